# Optimizing a Trainium2 kernel written in Bass

```python
import jax, jax.numpy as jnp
from jax import lax
import numpy as np

D_MODEL = 1024
BATCH = 2
SEQ = 8192
DEPTH = 2

N_EVEN = (DEPTH + 1) // 2
N_ODD = DEPTH // 2
BLOCK = 128
D_FF = 2816
EPS = 1e-6
SB_HEADS = 8
SB_HEAD_DIM = 64
SB_WIDTH = SB_HEADS * SB_HEAD_DIM
SG_GROUPS = 8
SG_GROUP_DIM = 64
SG_WIDTH = SG_GROUPS * SG_GROUP_DIM
SG_CHUNK = 128
EVEN_IN = 3 * SB_WIDTH + 2 * SG_WIDTH
EVEN_MIX = SB_WIDTH + SG_WIDTH
MLA_HEADS = 16
MLA_NOPE = 64
MLA_ROPE = 32
MLA_QK = MLA_NOPE + MLA_ROPE
MLA_V = 64
MLA_Q_LORA = 512
MLA_KV_LORA = 256
MLA_IN = MLA_Q_LORA + MLA_KV_LORA + MLA_ROPE
MLA_MIX = MLA_HEADS * MLA_V
ROPE_THETA = 10000.0
MEM_TOKENS = 256
MEM_HEADS = 4
MEM_HEAD_DIM = D_MODEL // MEM_HEADS

kernel_name = 'hybrid_sb_gmlp_mla_macaron_trunk'


def _rmsnorm(x, g):
    xf = x.astype(jnp.float32)
    y = xf * lax.rsqrt(jnp.mean(xf * xf, axis=-1, keepdims=True) + EPS)
    return (y * g.astype(jnp.float32)).astype(x.dtype)


def _layernorm(x, g, b):
    xf = x.astype(jnp.float32)
    mu = jnp.mean(xf, axis=-1, keepdims=True)
    var = jnp.mean(jnp.square(xf - mu), axis=-1, keepdims=True)
    y = (xf - mu) * lax.rsqrt(var + EPS)
    return (y * g.astype(jnp.float32) + b.astype(jnp.float32)).astype(x.dtype)


def _swiglu(h, w_gu, w_down):
    gate, up = jnp.split(h @ w_gu, 2, axis=-1)
    return (jax.nn.silu(gate) * up) @ w_down


def _to_blocks(t):
    b, s, h, d = t.shape
    return t.reshape(b, s // BLOCK, BLOCK, h, d).transpose(1, 0, 2, 3, 4)


def _from_blocks(t):
    nb, b, l, h, d = t.shape
    return t.transpose(1, 0, 2, 3, 4).reshape(b, nb * l, h, d)


def _rope(t, positions):
    half = t.shape[-1] // 2
    inv_freq = ROPE_THETA ** (-jnp.arange(half, dtype=jnp.float32) / half)
    ang = positions.astype(jnp.float32)[:, :, None, None] * inv_freq
    cos, sin = jnp.cos(ang), jnp.sin(ang)
    tf = t.astype(jnp.float32)
    t1, t2 = tf[..., :half], tf[..., half:]
    return jnp.concatenate([t1 * cos - t2 * sin, t1 * sin + t2 * cos], axis=-1).astype(t.dtype)


def _stick_breaking_attention(q, k, v):
    s_len = q.shape[1]
    scale = SB_HEAD_DIM ** -0.5
    k_pos = jnp.arange(s_len)

    def block(args):
        q_blk, i = args
        z = jnp.einsum('bqhd,bkhd->bhqk', q_blk, k).astype(jnp.float32) * scale
        q_pos = i * BLOCK + jnp.arange(BLOCK)
        strict = (k_pos[None, :] < q_pos[:, None])[None, None]
        log_stay = jnp.where(strict, jax.nn.log_sigmoid(-z), 0.0)
        log_rest = lax.cumsum(log_stay, axis=3, reverse=True) - log_stay
        w = jnp.where(strict, jnp.exp(jax.nn.log_sigmoid(z) + log_rest), 0.0)
        return jnp.einsum('bhqk,bkhd->bqhd', w.astype(v.dtype), v)

    out = lax.map(block, (_to_blocks(q), jnp.arange(s_len // BLOCK)))
    return _from_blocks(out)


def _causal_softmax_attention(q, k, v):
    s_len = q.shape[1]
    scale = q.shape[-1] ** -0.5
    k_pos = jnp.arange(s_len)

    def block(args):
        q_blk, i = args
        sc = jnp.einsum('bqhd,bkhd->bhqk', q_blk, k).astype(jnp.float32) * scale
        q_pos = i * BLOCK + jnp.arange(BLOCK)
        causal = (k_pos[None, :] <= q_pos[:, None])[None, None]
        p = jax.nn.softmax(jnp.where(causal, sc, -jnp.inf), axis=-1)
        return jnp.einsum('bhqk,bkhd->bqhd', p.astype(v.dtype), v)

    out = lax.map(block, (_to_blocks(q), jnp.arange(s_len // BLOCK)))
    return _from_blocks(out)


def _even_mixer(h, w_in, ln_g, ln_b, sgu_w, sgu_b, w_out):
    b, s, _ = h.shape
    q, k, v, z = jnp.split(h @ w_in, [SB_WIDTH, 2 * SB_WIDTH, 3 * SB_WIDTH], axis=-1)
    heads = lambda t: t.reshape(b, s, SB_HEADS, SB_HEAD_DIM)
    o_sb = _stick_breaking_attention(heads(q), heads(k), heads(v)).reshape(b, s, SB_WIDTH)
    u, g = jnp.split(jax.nn.gelu(z), 2, axis=-1)
    g = _layernorm(g, ln_g, ln_b).reshape(b, s // SG_CHUNK, SG_CHUNK, SG_GROUPS, SG_GROUP_DIM)
    tri = jnp.tril(jnp.ones((SG_CHUNK, SG_CHUNK), dtype=sgu_w.dtype))
    mixed = jnp.einsum('gts,bcsgd->bctgd', sgu_w * tri, g) + sgu_b.T[None, None, :, :, None]
    o_sg = u * mixed.reshape(b, s, SG_WIDTH)
    return jnp.concatenate([o_sb, o_sg], axis=-1) @ w_out


def _mla_mixer(h, positions, w_in, q_lora_g, kv_lora_g, w_uq, w_ukv, q_g, k_g, w_out):
    b, s, _ = h.shape
    c_q, c_kv, k_r = jnp.split(h @ w_in, [MLA_Q_LORA, MLA_Q_LORA + MLA_KV_LORA], axis=-1)
    q = (_rmsnorm(c_q, q_lora_g) @ w_uq).reshape(b, s, MLA_HEADS, MLA_QK)
    kv = (_rmsnorm(c_kv, kv_lora_g) @ w_ukv).reshape(b, s, MLA_HEADS, MLA_NOPE + MLA_V)
    k_nope, v = kv[..., :MLA_NOPE], kv[..., MLA_NOPE:]
    k_r = jnp.broadcast_to(k_r[:, :, None, :], (b, s, MLA_HEADS, MLA_ROPE))
    k = jnp.concatenate([k_nope, k_r], axis=-1)
    q = _rmsnorm(q, q_g)
    k = _rmsnorm(k, k_g)
    q = jnp.concatenate([q[..., :MLA_NOPE], _rope(q[..., MLA_NOPE:], positions)], axis=-1)
    k = jnp.concatenate([k[..., :MLA_NOPE], _rope(k[..., MLA_NOPE:], positions)], axis=-1)
    o = _causal_softmax_attention(q, k, v).reshape(b, s, MLA_MIX)
    return o @ w_out


def _memory_cross_attention(hq, hm, wq, wkv, q_g, k_g, wo):
    b, s, _ = hq.shape
    m = hm.shape[1]
    q = _rmsnorm((hq @ wq).reshape(b, s, MEM_HEADS, MEM_HEAD_DIM), q_g)
    k, v = jnp.split((hm @ wkv).reshape(b, m, MEM_HEADS, 2 * MEM_HEAD_DIM), 2, axis=-1)
    k = _rmsnorm(k, k_g)
    sc = jnp.einsum('bqhd,bmhd->bhqm', q, k).astype(jnp.float32) * (MEM_HEAD_DIM ** -0.5)
    p = jax.nn.softmax(sc, axis=-1)
    o = jnp.einsum('bhqm,bmhd->bqhd', p.astype(v.dtype), v).reshape(b, s, D_MODEL)
    return o @ wo


def _w(k, shape, fan_in):
    return jax.random.normal(k, shape, jnp.float32) * (fan_in ** -0.5)


def _gain(k, shape):
    return 1.0 + 0.02 * jax.random.normal(k, shape, jnp.float32)


def setup_inputs(seed: int = 0) -> dict:
    key = jax.random.key(seed)
    ks = list(jax.random.split(key, 32))
    nk = ks.pop
    inp = {}
    inp['x'] = jax.random.normal(nk(), (BATCH, SEQ, D_MODEL), jnp.float32)
    inp['mem'] = jax.random.normal(nk(), (BATCH, MEM_TOKENS, D_MODEL), jnp.float32)
    inp['positions'] = jnp.broadcast_to(jnp.arange(SEQ, dtype=jnp.int32)[None, :], (BATCH, SEQ))
    inp['ffn_pre_norm'] = _gain(nk(), (DEPTH, D_MODEL))
    inp['ffn_pre_w_gu'] = _w(nk(), (DEPTH, D_MODEL, 2 * D_FF), D_MODEL)
    inp['ffn_pre_w_down'] = _w(nk(), (DEPTH, D_FF, D_MODEL), D_FF)
    inp['mix_norm'] = _gain(nk(), (DEPTH, D_MODEL))
    inp['sbg_w_in'] = _w(nk(), (N_EVEN, D_MODEL, EVEN_IN), D_MODEL)
    inp['sgu_ln_gain'] = _gain(nk(), (N_EVEN, SG_WIDTH))
    inp['sgu_ln_bias'] = 0.02 * jax.random.normal(nk(), (N_EVEN, SG_WIDTH), jnp.float32)
    inp['sgu_w'] = _w(nk(), (N_EVEN, SG_GROUPS, SG_CHUNK, SG_CHUNK), SG_CHUNK)
    inp['sgu_b'] = 1.0 + 0.1 * jax.random.normal(nk(), (N_EVEN, SG_GROUPS, SG_CHUNK), jnp.float32)
    inp['sbg_w_out'] = _w(nk(), (N_EVEN, EVEN_MIX, D_MODEL), EVEN_MIX)
    inp['mla_w_in'] = _w(nk(), (N_ODD, D_MODEL, MLA_IN), D_MODEL)
    inp['mla_q_lora_gain'] = _gain(nk(), (N_ODD, MLA_Q_LORA))
    inp['mla_kv_lora_gain'] = _gain(nk(), (N_ODD, MLA_KV_LORA))
    inp['mla_w_uq'] = _w(nk(), (N_ODD, MLA_Q_LORA, MLA_HEADS * MLA_QK), MLA_Q_LORA)
    inp['mla_w_ukv'] = _w(nk(), (N_ODD, MLA_KV_LORA, MLA_HEADS * (MLA_NOPE + MLA_V)), MLA_KV_LORA)
    inp['mla_q_gain'] = _gain(nk(), (N_ODD, MLA_QK))
    inp['mla_k_gain'] = _gain(nk(), (N_ODD, MLA_QK))
    inp['mla_w_out'] = _w(nk(), (N_ODD, MLA_MIX, D_MODEL), MLA_MIX)
    inp['xmem_norm'] = _gain(nk(), (DEPTH, D_MODEL))
    inp['xmem_mem_norm'] = _gain(nk(), (DEPTH, D_MODEL))
    inp['xmem_wq'] = _w(nk(), (DEPTH, D_MODEL, D_MODEL), D_MODEL)
    inp['xmem_wkv'] = _w(nk(), (DEPTH, D_MODEL, 2 * D_MODEL), D_MODEL)
    inp['xmem_q_gain'] = _gain(nk(), (DEPTH, MEM_HEAD_DIM))
    inp['xmem_k_gain'] = _gain(nk(), (DEPTH, MEM_HEAD_DIM))
    inp['xmem_wo'] = _w(nk(), (DEPTH, D_MODEL, D_MODEL), D_MODEL)
    inp['ffn_post_norm'] = _gain(nk(), (DEPTH, D_MODEL))
    inp['ffn_post_w_gu'] = _w(nk(), (DEPTH, D_MODEL, 2 * D_FF), D_MODEL)
    inp['ffn_post_w_down'] = _w(nk(), (DEPTH, D_FF, D_MODEL), D_FF)
    return inp


def reference(x, mem, positions,
              ffn_pre_norm, ffn_pre_w_gu, ffn_pre_w_down,
              mix_norm,
              sbg_w_in, sgu_ln_gain, sgu_ln_bias, sgu_w, sgu_b, sbg_w_out,
              mla_w_in, mla_q_lora_gain, mla_kv_lora_gain, mla_w_uq, mla_w_ukv,
              mla_q_gain, mla_k_gain, mla_w_out,
              xmem_norm, xmem_mem_norm, xmem_wq, xmem_wkv, xmem_q_gain, xmem_k_gain, xmem_wo,
              ffn_post_norm, ffn_post_w_gu, ffn_post_w_down):
    for layer in range(DEPTH):
        x = x + 0.5 * _swiglu(_rmsnorm(x, ffn_pre_norm[layer]),
                              ffn_pre_w_gu[layer], ffn_pre_w_down[layer])
        h = _rmsnorm(x, mix_norm[layer])
        if layer % 2 == 0:
            e = layer // 2
            x = x + _even_mixer(h, sbg_w_in[e], sgu_ln_gain[e], sgu_ln_bias[e],
                                sgu_w[e], sgu_b[e], sbg_w_out[e])
        else:
            o = layer // 2
            x = x + _mla_mixer(h, positions, mla_w_in[o], mla_q_lora_gain[o],
                               mla_kv_lora_gain[o], mla_w_uq[o], mla_w_ukv[o],
                               mla_q_gain[o], mla_k_gain[o], mla_w_out[o])
        x = x + _memory_cross_attention(_rmsnorm(x, xmem_norm[layer]),
                                        _rmsnorm(mem, xmem_mem_norm[layer]),
                                        xmem_wq[layer], xmem_wkv[layer],
                                        xmem_q_gain[layer], xmem_k_gain[layer], xmem_wo[layer])
        x = x + 0.5 * _swiglu(_rmsnorm(x, ffn_post_norm[layer]),
                              ffn_post_w_gu[layer], ffn_post_w_down[layer])
    return x
```

```python
import types
import numpy as np
import ml_dtypes
import concourse.bass as bass
import concourse.mybir as mybir
from concourse.bass_utils import run_bass_kernel_spmd

F32 = mybir.dt.float32
BF16 = mybir.dt.bfloat16
I32 = mybir.dt.int32
AF = mybir.ActivationFunctionType
ALU = mybir.AluOpType

NCORES = 8
D = 1024
DFF = 2816
T = 2048
NT = 16
EPS = 1e-6
NEG = -30000.0


class Buf:
    __slots__ = ("name", "lw", "rd", "dkey", "dcnt")

    def __init__(self, name):
        self.name = name
        self.lw = None
        self.rd = {}
        self.dkey = None
        self.dcnt = 0


ENGS = ("pe", "act", "dve", "pool", "sp")


class Sched:
    def __init__(self, nc):
        self.nc = nc
        self.prog = {e: [] for e in ENGS}
        self.cnt = {e: 0 for e in ENGS}
        self.waited = {e: {} for e in ENGS}
        self.sems = {}
        self.alltoks = {}
        self.dfree = []
        self.dcum = {}
        self.dheld = []

    def sem(self, key):
        if key not in self.sems:
            self.sems[key] = self.nc.alloc_semaphore(name="s_%s" % (str(key).replace(" ", "")))
        return self.sems[key]

    def _wait(self, eng, tok):
        key, val = tok
        if key == eng and eng in ("pe", "sp"):
            return
        if self.waited[eng].get(key, 0) >= val:
            return
        self.waited[eng][key] = val
        h = self.sem(key)
        self.prog[eng].append(lambda E, h=h, val=val: E.wait_ge(h, val))

    def _deps(self, eng, reads, writes):
        for b in reads:
            if b.lw is not None:
                self._wait(eng, b.lw)
        for b in writes:
            if b.lw is not None and b.lw[0] != eng:
                self._wait(eng, b.lw)
            for k, v in b.rd.items():
                if k != eng:
                    self._wait(eng, (k, v))

    def _upd(self, tok, reads, writes):
        self.alltoks[tok[0]] = max(self.alltoks.get(tok[0], 0), tok[1])
        for b in reads:
            if b.rd.get(tok[0], 0) < tok[1]:
                b.rd[tok[0]] = tok[1]
        for b in writes:
            b.lw = tok
            b.rd = {}

    @staticmethod
    def _freeze(fn):
        if fn.__closure__:
            cells = tuple(types.CellType(c.cell_contents) for c in fn.__closure__)
            g = types.FunctionType(fn.__code__, fn.__globals__, fn.__name__, fn.__defaults__, cells)
            g.__kwdefaults__ = fn.__kwdefaults__
            return g
        return fn

    def op(self, eng, fn, reads=(), writes=()):
        fn = self._freeze(fn)
        self._deps(eng, reads, writes)
        self.cnt[eng] += 1
        tok = (eng, self.cnt[eng])
        h = self.sem(eng)
        self.prog[eng].append(lambda E, fn=fn, h=h: fn(E).then_inc(h, 1))
        self._upd(tok, reads, writes)
        return tok

    def dma(self, q, out_ap, in_ap, owner, reads=(), writes=()):
        self._deps(q, reads, writes)
        if owner.dkey is None:
            if self.dfree:
                owner.dkey = self.dfree.pop()
            else:
                owner.dkey = "q%d" % len(self.dcum)
                self.dcum[owner.dkey] = 0
            self.dheld.append(owner)
        self.dcum[owner.dkey] += 16
        tok = (owner.dkey, self.dcum[owner.dkey])
        h = self.sem(owner.dkey)
        self.prog[q].append(lambda E, o=out_ap, i=in_ap, h=h: E.dma_start(out=o, in_=i).then_inc(h, 16))
        self._upd(tok, reads, writes)
        return tok

    def barrier(self, release=True):
        for e in ENGS:
            for k, v in list(self.alltoks.items()):
                self._wait(e, (k, v))
        if release:
            for b in self.dheld:
                self.dfree.append(b.dkey)
                b.dkey = None
            self.dheld = []

    def run(self):
        nc = self.nc
        with nc.Block() as block:
            @block.tensor
            def _(E):
                for f in self.prog["pe"]:
                    f(E)

            @block.scalar
            def _(E):
                for f in self.prog["act"]:
                    f(E)

            @block.vector
            def _(E):
                for f in self.prog["dve"]:
                    f(E)

            @block.gpsimd
            def _(E):
                for f in self.prog["pool"]:
                    f(E)

            @block.sync
            def _(E):
                for f in self.prog["sp"]:
                    f(E)


class Rot:
    def __init__(self, slots):
        self.slots = slots
        self.i = 0

    def next(self):
        s = self.slots[self.i % len(self.slots)]
        self.i += 1
        return s


class Ctx:
    pass


def mk_tile(nc, name, shape, dt):
    return nc.alloc_sbuf_tensor(name, shape, dt)


import os
DBG = os.environ.get('KDBG', '')
RC_K0, RC_V0, RC_K1, RC_V1 = 256, 1024, 192, 512
GC = dict(ffn_pre=0, ffn_post=8, mix=32, xn=48, xmn=64, xqg=80, xkg=84, qlora=88, kvlora=92,
          mqg=94, mkg=95, mqg_sw=96, mkg_sw=97, invf=98, sgn=99, eps=100, halfpi=101, one=102, zero=103)
NGC = 104
TWO_PI = float(2 * np.pi)
PI = float(np.pi)


PHASES = ["F0pre", "A0", "B0", "X0", "F0post", "F1pre", "A1", "B1", "X1", "F1post"]


def build_program(lo=0, hi=10, exch="cc"):
    nc = bass.Bass("TRN2", target_bir_lowering=False)
    S = Sched(nc)
    outputs = ["x_out"]

    used_inputs = set()

    def din(name, shape, dt=F32):
        used_inputs.add(name)
        return nc.dram_tensor(name, list(shape), dt, kind="ExternalInput").ap()

    x_in = din("x_in", [T, D])
    x_out = nc.dram_tensor("x_out", [T, D], F32, kind="ExternalOutput").ap()
    consts_bf = din("consts_bf", [128, 4 * 128], BF16)
    ident_f_d = din("ident_f", [128, 128])
    gcols_d = din("gcols", [128, NGC])
    bc0_d = din("bc0", [128, 1536])
    trim_d = din("trimask", [128, 128])
    sguwT_d = din("sgu_wT", [128, 8 * 128])
    masks_d = din("masks", [128, 16 * 128], BF16)
    posrep_d = din("posrep", [32, T], I32)
    mem_d = din("mem", [256, D])
    WSHAPES = {"sbg_in": [D, 2560], "sbg_out": [D, D], "mla_in": [D, 800], "mla_uq": [512, 1536], "mla_uq_sw": [512, 512],
               "mla_uk": [256, 1024], "mla_uv": [256, 1024], "mla_out": [D, D]}

    class LazyW(dict):
        def __missing__(self, key):
            if isinstance(key, tuple) and key[0] in ("gu", "d"):
                name = "w_%s_%s_%d" % (key[1], key[0], key[2])
                shape = [D, 2 * DFF] if key[0] == "gu" else [DFF, D]
            elif isinstance(key, tuple):
                name = "w_%s_%d" % key
                shape = [D, D]
            else:
                name = "w_" + key
                shape = WSHAPES[key]
            self[key] = din(name, shape)
            return self[key]

    Wd = LazyW()
    def scratch(name, shape, prod, cons):
        if exch == "cc" or (lo <= prod < hi and lo <= cons < hi):
            return nc.dram_tensor(name, shape, BF16).ap()
        if lo <= prod < hi:
            outputs.append(name)
            return nc.dram_tensor(name, shape, BF16, kind="ExternalOutput").ap()
        if lo <= cons < hi:
            return din(name, shape, BF16)
        return None

    gp = 1 if exch == "cc" else -1
    q0_d = scratch("q0_d", [512, T], 1, 2)
    k0_d = scratch("k0_d", [512, T], 1, 2 if exch == "cc" else 99)
    v0_d = scratch("v0_d", [T, 512], 1, 2 if exch == "cc" else 99)
    osg_d = scratch("osg_d", [512, T], 1, 2)
    kg0_d = scratch("kg0_d", [4 * 512, T], gp, 2)
    vg0_d = scratch("vg0_d", [4 * T, 512], gp, 2)
    gp = 6 if exch == "cc" else -1
    q1_d = scratch("q1_d", [1536, T], 6, 7)
    k1_d = scratch("k1_d", [1536, T], 6, 7 if exch == "cc" else 99)
    v1_d = scratch("v1_d", [T, 1024], 6, 7 if exch == "cc" else 99)
    kg1_d = scratch("kg1_d", [4 * 1536, T], gp, 7)
    vg1_d = scratch("vg1_d", [4 * T, 1024], gp, 7)
    db = {n: Buf(n) for n in ("q0", "k0", "v0", "osg", "kg0", "vg0", "q1", "k1", "v1", "kg1", "vg1")}

    xT = nc.alloc_sbuf_tensor("xT", [128, 8, T], F32)
    xTb = [[Buf("xT_%d_%d" % (c, g)) for g in range(4)] for c in range(8)]
    cbf = nc.alloc_sbuf_tensor("cbf", [128, 4 * 128], BF16)
    cbf_b = Buf("cbf")
    ident_f = nc.alloc_sbuf_tensor("sb_ident_f", [128, 128], F32)
    ident_f_b = Buf("ident_f")
    gcols = nc.alloc_sbuf_tensor("sb_gcols", [128, NGC], F32)
    gcols_b = Buf("gcols")
    ident_bf = cbf[:, 0:128]
    ones_bf = cbf[:, 128:256]
    negU_bf = cbf[:, 256:384]
    swap_bf = cbf[:, 384:512]

    def gcol(name, i=0, p0=0, p1=128):
        return gcols[p0:p1, GC[name] + i:GC[name] + i + 1]

    S.dma("sp", cbf[:, :], consts_bf[:, :], cbf_b, writes=[cbf_b])
    S.dma("sp", ident_f[:, :], ident_f_d[:, :], ident_f_b, writes=[ident_f_b])
    S.dma("sp", gcols[:, :], gcols_d[:, :], gcols_b, writes=[gcols_b])

    posi = nc.alloc_sbuf_tensor("posi", [128, 512], I32)
    posi_b = Buf("posi")
    nint = nc.alloc_sbuf_tensor("nint", [128, 512], I32)
    nint_b = Buf("nint")
    AF32 = 11776
    ABF = 45056
    arena_f = nc.alloc_sbuf_tensor("arena_f", [128, AF32], F32)
    arena_h = nc.alloc_sbuf_tensor("arena_h", [128, ABF], BF16)
    psum = [nc.alloc_psum_tensor("ps%d" % i, [128, 512], F32) for i in range(8)]

    class Carver:
        def __init__(self, ar, size):
            self.ar, self.size, self.off = ar, size, 0

        def reset(self):
            self.off = 0

        def take(self, n):
            assert self.off + n <= self.size, ("arena overflow", self.off, n, self.size)
            a = self.ar[:, self.off:self.off + n]
            self.off += n
            return a

    cf = Carver(arena_f, AF32)
    ch = Carver(arena_h, ABF)
    nbuf = [0]

    def B(name="b"):
        nbuf[0] += 1
        return Buf("%s%d" % (name, nbuf[0]))

    def rotf(n, size, view=None):
        return Rot([((cf.take(size) if view is None else view(cf.take(size))), B()) for _ in range(n)])

    def roth(n, size, view=None):
        return Rot([((ch.take(size) if view is None else view(ch.take(size))), B()) for _ in range(n)])

    def psrot(idxs):
        return Rot([(psum[i], B("ps")) for i in idxs])

    W = Ctx()

    def wsetup(nst=2, stsize=2048):
        W.st = rotf(nst, stsize)

    castctr = [0]

    def cast(dst, dst_b, src, src_b):
        castctr[0] += 1
        if castctr[0] % 2 == 0:
            S.op("act", lambda E: E.copy(dst, src), reads=[src_b], writes=[dst_b])
        else:
            S.op("dve", lambda E: E.tensor_copy(dst, src), reads=[src_b], writes=[dst_b])

    def wload(view, kc, n, dst, dst_b, eng=None):
        assert kc * n <= 2048
        st, st_b = W.st.next()
        sv = st[:, 0:kc * n].rearrange("p (c n) -> p c n", c=kc)
        S.dma("sp", sv, view, st_b, writes=[st_b])
        cast(dst, dst_b, sv, st_b)

    def rstd_from_ss(out_ap, out_b, ps_ap, ps_b, n, tmp_ap, tmp_b, p0=0, p1=128):
        S.op("act", lambda E: E.activation(tmp_ap, ps_ap, AF.Ln, bias=gcol("eps", 0, p0, p1), scale=1.0 / n),
             reads=[ps_b, gcols_b], writes=[tmp_b])
        S.op("act", lambda E: E.activation(out_ap, tmp_ap, AF.Exp, scale=-0.5), reads=[tmp_b], writes=[out_b])

    def load_x():
        cf.reset()
        rot = rotf(2, 1024)
        psb = [B("ps"), B("ps")]
        for t in range(NT):
            ap, b = rot.next()
            S.dma("sp", ap, x_in[t * 128:(t + 1) * 128, :], b, writes=[b])
            for half in range(2):
                pb, ps = psb[half], psum[half]
                for cc in range(4):
                    c = half * 4 + cc
                    S.op("pe", lambda E, o=ps[:, cc * 128:(cc + 1) * 128], i=ap[:, c * 128:(c + 1) * 128]:
                         E.transpose(o, i, ident_f[:, :]), reads=[b, ident_f_b], writes=[pb])
                g = t // 4
                outap = xT[:, half * 4:half * 4 + 4, t * 128:(t + 1) * 128]
                inap = ps[:, :].rearrange("p (c n) -> p c n", c=4)
                wr = [xTb[half * 4 + cc][g] for cc in range(4)]
                if half == 0:
                    S.op("act", lambda E, o=outap, i=inap: E.copy(o, i), reads=[pb], writes=wr)
                else:
                    S.op("dve", lambda E, o=outap, i=inap: E.tensor_copy(o, i), reads=[pb], writes=wr)
        S.barrier()

    def store_x():
        cf.reset()
        rot = rotf(2, 1024)
        psb = [B("ps"), B("ps")]
        toks = []
        for t in range(NT):
            ap, b = rot.next()
            g = t // 4
            for half in range(2):
                pb, ps = psb[half], psum[half]
                for cc in range(4):
                    c = half * 4 + cc
                    S.op("pe", lambda E, o=ps[:, cc * 128:(cc + 1) * 128], i=xT[:, c, t * 128:(t + 1) * 128]:
                         E.transpose(o, i, ident_f[:, :]), reads=[xTb[c][g], ident_f_b], writes=[pb])
                if half == 0:
                    S.op("act", lambda E, o=ap[:, 0:512], i=ps[:, :]: E.copy(o, i), reads=[pb], writes=[b])
                else:
                    S.op("dve", lambda E, o=ap[:, 512:1024], i=ps[:, :]: E.tensor_copy(o, i), reads=[pb], writes=[b])
            toks.append(S.dma("sp", x_out[t * 128:(t + 1) * 128, :], ap, b, reads=[b]))
        for tk in toks:
            S._wait("sp", tk)
        S.barrier()

    def norm_prep(gname, gi0, grp, xb_ap, xb_b, rstd_ap, rstd_b, sq_ap, sq_b, ps_i, ps_b, tmp_ap, tmp_b, tokps=None):
        tsl = slice(grp * 512, (grp + 1) * 512)
        xbufs = [xTb[c][grp] for c in range(8)]
        S.op("act", lambda E: E.activation(sq_ap, xT[:, :, tsl], AF.Square), reads=xbufs, writes=[sq_b])
        ps = psum[ps_i]
        for c in range(8):
            S.op("pe", lambda E, c=c: E.matmul(ps[:, :], ones_bf, sq_ap[:, c, :], start=(c == 0), stop=(c == 7)),
                 reads=[sq_b, cbf_b], writes=[ps_b])
        rstd_from_ss(rstd_ap, rstd_b, ps[:, :], ps_b, D, tmp_ap, tmp_b)
        if tokps is not None:
            tp_ap, tp_b = tokps
            for tt in range(4):
                for c in range(8):
                    S.op("pe", lambda E, c=c, tt=tt: E.matmul(tp_ap[:, grp * 4 + tt:grp * 4 + tt + 1],
                                                              sq_ap[:, c, tt * 128:(tt + 1) * 128], ones_bf[:, 0:1],
                                                              start=(c == 0), stop=(c == 7)),
                         reads=[sq_b, cbf_b], writes=[tp_b])
        for c in range(8):
            S.op("dve", lambda E, c=c: E.tensor_scalar(xb_ap[:, c, :], xT[:, c, tsl], gcol(gname, gi0 + c), None, ALU.mult),
                 reads=[xTb[c][grp], gcols_b], writes=[xb_b])

    def ffn(wgu_d, wd_d, gname, gi0):
        for half in range(2):
            cf.reset()
            ch.reset()
            rstd = [cf.take(512) for _ in range(2)]
            rstd_b = [B() for _ in range(2)]
            tmp, tmp_b = cf.take(512), B()
            stg_gu = rotf(2, 2048, lambda a: a.rearrange("p (a c m) -> p a c m", a=2, c=8))
            stg_d = rotf(2, 1408, lambda a: a.rearrange("p (k m) -> p k m", k=11))
            a_rot, s_rot, t_rot = rotf(2, 512), rotf(2, 512), rotf(2, 512)
            xb = [ch.take(4096).rearrange("p (c n) -> p c n", c=8) for _ in range(2)]
            xb_b = [B() for _ in range(2)]
            hid = ch.take(22 * 1024).rearrange("p (k n) -> p k n", k=22)
            hid_b = [[B() for g in range(2)] for k in range(22)]
            wgu = roth(2, 2048, lambda a: a.rearrange("p (a c m) -> p a c m", a=2, c=8))
            wd = roth(2, 2816, lambda a: a.rearrange("p (k m) -> p k m", k=22))
            sq, sq_b = ch.take(4096).rearrange("p (c n) -> p c n", c=8), B()
            ps_ss_b = B("ps")
            ps_g, ps_u, ps_a = psrot([1, 2]), psrot([3, 4]), psrot([5, 6])
            for gi in range(2):
                norm_prep(gname, gi0, half * 2 + gi, xb[gi], xb_b[gi], rstd[gi], rstd_b[gi], sq, sq_b, 0, ps_ss_b, tmp, tmp_b)
            wgu_v = wgu_d.rearrange("(c p) n -> p c n", p=128)
            for j in range(22):
                st, st_b = stg_gu.next()
                S.dma("sp", st[:, 0], wgu_v[:, :, j * 128:(j + 1) * 128], st_b, writes=[st_b])
                S.dma("sp", st[:, 1], wgu_v[:, :, DFF + j * 128:DFF + (j + 1) * 128], st_b, writes=[st_b])
                wt, wt_b = wgu.next()
                cast(wt, wt_b, st, st_b)
                for gi in range(2):
                    pg, pg_b = ps_g.next()
                    pu, pu_b = ps_u.next()
                    for c in range(8):
                        S.op("pe", lambda E, c=c, pg=pg, wt=wt, gi=gi: E.matmul(pg[:, :], wt[:, 0, c, :], xb[gi][:, c, :],
                                                                                 start=(c == 0), stop=(c == 7)),
                             reads=[wt_b, xb_b[gi]], writes=[pg_b])
                    for c in range(8):
                        S.op("pe", lambda E, c=c, pu=pu, wt=wt, gi=gi: E.matmul(pu[:, :], wt[:, 1, c, :], xb[gi][:, c, :],
                                                                                 start=(c == 0), stop=(c == 7)),
                             reads=[wt_b, xb_b[gi]], writes=[pu_b])
                    a, a_b = a_rot.next()
                    s, s_b = s_rot.next()
                    t, t_b = t_rot.next()
                    S.op("dve", lambda E, a=a, pg=pg, gi=gi: E.tensor_tensor(a, pg[:, :], rstd[gi], ALU.mult),
                         reads=[pg_b, rstd_b[gi]], writes=[a_b])
                    S.op("act", lambda E, s=s, a=a: E.activation(s, a, AF.Silu), reads=[a_b], writes=[s_b])
                    S.op("dve", lambda E, t=t, pu=pu, gi=gi: E.tensor_tensor(t, pu[:, :], rstd[gi], ALU.mult),
                         reads=[pu_b, rstd_b[gi]], writes=[t_b])
                    S.op("pool", lambda E, s=s, t=t, j=j, gi=gi: E.tensor_tensor(hid[:, j, gi * 512:(gi + 1) * 512], s, t, ALU.mult),
                         reads=[s_b, t_b], writes=[hid_b[j][gi]])
            wd_v = wd_d.rearrange("(k p) n -> p k n", p=128)
            for i in range(8):
                wt, wt_b = wd.next()
                for hh in range(2):
                    st, st_b = stg_d.next()
                    S.dma("sp", st, wd_v[:, hh * 11:(hh + 1) * 11, i * 128:(i + 1) * 128], st_b, writes=[st_b])
                    cast(wt[:, hh * 11:(hh + 1) * 11, :], wt_b, st, st_b)
                for gi in range(2):
                    pa, pa_b = ps_a.next()
                    for k in range(22):
                        S.op("pe", lambda E, k=k, pa=pa, wt=wt, gi=gi: E.matmul(pa[:, :], wt[:, k, :], hid[:, k, gi * 512:(gi + 1) * 512],
                                                                                 start=(k == 0), stop=(k == 21)),
                             reads=[wt_b, hid_b[k][gi]], writes=[pa_b])
                    grp = half * 2 + gi
                    xs = xT[:, i, grp * 512:(grp + 1) * 512]
                    S.op("dve", lambda E, xs=xs, pa=pa: E.scalar_tensor_tensor(xs, pa[:, :], 0.5, xs, ALU.mult, ALU.add),
                         reads=[pa_b, xTb[i][grp]], writes=[xTb[i][grp]])
            S.barrier()

    def out_proj(mix, mix_b, w_dram, wfm, ps_o):
        wv = w_dram.rearrange("(c p) n -> p c n", p=128)
        for i in range(8):
            wt, wt_b = wfm.next()
            wload(wv[:, :, i * 128:(i + 1) * 128], 8, 128, wt, wt_b)
            for g in range(4):
                po, po_b = ps_o.next()
                for c in range(8):
                    mb = mix_b[c][g] if isinstance(mix_b[c], list) else mix_b[c]
                    S.op("pe", lambda E, c=c, po=po, wt=wt, g=g: E.matmul(po[:, :], wt[:, c, :], mix[c][:, g * 512:(g + 1) * 512],
                                                                           start=(c == 0), stop=(c == 7)),
                         reads=[wt_b, mb], writes=[po_b])
                xs = xT[:, i, g * 512:(g + 1) * 512]
                S.op("dve", lambda E, xs=xs, po=po: E.tensor_tensor(xs, po[:, :], xs, ALU.add),
                     reads=[po_b, xTb[i][g]], writes=[xTb[i][g]])

    def gelu(x_ap, x_b, out_ap, out_b, ga, gb):
        (a, a_b), (b, b_b) = ga, gb
        S.op("act", lambda E: E.activation(a, x_ap, AF.Square), reads=[x_b], writes=[a_b])
        S.op("dve", lambda E: E.tensor_scalar(a, a, 0.044715, 1.0, ALU.mult, ALU.add), reads=[a_b], writes=[a_b])
        S.op("dve", lambda E: E.tensor_tensor(a, a, x_ap, ALU.mult), reads=[a_b, x_b], writes=[a_b])
        S.op("act", lambda E: E.activation(b, a, AF.Sigmoid, scale=1.5957691216057308), reads=[a_b], writes=[b_b])
        S.op("pool", lambda E: E.tensor_tensor(out_ap, x_ap, b, ALU.mult), reads=[x_b, b_b], writes=[out_b])

    def allgather(src_ap, src_b, dst_ap, dst_b, rows, rc):
        S._deps("pool", [src_b], [dst_b])
        h = S.sem("cc")
        S.cnt.setdefault("cc", 0)
        for i in range(rows // rc):
            S.cnt["cc"] += 1
            S.prog["pool"].append(lambda E, i=i: E.collective_compute("AllGather", ALU.bypass, replica_groups=[[0, 1, 2, 3], [4, 5, 6, 7]],
                                                                      ins=[src_ap[i * rc:(i + 1) * rc, :]],
                                                                      outs=[dst_ap[i * 4 * rc:(i + 1) * 4 * rc, :]]).then_inc(h, 1))
        S._upd(("cc", S.cnt["cc"]), [src_b], [dst_b])

    def grow(rc, rk, row):
        return (row // rc) * 4 * rc + rk * rc + (row % rc)

    def phase_A0():
        cf.reset()
        ch.reset()
        rstd = [cf.take(512) for _ in range(4)]
        rstd_b = [B() for _ in range(4)]
        tmp, tmp_b = cf.take(512), B()
        rtok, rtok_b = cf.take(16), B()
        rtmp, rtmp_b = cf.take(16), B()
        wsetup()
        zt = rotf(2, 512)
        ga, gb = (cf.take(512), B()), (cf.take(512), B())
        gl = (cf.take(512), B())
        bc, bc_b = cf.take(1536), B()
        trim, trim_b = cf.take(128), B()
        st6, st6_b = cf.take(8), B()
        mv, mv_b = cf.take(4), B()
        xb = ch.take(16384).rearrange("p (c n) -> p c n", c=8)
        xb_b = [B() for _ in range(4)]
        sq, sq_b = ch.take(4096).rearrange("p (c n) -> p c n", c=8), B()
        Wv, Wv_b = ch.take(4096).rearrange("p (c n) -> p c n", c=8), B()
        Wzg, Wzg_b = ch.take(4096).rearrange("p (c n) -> p c n", c=8), B()
        wfm = roth(2, 1024, lambda a: a.rearrange("p (c n) -> p c n", c=8))
        uT = ch.take(8192).rearrange("p (c n) -> p c n", c=4)
        uT_b = [[B() for g in range(4)] for c in range(4)]
        WsT, WsT_b = ch.take(1024).rearrange("p (g n) -> p g n", g=8), B()
        ev = roth(2, 512)
        vev = roth(2, 512)
        gtok = roth(2, 512)
        osg = roth(2, 512, lambda a: a.rearrange("p (c n) -> p c n", c=4))
        ps_ss_b = B("ps")
        tokps = (psum[7][:, 0:16], B("ps"))
        ps_p = psrot([1, 2, 3])
        ps_m = psrot([4, 5])
        S.dma("sp", bc, bc0_d[:, :], bc_b, writes=[bc_b])
        S.dma("sp", trim, trim_d[:, :], trim_b, writes=[trim_b])
        for hh in range(4):
            st, st_b = W.st.next()
            S.dma("sp", st[:, 0:256], sguwT_d[:, hh * 256:(hh + 1) * 256], st_b, writes=[st_b])
            for k in range(2):
                gidx = hh * 2 + k
                S.op("dve", lambda E, st=st, k=k, gidx=gidx: E.tensor_tensor(WsT[:, gidx, :], st[:, k * 128:(k + 1) * 128], trim, ALU.mult),
                     reads=[st_b, trim_b], writes=[WsT_b])
        for g in range(4):
            norm_prep("mix", 0, g, xb[:, :, g * 512:(g + 1) * 512], xb_b[g], rstd[g], rstd_b[g], sq, sq_b, 0, ps_ss_b, tmp, tmp_b, tokps=tokps)
        rstd_from_ss(rtok, rtok_b, tokps[0], tokps[1], D, rtmp, rtmp_b)
        win = Wd["sbg_in"].rearrange("(c p) n -> p c n", p=128)
        for hh in range(2):
            wload(win[:, hh * 4:(hh + 1) * 4, 1024:1536], 4, 512, Wv[:, hh * 4:(hh + 1) * 4, :], Wv_b)
            wload(win[:, hh * 4:(hh + 1) * 4, 2048:2560], 4, 512, Wzg[:, hh * 4:(hh + 1) * 4, :], Wzg_b)
        for oc in list(range(0, 8)) + list(range(12, 16)):
            wt, wt_b = wfm.next()
            wload(win[:, :, oc * 128:(oc + 1) * 128], 8, 128, wt, wt_b)
            for g in range(4):
                pp, pp_b = ps_p.next()
                for c in range(8):
                    S.op("pe", lambda E, c=c, pp=pp, wt=wt, g=g: E.matmul(pp[:, :], wt[:, c, :], xb[:, c, g * 512:(g + 1) * 512],
                                                                           start=(c == 0), stop=(c == 7)),
                         reads=[wt_b, xb_b[g]], writes=[pp_b])
                if oc < 8:
                    e, e_b = ev.next()
                    sc = 0.125 if oc < 4 else 1.0
                    S.op("dve", lambda E, e=e, pp=pp, g=g, sc=sc: E.scalar_tensor_tensor(e, pp[:, :], sc, rstd[g], ALU.mult, ALU.mult),
                         reads=[pp_b, rstd_b[g]], writes=[e_b])
                    dst, dn = (q0_d, "q0") if oc < 4 else (k0_d, "k0")
                    r0 = (oc % 4) * 128
                    S.dma("sp", dst[r0:r0 + 128, g * 512:(g + 1) * 512], e, e_b, reads=[e_b], writes=[db[dn]])
                else:
                    z, z_b = zt.next()
                    S.op("dve", lambda E, z=z, pp=pp, g=g: E.tensor_tensor(z, pp[:, :], rstd[g], ALU.mult),
                         reads=[pp_b, rstd_b[g]], writes=[z_b])
                    gelu(z, z_b, uT[:, oc - 12, g * 512:(g + 1) * 512], uT_b[oc - 12][g], ga, gb)
        for t in range(NT):
            g = t // 4
            pp, pp_b = ps_p.next()
            for c in range(8):
                S.op("pe", lambda E, c=c, pp=pp, t=t: E.matmul(pp[:, :], xb[:, c, t * 128:(t + 1) * 128], Wv[:, c, :],
                                                               start=(c == 0), stop=(c == 7)),
                     reads=[Wv_b, xb_b[g]], writes=[pp_b])
            e, e_b = vev.next()
            S.op("dve", lambda E, e=e, pp=pp, t=t: E.tensor_scalar(e, pp[:, :], rtok[:, t:t + 1], None, ALU.mult),
                 reads=[pp_b, rtok_b], writes=[e_b])
            S.dma("sp", v0_d[t * 128:(t + 1) * 128, :], e, e_b, reads=[e_b], writes=[db["v0"]])
            pp, pp_b = ps_p.next()
            for c in range(8):
                S.op("pe", lambda E, c=c, pp=pp, t=t: E.matmul(pp[:, :], xb[:, c, t * 128:(t + 1) * 128], Wzg[:, c, :],
                                                               start=(c == 0), stop=(c == 7)),
                     reads=[Wzg_b, xb_b[g]], writes=[pp_b])
            z, z_b = zt.next()
            S.op("dve", lambda E, z=z, pp=pp, t=t: E.tensor_scalar(z, pp[:, :], rtok[:, t:t + 1], None, ALU.mult),
                 reads=[pp_b, rtok_b], writes=[z_b])
            gelu(z, z_b, gl[0], gl[1], ga, gb)
            S.op("dve", lambda E: E.bn_stats(st6[:, 0:6], gl[0]), reads=[gl[1]], writes=[st6_b])
            S.op("dve", lambda E: E.bn_aggr(mv[:, 0:2], st6[:, 0:6]), reads=[st6_b], writes=[mv_b])
            S.op("act", lambda E: E.activation(mv[:, 2:3], mv[:, 1:2], AF.Ln, bias=gcol("eps")), reads=[mv_b, gcols_b], writes=[mv_b])
            S.op("act", lambda E: E.activation(mv[:, 3:4], mv[:, 2:3], AF.Exp, scale=-0.5), reads=[mv_b], writes=[mv_b])
            S.op("dve", lambda E: E.tensor_scalar(gl[0], gl[0], mv[:, 0:1], mv[:, 3:4], ALU.subtract, ALU.mult),
                 reads=[gl[1], mv_b], writes=[gl[1]])
            S.op("dve", lambda E: E.tensor_tensor(gl[0], gl[0], bc[:, 0:512], ALU.mult), reads=[gl[1], bc_b], writes=[gl[1]])
            gt, gt_b = gtok.next()
            S.op("dve", lambda E, gt=gt: E.tensor_tensor(gt, gl[0], bc[:, 512:1024], ALU.add), reads=[gl[1], bc_b], writes=[gt_b])
            pm, pm_b = ps_m.next()
            for grp in range(8):
                hb = 64 * (grp % 2)
                cc = grp // 2
                S.op("pe", lambda E, pm=pm, gt=gt, grp=grp, hb=hb, cc=cc: E.matmul(pm[hb:hb + 64, cc * 128:(cc + 1) * 128],
                                                                                  gt[:, grp * 64:(grp + 1) * 64], WsT[:, grp, :],
                                                                                  start=True, stop=True),
                     reads=[gt_b, WsT_b], writes=[pm_b])
            z2, z2_b = zt.next()
            S.op("dve", lambda E, z2=z2, pm=pm: E.tensor_tensor(z2, pm[:, :], bc[:, 1024:1536], ALU.add), reads=[pm_b, bc_b], writes=[z2_b])
            og, og_b = osg.next()
            S.op("pool", lambda E, og=og, z2=z2, t=t: E.tensor_tensor(og, z2.rearrange("p (c n) -> p c n", c=4),
                                                                     uT[:, :, t * 128:(t + 1) * 128], ALU.mult),
                 reads=[z2_b] + [uT_b[c][g] for c in range(4)], writes=[og_b])
            S.dma("sp", osg_d.rearrange("(c p) n -> p c n", p=128)[:, :, t * 128:(t + 1) * 128], og, og_b, reads=[og_b], writes=[db["osg"]])
        S.barrier()
        if exch == "cc":
            allgather(k0_d, db["k0"], kg0_d, db["kg0"], 512, RC_K0)
            allgather(v0_d, db["v0"], vg0_d, db["vg0"], T, RC_V0)
        S.barrier()

    def key_loc(j):
        tj, mj = j // 4, j % 4
        rk = mj if tj % 2 == 0 else 3 - mj
        return rk, tj, mj

    def phase_B0():
        cf.reset()
        ch.reset()
        Rs = [(cf.take(512), B()) for _ in range(2)]
        ttr = rotf(4, 512)
        e2r = rotf(4, 512)
        wsetup()
        Kr = roth(2, 8192)
        Vc, Vc_b = ch.take(8192).rearrange("p (r t d) -> p r t d", r=4, t=16), B()
        qr = roth(2, 2048)
        Lpr = roth(4, 512)
        wvr = roth(6, 512)
        osb = ch.take(8192).rearrange("p (c n) -> p c n", c=4)
        osb_b = [[B() for g in range(4)] for c in range(4)]
        msk, msk_b = ch.take(1024).rearrange("p (v n) -> p v n", v=8), B()
        wfm = roth(2, 1024, lambda a: a.rearrange("p (c n) -> p c n", c=8))
        ps_e, ps_t, ps_o = psrot([0, 1, 2, 3]), psrot([4, 5]), psrot([6, 7])
        S.dma("sp", msk, masks_d[:, 0:1024].rearrange("p (v n) -> p v n", v=8), msk_b, writes=[msk_b])
        for c in range(4):
            Kc, Kc_b = Kr.next()
            for rk in range(4):
                S.dma("sp", Kc[:, rk * T:(rk + 1) * T], kg0_d[grow(RC_K0, rk, c * 128):grow(RC_K0, rk, c * 128) + 128, :], Kc_b,
                      reads=[db["kg0"]], writes=[Kc_b])
                for tq in range(4):
                    S.dma("sp", Vc[:, rk, tq * 4:(tq + 1) * 4, :],
                          vg0_d[grow(RC_V0, rk, tq * 512):grow(RC_V0, rk, tq * 512) + 512, c * 128:(c + 1) * 128].rearrange("(t p) d -> p t d", p=128), Vc_b,
                          reads=[db["vg0"]], writes=[Vc_b])
            qc, qc_b = qr.next()
            S.dma("sp", qc, q0_d[c * 128:(c + 1) * 128, :], qc_b, reads=[db["q0"]], writes=[qc_b])
            for u in range(4):
                js = list(range(16 * u + 15, -1, -1))
                pos_ = [ps_o.next() for _ in range(2)]
                st2q, st3q = [[], []], [[], []]
                for hh in range(2):
                    S.op("pool", lambda E: E.memset(Rs[hh][0], 0.0), writes=[Rs[hh][1]])
                for idx, j in enumerate(js):
                    rk, tj, mj = key_loc(j)
                    kcol = rk * T + tj * 128
                    diag = j >= 16 * u
                    col0, vidx = 0, 0
                    if diag:
                        m = (j - 16 * u) // 4
                        col0 = 128 * m
                        vidx = ((4 * u + m) % 2) * 4 + mj
                    for hh in range(2):
                        hb = 64 * hh
                        R, R_b = Rs[hh]
                        po, po_b = pos_[hh]
                        qsl = qc[hb:hb + 64, u * 512 + col0:(u + 1) * 512]
                        ksl = Kc[hb:hb + 64, kcol:kcol + 128]
                        pE, pE_b = ps_e.next()
                        tt, tt_b = ttr.next()
                        Lp, Lp_b = Lpr.next()
                        S.op("pe", lambda E: E.matmul(pE[:, col0:], ksl, qsl, start=True, stop=False), reads=[Kc_b, qc_b], writes=[pE_b])
                        if diag:
                            S.op("pe", lambda E: E.matmul(pE[:, col0:col0 + 128], ident_bf, msk[:, vidx, :], start=False, stop=False, skip_group_check=True),
                                 reads=[cbf_b, msk_b], writes=[pE_b])
                        S.op("act", lambda E: E.activation(tt[:, col0:], pE[:, col0:], AF.Exp), reads=[pE_b], writes=[tt_b])
                        S.op("act", lambda E: E.activation(Lp[:, col0:], tt[:, col0:], AF.Ln, bias=gcol("one")), reads=[tt_b, gcols_b], writes=[Lp_b])

                        def stage2(pE=pE, pE_b=pE_b, Lp=Lp, Lp_b=Lp_b, col0=col0, rk=rk, tj=tj, idx=idx, j=j, hh=hh, hb=hb, R=R, R_b=R_b, po=po, po_b=po_b):
                            pT, pT_b = ps_t.next()
                            e2, e2_b = e2r.next()
                            wv, wv_b = wvr.next()
                            S.op("pe", lambda E: E.matmul(pE[:, col0:], negU_bf, Lp[:, col0:], start=False, stop=True, skip_group_check=True),
                                 reads=[cbf_b, Lp_b], writes=[pE_b])
                            S.op("pe", lambda E: E.matmul(pT[:, col0:], ones_bf, Lp[:, col0:], start=True, stop=True), reads=[cbf_b, Lp_b], writes=[pT_b])
                            S.op("dve", lambda E: E.tensor_tensor(e2[:, col0:], pE[:, col0:], R[:, col0:], ALU.subtract), reads=[pE_b, R_b], writes=[e2_b])
                            S.op("act", lambda E: E.activation(wv[:, col0:], e2[:, col0:], AF.Exp), reads=[e2_b], writes=[wv_b])
                            S.op("dve", lambda E: E.tensor_tensor(R[:, col0:], pT[:, col0:], R[:, col0:], ALU.add), reads=[pT_b, R_b], writes=[R_b])

                            def stage3():
                                S.op("pe", lambda E: E.matmul(po[hb:hb + 64, col0:], Vc[:, rk, tj, hb:hb + 64], wv[:, col0:], start=(idx == 0), stop=(j == 0),
                                                              skip_group_check=True),
                                     reads=[Vc_b, wv_b], writes=[po_b])
                            st3q[hh].append(stage3)
                        st2q[hh].append(stage2)
                        if len(st2q[hh]) > 1:
                            st2q[hh].pop(0)()
                        if len(st3q[hh]) > 1:
                            st3q[hh].pop(0)()
                for hh in range(2):
                    while st2q[hh]:
                        st2q[hh].pop(0)()
                for hh in range(2):
                    while st3q[hh]:
                        st3q[hh].pop(0)()
                for hh in range(2):
                    hb = 64 * hh
                    po, po_b = pos_[hh]
                    S.op("dve", lambda E: E.tensor_copy(osb[hb:hb + 64, c, u * 512:(u + 1) * 512], po[hb:hb + 64, :]),
                         reads=[po_b], writes=[osb_b[c][u]])
        Kc, Kc_b = Kr.next()
        osgs = Kc.rearrange("p (c n) -> p c n", c=4)
        S.dma("sp", osgs, osg_d.rearrange("(c p) n -> p c n", p=128), Kc_b, reads=[db["osg"]], writes=[Kc_b])
        mix = [osb[:, c, :] for c in range(4)] + [osgs[:, c, :] for c in range(4)]
        mix_b = [osb_b[c] for c in range(4)] + [Kc_b] * 4
        out_proj(mix, mix_b, Wd["sbg_out"], wfm, psrot([0, 1, 2]))
        S.barrier()

    def phase_X(l):
        cf.reset()
        ch.reset()
        rstd1, rstd1_b = cf.take(512), B()
        tmp, tmp_b = cf.take(512), B()
        wsetup()
        mst = rotf(1, 1024)
        msmall, msmall_b = cf.take(8), B()
        qraw = rotf(1, 1024, lambda a: a.rearrange("p (c n) -> p c n", c=2))
        rq = rotf(2, 512)
        kraw = rotf(1, 512, lambda a: a.rearrange("p (c n) -> p c n", c=2))
        rk_ = rotf(2, 256)
        rcp = rotf(1, 512)
        xb = ch.take(4096).rearrange("p (c n) -> p c n", c=8)
        xb1_b = B()
        sq, sq_b = ch.take(4096).rearrange("p (c n) -> p c n", c=8), B()
        hmT, hmT_b = ch.take(2048).rearrange("p (c n) -> p c n", c=8), B()
        mrow = roth(1, 1024)
        kT, kT_b = ch.take(2048).rearrange("p (h c n) -> p h c n", h=4, c=2), B()
        Vm, Vm_b = ch.take(2048).rearrange("p (m n) -> p m n", m=2), B()
        wfm = roth(2, 1024, lambda a: a.rearrange("p (c n) -> p c n", c=8))
        Wv_ = roth(1, 4096, lambda a: a.rearrange("p (c n) -> p c n", c=8))
        qn = roth(2, 1024, lambda a: a.rearrange("p (c n) -> p c n", c=2))
        sqq = roth(2, 1024, lambda a: a.rearrange("p (c n) -> p c n", c=2))
        Pm = roth(2, 1024, lambda a: a.rearrange("p (m n) -> p m n", m=2))
        oT = ch.take(4096).rearrange("p (c n) -> p c n", c=8)
        oT_b = [B() for c in range(8)]
        ps_ss_b = B("ps")
        ps_p = psrot([1, 2])
        ps_q = psrot([3])
        ps_s2 = psrot([4, 5])
        ps_o2 = psrot([6, 7])
        for mt in range(2):
            ms, ms_b = mst.next()
            S.dma("sp", ms, mem_d[mt * 128:(mt + 1) * 128, :], ms_b, writes=[ms_b])
            mr, mr_b = mrow.next()
            S.op("act", lambda E, ms=ms, mr=mr: E.activation(mr, ms, AF.Square, accum_out=msmall[:, 0:1]), reads=[ms_b], writes=[mr_b, msmall_b])
            S.op("act", lambda E: E.activation(msmall[:, 1:2], msmall[:, 0:1], AF.Ln, bias=gcol("eps"), scale=1.0 / D),
                 reads=[msmall_b, gcols_b], writes=[msmall_b])
            S.op("act", lambda E: E.activation(msmall[:, 2:3], msmall[:, 1:2], AF.Exp, scale=-0.5), reads=[msmall_b], writes=[msmall_b])
            S.op("dve", lambda E, ms=ms: E.tensor_scalar(ms, ms, msmall[:, 2:3], None, ALU.mult), reads=[ms_b, msmall_b], writes=[ms_b])
            for half in range(2):
                pp, pp_b = ps_p.next()
                for cc in range(4):
                    c = half * 4 + cc
                    S.op("pe", lambda E, pp=pp, ms=ms, c=c, cc=cc: E.transpose(pp[:, cc * 128:(cc + 1) * 128], ms[:, c * 128:(c + 1) * 128], ident_f[:, :]),
                         reads=[ms_b, ident_f_b], writes=[pp_b])
                for cc in range(4):
                    c = half * 4 + cc
                    S.op("dve", lambda E, pp=pp, c=c, cc=cc, mt=mt: E.tensor_scalar(hmT[:, c, mt * 128:(mt + 1) * 128], pp[:, cc * 128:(cc + 1) * 128],
                                                                                     gcol("xmn", 8 * l + c), None, ALU.mult),
                         reads=[pp_b, gcols_b], writes=[hmT_b])
        wk = Wd[("xk", l)].rearrange("(c p) n -> p c n", p=128)
        for hd in range(4):
            kr_, kr_b = kraw.next()
            sk, sk_b = sqq.next()
            for dc in range(2):
                oc = hd * 2 + dc
                wt, wt_b = wfm.next()
                wload(wk[:, :, oc * 128:(oc + 1) * 128], 8, 128, wt, wt_b)
                pp, pp_b = ps_p.next()
                for c in range(8):
                    S.op("pe", lambda E, c=c, pp=pp, wt=wt: E.matmul(pp[:, 0:256], wt[:, c, :], hmT[:, c, :], start=(c == 0), stop=(c == 7)),
                         reads=[wt_b, hmT_b], writes=[pp_b])
                S.op("act", lambda E, kr_=kr_, pp=pp, dc=dc: E.copy(kr_[:, dc, :], pp[:, 0:256]), reads=[pp_b], writes=[kr_b])
                S.op("act", lambda E, sk=sk, pp=pp, dc=dc: E.activation(sk[:, dc, 0:256], pp[:, 0:256], AF.Square), reads=[pp_b], writes=[sk_b])
            pq, pq_b = ps_q.next()
            for dc in range(2):
                S.op("pe", lambda E, pq=pq, sk=sk, dc=dc: E.matmul(pq[:, 0:256], ones_bf, sk[:, dc, 0:256], start=(dc == 0), stop=(dc == 1)),
                     reads=[sk_b, cbf_b], writes=[pq_b])
            rr, rr_b = rk_.next()
            rstd_from_ss(rr, rr_b, pq[:, 0:256], pq_b, 256, tmp[:, 0:256], tmp_b)
            for dc in range(2):
                S.op("dve", lambda E, kr_=kr_, rr=rr, dc=dc, hd=hd: E.scalar_tensor_tensor(kT[:, hd, dc, :], kr_[:, dc, :], gcol("xkg", 2 * l + dc), rr,
                                                                                           ALU.mult, ALU.mult),
                     reads=[kr_b, rr_b, gcols_b], writes=[kT_b])
        wvv = Wd[("xv", l)].rearrange("(c p) n -> p c n", p=128)
        for nh in range(2):
            wt, wt_b = Wv_.next()
            for hh in range(2):
                wload(wvv[:, hh * 4:(hh + 1) * 4, nh * 512:(nh + 1) * 512], 4, 512, wt[:, hh * 4:(hh + 1) * 4, :], wt_b)
            for mt in range(2):
                pp, pp_b = ps_p.next()
                for c in range(8):
                    S.op("pe", lambda E, c=c, pp=pp, wt=wt, mt=mt: E.matmul(pp[:, :], hmT[:, c, mt * 128:(mt + 1) * 128], wt[:, c, :],
                                                                             start=(c == 0), stop=(c == 7)),
                         reads=[wt_b, hmT_b], writes=[pp_b])
                S.op("act", lambda E, pp=pp, mt=mt, nh=nh: E.copy(Vm[:, mt, nh * 512:(nh + 1) * 512], pp[:, :]), reads=[pp_b], writes=[Vm_b])
        wq = Wd[("xq", l)].rearrange("(c p) n -> p c n", p=128)
        wo = Wd[("xo", l)].rearrange("(c p) n -> p c n", p=128)
        ps_x = psrot([1, 2])
        for g in range(4):
            norm_prep("xn", 8 * l, g, xb, xb1_b, rstd1, rstd1_b, sq, sq_b, 0, ps_ss_b, tmp, tmp_b)
            for hd in range(4):
                qr_, qr_b = qraw.next()
                sk, sk_b = sqq.next()
                for dc in range(2):
                    oc = hd * 2 + dc
                    wt, wt_b = wfm.next()
                    wload(wq[:, :, oc * 128:(oc + 1) * 128], 8, 128, wt, wt_b)
                    pp, pp_b = ps_p.next()
                    for c in range(8):
                        S.op("pe", lambda E, c=c, pp=pp, wt=wt: E.matmul(pp[:, :], wt[:, c, :], xb[:, c, :], start=(c == 0), stop=(c == 7)),
                             reads=[wt_b, xb1_b], writes=[pp_b])
                    S.op("dve", lambda E, qr_=qr_, pp=pp, dc=dc: E.tensor_tensor(qr_[:, dc, :], pp[:, :], rstd1, ALU.mult),
                         reads=[pp_b, rstd1_b], writes=[qr_b])
                    S.op("act", lambda E, sk=sk, qr_=qr_, dc=dc: E.activation(sk[:, dc, :], qr_[:, dc, :], AF.Square), reads=[qr_b], writes=[sk_b])
                pq, pq_b = ps_q.next()
                for dc in range(2):
                    S.op("pe", lambda E, pq=pq, sk=sk, dc=dc: E.matmul(pq[:, :], ones_bf, sk[:, dc, :], start=(dc == 0), stop=(dc == 1)),
                         reads=[sk_b, cbf_b], writes=[pq_b])
                rr, rr_b = rq.next()
                rstd_from_ss(rr, rr_b, pq[:, :], pq_b, 256, tmp, tmp_b)
                qq, qq_b = qn.next()
                for dc in range(2):
                    S.op("dve", lambda E, qq=qq, qr_=qr_, rr=rr, dc=dc: E.scalar_tensor_tensor(qq[:, dc, :], qr_[:, dc, :], gcol("xqg", 2 * l + dc), rr,
                                                                                              ALU.mult, ALU.mult),
                         reads=[qr_b, rr_b, gcols_b], writes=[qq_b])
                pm_, pm_b = Pm.next()
                for mt in range(2):
                    pS, pS_b = ps_s2.next()
                    for dc in range(2):
                        S.op("pe", lambda E, pS=pS, qq=qq, dc=dc, mt=mt, hd=hd: E.matmul(pS[:, :], kT[:, hd, dc, mt * 128:(mt + 1) * 128], qq[:, dc, :],
                                                                                         start=(dc == 0), stop=(dc == 1)),
                             reads=[kT_b, qq_b], writes=[pS_b])
                    S.op("act", lambda E, pm_=pm_, pS=pS, mt=mt: E.activation(pm_[:, mt, :], pS[:, :], AF.Exp, scale=1.0 / 16.0),
                         reads=[pS_b], writes=[pm_b])
                pq, pq_b = ps_q.next()
                for mt in range(2):
                    S.op("pe", lambda E, pq=pq, pm_=pm_, mt=mt: E.matmul(pq[:, :], ones_bf, pm_[:, mt, :], start=(mt == 0), stop=(mt == 1)),
                         reads=[pm_b, cbf_b], writes=[pq_b])
                rc, rc_b = rcp.next()
                S.op("dve", lambda E, rc=rc, pq=pq: E.reciprocal(rc, pq[:, :]), reads=[pq_b], writes=[rc_b])
                for dc in range(2):
                    po, po_b = ps_o2.next()
                    for mt in range(2):
                        S.op("pe", lambda E, po=po, pm_=pm_, mt=mt, dc=dc, hd=hd: E.matmul(po[:, :], Vm[:, mt, hd * 256 + dc * 128:hd * 256 + (dc + 1) * 128],
                                                                                           pm_[:, mt, :], start=(mt == 0), stop=(mt == 1)),
                             reads=[Vm_b, pm_b], writes=[po_b])
                    S.op("dve", lambda E, po=po, rc=rc, hd=hd, dc=dc: E.tensor_tensor(oT[:, hd * 2 + dc, :], po[:, :], rc, ALU.mult),
                         reads=[po_b, rc_b], writes=[oT_b[hd * 2 + dc]])
            for i in range(8):
                wt, wt_b = wfm.next()
                wload(wo[:, :, i * 128:(i + 1) * 128], 8, 128, wt, wt_b)
                po, po_b = ps_x.next()
                for c in range(8):
                    S.op("pe", lambda E, c=c, po=po, wt=wt: E.matmul(po[:, :], wt[:, c, :], oT[:, c, :], start=(c == 0), stop=(c == 7)),
                         reads=[wt_b, oT_b[c]], writes=[po_b])
                xs = xT[:, i, g * 512:(g + 1) * 512]
                S.op("dve", lambda E, xs=xs, po=po: E.tensor_tensor(xs, po[:, :], xs, ALU.add), reads=[po_b, xTb[i][g]], writes=[xTb[i][g]])
        S.barrier()

    def sin_reduced(out_ap, out_b, ang_ap, ang_b, wk, wk_b, nf, nf_b, p0, p1, add=0.0):
        ni = nint[p0:p1, :]
        S.op("dve", lambda E: E.tensor_scalar(wk, ang_ap, float(add), None, ALU.add), reads=[ang_b], writes=[wk_b])
        S.op("dve", lambda E: E.tensor_scalar(ni, wk, float(1 / TWO_PI), None, ALU.mult), reads=[wk_b], writes=[nint_b])
        S.op("dve", lambda E: E.tensor_copy(nf, ni), reads=[nint_b], writes=[nf_b])
        S.op("dve", lambda E: E.scalar_tensor_tensor(wk, nf, -TWO_PI, wk, ALU.mult, ALU.add), reads=[nf_b, wk_b], writes=[wk_b])
        S.op("dve", lambda E: E.tensor_scalar(nf, wk, PI, TWO_PI, ALU.is_gt, ALU.mult), reads=[wk_b], writes=[nf_b])
        S.op("dve", lambda E: E.tensor_tensor(wk, wk, nf, ALU.subtract), reads=[nf_b, wk_b], writes=[wk_b])
        S.op("dve", lambda E: E.tensor_scalar(nf, wk, -PI, TWO_PI, ALU.is_lt, ALU.mult), reads=[wk_b], writes=[nf_b])
        S.op("dve", lambda E: E.tensor_tensor(wk, wk, nf, ALU.add), reads=[nf_b, wk_b], writes=[wk_b])
        S.op("act", lambda E: E.activation(out_ap, wk, AF.Sin), reads=[wk_b], writes=[out_b])

    class _Stop(Exception):
        pass

    def _stage(n):
        for k in range(1, 9):
            if ('a1s%d' % k) in DBG and n >= k:
                raise _Stop()

    def phase_A1():
        try:
            phase_A1_()
        except _Stop:
            S.barrier()

    def phase_A1_():
        cf.reset()
        ch.reset()
        rstd = [cf.take(512) for _ in range(4)]
        rstd_b = [B() for _ in range(4)]
        tmp, tmp_b = cf.take(512), B()
        wsetup(2, 1024)
        tabs = [cf.take(512) for _ in range(4)]
        tab_b = [B() for _ in range(4)]
        cqraw, cqraw_b = cf.take(2048).rearrange("p (c n) -> p c n", c=4), B()
        fa, fa_b = cf.take(512), B()
        fb, fb_b = cf.take(512), B()
        fc, fc_b = cf.take(512), B()
        kra, kra_b = cf.take(512), B()
        rqk = rotf(2, 512)
        xb = ch.take(16384).rearrange("p (c n) -> p c n", c=8)
        xb_b = [B() for _ in range(4)]
        sq, sq_b = ch.take(4096).rearrange("p (c n) -> p c n", c=8), B()
        Win, Win_b = ch.take(6400).rearrange("p (c n) -> p c n", c=8), B()
        Wuq, Wuq_b = ch.take(6144).rearrange("p (c n) -> p c n", c=4), B()
        Wqs, Wqs_b = ch.take(2048).rearrange("p (c n) -> p c n", c=4), B()
        Wuk, Wuk_b = ch.take(2048).rearrange("p (c n) -> p c n", c=2), B()
        Wuv, Wuv_b = ch.take(2048).rearrange("p (c n) -> p c n", c=2), B()
        cqn, cqn_b = ch.take(2048).rearrange("p (c n) -> p c n", c=4), B()
        ckvn, ckvn_b = ch.take(1024).rearrange("p (c n) -> p c n", c=2), B()
        krb, krb_b = ch.take(512), B()
        sqh = ch.take(512)
        sqh_lo, sqh_hi = B(), B()
        sqq, sqq_b = ch.take(512), B()
        qst = roth(1, 512)
        vst = roth(1, 512)
        ps_ss_b = B("ps")
        ps_p = psrot([1, 2])
        ps_a = psrot([3, 4])
        ps_b2 = psrot([5])
        ps_q = psrot([6])
        ps_r = psrot([7])

        def wl(src, kc, ncol, dst, dst_b):
            v = src.rearrange("(c p) n -> p c n", p=128)
            step = max(128, (1024 // kc) // 128 * 128) if ncol >= 128 else ncol
            for n0 in range(0, ncol, step):
                n1 = min(ncol, n0 + step)
                wload(v[:, :, n0:n1], kc, n1 - n0, dst[:, :, n0:n1], dst_b)

        wl(Wd["mla_in"], 8, 768, Win, Win_b)
        wload(Wd["mla_in"].rearrange("(c p) n -> p c n", p=128)[:, :, 768:800], 8, 32, Win[:, :, 768:800], Win_b)
        wl(Wd["mla_uq"], 4, 1536, Wuq, Wuq_b)
        wl(Wd["mla_uq_sw"], 4, 512, Wqs, Wqs_b)
        wl(Wd["mla_uk"], 2, 1024, Wuk, Wuk_b)
        wl(Wd["mla_uv"], 2, 1024, Wuv, Wuv_b)
        for g in range(4):
            norm_prep("mix", 8, g, xb[:, :, g * 512:(g + 1) * 512], xb_b[g], rstd[g], rstd_b[g], sq, sq_b, 0, ps_ss_b, tmp, tmp_b)
        _stage(1)
        P0, P1 = 64, 96
        for g in range(1 if 'a1small' in DBG else 4):
            gs = slice(g * 512, (g + 1) * 512)
            S.dma("sp", posi[P0:P1, :], posrep_d[:, gs], posi_b, writes=[posi_b])
            S.op("dve", lambda E: E.tensor_copy(fa[P0:P1, :], posi[P0:P1, :]), reads=[posi_b], writes=[fa_b])
            S.op("dve", lambda E: E.tensor_scalar(fa[P0:P1, :], fa[P0:P1, :], gcol("invf", 0, P0, P1), None, ALU.mult),
                 reads=[fa_b, gcols_b], writes=[fa_b])
            sin_reduced(tabs[0][P0:P1, :], tab_b[0], fa[P0:P1, :], fa_b, fb[P0:P1, :], fb_b, fc[P0:P1, :], fc_b, P0, P1, add=PI / 2)
            sin_reduced(tabs[1][P0:P1, :], tab_b[1], fa[P0:P1, :], fa_b, fb[P0:P1, :], fb_b, fc[P0:P1, :], fc_b, P0, P1, add=0.0)
            S.op("dve", lambda E: E.tensor_scalar(tabs[2][P0:P1, :], tabs[0][P0:P1, :], gcol("mkg", 0, P0, P1), None, ALU.mult),
                 reads=[tab_b[0], gcols_b], writes=[tab_b[2]])
            S.op("dve", lambda E: E.tensor_scalar(tabs[3][P0:P1, :], tabs[1][P0:P1, :], gcol("sgn", 0, P0, P1), gcol("mkg_sw", 0, P0, P1), ALU.mult, ALU.mult),
                 reads=[tab_b[1], gcols_b], writes=[tab_b[3]])
            S.op("dve", lambda E: E.tensor_scalar(tabs[0][P0:P1, :], tabs[0][P0:P1, :], gcol("mqg", 0, P0, P1), None, ALU.mult),
                 reads=[tab_b[0], gcols_b], writes=[tab_b[0]])
            S.op("dve", lambda E: E.tensor_scalar(tabs[1][P0:P1, :], tabs[1][P0:P1, :], gcol("sgn", 0, P0, P1), gcol("mqg_sw", 0, P0, P1), ALU.mult, ALU.mult),
                 reads=[tab_b[1], gcols_b], writes=[tab_b[1]])
            _stage(2)
            for (n_ch, col0, dst, dst_b, gname, nfeat) in ((4, 0, cqn, cqn_b, "qlora", 512), (2, 512, ckvn, ckvn_b, "kvlora", 256)):
                for oc in range(n_ch):
                    pp, pp_b = ps_p.next()
                    for c in range(8):
                        S.op("pe", lambda E, c=c, pp=pp, oc=oc, col0=col0: E.matmul(pp[:, :], Win[:, c, col0 + oc * 128:col0 + (oc + 1) * 128], xb[:, c, gs],
                                                                                    start=(c == 0), stop=(c == 7)),
                             reads=[Win_b, xb_b[g]], writes=[pp_b])
                    S.op("dve", lambda E, pp=pp, oc=oc: E.tensor_tensor(cqraw[:, oc, :], pp[:, :], rstd[g], ALU.mult),
                         reads=[pp_b, rstd_b[g]], writes=[cqraw_b])
                    S.op("act", lambda E, oc=oc: E.activation(sq[:, oc, :], cqraw[:, oc, :], AF.Square), reads=[cqraw_b], writes=[sq_b])
                pq, pq_b = ps_q.next()
                for oc in range(n_ch):
                    S.op("pe", lambda E, pq=pq, oc=oc, n_ch=n_ch: E.matmul(pq[:, :], ones_bf, sq[:, oc, :], start=(oc == 0), stop=(oc == n_ch - 1)),
                         reads=[sq_b, cbf_b], writes=[pq_b])
                rr, rr_b = rqk.next()
                rstd_from_ss(rr, rr_b, pq[:, :], pq_b, nfeat, tmp, tmp_b)
                for oc in range(n_ch):
                    S.op("dve", lambda E, oc=oc, dst=dst, rr=rr, gname=gname: E.scalar_tensor_tensor(dst[:, oc, :], cqraw[:, oc, :], gcol(gname, oc), rr, ALU.mult, ALU.mult),
                         reads=[cqraw_b, rr_b, gcols_b], writes=[dst_b])
            _stage(3)
            pp, pp_b = ps_p.next()
            for c in range(8):
                S.op("pe", lambda E, c=c, pp=pp: E.matmul(pp[0:32, :], Win[:, c, 768:800], xb[:, c, gs], start=(c == 0), stop=(c == 7)),
                     reads=[Win_b, xb_b[g]], writes=[pp_b])
            S.op("dve", lambda E, pp=pp: E.tensor_tensor(krb[0:32, :], pp[0:32, :], rstd[g][0:32, :], ALU.mult), reads=[pp_b, rstd_b[g]], writes=[krb_b])
            if os.environ.get("KSUB") == "1":
                raise _Stop()
            pa, pa_b = ps_a.next()
            pb_, pb_b = ps_b2.next()
            S.op("pe", lambda E, pa=pa: E.matmul(pa[P0:P1, :], ident_bf[0:32, 0:32], krb[0:32, :], start=True, stop=True), reads=[krb_b, cbf_b], writes=[pa_b])
            S.op("pe", lambda E, pb_=pb_: E.matmul(pb_[P0:P1, :], swap_bf[0:32, 0:32], krb[0:32, :], start=True, stop=True), reads=[krb_b, cbf_b], writes=[pb_b])
            if os.environ.get("KSUB") == "2":
                raise _Stop()
            S.op("act", lambda E, pa=pa: E.activation(sqh[P0:P1, :], pa[P0:P1, :], AF.Square), reads=[pa_b], writes=[sqh_hi])
            if os.environ.get("KSUB") == "3":
                raise _Stop()
            S.op("act", lambda E, pa=pa: E.copy(kra[P0:P1, :], pa[P0:P1, :]), reads=[pa_b], writes=[kra_b])
            S.op("act", lambda E, pb_=pb_: E.copy(fb[P0:P1, :], pb_[P0:P1, :]), reads=[pb_b], writes=[fb_b])
            S.op("dve", lambda E: E.tensor_tensor(kra[P0:P1, :], kra[P0:P1, :], tabs[2][P0:P1, :], ALU.mult), reads=[kra_b, tab_b[2]], writes=[kra_b])
            S.op("dve", lambda E: E.tensor_tensor(fb[P0:P1, :], fb[P0:P1, :], tabs[3][P0:P1, :], ALU.mult), reads=[fb_b, tab_b[3]], writes=[fb_b])
            S.op("dve", lambda E: E.tensor_tensor(kra[P0:P1, :], kra[P0:P1, :], fb[P0:P1, :], ALU.add), reads=[kra_b, fb_b], writes=[kra_b])
            _stage(4)
            for h in range(2 if 'a1small' in DBG else 16):
                pa, pa_b = ps_a.next()
                pb_, pb_b = ps_b2.next()
                for c in range(4):
                    S.op("pe", lambda E, c=c, pa=pa, h=h: E.matmul(pa[0:96, :], Wuq[:, c, h * 96:(h + 1) * 96], cqn[:, c, :], start=(c == 0), stop=(c == 3)),
                         reads=[Wuq_b, cqn_b], writes=[pa_b])
                for c in range(4):
                    S.op("pe", lambda E, c=c, pb_=pb_, h=h: E.matmul(pb_[P0:P1, :], Wqs[:, c, h * 32:(h + 1) * 32], cqn[:, c, :], start=(c == 0), stop=(c == 3)),
                         reads=[Wqs_b, cqn_b], writes=[pb_b])
                S.op("act", lambda E, pa=pa: E.activation(sqq[0:96, :], pa[0:96, :], AF.Square), reads=[pa_b], writes=[sqq_b])
                pr, pr_b = ps_r.next()
                S.op("pe", lambda E, pr=pr: E.matmul(pr[0:96, :], ones_bf[0:96, 0:96], sqq[0:96, :], start=True, stop=True), reads=[sqq_b, cbf_b], writes=[pr_b])
                rr, rr_b = rqk.next()
                rstd_from_ss(rr[0:96, :], rr_b, pr[0:96, :], pr_b, 96, tmp[0:96, :], tmp_b, 0, 96)
                qs, qs_b = qst.next()
                S.op("dve", lambda E, qs=qs, pa=pa, rr=rr: E.scalar_tensor_tensor(qs[0:64, :], pa[0:64, :], gcol("mqg", 0, 0, 64), rr[0:64, :], ALU.mult, ALU.mult),
                     reads=[pa_b, rr_b, gcols_b], writes=[qs_b])
                S.op("act", lambda E, pa=pa: E.copy(fa[P0:P1, :], pa[P0:P1, :]), reads=[pa_b], writes=[fa_b])
                S.op("act", lambda E, pb_=pb_: E.copy(fb[P0:P1, :], pb_[P0:P1, :]), reads=[pb_b], writes=[fb_b])
                S.op("dve", lambda E: E.tensor_tensor(fa[P0:P1, :], fa[P0:P1, :], tabs[0][P0:P1, :], ALU.mult), reads=[fa_b, tab_b[0]], writes=[fa_b])
                S.op("dve", lambda E: E.tensor_tensor(fb[P0:P1, :], fb[P0:P1, :], tabs[1][P0:P1, :], ALU.mult), reads=[fb_b, tab_b[1]], writes=[fb_b])
                S.op("dve", lambda E: E.tensor_tensor(fa[P0:P1, :], fa[P0:P1, :], fb[P0:P1, :], ALU.add), reads=[fa_b, fb_b], writes=[fa_b])
                S.op("dve", lambda E, qs=qs, rr=rr: E.tensor_tensor(qs[P0:P1, :], fa[P0:P1, :], rr[P0:P1, :], ALU.mult), reads=[fa_b, rr_b], writes=[qs_b])
                S.dma("sp", q1_d[h * 96:(h + 1) * 96, gs], qs[0:96, :], qs_b, reads=[qs_b], writes=[db["q1"]])
                _stage(5)
                pa, pa_b = ps_a.next()
                for c in range(2):
                    S.op("pe", lambda E, c=c, pa=pa, h=h: E.matmul(pa[0:64, :], Wuk[:, c, h * 64:(h + 1) * 64], ckvn[:, c, :], start=(c == 0), stop=(c == 1)),
                         reads=[Wuk_b, ckvn_b], writes=[pa_b])
                S.op("act", lambda E, pa=pa: E.activation(sqh[0:64, :], pa[0:64, :], AF.Square), reads=[pa_b], writes=[sqh_lo])
                pr, pr_b = ps_r.next()
                S.op("pe", lambda E, pr=pr: E.matmul(pr[0:96, :], ones_bf[0:96, 0:96], sqh[0:96, :], start=True, stop=True), reads=[sqh_lo, sqh_hi, cbf_b], writes=[pr_b])
                rr, rr_b = rqk.next()
                rstd_from_ss(rr[0:96, :], rr_b, pr[0:96, :], pr_b, 96, tmp[0:96, :], tmp_b, 0, 96)
                ks, ks_b = qst.next()
                S.op("dve", lambda E, ks=ks, pa=pa, rr=rr: E.scalar_tensor_tensor(ks[0:64, :], pa[0:64, :], gcol("mkg", 0, 0, 64), rr[0:64, :], ALU.mult, ALU.mult),
                     reads=[pa_b, rr_b, gcols_b], writes=[ks_b])
                S.op("dve", lambda E, ks=ks, rr=rr: E.tensor_tensor(ks[P0:P1, :], kra[P0:P1, :], rr[P0:P1, :], ALU.mult), reads=[kra_b, rr_b], writes=[ks_b])
                S.dma("sp", k1_d[h * 96:(h + 1) * 96, gs], ks[0:96, :], ks_b, reads=[ks_b], writes=[db["k1"]])
            _stage(6)
            for tt in range(4):
                t = g * 4 + tt
                for half in range(2):
                    pp, pp_b = ps_p.next()
                    for c in range(2):
                        S.op("pe", lambda E, c=c, pp=pp, tt=tt, half=half: E.matmul(pp[:, :], ckvn[:, c, tt * 128:(tt + 1) * 128], Wuv[:, c, half * 512:(half + 1) * 512],
                                                                                     start=(c == 0), stop=(c == 1)),
                             reads=[Wuv_b, ckvn_b], writes=[pp_b])
                    vs, vs_b = vst.next()
                    S.op("act", lambda E, vs=vs, pp=pp: E.copy(vs, pp[:, :]), reads=[pp_b], writes=[vs_b])
                    S.dma("sp", v1_d[t * 128:(t + 1) * 128, half * 512:(half + 1) * 512], vs, vs_b, reads=[vs_b], writes=[db["v1"]])
        S.barrier()
        if exch == "cc":
            allgather(k1_d, db["k1"], kg1_d, db["kg1"], 1536, RC_K1)
            allgather(v1_d, db["v1"], vg1_d, db["vg1"], T, RC_V1)
        S.barrier()

    def phase_B1():
        cf.reset()
        ch.reset()
        wsetup()
        tmpA, tmpA_b = cf.take(512), B()
        rcf, rcf_b = cf.take(512), B()
        rcs, rcs_b = cf.take(512), B()
        Kh, Kh_b = ch.take(8192), B()
        Vp, Vp_b = ch.take(12288).rearrange("p (r t d) -> p r t d", r=4, t=16), B()
        qh, qh_b = ch.take(2048), B()
        Pr = roth(3, 512)
        oT = ch.take(8 * T).rearrange("p (c n) -> p c n", c=8)
        oT_b = [[B() for g in range(4)] for c in range(8)]
        msk, msk_b = ch.take(1024).rearrange("p (v n) -> p v n", v=8), B()
        wfm = roth(2, 1024, lambda a: a.rearrange("p (c n) -> p c n", c=8))
        rhi, rhi_b = ch.take(512), B()
        rlo, rlo_b = ch.take(512), B()
        ps_s = psrot([0, 1, 2])
        ps_o = psrot([3, 4])
        ps_r = psrot([5])
        S.dma("sp", msk, masks_d[:, 1024:2048].rearrange("p (v n) -> p v n", v=8), msk_b, writes=[msk_b])
        for rk in range(4):
            S.op("pool", lambda E: E.memset(Vp[:, rk, :, 64:128], 1.0), writes=[Vp_b])
        SC = float(96 ** -0.5)
        for h in range(16):
            hh = h % 2
            hb = 64 * hh
            dbp = 64 - hb
            for rk in range(4):
                S.dma("sp", Kh[0:96, rk * T:(rk + 1) * T], kg1_d[grow(RC_K1, rk, h * 96):grow(RC_K1, rk, h * 96) + 96, :], Kh_b,
                      reads=[db["kg1"]], writes=[Kh_b])
                if hh == 0:
                    for tq in range(4):
                        r0 = grow(RC_V1, rk, tq * 512)
                        c0 = (h // 2) * 128
                        S.dma("sp", Vp[:, rk, tq * 4:(tq + 1) * 4, 0:64],
                              vg1_d[r0:r0 + 512, c0:c0 + 64].rearrange("(t p) d -> p t d", p=128), Vp_b, reads=[db["vg1"]], writes=[Vp_b])
                        S.dma("sp", Vp[:, rk, tq * 4:(tq + 1) * 4, 128:192],
                              vg1_d[r0:r0 + 512, c0 + 64:c0 + 128].rearrange("(t p) d -> p t d", p=128), Vp_b, reads=[db["vg1"]], writes=[Vp_b])
            S.dma("sp", qh[0:96, :], q1_d[h * 96:(h + 1) * 96, :], qh_b, reads=[db["q1"]], writes=[qh_b])
            for u in range(4):
                po, po_b = ps_o.next()
                nblk = 16 * u + 16
                pend = []
                for j in range(nblk):
                    rk, tj, mj = key_loc(j)
                    kcol = rk * T + tj * 128
                    diag = j >= 16 * u
                    col0, vidx = 0, 0
                    if diag:
                        m = (j - 16 * u) // 4
                        col0 = 128 * m
                        vidx = ((4 * u + m) % 2) * 4 + mj
                    pS, pS_b = ps_s.next()
                    S.op("pe", lambda E: E.matmul(pS[:, col0:], Kh[0:96, kcol:kcol + 128], qh[0:96, u * 512 + col0:(u + 1) * 512], start=True, stop=not diag),
                         reads=[Kh_b, qh_b], writes=[pS_b])
                    if diag:
                        S.op("pe", lambda E: E.matmul(pS[:, col0:col0 + 128], ident_bf, msk[:, vidx, :], start=False, stop=True, skip_group_check=True),
                             reads=[cbf_b, msk_b], writes=[pS_b])
                    P_, P_b = Pr.next()
                    S.op("act", lambda E: E.activation(P_[:, col0:], pS[:, col0:], AF.Exp, scale=SC), reads=[pS_b], writes=[P_b])

                    def pv(po=po, po_b=po_b, P_=P_, P_b=P_b, col0=col0, rk=rk, tj=tj, j=j, nblk=nblk):
                        S.op("pe", lambda E: E.matmul(po[:, col0:], Vp[:, rk, tj, hb:hb + 128], P_[:, col0:], start=(j == 0), stop=(j == nblk - 1),
                                                      skip_group_check=True),
                             reads=[Vp_b, P_b], writes=[po_b])
                    if pend:
                        pend.pop()()
                    pend.append(pv)
                pend.pop()()
                dsl = slice(dbp, dbp + 64)
                osl = slice(hb, hb + 64)
                S.op("act", lambda E: E.activation(tmpA[dsl, :], po[dsl, :], AF.Ln), reads=[po_b], writes=[tmpA_b])
                S.op("act", lambda E: E.activation(rcf[dsl, :], tmpA[dsl, :], AF.Exp, scale=-1.0), reads=[tmpA_b], writes=[rcf_b])
                S.op("dve", lambda E: E.tensor_copy(rhi[dsl, :], rcf[dsl, :]), reads=[rcf_b], writes=[rhi_b])
                S.op("dve", lambda E: E.tensor_tensor(rlo[dsl, :], rcf[dsl, :], rhi[dsl, :], ALU.subtract), reads=[rcf_b, rhi_b], writes=[rlo_b])
                pr, pr_b = ps_r.next()
                S.op("pe", lambda E: E.matmul(pr[osl, :], ident_bf[dsl, dbp:dbp + 64], rhi[dsl, :], start=True, stop=False), reads=[cbf_b, rhi_b], writes=[pr_b])
                S.op("pe", lambda E: E.matmul(pr[osl, :], ident_bf[dsl, dbp:dbp + 64], rlo[dsl, :], start=False, stop=True), reads=[cbf_b, rlo_b], writes=[pr_b])
                S.op("act", lambda E: E.copy(rcs[osl, :], pr[osl, :]), reads=[pr_b], writes=[rcs_b])
                S.op("dve", lambda E: E.tensor_tensor(oT[osl, h // 2, u * 512:(u + 1) * 512], po[osl, :], rcs[osl, :], ALU.mult),
                     reads=[po_b, rcs_b], writes=[oT_b[h // 2][u]])
        out_proj([oT[:, c, :] for c in range(8)], oT_b, Wd["mla_out"], wfm, psrot([0, 1, 2]))
        S.barrier()

    load_x()
    seq = [lambda: ffn(Wd[("gu", "pre", 0)], Wd[("d", "pre", 0)], "ffn_pre", 0),
           phase_A0, phase_B0, lambda: phase_X(0),
           lambda: ffn(Wd[("gu", "post", 0)], Wd[("d", "post", 0)], "ffn_post", 0),
           lambda: ffn(Wd[("gu", "pre", 1)], Wd[("d", "pre", 1)], "ffn_pre", 16),
           phase_A1, phase_B1, lambda: phase_X(1),
           lambda: ffn(Wd[("gu", "post", 1)], Wd[("d", "post", 1)], "ffn_post", 16)]
    for i in range(lo, hi):
        seq[i]()
    for n in db:
        if db[n].lw is not None:
            S._wait("sp", db[n].lw)
    store_x()
    S.run()
    print('[build] phases', lo, hi, 'counts', dict(S.cnt), 'nsems', len(S.sems), 'maxdma', max([0] + [v for k, v in S.alltoks.items() if str(k).startswith('d')]), flush=True)
    return nc, used_inputs, outputs


def zig(t, r):
    return 4 * t + (r if t % 2 == 0 else 3 - r)


def shard_tokens(x):
    outs = []
    for c in range(NCORES):
        b, r = c // 4, c % 4
        blocks = [zig(t, r) for t in range(NT)]
        xb = x[b].reshape(64, 128, -1)
        outs.append(np.ascontiguousarray(xb[blocks].reshape(T, -1)))
    return outs


def unshard_tokens(outs, dtype=np.float32):
    full = np.zeros((2, 64, 128, D), dtype)
    for c in range(NCORES):
        b, r = c // 4, c % 4
        blocks = [zig(t, r) for t in range(NT)]
        full[b, blocks] = outs[c].reshape(NT, 128, D)
    return full.reshape(2, 8192, D)


def col(v):
    v = np.asarray(v, np.float32)
    return np.ascontiguousarray(v.reshape(-1, 128).T)


def make_masks(r):
    k = np.arange(128)[:, None]
    q = np.arange(128)[None, :]
    m = np.zeros((128, 16, 128), np.float32)
    for kind in range(2):
        tri = (k < q) if kind == 0 else (k <= q)
        for p in range(2):
            rr = r if p == 0 else 3 - r
            for mj in range(4):
                if mj < rr:
                    v = np.zeros((128, 128), np.float32)
                elif mj == rr:
                    v = np.where(tri, 0.0, NEG).astype(np.float32)
                else:
                    v = np.full((128, 128), NEG, np.float32)
                m[:, kind * 8 + p * 4 + mj, :] = v
    return m.reshape(128, 16 * 128).astype(ml_dtypes.bfloat16)


_PROG_CACHE = {}


def prepare_inputs(inp):
    f32 = np.float32
    ident = np.eye(128, dtype=f32)
    ones = np.ones((128, 128), f32)
    negU = -np.tril(np.ones((128, 128), f32))
    sw = np.zeros((128, 128), f32)
    for i in range(16):
        sw[16 + i, i] = 1.0
        sw[i, 16 + i] = 1.0
    cb = np.concatenate([ident, ones, negU, sw], axis=1).astype(ml_dtypes.bfloat16)
    gc = np.zeros((128, NGC), f32)
    for l in range(2):
        gc[:, 16 * l:16 * l + 8] = col(inp["ffn_pre_norm"][l])
        gc[:, 16 * l + 8:16 * l + 16] = col(inp["ffn_post_norm"][l])
        gc[:, 32 + 8 * l:40 + 8 * l] = col(inp["mix_norm"][l])
        gc[:, 48 + 8 * l:56 + 8 * l] = col(inp["xmem_norm"][l])
        gc[:, 64 + 8 * l:72 + 8 * l] = col(inp["xmem_mem_norm"][l])
        gc[:, 80 + 2 * l:82 + 2 * l] = col(inp["xmem_q_gain"][l])
        gc[:, 84 + 2 * l:86 + 2 * l] = col(inp["xmem_k_gain"][l])
    gc[:, 88:92] = col(inp["mla_q_lora_gain"][0])
    gc[:, 92:94] = col(inp["mla_kv_lora_gain"][0])
    perm = np.arange(96)
    perm[64:80] = np.arange(80, 96)
    perm[80:96] = np.arange(64, 80)
    gc[:96, 94] = inp["mla_q_gain"][0]
    gc[:96, 95] = inp["mla_k_gain"][0]
    gc[:96, 96] = inp["mla_q_gain"][0][perm]
    gc[:96, 97] = inp["mla_k_gain"][0][perm]
    half = 16
    invf = (f32(10000.0) ** (-np.arange(half, dtype=f32) / f32(half))).astype(f32)
    gc[64:80, 98] = invf
    gc[80:96, 98] = invf
    gc[64:80, 99] = -1.0
    gc[80:96, 99] = 1.0
    gc[:, 100] = EPS
    gc[:, 101] = np.pi / 2
    gc[:, 102] = 1.0
    bc0 = np.zeros((128, 1536), f32)
    bc0[:, 0:512] = inp["sgu_ln_gain"][0][None, :]
    bc0[:, 512:1024] = inp["sgu_ln_bias"][0][None, :]
    sb = inp["sgu_b"][0]
    for cc in range(4):
        bc0[0:64, 1024 + cc * 128:1024 + (cc + 1) * 128] = sb[2 * cc][None, :]
        bc0[64:128, 1024 + cc * 128:1024 + (cc + 1) * 128] = sb[2 * cc + 1][None, :]
    trim = (np.arange(128)[:, None] <= np.arange(128)[None, :]).astype(f32)
    sguwT = np.ascontiguousarray(inp["sgu_w"][0].transpose(2, 0, 1)).reshape(128, 8 * 128)
    common = {"consts_bf": cb, "ident_f": ident, "gcols": gc, "bc0": bc0, "trimask": trim, "sgu_wT": sguwT}
    for l in range(2):
        for which in ("pre", "post"):
            common["w_%s_gu_%d" % (which, l)] = np.ascontiguousarray(inp["ffn_%s_w_gu" % which][l])
            common["w_%s_d_%d" % (which, l)] = np.ascontiguousarray(inp["ffn_%s_w_down" % which][l])
        common["w_xq_%d" % l] = np.ascontiguousarray(inp["xmem_wq"][l])
        wkv = inp["xmem_wkv"][l].reshape(D, 4, 2, 256)
        common["w_xk_%d" % l] = np.ascontiguousarray(wkv[:, :, 0, :].reshape(D, D))
        common["w_xv_%d" % l] = np.ascontiguousarray(wkv[:, :, 1, :].reshape(D, D))
        common["w_xo_%d" % l] = np.ascontiguousarray(inp["xmem_wo"][l])
    common["w_sbg_in"] = np.ascontiguousarray(inp["sbg_w_in"][0])
    common["w_sbg_out"] = np.ascontiguousarray(inp["sbg_w_out"][0])
    common["w_mla_in"] = np.ascontiguousarray(inp["mla_w_in"][0])
    uq = inp["mla_w_uq"][0]
    common["w_mla_uq"] = np.ascontiguousarray(uq)
    uq_h = uq.reshape(512, 16, 96)
    common["w_mla_uq_sw"] = np.ascontiguousarray(np.concatenate([uq_h[:, :, 80:96], uq_h[:, :, 64:80]], axis=2).reshape(512, 512))
    ukv = inp["mla_w_ukv"][0].reshape(256, 16, 128)
    common["w_mla_uk"] = np.ascontiguousarray(ukv[:, :, 0:64].reshape(256, 1024))
    common["w_mla_uv"] = np.ascontiguousarray(ukv[:, :, 64:128].reshape(256, 1024))
    common["w_mla_out"] = np.ascontiguousarray(inp["mla_w_out"][0])
    xs = shard_tokens(inp["x"])
    pos = inp["positions"].astype(np.int32)
    in_maps = []
    for c in range(NCORES):
        b, r = c // 4, c % 4
        blocks = [zig(t, r) for t in range(NT)]
        pl = pos[b].reshape(64, 128)[blocks].reshape(T)
        m = dict(common)
        m["x_in"] = xs[c]
        m["masks"] = make_masks(r)
        m["posrep"] = np.ascontiguousarray(np.broadcast_to(pl[None, :], (32, T))).astype(np.int32)
        m["mem"] = np.ascontiguousarray(inp["mem"][b])
        in_maps.append(m)
    return in_maps


EXCH = "cc"
DEBUG_X = {}


def _run(key, in_maps):
    if key not in _PROG_CACHE:
        _PROG_CACHE[key] = build_program(*key)
    nc, used, outs = _PROG_CACHE[key]
    maps = [{k: v for k, v in m.items() if k in used} for m in in_maps]
    res = run_bass_kernel_spmd(nc, maps, core_ids=list(range(NCORES)))
    return res.results


def _gather(results, name, rc):
    out = []
    for c in range(NCORES):
        b = c // 4
        rows = results[c][name].shape[0]
        parts = []
        for i in range(rows // rc):
            for rk in range(4):
                parts.append(results[4 * b + rk][name][i * rc:(i + 1) * rc])
        out.append(np.ascontiguousarray(np.concatenate(parts, axis=0)))
    return out


def kernel(**inputs):
    inp = {k: np.asarray(v) for k, v in inputs.items()}
    in_maps = prepare_inputs(inp)
    if EXCH == "cc":
        res = _run((0, 10, "cc"), in_maps)
        return unshard_tokens([r["x_out"] for r in res])
    r1 = _run((0, 2, "host"), in_maps)
    kg0, vg0 = _gather(r1, "k0_d", RC_K0), _gather(r1, "v0_d", RC_V0)
    for c in range(NCORES):
        in_maps[c].update({"x_in": r1[c]["x_out"], "q0_d": r1[c]["q0_d"], "osg_d": r1[c]["osg_d"], "kg0_d": kg0[c], "vg0_d": vg0[c]})
    r2 = _run((2, 7, "host"), in_maps)
    DEBUG_X["p2"] = [r["x_out"] for r in r2]
    kg1, vg1 = _gather(r2, "k1_d", RC_K1), _gather(r2, "v1_d", RC_V1)
    for c in range(NCORES):
        in_maps[c].update({"x_in": r2[c]["x_out"], "q1_d": r2[c]["q1_d"], "kg1_d": kg1[c], "vg1_d": vg1[c]})
    r3 = _run((7, 10, "host"), in_maps)
    return unshard_tokens([r["x_out"] for r in r3])
```

```python
import types
import numpy as np
import ml_dtypes
import concourse.bass as bass
import concourse.mybir as mybir
from concourse.bass_utils import run_bass_kernel_spmd

F32 = mybir.dt.float32
BF16 = mybir.dt.bfloat16
I32 = mybir.dt.int32
AF = mybir.ActivationFunctionType
ALU = mybir.AluOpType

NCORES = 8
D = 1024
DFF = 2816
T = 2048
NT = 16
EPS = 1e-6
NEG = -30000.0


class Buf:
    __slots__ = ("name", "lw", "rd", "dkey", "dcnt")

    def __init__(self, name):
        self.name = name
        self.lw = None
        self.rd = {}
        self.dkey = None
        self.dcnt = 0


ENGS = ("pe", "act", "dve", "pool", "sp")


class Sched:
    def __init__(self, nc):
        self.nc = nc
        self.prog = {e: [] for e in ENGS}
        self.cnt = {e: 0 for e in ENGS}
        self.waited = {e: {} for e in ENGS}
        self.sems = {}
        self.alltoks = {}
        self.dfree = []
        self.dcum = {}
        self.dheld = []

    def sem(self, key):
        if key not in self.sems:
            self.sems[key] = self.nc.alloc_semaphore(name="s_%s" % (str(key).replace(" ", "")))
        return self.sems[key]

    def _wait(self, eng, tok):
        key, val = tok
        if key == eng and eng in ("pe", "sp"):
            return
        if self.waited[eng].get(key, 0) >= val:
            return
        self.waited[eng][key] = val
        h = self.sem(key)
        self.prog[eng].append(lambda E, h=h, val=val: E.wait_ge(h, val))

    def _deps(self, eng, reads, writes):
        for b in reads:
            if b.lw is not None:
                self._wait(eng, b.lw)
        for b in writes:
            if b.lw is not None and b.lw[0] != eng:
                self._wait(eng, b.lw)
            for k, v in b.rd.items():
                if k != eng:
                    self._wait(eng, (k, v))

    def _upd(self, tok, reads, writes):
        self.alltoks[tok[0]] = max(self.alltoks.get(tok[0], 0), tok[1])
        for b in reads:
            if b.rd.get(tok[0], 0) < tok[1]:
                b.rd[tok[0]] = tok[1]
        for b in writes:
            b.lw = tok
            b.rd = {}

    @staticmethod
    def _freeze(fn):
        if fn.__closure__:
            cells = tuple(types.CellType(c.cell_contents) for c in fn.__closure__)
            g = types.FunctionType(fn.__code__, fn.__globals__, fn.__name__, fn.__defaults__, cells)
            g.__kwdefaults__ = fn.__kwdefaults__
            return g
        return fn

    def op(self, eng, fn, reads=(), writes=()):
        fn = self._freeze(fn)
        self._deps(eng, reads, writes)
        self.cnt[eng] += 1
        tok = (eng, self.cnt[eng])
        h = self.sem(eng)
        self.prog[eng].append(lambda E, fn=fn, h=h: fn(E).then_inc(h, 1))
        self._upd(tok, reads, writes)
        return tok

    def dma(self, q, out_ap, in_ap, owner, reads=(), writes=()):
        self._deps(q, reads, writes)
        if owner.dkey is None:
            if self.dfree:
                owner.dkey = self.dfree.pop()
            else:
                owner.dkey = "q%d" % len(self.dcum)
                self.dcum[owner.dkey] = 0
            self.dheld.append(owner)
        self.dcum[owner.dkey] += 16
        tok = (owner.dkey, self.dcum[owner.dkey])
        h = self.sem(owner.dkey)
        self.prog[q].append(lambda E, o=out_ap, i=in_ap, h=h: E.dma_start(out=o, in_=i).then_inc(h, 16))
        self._upd(tok, reads, writes)
        return tok

    def barrier(self, release=True):
        for e in ENGS:
            for k, v in list(self.alltoks.items()):
                self._wait(e, (k, v))
        if release:
            for b in self.dheld:
                self.dfree.append(b.dkey)
                b.dkey = None
            self.dheld = []

    def run(self):
        nc = self.nc
        with nc.Block() as block:
            @block.tensor
            def _(E):
                for f in self.prog["pe"]:
                    f(E)

            @block.scalar
            def _(E):
                for f in self.prog["act"]:
                    f(E)

            @block.vector
            def _(E):
                for f in self.prog["dve"]:
                    f(E)

            @block.gpsimd
            def _(E):
                for f in self.prog["pool"]:
                    f(E)

            @block.sync
            def _(E):
                for f in self.prog["sp"]:
                    f(E)


class Rot:
    def __init__(self, slots):
        self.slots = slots
        self.i = 0

    def next(self):
        s = self.slots[self.i % len(self.slots)]
        self.i += 1
        return s


class Ctx:
    pass


def mk_tile(nc, name, shape, dt):
    return nc.alloc_sbuf_tensor(name, shape, dt)


import os
DBG = os.environ.get('KDBG', '')
RC_K0, RC_V0, RC_K1, RC_V1 = 256, 1024, 192, 512
GC = dict(ffn_pre=0, ffn_post=8, mix=32, xn=48, xmn=64, xqg=80, xkg=84, qlora=88, kvlora=92,
          mqg=94, mkg=95, mqg_sw=96, mkg_sw=97, invf=98, sgn=99, eps=100, halfpi=101, one=102, zero=103)
NGC = 104
TWO_PI = float(2 * np.pi)
PI = float(np.pi)


PHASES = ["F0pre", "A0", "B0", "X0", "F0post", "F1pre", "A1", "B1", "X1", "F1post"]


def build_program(lo=0, hi=10, exch="cc"):
    nc = bass.Bass("TRN2", target_bir_lowering=False)
    S = Sched(nc)
    outputs = ["x_out"]

    used_inputs = set()

    def din(name, shape, dt=F32):
        used_inputs.add(name)
        return nc.dram_tensor(name, list(shape), dt, kind="ExternalInput").ap()

    x_in = din("x_in", [T, D])
    x_out = nc.dram_tensor("x_out", [T, D], F32, kind="ExternalOutput").ap()
    consts_bf = din("consts_bf", [128, 4 * 128], BF16)
    ident_f_d = din("ident_f", [128, 128])
    gcols_d = din("gcols", [128, NGC])
    bc0_d = din("bc0", [128, 1536])
    trim_d = din("trimask", [128, 128])
    sguwT_d = din("sgu_wT", [128, 8 * 128])
    masks_d = din("masks", [128, 16 * 128], BF16)
    posrep_d = din("posrep", [32, T], I32)
    mem_d = din("mem", [256, D])
    WSHAPES = {"sbg_in": [D, 2560], "sbg_out": [D, D], "mla_in": [D, 800], "mla_uq": [512, 1536], "mla_uq_sw": [512, 512],
               "mla_uk": [256, 1024], "mla_uv": [256, 1024], "mla_out": [D, D]}

    class LazyW(dict):
        def __missing__(self, key):
            if isinstance(key, tuple) and key[0] in ("gu", "d"):
                name = "w_%s_%s_%d" % (key[1], key[0], key[2])
                shape = [D, 2 * DFF] if key[0] == "gu" else [DFF, D]
            elif isinstance(key, tuple):
                name = "w_%s_%d" % key
                shape = [D, D]
            else:
                name = "w_" + key
                shape = WSHAPES[key]
            self[key] = din(name, shape)
            return self[key]

    Wd = LazyW()
    def scratch(name, shape, prod, cons):
        if exch == "cc" or (lo <= prod < hi and lo <= cons < hi):
            return nc.dram_tensor(name, shape, BF16).ap()
        if lo <= prod < hi:
            outputs.append(name)
            return nc.dram_tensor(name, shape, BF16, kind="ExternalOutput").ap()
        if lo <= cons < hi:
            return din(name, shape, BF16)
        return None

    gp = 1 if exch == "cc" else -1
    q0_d = scratch("q0_d", [512, T], 1, 2)
    k0_d = scratch("k0_d", [512, T], 1, 2 if exch == "cc" else 99)
    v0_d = scratch("v0_d", [T, 512], 1, 2 if exch == "cc" else 99)
    osg_d = scratch("osg_d", [512, T], 1, 2)
    kg0_d = scratch("kg0_d", [4 * 512, T], gp, 2)
    vg0_d = scratch("vg0_d", [4 * T, 512], gp, 2)
    gp = 6 if exch == "cc" else -1
    q1_d = scratch("q1_d", [1536, T], 6, 7)
    k1_d = scratch("k1_d", [1536, T], 6, 7 if exch == "cc" else 99)
    v1_d = scratch("v1_d", [T, 1024], 6, 7 if exch == "cc" else 99)
    kg1_d = scratch("kg1_d", [4 * 1536, T], gp, 7)
    vg1_d = scratch("vg1_d", [4 * T, 1024], gp, 7)
    db = {n: Buf(n) for n in ("q0", "k0", "v0", "osg", "kg0", "vg0", "q1", "k1", "v1", "kg1", "vg1")}

    xT = nc.alloc_sbuf_tensor("xT", [128, 8, T], F32)
    xTb = [[Buf("xT_%d_%d" % (c, g)) for g in range(4)] for c in range(8)]
    cbf = nc.alloc_sbuf_tensor("cbf", [128, 4 * 128], BF16)
    cbf_b = Buf("cbf")
    ident_f = nc.alloc_sbuf_tensor("sb_ident_f", [128, 128], F32)
    ident_f_b = Buf("ident_f")
    gcols = nc.alloc_sbuf_tensor("sb_gcols", [128, NGC], F32)
    gcols_b = Buf("gcols")
    ident_bf = cbf[:, 0:128]
    ones_bf = cbf[:, 128:256]
    negU_bf = cbf[:, 256:384]
    swap_bf = cbf[:, 384:512]

    def gcol(name, i=0, p0=0, p1=128):
        return gcols[p0:p1, GC[name] + i:GC[name] + i + 1]

    S.dma("sp", cbf[:, :], consts_bf[:, :], cbf_b, writes=[cbf_b])
    S.dma("sp", ident_f[:, :], ident_f_d[:, :], ident_f_b, writes=[ident_f_b])
    S.dma("sp", gcols[:, :], gcols_d[:, :], gcols_b, writes=[gcols_b])

    posi = nc.alloc_sbuf_tensor("posi", [128, 512], I32)
    posi_b = Buf("posi")
    nint = nc.alloc_sbuf_tensor("nint", [128, 512], I32)
    nint_b = Buf("nint")
    AF32 = 11776
    ABF = 45056
    arena_f = nc.alloc_sbuf_tensor("arena_f", [128, AF32], F32)
    arena_h = nc.alloc_sbuf_tensor("arena_h", [128, ABF], BF16)
    psum = [nc.alloc_psum_tensor("ps%d" % i, [128, 512], F32) for i in range(8)]

    class Carver:
        def __init__(self, ar, size):
            self.ar, self.size, self.off = ar, size, 0

        def reset(self):
            self.off = 0

        def take(self, n):
            assert self.off + n <= self.size, ("arena overflow", self.off, n, self.size)
            a = self.ar[:, self.off:self.off + n]
            self.off += n
            return a

    cf = Carver(arena_f, AF32)
    ch = Carver(arena_h, ABF)
    nbuf = [0]

    def B(name="b"):
        nbuf[0] += 1
        return Buf("%s%d" % (name, nbuf[0]))

    def rotf(n, size, view=None):
        return Rot([((cf.take(size) if view is None else view(cf.take(size))), B()) for _ in range(n)])

    def roth(n, size, view=None):
        return Rot([((ch.take(size) if view is None else view(ch.take(size))), B()) for _ in range(n)])

    def psrot(idxs):
        return Rot([(psum[i], B("ps")) for i in idxs])

    W = Ctx()

    def wsetup(nst=2, stsize=2048):
        W.st = rotf(nst, stsize)

    castctr = [0]

    def cast(dst, dst_b, src, src_b):
        castctr[0] += 1
        if castctr[0] % 2 == 0:
            S.op("act", lambda E: E.copy(dst, src), reads=[src_b], writes=[dst_b])
        else:
            S.op("dve", lambda E: E.tensor_copy(dst, src), reads=[src_b], writes=[dst_b])

    def wload(view, kc, n, dst, dst_b, eng=None):
        assert kc * n <= 2048
        st, st_b = W.st.next()
        sv = st[:, 0:kc * n].rearrange("p (c n) -> p c n", c=kc)
        S.dma("sp", sv, view, st_b, writes=[st_b])
        cast(dst, dst_b, sv, st_b)

    def rstd_from_ss(out_ap, out_b, ps_ap, ps_b, n, tmp_ap, tmp_b, p0=0, p1=128):
        S.op("act", lambda E: E.activation(tmp_ap, ps_ap, AF.Ln, bias=gcol("eps", 0, p0, p1), scale=1.0 / n),
             reads=[ps_b, gcols_b], writes=[tmp_b])
        S.op("act", lambda E: E.activation(out_ap, tmp_ap, AF.Exp, scale=-0.5), reads=[tmp_b], writes=[out_b])

    def load_x():
        cf.reset()
        rot = rotf(2, 1024)
        psb = [B("ps"), B("ps")]
        for t in range(NT):
            ap, b = rot.next()
            S.dma("sp", ap, x_in[t * 128:(t + 1) * 128, :], b, writes=[b])
            for half in range(2):
                pb, ps = psb[half], psum[half]
                for cc in range(4):
                    c = half * 4 + cc
                    S.op("pe", lambda E, o=ps[:, cc * 128:(cc + 1) * 128], i=ap[:, c * 128:(c + 1) * 128]:
                         E.transpose(o, i, ident_f[:, :]), reads=[b, ident_f_b], writes=[pb])
                g = t // 4
                outap = xT[:, half * 4:half * 4 + 4, t * 128:(t + 1) * 128]
                inap = ps[:, :].rearrange("p (c n) -> p c n", c=4)
                wr = [xTb[half * 4 + cc][g] for cc in range(4)]
                if half == 0:
                    S.op("act", lambda E, o=outap, i=inap: E.copy(o, i), reads=[pb], writes=wr)
                else:
                    S.op("dve", lambda E, o=outap, i=inap: E.tensor_copy(o, i), reads=[pb], writes=wr)
        S.barrier()

    def store_x():
        cf.reset()
        rot = rotf(2, 1024)
        psb = [B("ps"), B("ps")]
        toks = []
        for t in range(NT):
            ap, b = rot.next()
            g = t // 4
            for half in range(2):
                pb, ps = psb[half], psum[half]
                for cc in range(4):
                    c = half * 4 + cc
                    S.op("pe", lambda E, o=ps[:, cc * 128:(cc + 1) * 128], i=xT[:, c, t * 128:(t + 1) * 128]:
                         E.transpose(o, i, ident_f[:, :]), reads=[xTb[c][g], ident_f_b], writes=[pb])
                if half == 0:
                    S.op("act", lambda E, o=ap[:, 0:512], i=ps[:, :]: E.copy(o, i), reads=[pb], writes=[b])
                else:
                    S.op("dve", lambda E, o=ap[:, 512:1024], i=ps[:, :]: E.tensor_copy(o, i), reads=[pb], writes=[b])
            toks.append(S.dma("sp", x_out[t * 128:(t + 1) * 128, :], ap, b, reads=[b]))
        for tk in toks:
            S._wait("sp", tk)
        S.barrier()

    def norm_prep(gname, gi0, grp, xb_ap, xb_b, rstd_ap, rstd_b, sq_ap, sq_b, ps_i, ps_b, tmp_ap, tmp_b, tokps=None):
        tsl = slice(grp * 512, (grp + 1) * 512)
        xbufs = [xTb[c][grp] for c in range(8)]
        S.op("act", lambda E: E.activation(sq_ap, xT[:, :, tsl], AF.Square), reads=xbufs, writes=[sq_b])
        ps = psum[ps_i]
        for c in range(8):
            S.op("pe", lambda E, c=c: E.matmul(ps[:, :], ones_bf, sq_ap[:, c, :], start=(c == 0), stop=(c == 7)),
                 reads=[sq_b, cbf_b], writes=[ps_b])
        rstd_from_ss(rstd_ap, rstd_b, ps[:, :], ps_b, D, tmp_ap, tmp_b)
        if tokps is not None:
            tp_ap, tp_b = tokps
            for tt in range(4):
                for c in range(8):
                    S.op("pe", lambda E, c=c, tt=tt: E.matmul(tp_ap[:, grp * 4 + tt:grp * 4 + tt + 1],
                                                              sq_ap[:, c, tt * 128:(tt + 1) * 128], ones_bf[:, 0:1],
                                                              start=(c == 0), stop=(c == 7)),
                         reads=[sq_b, cbf_b], writes=[tp_b])
        for c in range(8):
            S.op("dve", lambda E, c=c: E.tensor_scalar(xb_ap[:, c, :], xT[:, c, tsl], gcol(gname, gi0 + c), None, ALU.mult),
                 reads=[xTb[c][grp], gcols_b], writes=[xb_b])

    def ffn(wgu_d, wd_d, gname, gi0):
        for half in range(2):
            cf.reset()
            ch.reset()
            rstd = [cf.take(512) for _ in range(2)]
            rstd_b = [B() for _ in range(2)]
            tmp, tmp_b = cf.take(512), B()
            stg_gu = rotf(2, 2048, lambda a: a.rearrange("p (a c m) -> p a c m", a=2, c=8))
            stg_d = rotf(2, 1408, lambda a: a.rearrange("p (k m) -> p k m", k=11))
            a_rot, s_rot, t_rot = rotf(2, 512), rotf(2, 512), rotf(2, 512)
            xb = [ch.take(4096).rearrange("p (c n) -> p c n", c=8) for _ in range(2)]
            xb_b = [B() for _ in range(2)]
            hid = ch.take(22 * 1024).rearrange("p (k n) -> p k n", k=22)
            hid_b = [[B() for g in range(2)] for k in range(22)]
            wgu = roth(2, 2048, lambda a: a.rearrange("p (a c m) -> p a c m", a=2, c=8))
            wd = roth(2, 2816, lambda a: a.rearrange("p (k m) -> p k m", k=22))
            sq, sq_b = ch.take(4096).rearrange("p (c n) -> p c n", c=8), B()
            ps_ss_b = B("ps")
            ps_g, ps_u, ps_a = psrot([1, 2]), psrot([3, 4]), psrot([5, 6])
            for gi in range(2):
                norm_prep(gname, gi0, half * 2 + gi, xb[gi], xb_b[gi], rstd[gi], rstd_b[gi], sq, sq_b, 0, ps_ss_b, tmp, tmp_b)
            wgu_v = wgu_d.rearrange("(c p) n -> p c n", p=128)
            for j in range(22):
                st, st_b = stg_gu.next()
                S.dma("sp", st[:, 0], wgu_v[:, :, j * 128:(j + 1) * 128], st_b, writes=[st_b])
                S.dma("sp", st[:, 1], wgu_v[:, :, DFF + j * 128:DFF + (j + 1) * 128], st_b, writes=[st_b])
                wt, wt_b = wgu.next()
                cast(wt, wt_b, st, st_b)
                for gi in range(2):
                    pg, pg_b = ps_g.next()
                    pu, pu_b = ps_u.next()
                    for c in range(8):
                        S.op("pe", lambda E, c=c, pg=pg, wt=wt, gi=gi: E.matmul(pg[:, :], wt[:, 0, c, :], xb[gi][:, c, :],
                                                                                 start=(c == 0), stop=(c == 7)),
                             reads=[wt_b, xb_b[gi]], writes=[pg_b])
                    for c in range(8):
                        S.op("pe", lambda E, c=c, pu=pu, wt=wt, gi=gi: E.matmul(pu[:, :], wt[:, 1, c, :], xb[gi][:, c, :],
                                                                                 start=(c == 0), stop=(c == 7)),
                             reads=[wt_b, xb_b[gi]], writes=[pu_b])
                    a, a_b = a_rot.next()
                    s, s_b = s_rot.next()
                    t, t_b = t_rot.next()
                    S.op("dve", lambda E, a=a, pg=pg, gi=gi: E.tensor_tensor(a, pg[:, :], rstd[gi], ALU.mult),
                         reads=[pg_b, rstd_b[gi]], writes=[a_b])
                    S.op("act", lambda E, s=s, a=a: E.activation(s, a, AF.Silu), reads=[a_b], writes=[s_b])
                    S.op("dve", lambda E, t=t, pu=pu, gi=gi: E.tensor_tensor(t, pu[:, :], rstd[gi], ALU.mult),
                         reads=[pu_b, rstd_b[gi]], writes=[t_b])
                    S.op("pool", lambda E, s=s, t=t, j=j, gi=gi: E.tensor_tensor(hid[:, j, gi * 512:(gi + 1) * 512], s, t, ALU.mult),
                         reads=[s_b, t_b], writes=[hid_b[j][gi]])
            wd_v = wd_d.rearrange("(k p) n -> p k n", p=128)
            for i in range(8):
                wt, wt_b = wd.next()
                for hh in range(2):
                    st, st_b = stg_d.next()
                    S.dma("sp", st, wd_v[:, hh * 11:(hh + 1) * 11, i * 128:(i + 1) * 128], st_b, writes=[st_b])
                    cast(wt[:, hh * 11:(hh + 1) * 11, :], wt_b, st, st_b)
                for gi in range(2):
                    pa, pa_b = ps_a.next()
                    for k in range(22):
                        S.op("pe", lambda E, k=k, pa=pa, wt=wt, gi=gi: E.matmul(pa[:, :], wt[:, k, :], hid[:, k, gi * 512:(gi + 1) * 512],
                                                                                 start=(k == 0), stop=(k == 21)),
                             reads=[wt_b, hid_b[k][gi]], writes=[pa_b])
                    grp = half * 2 + gi
                    xs = xT[:, i, grp * 512:(grp + 1) * 512]
                    S.op("dve", lambda E, xs=xs, pa=pa: E.scalar_tensor_tensor(xs, pa[:, :], 0.5, xs, ALU.mult, ALU.add),
                         reads=[pa_b, xTb[i][grp]], writes=[xTb[i][grp]])
            S.barrier()

    def out_proj(mix, mix_b, w_dram, wfm, ps_o):
        wv = w_dram.rearrange("(c p) n -> p c n", p=128)
        for i in range(8):
            wt, wt_b = wfm.next()
            wload(wv[:, :, i * 128:(i + 1) * 128], 8, 128, wt, wt_b)
            for g in range(4):
                po, po_b = ps_o.next()
                for c in range(8):
                    mb = mix_b[c][g] if isinstance(mix_b[c], list) else mix_b[c]
                    S.op("pe", lambda E, c=c, po=po, wt=wt, g=g: E.matmul(po[:, :], wt[:, c, :], mix[c][:, g * 512:(g + 1) * 512],
                                                                           start=(c == 0), stop=(c == 7)),
                         reads=[wt_b, mb], writes=[po_b])
                xs = xT[:, i, g * 512:(g + 1) * 512]
                S.op("dve", lambda E, xs=xs, po=po: E.tensor_tensor(xs, po[:, :], xs, ALU.add),
                     reads=[po_b, xTb[i][g]], writes=[xTb[i][g]])

    def gelu(x_ap, x_b, out_ap, out_b, ga, gb):
        (a, a_b), (b, b_b) = ga, gb
        S.op("act", lambda E: E.activation(a, x_ap, AF.Square), reads=[x_b], writes=[a_b])
        S.op("dve", lambda E: E.tensor_scalar(a, a, 0.044715, 1.0, ALU.mult, ALU.add), reads=[a_b], writes=[a_b])
        S.op("dve", lambda E: E.tensor_tensor(a, a, x_ap, ALU.mult), reads=[a_b, x_b], writes=[a_b])
        S.op("act", lambda E: E.activation(b, a, AF.Sigmoid, scale=1.5957691216057308), reads=[a_b], writes=[b_b])
        S.op("pool", lambda E: E.tensor_tensor(out_ap, x_ap, b, ALU.mult), reads=[x_b, b_b], writes=[out_b])

    def allgather(src_ap, src_b, dst_ap, dst_b, rows, rc):
        S._deps("pool", [src_b], [dst_b])
        h = S.sem("cc")
        S.cnt.setdefault("cc", 0)
        for i in range(rows // rc):
            S.cnt["cc"] += 1
            S.prog["pool"].append(lambda E, i=i: E.collective_compute("AllGather", ALU.bypass, replica_groups=[[0, 1, 2, 3], [4, 5, 6, 7]],
                                                                      ins=[src_ap[i * rc:(i + 1) * rc, :]],
                                                                      outs=[dst_ap[i * 4 * rc:(i + 1) * 4 * rc, :]]).then_inc(h, 1))
        S._upd(("cc", S.cnt["cc"]), [src_b], [dst_b])

    def grow(rc, rk, row):
        return (row // rc) * 4 * rc + rk * rc + (row % rc)

    def phase_A0():
        cf.reset()
        ch.reset()
        rstd = [cf.take(512) for _ in range(4)]
        rstd_b = [B() for _ in range(4)]
        tmp, tmp_b = cf.take(512), B()
        rtok, rtok_b = cf.take(16), B()
        rtmp, rtmp_b = cf.take(16), B()
        wsetup()
        zt = rotf(2, 512)
        ga, gb = (cf.take(512), B()), (cf.take(512), B())
        gl = (cf.take(512), B())
        bc, bc_b = cf.take(1536), B()
        trim, trim_b = cf.take(128), B()
        st6, st6_b = cf.take(8), B()
        mv, mv_b = cf.take(4), B()
        xb = ch.take(16384).rearrange("p (c n) -> p c n", c=8)
        xb_b = [B() for _ in range(4)]
        sq, sq_b = ch.take(4096).rearrange("p (c n) -> p c n", c=8), B()
        Wv, Wv_b = ch.take(4096).rearrange("p (c n) -> p c n", c=8), B()
        Wzg, Wzg_b = ch.take(4096).rearrange("p (c n) -> p c n", c=8), B()
        wfm = roth(2, 1024, lambda a: a.rearrange("p (c n) -> p c n", c=8))
        uT = ch.take(8192).rearrange("p (c n) -> p c n", c=4)
        uT_b = [[B() for g in range(4)] for c in range(4)]
        WsT, WsT_b = ch.take(1024).rearrange("p (g n) -> p g n", g=8), B()
        ev = roth(2, 512)
        vev = roth(2, 512)
        gtok = roth(2, 512)
        osg = roth(2, 512, lambda a: a.rearrange("p (c n) -> p c n", c=4))
        ps_ss_b = B("ps")
        tokps = (psum[7][:, 0:16], B("ps"))
        ps_p = psrot([1, 2, 3])
        ps_m = psrot([4, 5])
        S.dma("sp", bc, bc0_d[:, :], bc_b, writes=[bc_b])
        S.dma("sp", trim, trim_d[:, :], trim_b, writes=[trim_b])
        for hh in range(4):
            st, st_b = W.st.next()
            S.dma("sp", st[:, 0:256], sguwT_d[:, hh * 256:(hh + 1) * 256], st_b, writes=[st_b])
            for k in range(2):
                gidx = hh * 2 + k
                S.op("dve", lambda E, st=st, k=k, gidx=gidx: E.tensor_tensor(WsT[:, gidx, :], st[:, k * 128:(k + 1) * 128], trim, ALU.mult),
                     reads=[st_b, trim_b], writes=[WsT_b])
        for g in range(4):
            norm_prep("mix", 0, g, xb[:, :, g * 512:(g + 1) * 512], xb_b[g], rstd[g], rstd_b[g], sq, sq_b, 0, ps_ss_b, tmp, tmp_b, tokps=tokps)
        rstd_from_ss(rtok, rtok_b, tokps[0], tokps[1], D, rtmp, rtmp_b)
        win = Wd["sbg_in"].rearrange("(c p) n -> p c n", p=128)
        for hh in range(2):
            wload(win[:, hh * 4:(hh + 1) * 4, 1024:1536], 4, 512, Wv[:, hh * 4:(hh + 1) * 4, :], Wv_b)
            wload(win[:, hh * 4:(hh + 1) * 4, 2048:2560], 4, 512, Wzg[:, hh * 4:(hh + 1) * 4, :], Wzg_b)
        for oc in list(range(0, 8)) + list(range(12, 16)):
            wt, wt_b = wfm.next()
            wload(win[:, :, oc * 128:(oc + 1) * 128], 8, 128, wt, wt_b)
            for g in range(4):
                pp, pp_b = ps_p.next()
                for c in range(8):
                    S.op("pe", lambda E, c=c, pp=pp, wt=wt, g=g: E.matmul(pp[:, :], wt[:, c, :], xb[:, c, g * 512:(g + 1) * 512],
                                                                           start=(c == 0), stop=(c == 7)),
                         reads=[wt_b, xb_b[g]], writes=[pp_b])
                if oc < 8:
                    e, e_b = ev.next()
                    sc = 0.125 if oc < 4 else 1.0
                    S.op("dve", lambda E, e=e, pp=pp, g=g, sc=sc: E.scalar_tensor_tensor(e, pp[:, :], sc, rstd[g], ALU.mult, ALU.mult),
                         reads=[pp_b, rstd_b[g]], writes=[e_b])
                    dst, dn = (q0_d, "q0") if oc < 4 else (k0_d, "k0")
                    r0 = (oc % 4) * 128
                    S.dma("sp", dst[r0:r0 + 128, g * 512:(g + 1) * 512], e, e_b, reads=[e_b], writes=[db[dn]])
                else:
                    z, z_b = zt.next()
                    S.op("dve", lambda E, z=z, pp=pp, g=g: E.tensor_tensor(z, pp[:, :], rstd[g], ALU.mult),
                         reads=[pp_b, rstd_b[g]], writes=[z_b])
                    gelu(z, z_b, uT[:, oc - 12, g * 512:(g + 1) * 512], uT_b[oc - 12][g], ga, gb)
        for t in range(NT):
            g = t // 4
            pp, pp_b = ps_p.next()
            for c in range(8):
                S.op("pe", lambda E, c=c, pp=pp, t=t: E.matmul(pp[:, :], xb[:, c, t * 128:(t + 1) * 128], Wv[:, c, :],
                                                               start=(c == 0), stop=(c == 7)),
                     reads=[Wv_b, xb_b[g]], writes=[pp_b])
            e, e_b = vev.next()
            S.op("dve", lambda E, e=e, pp=pp, t=t: E.tensor_scalar(e, pp[:, :], rtok[:, t:t + 1], None, ALU.mult),
                 reads=[pp_b, rtok_b], writes=[e_b])
            S.dma("sp", v0_d[t * 128:(t + 1) * 128, :], e, e_b, reads=[e_b], writes=[db["v0"]])
            pp, pp_b = ps_p.next()
            for c in range(8):
                S.op("pe", lambda E, c=c, pp=pp, t=t: E.matmul(pp[:, :], xb[:, c, t * 128:(t + 1) * 128], Wzg[:, c, :],
                                                               start=(c == 0), stop=(c == 7)),
                     reads=[Wzg_b, xb_b[g]], writes=[pp_b])
            z, z_b = zt.next()
            S.op("dve", lambda E, z=z, pp=pp, t=t: E.tensor_scalar(z, pp[:, :], rtok[:, t:t + 1], None, ALU.mult),
                 reads=[pp_b, rtok_b], writes=[z_b])
            gelu(z, z_b, gl[0], gl[1], ga, gb)
            S.op("dve", lambda E: E.bn_stats(st6[:, 0:6], gl[0]), reads=[gl[1]], writes=[st6_b])
            S.op("dve", lambda E: E.bn_aggr(mv[:, 0:2], st6[:, 0:6]), reads=[st6_b], writes=[mv_b])
            S.op("act", lambda E: E.activation(mv[:, 2:3], mv[:, 1:2], AF.Ln, bias=gcol("eps")), reads=[mv_b, gcols_b], writes=[mv_b])
            S.op("act", lambda E: E.activation(mv[:, 3:4], mv[:, 2:3], AF.Exp, scale=-0.5), reads=[mv_b], writes=[mv_b])
            S.op("dve", lambda E: E.tensor_scalar(gl[0], gl[0], mv[:, 0:1], mv[:, 3:4], ALU.subtract, ALU.mult),
                 reads=[gl[1], mv_b], writes=[gl[1]])
            S.op("dve", lambda E: E.tensor_tensor(gl[0], gl[0], bc[:, 0:512], ALU.mult), reads=[gl[1], bc_b], writes=[gl[1]])
            gt, gt_b = gtok.next()
            S.op("dve", lambda E, gt=gt: E.tensor_tensor(gt, gl[0], bc[:, 512:1024], ALU.add), reads=[gl[1], bc_b], writes=[gt_b])
            pm, pm_b = ps_m.next()
            for grp in range(8):
                hb = 64 * (grp % 2)
                cc = grp // 2
                S.op("pe", lambda E, pm=pm, gt=gt, grp=grp, hb=hb, cc=cc: E.matmul(pm[hb:hb + 64, cc * 128:(cc + 1) * 128],
                                                                                  gt[:, grp * 64:(grp + 1) * 64], WsT[:, grp, :],
                                                                                  start=True, stop=True),
                     reads=[gt_b, WsT_b], writes=[pm_b])
            z2, z2_b = zt.next()
            S.op("dve", lambda E, z2=z2, pm=pm: E.tensor_tensor(z2, pm[:, :], bc[:, 1024:1536], ALU.add), reads=[pm_b, bc_b], writes=[z2_b])
            og, og_b = osg.next()
            S.op("pool", lambda E, og=og, z2=z2, t=t: E.tensor_tensor(og, z2.rearrange("p (c n) -> p c n", c=4),
                                                                     uT[:, :, t * 128:(t + 1) * 128], ALU.mult),
                 reads=[z2_b] + [uT_b[c][g] for c in range(4)], writes=[og_b])
            S.dma("sp", osg_d.rearrange("(c p) n -> p c n", p=128)[:, :, t * 128:(t + 1) * 128], og, og_b, reads=[og_b], writes=[db["osg"]])
        S.barrier()
        if exch == "cc":
            allgather(k0_d, db["k0"], kg0_d, db["kg0"], 512, RC_K0)
            allgather(v0_d, db["v0"], vg0_d, db["vg0"], T, RC_V0)
        S.barrier()

    def key_loc(j):
        tj, mj = j // 4, j % 4
        rk = mj if tj % 2 == 0 else 3 - mj
        return rk, tj, mj

    def phase_B0():
        cf.reset()
        ch.reset()
        Rs = [(cf.take(512), B()) for _ in range(2)]
        ttr = rotf(4, 512)
        e2r = rotf(4, 512)
        wsetup()
        Kr = roth(2, 8192)
        Vc, Vc_b = ch.take(8192).rearrange("p (r t d) -> p r t d", r=4, t=16), B()
        qr = roth(2, 2048)
        Lpr = roth(4, 512)
        wvr = roth(6, 512)
        osb = ch.take(8192).rearrange("p (c n) -> p c n", c=4)
        osb_b = [[B() for g in range(4)] for c in range(4)]
        msk, msk_b = ch.take(1024).rearrange("p (v n) -> p v n", v=8), B()
        wfm = roth(2, 1024, lambda a: a.rearrange("p (c n) -> p c n", c=8))
        ps_e, ps_t, ps_o = psrot([0, 1, 2, 3]), psrot([4, 5]), psrot([6, 7])
        S.dma("sp", msk, masks_d[:, 0:1024].rearrange("p (v n) -> p v n", v=8), msk_b, writes=[msk_b])
        for c in range(4):
            Kc, Kc_b = Kr.next()
            for rk in range(4):
                S.dma("sp", Kc[:, rk * T:(rk + 1) * T], kg0_d[grow(RC_K0, rk, c * 128):grow(RC_K0, rk, c * 128) + 128, :], Kc_b,
                      reads=[db["kg0"]], writes=[Kc_b])
                for tq in range(4):
                    S.dma("sp", Vc[:, rk, tq * 4:(tq + 1) * 4, :],
                          vg0_d[grow(RC_V0, rk, tq * 512):grow(RC_V0, rk, tq * 512) + 512, c * 128:(c + 1) * 128].rearrange("(t p) d -> p t d", p=128), Vc_b,
                          reads=[db["vg0"]], writes=[Vc_b])
            qc, qc_b = qr.next()
            S.dma("sp", qc, q0_d[c * 128:(c + 1) * 128, :], qc_b, reads=[db["q0"]], writes=[qc_b])
            for u in range(4):
                js = list(range(16 * u + 15, -1, -1))
                pos_ = [ps_o.next() for _ in range(2)]
                st2q, st3q = [[], []], [[], []]
                for hh in range(2):
                    S.op("pool", lambda E: E.memset(Rs[hh][0], 0.0), writes=[Rs[hh][1]])
                for idx, j in enumerate(js):
                    rk, tj, mj = key_loc(j)
                    kcol = rk * T + tj * 128
                    diag = j >= 16 * u
                    col0, vidx = 0, 0
                    if diag:
                        m = (j - 16 * u) // 4
                        col0 = 128 * m
                        vidx = ((4 * u + m) % 2) * 4 + mj
                    for hh in range(2):
                        hb = 64 * hh
                        R, R_b = Rs[hh]
                        po, po_b = pos_[hh]
                        qsl = qc[hb:hb + 64, u * 512 + col0:(u + 1) * 512]
                        ksl = Kc[hb:hb + 64, kcol:kcol + 128]
                        pE, pE_b = ps_e.next()
                        tt, tt_b = ttr.next()
                        Lp, Lp_b = Lpr.next()
                        S.op("pe", lambda E: E.matmul(pE[:, col0:], ksl, qsl, start=True, stop=False), reads=[Kc_b, qc_b], writes=[pE_b])
                        if diag:
                            S.op("pe", lambda E: E.matmul(pE[:, col0:col0 + 128], ident_bf, msk[:, vidx, :], start=False, stop=False, skip_group_check=True),
                                 reads=[cbf_b, msk_b], writes=[pE_b])
                        S.op("act", lambda E: E.activation(tt[:, col0:], pE[:, col0:], AF.Exp), reads=[pE_b], writes=[tt_b])
                        S.op("act", lambda E: E.activation(Lp[:, col0:], tt[:, col0:], AF.Ln, bias=gcol("one")), reads=[tt_b, gcols_b], writes=[Lp_b])

                        def stage2(pE=pE, pE_b=pE_b, Lp=Lp, Lp_b=Lp_b, col0=col0, rk=rk, tj=tj, idx=idx, j=j, hh=hh, hb=hb, R=R, R_b=R_b, po=po, po_b=po_b):
                            pT, pT_b = ps_t.next()
                            e2, e2_b = e2r.next()
                            wv, wv_b = wvr.next()
                            S.op("pe", lambda E: E.matmul(pE[:, col0:], negU_bf, Lp[:, col0:], start=False, stop=True, skip_group_check=True),
                                 reads=[cbf_b, Lp_b], writes=[pE_b])
                            S.op("pe", lambda E: E.matmul(pT[:, col0:], ones_bf, Lp[:, col0:], start=True, stop=True), reads=[cbf_b, Lp_b], writes=[pT_b])
                            S.op("dve", lambda E: E.tensor_tensor(e2[:, col0:], pE[:, col0:], R[:, col0:], ALU.subtract), reads=[pE_b, R_b], writes=[e2_b])
                            S.op("act", lambda E: E.activation(wv[:, col0:], e2[:, col0:], AF.Exp), reads=[e2_b], writes=[wv_b])
                            S.op("dve", lambda E: E.tensor_tensor(R[:, col0:], pT[:, col0:], R[:, col0:], ALU.add), reads=[pT_b, R_b], writes=[R_b])

                            def stage3():
                                S.op("pe", lambda E: E.matmul(po[hb:hb + 64, col0:], Vc[:, rk, tj, hb:hb + 64], wv[:, col0:], start=(idx == 0), stop=(j == 0),
                                                              skip_group_check=True),
                                     reads=[Vc_b, wv_b], writes=[po_b])
                            st3q[hh].append(stage3)
                        st2q[hh].append(stage2)
                        if len(st2q[hh]) > 1:
                            st2q[hh].pop(0)()
                        if len(st3q[hh]) > 1:
                            st3q[hh].pop(0)()
                for hh in range(2):
                    while st2q[hh]:
                        st2q[hh].pop(0)()
                for hh in range(2):
                    while st3q[hh]:
                        st3q[hh].pop(0)()
                for hh in range(2):
                    hb = 64 * hh
                    po, po_b = pos_[hh]
                    S.op("dve", lambda E: E.tensor_copy(osb[hb:hb + 64, c, u * 512:(u + 1) * 512], po[hb:hb + 64, :]),
                         reads=[po_b], writes=[osb_b[c][u]])
        Kc, Kc_b = Kr.next()
        osgs = Kc.rearrange("p (c n) -> p c n", c=4)
        S.dma("sp", osgs, osg_d.rearrange("(c p) n -> p c n", p=128), Kc_b, reads=[db["osg"]], writes=[Kc_b])
        mix = [osb[:, c, :] for c in range(4)] + [osgs[:, c, :] for c in range(4)]
        mix_b = [osb_b[c] for c in range(4)] + [Kc_b] * 4
        out_proj(mix, mix_b, Wd["sbg_out"], wfm, psrot([0, 1, 2]))
        S.barrier()

    def phase_X(l):
        cf.reset()
        ch.reset()
        rstd1, rstd1_b = cf.take(512), B()
        tmp, tmp_b = cf.take(512), B()
        wsetup()
        mst = rotf(1, 1024)
        msmall, msmall_b = cf.take(8), B()
        qraw = rotf(1, 1024, lambda a: a.rearrange("p (c n) -> p c n", c=2))
        rq = rotf(2, 512)
        kraw = rotf(1, 512, lambda a: a.rearrange("p (c n) -> p c n", c=2))
        rk_ = rotf(2, 256)
        rcp = rotf(1, 512)
        xb = ch.take(4096).rearrange("p (c n) -> p c n", c=8)
        xb1_b = B()
        sq, sq_b = ch.take(4096).rearrange("p (c n) -> p c n", c=8), B()
        hmT, hmT_b = ch.take(2048).rearrange("p (c n) -> p c n", c=8), B()
        mrow = roth(1, 1024)
        kT, kT_b = ch.take(2048).rearrange("p (h c n) -> p h c n", h=4, c=2), B()
        Vm, Vm_b = ch.take(2048).rearrange("p (m n) -> p m n", m=2), B()
        wfm = roth(2, 1024, lambda a: a.rearrange("p (c n) -> p c n", c=8))
        Wv_ = roth(1, 4096, lambda a: a.rearrange("p (c n) -> p c n", c=8))
        qn = roth(2, 1024, lambda a: a.rearrange("p (c n) -> p c n", c=2))
        sqq = roth(2, 1024, lambda a: a.rearrange("p (c n) -> p c n", c=2))
        Pm = roth(2, 1024, lambda a: a.rearrange("p (m n) -> p m n", m=2))
        oT = ch.take(4096).rearrange("p (c n) -> p c n", c=8)
        oT_b = [B() for c in range(8)]
        ps_ss_b = B("ps")
        ps_p = psrot([1, 2])
        ps_q = psrot([3])
        ps_s2 = psrot([4, 5])
        ps_o2 = psrot([6, 7])
        for mt in range(2):
            ms, ms_b = mst.next()
            S.dma("sp", ms, mem_d[mt * 128:(mt + 1) * 128, :], ms_b, writes=[ms_b])
            mr, mr_b = mrow.next()
            S.op("act", lambda E, ms=ms, mr=mr: E.activation(mr, ms, AF.Square, accum_out=msmall[:, 0:1]), reads=[ms_b], writes=[mr_b, msmall_b])
            S.op("act", lambda E: E.activation(msmall[:, 1:2], msmall[:, 0:1], AF.Ln, bias=gcol("eps"), scale=1.0 / D),
                 reads=[msmall_b, gcols_b], writes=[msmall_b])
            S.op("act", lambda E: E.activation(msmall[:, 2:3], msmall[:, 1:2], AF.Exp, scale=-0.5), reads=[msmall_b], writes=[msmall_b])
            S.op("dve", lambda E, ms=ms: E.tensor_scalar(ms, ms, msmall[:, 2:3], None, ALU.mult), reads=[ms_b, msmall_b], writes=[ms_b])
            for half in range(2):
                pp, pp_b = ps_p.next()
                for cc in range(4):
                    c = half * 4 + cc
                    S.op("pe", lambda E, pp=pp, ms=ms, c=c, cc=cc: E.transpose(pp[:, cc * 128:(cc + 1) * 128], ms[:, c * 128:(c + 1) * 128], ident_f[:, :]),
                         reads=[ms_b, ident_f_b], writes=[pp_b])
                for cc in range(4):
                    c = half * 4 + cc
                    S.op("dve", lambda E, pp=pp, c=c, cc=cc, mt=mt: E.tensor_scalar(hmT[:, c, mt * 128:(mt + 1) * 128], pp[:, cc * 128:(cc + 1) * 128],
                                                                                     gcol("xmn", 8 * l + c), None, ALU.mult),
                         reads=[pp_b, gcols_b], writes=[hmT_b])
        wk = Wd[("xk", l)].rearrange("(c p) n -> p c n", p=128)
        for hd in range(4):
            kr_, kr_b = kraw.next()
            sk, sk_b = sqq.next()
            for dc in range(2):
                oc = hd * 2 + dc
                wt, wt_b = wfm.next()
                wload(wk[:, :, oc * 128:(oc + 1) * 128], 8, 128, wt, wt_b)
                pp, pp_b = ps_p.next()
                for c in range(8):
                    S.op("pe", lambda E, c=c, pp=pp, wt=wt: E.matmul(pp[:, 0:256], wt[:, c, :], hmT[:, c, :], start=(c == 0), stop=(c == 7)),
                         reads=[wt_b, hmT_b], writes=[pp_b])
                S.op("act", lambda E, kr_=kr_, pp=pp, dc=dc: E.copy(kr_[:, dc, :], pp[:, 0:256]), reads=[pp_b], writes=[kr_b])
                S.op("act", lambda E, sk=sk, pp=pp, dc=dc: E.activation(sk[:, dc, 0:256], pp[:, 0:256], AF.Square), reads=[pp_b], writes=[sk_b])
            pq, pq_b = ps_q.next()
            for dc in range(2):
                S.op("pe", lambda E, pq=pq, sk=sk, dc=dc: E.matmul(pq[:, 0:256], ones_bf, sk[:, dc, 0:256], start=(dc == 0), stop=(dc == 1)),
                     reads=[sk_b, cbf_b], writes=[pq_b])
            rr, rr_b = rk_.next()
            rstd_from_ss(rr, rr_b, pq[:, 0:256], pq_b, 256, tmp[:, 0:256], tmp_b)
            for dc in range(2):
                S.op("dve", lambda E, kr_=kr_, rr=rr, dc=dc, hd=hd: E.scalar_tensor_tensor(kT[:, hd, dc, :], kr_[:, dc, :], gcol("xkg", 2 * l + dc), rr,
                                                                                           ALU.mult, ALU.mult),
                     reads=[kr_b, rr_b, gcols_b], writes=[kT_b])
        wvv = Wd[("xv", l)].rearrange("(c p) n -> p c n", p=128)
        for nh in range(2):
            wt, wt_b = Wv_.next()
            for hh in range(2):
                wload(wvv[:, hh * 4:(hh + 1) * 4, nh * 512:(nh + 1) * 512], 4, 512, wt[:, hh * 4:(hh + 1) * 4, :], wt_b)
            for mt in range(2):
                pp, pp_b = ps_p.next()
                for c in range(8):
                    S.op("pe", lambda E, c=c, pp=pp, wt=wt, mt=mt: E.matmul(pp[:, :], hmT[:, c, mt * 128:(mt + 1) * 128], wt[:, c, :],
                                                                             start=(c == 0), stop=(c == 7)),
                         reads=[wt_b, hmT_b], writes=[pp_b])
                S.op("act", lambda E, pp=pp, mt=mt, nh=nh: E.copy(Vm[:, mt, nh * 512:(nh + 1) * 512], pp[:, :]), reads=[pp_b], writes=[Vm_b])
        wq = Wd[("xq", l)].rearrange("(c p) n -> p c n", p=128)
        wo = Wd[("xo", l)].rearrange("(c p) n -> p c n", p=128)
        ps_x = psrot([1, 2])
        for g in range(4):
            norm_prep("xn", 8 * l, g, xb, xb1_b, rstd1, rstd1_b, sq, sq_b, 0, ps_ss_b, tmp, tmp_b)
            for hd in range(4):
                qr_, qr_b = qraw.next()
                sk, sk_b = sqq.next()
                for dc in range(2):
                    oc = hd * 2 + dc
                    wt, wt_b = wfm.next()
                    wload(wq[:, :, oc * 128:(oc + 1) * 128], 8, 128, wt, wt_b)
                    pp, pp_b = ps_p.next()
                    for c in range(8):
                        S.op("pe", lambda E, c=c, pp=pp, wt=wt: E.matmul(pp[:, :], wt[:, c, :], xb[:, c, :], start=(c == 0), stop=(c == 7)),
                             reads=[wt_b, xb1_b], writes=[pp_b])
                    S.op("dve", lambda E, qr_=qr_, pp=pp, dc=dc: E.tensor_tensor(qr_[:, dc, :], pp[:, :], rstd1, ALU.mult),
                         reads=[pp_b, rstd1_b], writes=[qr_b])
                    S.op("act", lambda E, sk=sk, qr_=qr_, dc=dc: E.activation(sk[:, dc, :], qr_[:, dc, :], AF.Square), reads=[qr_b], writes=[sk_b])
                pq, pq_b = ps_q.next()
                for dc in range(2):
                    S.op("pe", lambda E, pq=pq, sk=sk, dc=dc: E.matmul(pq[:, :], ones_bf, sk[:, dc, :], start=(dc == 0), stop=(dc == 1)),
                         reads=[sk_b, cbf_b], writes=[pq_b])
                rr, rr_b = rq.next()
                rstd_from_ss(rr, rr_b, pq[:, :], pq_b, 256, tmp, tmp_b)
                qq, qq_b = qn.next()
                for dc in range(2):
                    S.op("dve", lambda E, qq=qq, qr_=qr_, rr=rr, dc=dc: E.scalar_tensor_tensor(qq[:, dc, :], qr_[:, dc, :], gcol("xqg", 2 * l + dc), rr,
                                                                                              ALU.mult, ALU.mult),
                         reads=[qr_b, rr_b, gcols_b], writes=[qq_b])
                pm_, pm_b = Pm.next()
                for mt in range(2):
                    pS, pS_b = ps_s2.next()
                    for dc in range(2):
                        S.op("pe", lambda E, pS=pS, qq=qq, dc=dc, mt=mt, hd=hd: E.matmul(pS[:, :], kT[:, hd, dc, mt * 128:(mt + 1) * 128], qq[:, dc, :],
                                                                                         start=(dc == 0), stop=(dc == 1)),
                             reads=[kT_b, qq_b], writes=[pS_b])
                    S.op("act", lambda E, pm_=pm_, pS=pS, mt=mt: E.activation(pm_[:, mt, :], pS[:, :], AF.Exp, scale=1.0 / 16.0),
                         reads=[pS_b], writes=[pm_b])
                pq, pq_b = ps_q.next()
                for mt in range(2):
                    S.op("pe", lambda E, pq=pq, pm_=pm_, mt=mt: E.matmul(pq[:, :], ones_bf, pm_[:, mt, :], start=(mt == 0), stop=(mt == 1)),
                         reads=[pm_b, cbf_b], writes=[pq_b])
                rc, rc_b = rcp.next()
                S.op("dve", lambda E, rc=rc, pq=pq: E.reciprocal(rc, pq[:, :]), reads=[pq_b], writes=[rc_b])
                for dc in range(2):
                    po, po_b = ps_o2.next()
                    for mt in range(2):
                        S.op("pe", lambda E, po=po, pm_=pm_, mt=mt, dc=dc, hd=hd: E.matmul(po[:, :], Vm[:, mt, hd * 256 + dc * 128:hd * 256 + (dc + 1) * 128],
                                                                                           pm_[:, mt, :], start=(mt == 0), stop=(mt == 1)),
                             reads=[Vm_b, pm_b], writes=[po_b])
                    S.op("dve", lambda E, po=po, rc=rc, hd=hd, dc=dc: E.tensor_tensor(oT[:, hd * 2 + dc, :], po[:, :], rc, ALU.mult),
                         reads=[po_b, rc_b], writes=[oT_b[hd * 2 + dc]])
            for i in range(8):
                wt, wt_b = wfm.next()
                wload(wo[:, :, i * 128:(i + 1) * 128], 8, 128, wt, wt_b)
                po, po_b = ps_x.next()
                for c in range(8):
                    S.op("pe", lambda E, c=c, po=po, wt=wt: E.matmul(po[:, :], wt[:, c, :], oT[:, c, :], start=(c == 0), stop=(c == 7)),
                         reads=[wt_b, oT_b[c]], writes=[po_b])
                xs = xT[:, i, g * 512:(g + 1) * 512]
                S.op("dve", lambda E, xs=xs, po=po: E.tensor_tensor(xs, po[:, :], xs, ALU.add), reads=[po_b, xTb[i][g]], writes=[xTb[i][g]])
        S.barrier()

    def sin_reduced(out_ap, out_b, ang_ap, ang_b, wk, wk_b, nf, nf_b, p0, p1, add=0.0):
        ni = nint[p0:p1, :]
        S.op("dve", lambda E: E.tensor_scalar(wk, ang_ap, float(add), None, ALU.add), reads=[ang_b], writes=[wk_b])
        S.op("dve", lambda E: E.tensor_scalar(ni, wk, float(1 / TWO_PI), None, ALU.mult), reads=[wk_b], writes=[nint_b])
        S.op("dve", lambda E: E.tensor_copy(nf, ni), reads=[nint_b], writes=[nf_b])
        S.op("dve", lambda E: E.scalar_tensor_tensor(wk, nf, -TWO_PI, wk, ALU.mult, ALU.add), reads=[nf_b, wk_b], writes=[wk_b])
        S.op("dve", lambda E: E.tensor_scalar(nf, wk, PI, TWO_PI, ALU.is_gt, ALU.mult), reads=[wk_b], writes=[nf_b])
        S.op("dve", lambda E: E.tensor_tensor(wk, wk, nf, ALU.subtract), reads=[nf_b, wk_b], writes=[wk_b])
        S.op("dve", lambda E: E.tensor_scalar(nf, wk, -PI, TWO_PI, ALU.is_lt, ALU.mult), reads=[wk_b], writes=[nf_b])
        S.op("dve", lambda E: E.tensor_tensor(wk, wk, nf, ALU.add), reads=[nf_b, wk_b], writes=[wk_b])
        S.op("act", lambda E: E.activation(out_ap, wk, AF.Sin), reads=[wk_b], writes=[out_b])

    class _Stop(Exception):
        pass

    def _stage(n):
        for k in range(1, 9):
            if ('a1s%d' % k) in DBG and n >= k:
                raise _Stop()

    def phase_A1():
        try:
            phase_A1_()
        except _Stop:
            S.barrier()

    def phase_A1_():
        cf.reset()
        ch.reset()
        rstd = [cf.take(512) for _ in range(4)]
        rstd_b = [B() for _ in range(4)]
        tmp, tmp_b = cf.take(512), B()
        wsetup(2, 1024)
        tabs = [cf.take(512) for _ in range(4)]
        tab_b = [B() for _ in range(4)]
        cqraw, cqraw_b = cf.take(2048).rearrange("p (c n) -> p c n", c=4), B()
        fa, fa_b = cf.take(512), B()
        fb, fb_b = cf.take(512), B()
        fc, fc_b = cf.take(512), B()
        kra, kra_b = cf.take(512), B()
        rqk = rotf(2, 512)
        xb = ch.take(16384).rearrange("p (c n) -> p c n", c=8)
        xb_b = [B() for _ in range(4)]
        sq, sq_b = ch.take(4096).rearrange("p (c n) -> p c n", c=8), B()
        Win, Win_b = ch.take(6400).rearrange("p (c n) -> p c n", c=8), B()
        Wuq, Wuq_b = ch.take(6144).rearrange("p (c n) -> p c n", c=4), B()
        Wqs, Wqs_b = ch.take(2048).rearrange("p (c n) -> p c n", c=4), B()
        Wuk, Wuk_b = ch.take(2048).rearrange("p (c n) -> p c n", c=2), B()
        Wuv, Wuv_b = ch.take(2048).rearrange("p (c n) -> p c n", c=2), B()
        cqn, cqn_b = ch.take(2048).rearrange("p (c n) -> p c n", c=4), B()
        ckvn, ckvn_b = ch.take(1024).rearrange("p (c n) -> p c n", c=2), B()
        krb, krb_b = ch.take(512), B()
        sqh = ch.take(512)
        sqh_lo, sqh_hi = B(), B()
        sqq, sqq_b = ch.take(512), B()
        qst = roth(1, 512)
        vst = roth(1, 512)
        ps_ss_b = B("ps")
        ps_p = psrot([1, 2])
        ps_a = psrot([3, 4])
        ps_b2 = psrot([5])
        ps_q = psrot([6])
        ps_r = psrot([7])

        def wl(src, kc, ncol, dst, dst_b):
            v = src.rearrange("(c p) n -> p c n", p=128)
            step = max(128, (1024 // kc) // 128 * 128) if ncol >= 128 else ncol
            for n0 in range(0, ncol, step):
                n1 = min(ncol, n0 + step)
                wload(v[:, :, n0:n1], kc, n1 - n0, dst[:, :, n0:n1], dst_b)

        wl(Wd["mla_in"], 8, 768, Win, Win_b)
        wload(Wd["mla_in"].rearrange("(c p) n -> p c n", p=128)[:, :, 768:800], 8, 32, Win[:, :, 768:800], Win_b)
        wl(Wd["mla_uq"], 4, 1536, Wuq, Wuq_b)
        wl(Wd["mla_uq_sw"], 4, 512, Wqs, Wqs_b)
        wl(Wd["mla_uk"], 2, 1024, Wuk, Wuk_b)
        wl(Wd["mla_uv"], 2, 1024, Wuv, Wuv_b)
        for g in range(4):
            norm_prep("mix", 8, g, xb[:, :, g * 512:(g + 1) * 512], xb_b[g], rstd[g], rstd_b[g], sq, sq_b, 0, ps_ss_b, tmp, tmp_b)
        _stage(1)
        P0, P1 = 64, 96
        for g in range(1 if 'a1small' in DBG else 4):
            gs = slice(g * 512, (g + 1) * 512)
            S.dma("sp", posi[P0:P1, :], posrep_d[:, gs], posi_b, writes=[posi_b])
            S.op("dve", lambda E: E.tensor_copy(fa[P0:P1, :], posi[P0:P1, :]), reads=[posi_b], writes=[fa_b])
            S.op("dve", lambda E: E.tensor_scalar(fa[P0:P1, :], fa[P0:P1, :], gcol("invf", 0, P0, P1), None, ALU.mult),
                 reads=[fa_b, gcols_b], writes=[fa_b])
            sin_reduced(tabs[0][P0:P1, :], tab_b[0], fa[P0:P1, :], fa_b, fb[P0:P1, :], fb_b, fc[P0:P1, :], fc_b, P0, P1, add=PI / 2)
            sin_reduced(tabs[1][P0:P1, :], tab_b[1], fa[P0:P1, :], fa_b, fb[P0:P1, :], fb_b, fc[P0:P1, :], fc_b, P0, P1, add=0.0)
            S.op("dve", lambda E: E.tensor_scalar(tabs[2][P0:P1, :], tabs[0][P0:P1, :], gcol("mkg", 0, P0, P1), None, ALU.mult),
                 reads=[tab_b[0], gcols_b], writes=[tab_b[2]])
            S.op("dve", lambda E: E.tensor_scalar(tabs[3][P0:P1, :], tabs[1][P0:P1, :], gcol("sgn", 0, P0, P1), gcol("mkg_sw", 0, P0, P1), ALU.mult, ALU.mult),
                 reads=[tab_b[1], gcols_b], writes=[tab_b[3]])
            S.op("dve", lambda E: E.tensor_scalar(tabs[0][P0:P1, :], tabs[0][P0:P1, :], gcol("mqg", 0, P0, P1), None, ALU.mult),
                 reads=[tab_b[0], gcols_b], writes=[tab_b[0]])
            S.op("dve", lambda E: E.tensor_scalar(tabs[1][P0:P1, :], tabs[1][P0:P1, :], gcol("sgn", 0, P0, P1), gcol("mqg_sw", 0, P0, P1), ALU.mult, ALU.mult),
                 reads=[tab_b[1], gcols_b], writes=[tab_b[1]])
            _stage(2)
            for (n_ch, col0, dst, dst_b, gname, nfeat) in ((4, 0, cqn, cqn_b, "qlora", 512), (2, 512, ckvn, ckvn_b, "kvlora", 256)):
                for oc in range(n_ch):
                    pp, pp_b = ps_p.next()
                    for c in range(8):
                        S.op("pe", lambda E, c=c, pp=pp, oc=oc, col0=col0: E.matmul(pp[:, :], Win[:, c, col0 + oc * 128:col0 + (oc + 1) * 128], xb[:, c, gs],
                                                                                    start=(c == 0), stop=(c == 7)),
                             reads=[Win_b, xb_b[g]], writes=[pp_b])
                    S.op("dve", lambda E, pp=pp, oc=oc: E.tensor_tensor(cqraw[:, oc, :], pp[:, :], rstd[g], ALU.mult),
                         reads=[pp_b, rstd_b[g]], writes=[cqraw_b])
                    S.op("act", lambda E, oc=oc: E.activation(sq[:, oc, :], cqraw[:, oc, :], AF.Square), reads=[cqraw_b], writes=[sq_b])
                pq, pq_b = ps_q.next()
                for oc in range(n_ch):
                    S.op("pe", lambda E, pq=pq, oc=oc, n_ch=n_ch: E.matmul(pq[:, :], ones_bf, sq[:, oc, :], start=(oc == 0), stop=(oc == n_ch - 1)),
                         reads=[sq_b, cbf_b], writes=[pq_b])
                rr, rr_b = rqk.next()
                rstd_from_ss(rr, rr_b, pq[:, :], pq_b, nfeat, tmp, tmp_b)
                for oc in range(n_ch):
                    S.op("dve", lambda E, oc=oc, dst=dst, rr=rr, gname=gname: E.scalar_tensor_tensor(dst[:, oc, :], cqraw[:, oc, :], gcol(gname, oc), rr, ALU.mult, ALU.mult),
                         reads=[cqraw_b, rr_b, gcols_b], writes=[dst_b])
            _stage(3)
            pp, pp_b = ps_p.next()
            for c in range(8):
                S.op("pe", lambda E, c=c, pp=pp: E.matmul(pp[0:32, :], Win[:, c, 768:800], xb[:, c, gs], start=(c == 0), stop=(c == 7)),
                     reads=[Win_b, xb_b[g]], writes=[pp_b])
            S.op("dve", lambda E, pp=pp: E.tensor_tensor(krb[0:32, :], pp[0:32, :], rstd[g][0:32, :], ALU.mult), reads=[pp_b, rstd_b[g]], writes=[krb_b])
            if os.environ.get("KSUB") == "1":
                raise _Stop()
            pa, pa_b = ps_a.next()
            pb_, pb_b = ps_b2.next()
            S.op("pe", lambda E, pa=pa: E.matmul(pa[P0:P1, :], ident_bf[0:32, 0:32], krb[0:32, :], start=True, stop=True), reads=[krb_b, cbf_b], writes=[pa_b])
            S.op("pe", lambda E, pb_=pb_: E.matmul(pb_[P0:P1, :], swap_bf[0:32, 0:32], krb[0:32, :], start=True, stop=True), reads=[krb_b, cbf_b], writes=[pb_b])
            if os.environ.get("KSUB") == "2":
                raise _Stop()
            S.op("act", lambda E, pa=pa: E.activation(sqh[P0:P1, :], pa[P0:P1, :], AF.Square), reads=[pa_b], writes=[sqh_hi])
            if os.environ.get("KSUB") == "3":
                raise _Stop()
            S.op("act", lambda E, pa=pa: E.copy(kra[P0:P1, :], pa[P0:P1, :]), reads=[pa_b], writes=[kra_b])
            S.op("act", lambda E, pb_=pb_: E.copy(fb[P0:P1, :], pb_[P0:P1, :]), reads=[pb_b], writes=[fb_b])
            S.op("dve", lambda E: E.tensor_tensor(kra[P0:P1, :], kra[P0:P1, :], tabs[2][P0:P1, :], ALU.mult), reads=[kra_b, tab_b[2]], writes=[kra_b])
            S.op("dve", lambda E: E.tensor_tensor(fb[P0:P1, :], fb[P0:P1, :], tabs[3][P0:P1, :], ALU.mult), reads=[fb_b, tab_b[3]], writes=[fb_b])
            S.op("dve", lambda E: E.tensor_tensor(kra[P0:P1, :], kra[P0:P1, :], fb[P0:P1, :], ALU.add), reads=[kra_b, fb_b], writes=[kra_b])
            _stage(4)
            for h in range(2 if 'a1small' in DBG else 16):
                pa, pa_b = ps_a.next()
                pb_, pb_b = ps_b2.next()
                for c in range(4):
                    S.op("pe", lambda E, c=c, pa=pa, h=h: E.matmul(pa[0:96, :], Wuq[:, c, h * 96:(h + 1) * 96], cqn[:, c, :], start=(c == 0), stop=(c == 3)),
                         reads=[Wuq_b, cqn_b], writes=[pa_b])
                for c in range(4):
                    S.op("pe", lambda E, c=c, pb_=pb_, h=h: E.matmul(pb_[P0:P1, :], Wqs[:, c, h * 32:(h + 1) * 32], cqn[:, c, :], start=(c == 0), stop=(c == 3)),
                         reads=[Wqs_b, cqn_b], writes=[pb_b])
                S.op("act", lambda E, pa=pa: E.activation(sqq[0:96, :], pa[0:96, :], AF.Square), reads=[pa_b], writes=[sqq_b])
                pr, pr_b = ps_r.next()
                S.op("pe", lambda E, pr=pr: E.matmul(pr[0:96, :], ones_bf[0:96, 0:96], sqq[0:96, :], start=True, stop=True), reads=[sqq_b, cbf_b], writes=[pr_b])
                rr, rr_b = rqk.next()
                rstd_from_ss(rr[0:96, :], rr_b, pr[0:96, :], pr_b, 96, tmp[0:96, :], tmp_b, 0, 96)
                qs, qs_b = qst.next()
                S.op("dve", lambda E, qs=qs, pa=pa, rr=rr: E.scalar_tensor_tensor(qs[0:64, :], pa[0:64, :], gcol("mqg", 0, 0, 64), rr[0:64, :], ALU.mult, ALU.mult),
                     reads=[pa_b, rr_b, gcols_b], writes=[qs_b])
                S.op("act", lambda E, pa=pa: E.copy(fa[P0:P1, :], pa[P0:P1, :]), reads=[pa_b], writes=[fa_b])
                S.op("act", lambda E, pb_=pb_: E.copy(fb[P0:P1, :], pb_[P0:P1, :]), reads=[pb_b], writes=[fb_b])
                S.op("dve", lambda E: E.tensor_tensor(fa[P0:P1, :], fa[P0:P1, :], tabs[0][P0:P1, :], ALU.mult), reads=[fa_b, tab_b[0]], writes=[fa_b])
                S.op("dve", lambda E: E.tensor_tensor(fb[P0:P1, :], fb[P0:P1, :], tabs[1][P0:P1, :], ALU.mult), reads=[fb_b, tab_b[1]], writes=[fb_b])
                S.op("dve", lambda E: E.tensor_tensor(fa[P0:P1, :], fa[P0:P1, :], fb[P0:P1, :], ALU.add), reads=[fa_b, fb_b], writes=[fa_b])
                S.op("dve", lambda E, qs=qs, rr=rr: E.tensor_tensor(qs[P0:P1, :], fa[P0:P1, :], rr[P0:P1, :], ALU.mult), reads=[fa_b, rr_b], writes=[qs_b])
                S.dma("sp", q1_d[h * 96:(h + 1) * 96, gs], qs[0:96, :], qs_b, reads=[qs_b], writes=[db["q1"]])
                _stage(5)
                pa, pa_b = ps_a.next()
                for c in range(2):
                    S.op("pe", lambda E, c=c, pa=pa, h=h: E.matmul(pa[0:64, :], Wuk[:, c, h * 64:(h + 1) * 64], ckvn[:, c, :], start=(c == 0), stop=(c == 1)),
                         reads=[Wuk_b, ckvn_b], writes=[pa_b])
                S.op("act", lambda E, pa=pa: E.activation(sqh[0:64, :], pa[0:64, :], AF.Square), reads=[pa_b], writes=[sqh_lo])
                pr, pr_b = ps_r.next()
                S.op("pe", lambda E, pr=pr: E.matmul(pr[0:96, :], ones_bf[0:96, 0:96], sqh[0:96, :], start=True, stop=True), reads=[sqh_lo, sqh_hi, cbf_b], writes=[pr_b])
                rr, rr_b = rqk.next()
                rstd_from_ss(rr[0:96, :], rr_b, pr[0:96, :], pr_b, 96, tmp[0:96, :], tmp_b, 0, 96)
                ks, ks_b = qst.next()
                S.op("dve", lambda E, ks=ks, pa=pa, rr=rr: E.scalar_tensor_tensor(ks[0:64, :], pa[0:64, :], gcol("mkg", 0, 0, 64), rr[0:64, :], ALU.mult, ALU.mult),
                     reads=[pa_b, rr_b, gcols_b], writes=[ks_b])
                S.op("dve", lambda E, ks=ks, rr=rr: E.tensor_tensor(ks[P0:P1, :], kra[P0:P1, :], rr[P0:P1, :], ALU.mult), reads=[kra_b, rr_b], writes=[ks_b])
                S.dma("sp", k1_d[h * 96:(h + 1) * 96, gs], ks[0:96, :], ks_b, reads=[ks_b], writes=[db["k1"]])
            _stage(6)
            for tt in range(4):
                t = g * 4 + tt
                for half in range(2):
                    pp, pp_b = ps_p.next()
                    for c in range(2):
                        S.op("pe", lambda E, c=c, pp=pp, tt=tt, half=half: E.matmul(pp[:, :], ckvn[:, c, tt * 128:(tt + 1) * 128], Wuv[:, c, half * 512:(half + 1) * 512],
                                                                                     start=(c == 0), stop=(c == 1)),
                             reads=[Wuv_b, ckvn_b], writes=[pp_b])
                    vs, vs_b = vst.next()
                    S.op("act", lambda E, vs=vs, pp=pp: E.copy(vs, pp[:, :]), reads=[pp_b], writes=[vs_b])
                    S.dma("sp", v1_d[t * 128:(t + 1) * 128, half * 512:(half + 1) * 512], vs, vs_b, reads=[vs_b], writes=[db["v1"]])
        S.barrier()
        if exch == "cc":
            allgather(k1_d, db["k1"], kg1_d, db["kg1"], 1536, RC_K1)
            allgather(v1_d, db["v1"], vg1_d, db["vg1"], T, RC_V1)
        S.barrier()

    o1_d = nc.dram_tensor("o1_d", [D, T], BF16).ap()
    o1_b = Buf("o1")

    def phase_B1():
        cf.reset()
        ch.reset()
        wsetup()
        tmpA, tmpA_b = cf.take(512), B()
        rcf, rcf_b = cf.take(512), B()
        rcs, rcs_b = cf.take(512), B()
        KK = ch.take(16384)
        Kr = Rot([(KK[:, 0:8192], B()), (KK[:, 8192:16384], B())])
        Vr = roth(2, 8192, lambda a: a.rearrange("p (r t d) -> p r t d", r=4, t=16))
        qr = roth(2, 2048)
        Pr = roth(3, 512)
        msk, msk_b = ch.take(1024).rearrange("p (v n) -> p v n", v=8), B()
        rhi, rhi_b = ch.take(512), B()
        rlo, rlo_b = ch.take(512), B()
        ost = roth(2, 512)
        wfm = roth(2, 1024, lambda a: a.rearrange("p (c n) -> p c n", c=8))
        ps_s = psrot([0, 1, 2])
        ps_o = psrot([3, 4])
        ps_r = psrot([5])
        S.dma("sp", msk, masks_d[:, 1024:2048].rearrange("p (v n) -> p v n", v=8), msk_b, writes=[msk_b])
        for (Vt, Vt_b) in Vr.slots:
            for rk in range(4):
                S.op("pool", lambda E: E.memset(Vt[:, rk, :, 64:128], 1.0), writes=[Vt_b])
        SC = float(96 ** -0.5)
        dsl = slice(64, 128)
        osl = slice(0, 64)
        def issue_loads(h):
            Kh, Kh_b = Kr.next()
            Vh, Vh_b = Vr.next()
            qh, qh_b = qr.next()
            S.dma("sp", qh[0:96, :], q1_d[h * 96:(h + 1) * 96, :], qh_b, reads=[db["q1"]], writes=[qh_b])
            for rk in range(4):
                S.dma("sp", Kh[0:96, rk * T:(rk + 1) * T], kg1_d[grow(RC_K1, rk, h * 96):grow(RC_K1, rk, h * 96) + 96, :], Kh_b,
                      reads=[db["kg1"]], writes=[Kh_b])
                for tq in range(4):
                    r0 = grow(RC_V1, rk, tq * 512)
                    S.dma("sp", Vh[:, rk, tq * 4:(tq + 1) * 4, 0:64],
                          vg1_d[r0:r0 + 512, h * 64:(h + 1) * 64].rearrange("(t p) d -> p t d", p=128), Vh_b, reads=[db["vg1"]], writes=[Vh_b])
            return Kh, Kh_b, Vh, Vh_b, qh, qh_b

        nxt = issue_loads(0)
        for h in range(16):
            Kh, Kh_b, Vh, Vh_b, qh, qh_b = nxt
            if h + 1 < 16:
                nxt = issue_loads(h + 1)
            for u in range(4):
                po, po_b = ps_o.next()
                nblk = 16 * u + 16
                pend = []
                for j in range(nblk):
                    rk, tj, mj = key_loc(j)
                    kcol = rk * T + tj * 128
                    diag = j >= 16 * u
                    col0, vidx = 0, 0
                    if diag:
                        m = (j - 16 * u) // 4
                        col0 = 128 * m
                        vidx = ((4 * u + m) % 2) * 4 + mj
                    pS, pS_b = ps_s.next()
                    S.op("pe", lambda E: E.matmul(pS[:, col0:], Kh[0:96, kcol:kcol + 128], qh[0:96, u * 512 + col0:(u + 1) * 512], start=True, stop=not diag),
                         reads=[Kh_b, qh_b], writes=[pS_b])
                    if diag:
                        S.op("pe", lambda E: E.matmul(pS[:, col0:col0 + 128], ident_bf, msk[:, vidx, :], start=False, stop=True, skip_group_check=True),
                             reads=[cbf_b, msk_b], writes=[pS_b])
                    P_, P_b = Pr.next()
                    S.op("act", lambda E: E.activation(P_[:, col0:], pS[:, col0:], AF.Exp, scale=SC), reads=[pS_b], writes=[P_b])

                    def pv(po=po, po_b=po_b, P_=P_, P_b=P_b, col0=col0, rk=rk, tj=tj, j=j, nblk=nblk, Vh=Vh, Vh_b=Vh_b):
                        S.op("pe", lambda E: E.matmul(po[:, col0:], Vh[:, rk, tj, :], P_[:, col0:], start=(j == 0), stop=(j == nblk - 1),
                                                      skip_group_check=True),
                             reads=[Vh_b, P_b], writes=[po_b])
                    if pend:
                        pend.pop()()
                    pend.append(pv)
                pend.pop()()
                S.op("act", lambda E: E.activation(tmpA[dsl, :], po[dsl, :], AF.Ln), reads=[po_b], writes=[tmpA_b])
                S.op("act", lambda E: E.activation(rcf[dsl, :], tmpA[dsl, :], AF.Exp, scale=-1.0), reads=[tmpA_b], writes=[rcf_b])
                S.op("dve", lambda E: E.tensor_copy(rhi[dsl, :], rcf[dsl, :]), reads=[rcf_b], writes=[rhi_b])
                S.op("dve", lambda E: E.tensor_tensor(rlo[dsl, :], rcf[dsl, :], rhi[dsl, :], ALU.subtract), reads=[rcf_b, rhi_b], writes=[rlo_b])
                pr, pr_b = ps_r.next()
                S.op("pe", lambda E: E.matmul(pr[osl, :], ident_bf[dsl, 64:128], rhi[dsl, :], start=True, stop=False), reads=[cbf_b, rhi_b], writes=[pr_b])
                S.op("pe", lambda E: E.matmul(pr[osl, :], ident_bf[dsl, 64:128], rlo[dsl, :], start=False, stop=True), reads=[cbf_b, rlo_b], writes=[pr_b])
                S.op("act", lambda E: E.copy(rcs[osl, :], pr[osl, :]), reads=[pr_b], writes=[rcs_b])
                os_, os_b = ost.next()
                S.op("dve", lambda E: E.tensor_tensor(os_[osl, :], po[osl, :], rcs[osl, :], ALU.mult), reads=[po_b, rcs_b], writes=[os_b])
                S.dma("pool", o1_d[h * 64:(h + 1) * 64, u * 512:(u + 1) * 512], os_[osl, :], os_b, reads=[os_b], writes=[o1_b])
        S.barrier()
        oT = KK.rearrange("p (c n) -> p c n", c=8)
        oT_b = B()
        S.dma("sp", oT, o1_d.rearrange("(c p) n -> p c n", p=128), oT_b, reads=[o1_b], writes=[oT_b])
        out_proj([oT[:, c, :] for c in range(8)], [oT_b] * 8, Wd["mla_out"], wfm, psrot([0, 1, 2]))
        S.barrier()

    load_x()
    seq = [lambda: ffn(Wd[("gu", "pre", 0)], Wd[("d", "pre", 0)], "ffn_pre", 0),
           phase_A0, phase_B0, lambda: phase_X(0),
           lambda: ffn(Wd[("gu", "post", 0)], Wd[("d", "post", 0)], "ffn_post", 0),
           lambda: ffn(Wd[("gu", "pre", 1)], Wd[("d", "pre", 1)], "ffn_pre", 16),
           phase_A1, phase_B1, lambda: phase_X(1),
           lambda: ffn(Wd[("gu", "post", 1)], Wd[("d", "post", 1)], "ffn_post", 16)]
    for i in range(lo, hi):
        seq[i]()
    for n in db:
        if db[n].lw is not None:
            S._wait("sp", db[n].lw)
    store_x()
    S.run()
    print('[build] phases', lo, hi, 'counts', dict(S.cnt), 'nsems', len(S.sems), 'maxdma', max([0] + [v for k, v in S.alltoks.items() if str(k).startswith('d')]), flush=True)
    return nc, used_inputs, outputs


def zig(t, r):
    return 4 * t + (r if t % 2 == 0 else 3 - r)


def shard_tokens(x):
    outs = []
    for c in range(NCORES):
        b, r = c // 4, c % 4
        blocks = [zig(t, r) for t in range(NT)]
        xb = x[b].reshape(64, 128, -1)
        outs.append(np.ascontiguousarray(xb[blocks].reshape(T, -1)))
    return outs


def unshard_tokens(outs, dtype=np.float32):
    full = np.zeros((2, 64, 128, D), dtype)
    for c in range(NCORES):
        b, r = c // 4, c % 4
        blocks = [zig(t, r) for t in range(NT)]
        full[b, blocks] = outs[c].reshape(NT, 128, D)
    return full.reshape(2, 8192, D)


def col(v):
    v = np.asarray(v, np.float32)
    return np.ascontiguousarray(v.reshape(-1, 128).T)


def make_masks(r):
    k = np.arange(128)[:, None]
    q = np.arange(128)[None, :]
    m = np.zeros((128, 16, 128), np.float32)
    for kind in range(2):
        tri = (k < q) if kind == 0 else (k <= q)
        for p in range(2):
            rr = r if p == 0 else 3 - r
            for mj in range(4):
                if mj < rr:
                    v = np.zeros((128, 128), np.float32)
                elif mj == rr:
                    v = np.where(tri, 0.0, NEG).astype(np.float32)
                else:
                    v = np.full((128, 128), NEG, np.float32)
                m[:, kind * 8 + p * 4 + mj, :] = v
    return m.reshape(128, 16 * 128).astype(ml_dtypes.bfloat16)


_PROG_CACHE = {}


def prepare_inputs(inp):
    f32 = np.float32
    ident = np.eye(128, dtype=f32)
    ones = np.ones((128, 128), f32)
    negU = -np.tril(np.ones((128, 128), f32))
    sw = np.zeros((128, 128), f32)
    for i in range(16):
        sw[16 + i, i] = 1.0
        sw[i, 16 + i] = 1.0
    cb = np.concatenate([ident, ones, negU, sw], axis=1).astype(ml_dtypes.bfloat16)
    gc = np.zeros((128, NGC), f32)
    for l in range(2):
        gc[:, 16 * l:16 * l + 8] = col(inp["ffn_pre_norm"][l])
        gc[:, 16 * l + 8:16 * l + 16] = col(inp["ffn_post_norm"][l])
        gc[:, 32 + 8 * l:40 + 8 * l] = col(inp["mix_norm"][l])
        gc[:, 48 + 8 * l:56 + 8 * l] = col(inp["xmem_norm"][l])
        gc[:, 64 + 8 * l:72 + 8 * l] = col(inp["xmem_mem_norm"][l])
        gc[:, 80 + 2 * l:82 + 2 * l] = col(inp["xmem_q_gain"][l])
        gc[:, 84 + 2 * l:86 + 2 * l] = col(inp["xmem_k_gain"][l])
    gc[:, 88:92] = col(inp["mla_q_lora_gain"][0])
    gc[:, 92:94] = col(inp["mla_kv_lora_gain"][0])
    perm = np.arange(96)
    perm[64:80] = np.arange(80, 96)
    perm[80:96] = np.arange(64, 80)
    gc[:96, 94] = inp["mla_q_gain"][0]
    gc[:96, 95] = inp["mla_k_gain"][0]
    gc[:96, 96] = inp["mla_q_gain"][0][perm]
    gc[:96, 97] = inp["mla_k_gain"][0][perm]
    half = 16
    invf = (f32(10000.0) ** (-np.arange(half, dtype=f32) / f32(half))).astype(f32)
    gc[64:80, 98] = invf
    gc[80:96, 98] = invf
    gc[64:80, 99] = -1.0
    gc[80:96, 99] = 1.0
    gc[:, 100] = EPS
    gc[:, 101] = np.pi / 2
    gc[:, 102] = 1.0
    bc0 = np.zeros((128, 1536), f32)
    bc0[:, 0:512] = inp["sgu_ln_gain"][0][None, :]
    bc0[:, 512:1024] = inp["sgu_ln_bias"][0][None, :]
    sb = inp["sgu_b"][0]
    for cc in range(4):
        bc0[0:64, 1024 + cc * 128:1024 + (cc + 1) * 128] = sb[2 * cc][None, :]
        bc0[64:128, 1024 + cc * 128:1024 + (cc + 1) * 128] = sb[2 * cc + 1][None, :]
    trim = (np.arange(128)[:, None] <= np.arange(128)[None, :]).astype(f32)
    sguwT = np.ascontiguousarray(inp["sgu_w"][0].transpose(2, 0, 1)).reshape(128, 8 * 128)
    common = {"consts_bf": cb, "ident_f": ident, "gcols": gc, "bc0": bc0, "trimask": trim, "sgu_wT": sguwT}
    for l in range(2):
        for which in ("pre", "post"):
            common["w_%s_gu_%d" % (which, l)] = np.ascontiguousarray(inp["ffn_%s_w_gu" % which][l])
            common["w_%s_d_%d" % (which, l)] = np.ascontiguousarray(inp["ffn_%s_w_down" % which][l])
        common["w_xq_%d" % l] = np.ascontiguousarray(inp["xmem_wq"][l])
        wkv = inp["xmem_wkv"][l].reshape(D, 4, 2, 256)
        common["w_xk_%d" % l] = np.ascontiguousarray(wkv[:, :, 0, :].reshape(D, D))
        common["w_xv_%d" % l] = np.ascontiguousarray(wkv[:, :, 1, :].reshape(D, D))
        common["w_xo_%d" % l] = np.ascontiguousarray(inp["xmem_wo"][l])
    common["w_sbg_in"] = np.ascontiguousarray(inp["sbg_w_in"][0])
    common["w_sbg_out"] = np.ascontiguousarray(inp["sbg_w_out"][0])
    common["w_mla_in"] = np.ascontiguousarray(inp["mla_w_in"][0])
    uq = inp["mla_w_uq"][0]
    common["w_mla_uq"] = np.ascontiguousarray(uq)
    uq_h = uq.reshape(512, 16, 96)
    common["w_mla_uq_sw"] = np.ascontiguousarray(np.concatenate([uq_h[:, :, 80:96], uq_h[:, :, 64:80]], axis=2).reshape(512, 512))
    ukv = inp["mla_w_ukv"][0].reshape(256, 16, 128)
    common["w_mla_uk"] = np.ascontiguousarray(ukv[:, :, 0:64].reshape(256, 1024))
    common["w_mla_uv"] = np.ascontiguousarray(ukv[:, :, 64:128].reshape(256, 1024))
    common["w_mla_out"] = np.ascontiguousarray(inp["mla_w_out"][0])
    xs = shard_tokens(inp["x"])
    pos = inp["positions"].astype(np.int32)
    in_maps = []
    for c in range(NCORES):
        b, r = c // 4, c % 4
        blocks = [zig(t, r) for t in range(NT)]
        pl = pos[b].reshape(64, 128)[blocks].reshape(T)
        m = dict(common)
        m["x_in"] = xs[c]
        m["masks"] = make_masks(r)
        m["posrep"] = np.ascontiguousarray(np.broadcast_to(pl[None, :], (32, T))).astype(np.int32)
        m["mem"] = np.ascontiguousarray(inp["mem"][b])
        in_maps.append(m)
    return in_maps


EXCH = "cc"
DEBUG_X = {}


def _run(key, in_maps):
    if key not in _PROG_CACHE:
        _PROG_CACHE[key] = build_program(*key)
    nc, used, outs = _PROG_CACHE[key]
    maps = [{k: v for k, v in m.items() if k in used} for m in in_maps]
    res = run_bass_kernel_spmd(nc, maps, core_ids=list(range(NCORES)))
    return res.results


def _gather(results, name, rc):
    out = []
    for c in range(NCORES):
        b = c // 4
        rows = results[c][name].shape[0]
        parts = []
        for i in range(rows // rc):
            for rk in range(4):
                parts.append(results[4 * b + rk][name][i * rc:(i + 1) * rc])
        out.append(np.ascontiguousarray(np.concatenate(parts, axis=0)))
    return out


def kernel(**inputs):
    inp = {k: np.asarray(v) for k, v in inputs.items()}
    in_maps = prepare_inputs(inp)
    if EXCH == "cc":
        res = _run((0, 10, "cc"), in_maps)
        return unshard_tokens([r["x_out"] for r in res])
    r1 = _run((0, 2, "host"), in_maps)
    kg0, vg0 = _gather(r1, "k0_d", RC_K0), _gather(r1, "v0_d", RC_V0)
    for c in range(NCORES):
        in_maps[c].update({"x_in": r1[c]["x_out"], "q0_d": r1[c]["q0_d"], "osg_d": r1[c]["osg_d"], "kg0_d": kg0[c], "vg0_d": vg0[c]})
    r2 = _run((2, 7, "host"), in_maps)
    DEBUG_X["p2"] = [r["x_out"] for r in r2]
    kg1, vg1 = _gather(r2, "k1_d", RC_K1), _gather(r2, "v1_d", RC_V1)
    for c in range(NCORES):
        in_maps[c].update({"x_in": r2[c]["x_out"], "q1_d": r2[c]["q1_d"], "kg1_d": kg1[c], "vg1_d": vg1[c]})
    r3 = _run((7, 10, "host"), in_maps)
    return unshard_tokens([r["x_out"] for r in r3])
```

```python
import types
import numpy as np
import ml_dtypes
import concourse.bass as bass
import concourse.mybir as mybir
from concourse.bass_utils import run_bass_kernel_spmd

F32 = mybir.dt.float32
BF16 = mybir.dt.bfloat16
I32 = mybir.dt.int32
AF = mybir.ActivationFunctionType
ALU = mybir.AluOpType

NCORES = 8
D = 1024
DFF = 2816
T = 2048
NT = 16
EPS = 1e-6
NEG = -30000.0


class Buf:
    __slots__ = ("name", "lw", "rd", "dkey", "dcnt")

    def __init__(self, name):
        self.name = name
        self.lw = None
        self.rd = {}
        self.dkey = None
        self.dcnt = 0


ENGS = ("pe", "act", "dve", "pool", "sp")


class Sched:
    def __init__(self, nc):
        self.nc = nc
        self.prog = {e: [] for e in ENGS}
        self.cnt = {e: 0 for e in ENGS}
        self.waited = {e: {} for e in ENGS}
        self.sems = {}
        self.alltoks = {}
        self.dfree = []
        self.dcum = {}
        self.dheld = []

    def sem(self, key):
        if key not in self.sems:
            self.sems[key] = self.nc.alloc_semaphore(name="s_%s" % (str(key).replace(" ", "")))
        return self.sems[key]

    def _wait(self, eng, tok):
        key, val = tok
        if key == eng and eng in ("pe", "sp"):
            return
        if self.waited[eng].get(key, 0) >= val:
            return
        self.waited[eng][key] = val
        h = self.sem(key)
        self.prog[eng].append(lambda E, h=h, val=val: E.wait_ge(h, val))

    def _deps(self, eng, reads, writes):
        for b in reads:
            if b.lw is not None:
                self._wait(eng, b.lw)
        for b in writes:
            if b.lw is not None and b.lw[0] != eng:
                self._wait(eng, b.lw)
            for k, v in b.rd.items():
                if k != eng:
                    self._wait(eng, (k, v))

    def _upd(self, tok, reads, writes):
        self.alltoks[tok[0]] = max(self.alltoks.get(tok[0], 0), tok[1])
        for b in reads:
            if b.rd.get(tok[0], 0) < tok[1]:
                b.rd[tok[0]] = tok[1]
        for b in writes:
            b.lw = tok
            b.rd = {}

    @staticmethod
    def _freeze(fn):
        if fn.__closure__:
            cells = tuple(types.CellType(c.cell_contents) for c in fn.__closure__)
            g = types.FunctionType(fn.__code__, fn.__globals__, fn.__name__, fn.__defaults__, cells)
            g.__kwdefaults__ = fn.__kwdefaults__
            return g
        return fn

    def op(self, eng, fn, reads=(), writes=()):
        fn = self._freeze(fn)
        self._deps(eng, reads, writes)
        self.cnt[eng] += 1
        tok = (eng, self.cnt[eng])
        h = self.sem(eng)
        self.prog[eng].append(lambda E, fn=fn, h=h: fn(E).then_inc(h, 1))
        self._upd(tok, reads, writes)
        return tok

    def dma(self, q, out_ap, in_ap, owner, reads=(), writes=()):
        self._deps(q, reads, writes)
        if owner.dkey is None:
            if self.dfree:
                owner.dkey = self.dfree.pop()
            else:
                owner.dkey = "q%d" % len(self.dcum)
                self.dcum[owner.dkey] = 0
            self.dheld.append(owner)
        self.dcum[owner.dkey] += 16
        tok = (owner.dkey, self.dcum[owner.dkey])
        h = self.sem(owner.dkey)
        self.prog[q].append(lambda E, o=out_ap, i=in_ap, h=h: E.dma_start(out=o, in_=i).then_inc(h, 16))
        self._upd(tok, reads, writes)
        return tok

    def barrier(self, release=True):
        for e in ENGS:
            for k, v in list(self.alltoks.items()):
                self._wait(e, (k, v))
        if release:
            for b in self.dheld:
                self.dfree.append(b.dkey)
                b.dkey = None
            self.dheld = []

    def run(self):
        nc = self.nc
        with nc.Block() as block:
            @block.tensor
            def _(E):
                for f in self.prog["pe"]:
                    f(E)

            @block.scalar
            def _(E):
                for f in self.prog["act"]:
                    f(E)

            @block.vector
            def _(E):
                for f in self.prog["dve"]:
                    f(E)

            @block.gpsimd
            def _(E):
                for f in self.prog["pool"]:
                    f(E)

            @block.sync
            def _(E):
                for f in self.prog["sp"]:
                    f(E)


class Rot:
    def __init__(self, slots):
        self.slots = slots
        self.i = 0

    def next(self):
        s = self.slots[self.i % len(self.slots)]
        self.i += 1
        return s


class Ctx:
    pass


def mk_tile(nc, name, shape, dt):
    return nc.alloc_sbuf_tensor(name, shape, dt)


import os
DBG = os.environ.get('KDBG', '')
RC_K0, RC_V0, RC_K1, RC_V1 = 256, 1024, 192, 512
GC = dict(ffn_pre=0, ffn_post=8, mix=32, xn=48, xmn=64, xqg=80, xkg=84, qlora=88, kvlora=92,
          mqg=94, mkg=95, mqg_sw=96, mkg_sw=97, invf=98, sgn=99, eps=100, halfpi=101, one=102, zero=103)
NGC = 104
TWO_PI = float(2 * np.pi)
PI = float(np.pi)


PHASES = ["F0pre", "A0", "B0", "X0", "F0post", "F1pre", "A1", "B1", "X1", "F1post"]


def build_program(lo=0, hi=10, exch="cc"):
    nc = bass.Bass("TRN2", target_bir_lowering=False)
    S = Sched(nc)
    outputs = ["x_out"]

    used_inputs = set()

    def din(name, shape, dt=F32):
        used_inputs.add(name)
        return nc.dram_tensor(name, list(shape), dt, kind="ExternalInput").ap()

    x_in = din("x_in", [T, D])
    x_out = nc.dram_tensor("x_out", [T, D], F32, kind="ExternalOutput").ap()
    consts_bf = din("consts_bf", [128, 4 * 128], BF16)
    ident_f_d = din("ident_f", [128, 128])
    gcols_d = din("gcols", [128, NGC])
    bc0_d = din("bc0", [128, 1536])
    trim_d = din("trimask", [128, 128])
    sguwT_d = din("sgu_wT", [128, 8 * 128])
    masks_d = din("masks", [128, 16 * 128], BF16)
    posrep_d = din("posrep", [32, T], I32)
    mem_d = din("mem", [256, D])
    WSHAPES = {"sbg_in": [D, 2560], "sbg_out": [D, D], "mla_in": [D, 800], "mla_uq": [512, 1536], "mla_uq_sw": [512, 512],
               "mla_uk": [256, 1024], "mla_uv": [256, 1024], "mla_out": [D, D]}

    class LazyW(dict):
        def __missing__(self, key):
            if isinstance(key, tuple) and key[0] in ("gu", "d"):
                name = "w_%s_%s_%d" % (key[1], key[0], key[2])
                shape = [D, 2 * DFF] if key[0] == "gu" else [DFF, D]
            elif isinstance(key, tuple):
                name = "w_%s_%d" % key
                shape = [D, D]
            else:
                name = "w_" + key
                shape = WSHAPES[key]
            self[key] = din(name, shape)
            return self[key]

    Wd = LazyW()
    def scratch(name, shape, prod, cons):
        if exch == "cc" or (lo <= prod < hi and lo <= cons < hi):
            return nc.dram_tensor(name, shape, BF16).ap()
        if lo <= prod < hi:
            outputs.append(name)
            return nc.dram_tensor(name, shape, BF16, kind="ExternalOutput").ap()
        if lo <= cons < hi:
            return din(name, shape, BF16)
        return None

    gp = 1 if exch == "cc" else -1
    q0_d = scratch("q0_d", [512, T], 1, 2)
    k0_d = scratch("k0_d", [512, T], 1, 2 if exch == "cc" else 99)
    v0_d = scratch("v0_d", [T, 512], 1, 2 if exch == "cc" else 99)
    osg_d = scratch("osg_d", [512, T], 1, 2)
    kg0_d = scratch("kg0_d", [4 * 512, T], gp, 2)
    vg0_d = scratch("vg0_d", [4 * T, 512], gp, 2)
    gp = 6 if exch == "cc" else -1
    q1_d = scratch("q1_d", [1536, T], 6, 7)
    k1_d = scratch("k1_d", [1536, T], 6, 7 if exch == "cc" else 99)
    v1_d = scratch("v1_d", [T, 1024], 6, 7 if exch == "cc" else 99)
    kg1_d = scratch("kg1_d", [4 * 1536, T], gp, 7)
    vg1_d = scratch("vg1_d", [4 * T, 1024], gp, 7)
    db = {n: Buf(n) for n in ("q0", "k0", "v0", "osg", "kg0", "vg0", "q1", "k1", "v1", "kg1", "vg1")}

    xT = nc.alloc_sbuf_tensor("xT", [128, 8, T], F32)
    xTb = [[Buf("xT_%d_%d" % (c, g)) for g in range(4)] for c in range(8)]
    cbf = nc.alloc_sbuf_tensor("cbf", [128, 4 * 128], BF16)
    cbf_b = Buf("cbf")
    ident_f = nc.alloc_sbuf_tensor("sb_ident_f", [128, 128], F32)
    ident_f_b = Buf("ident_f")
    gcols = nc.alloc_sbuf_tensor("sb_gcols", [128, NGC], F32)
    gcols_b = Buf("gcols")
    ident_bf = cbf[:, 0:128]
    ones_bf = cbf[:, 128:256]
    negU_bf = cbf[:, 256:384]
    swap_bf = cbf[:, 384:512]

    def gcol(name, i=0, p0=0, p1=128):
        return gcols[p0:p1, GC[name] + i:GC[name] + i + 1]

    S.dma("sp", cbf[:, :], consts_bf[:, :], cbf_b, writes=[cbf_b])
    S.dma("sp", ident_f[:, :], ident_f_d[:, :], ident_f_b, writes=[ident_f_b])
    S.dma("sp", gcols[:, :], gcols_d[:, :], gcols_b, writes=[gcols_b])

    posi = nc.alloc_sbuf_tensor("posi", [128, 512], I32)
    posi_b = Buf("posi")
    nint = nc.alloc_sbuf_tensor("nint", [128, 512], I32)
    nint_b = Buf("nint")
    AF32 = 11776
    ABF = 45056
    arena_f = nc.alloc_sbuf_tensor("arena_f", [128, AF32], F32)
    arena_h = nc.alloc_sbuf_tensor("arena_h", [128, ABF], BF16)
    psum = [nc.alloc_psum_tensor("ps%d" % i, [128, 512], F32) for i in range(8)]

    class Carver:
        def __init__(self, ar, size):
            self.ar, self.size, self.off = ar, size, 0

        def reset(self):
            self.off = 0

        def take(self, n):
            assert self.off + n <= self.size, ("arena overflow", self.off, n, self.size)
            a = self.ar[:, self.off:self.off + n]
            self.off += n
            return a

    cf = Carver(arena_f, AF32)
    ch = Carver(arena_h, ABF)
    nbuf = [0]

    def B(name="b"):
        nbuf[0] += 1
        return Buf("%s%d" % (name, nbuf[0]))

    def rotf(n, size, view=None):
        return Rot([((cf.take(size) if view is None else view(cf.take(size))), B()) for _ in range(n)])

    def roth(n, size, view=None):
        return Rot([((ch.take(size) if view is None else view(ch.take(size))), B()) for _ in range(n)])

    def psrot(idxs):
        return Rot([(psum[i], B("ps")) for i in idxs])

    W = Ctx()

    def wsetup(nst=2, stsize=2048):
        W.st = rotf(nst, stsize)

    castctr = [0]

    def cast(dst, dst_b, src, src_b):
        castctr[0] += 1
        if castctr[0] % 2 == 0:
            S.op("act", lambda E: E.copy(dst, src), reads=[src_b], writes=[dst_b])
        else:
            S.op("dve", lambda E: E.tensor_copy(dst, src), reads=[src_b], writes=[dst_b])

    def wload(view, kc, n, dst, dst_b, eng=None):
        assert kc * n <= 2048
        st, st_b = W.st.next()
        sv = st[:, 0:kc * n].rearrange("p (c n) -> p c n", c=kc)
        S.dma("sp", sv, view, st_b, writes=[st_b])
        cast(dst, dst_b, sv, st_b)

    def rstd_from_ss(out_ap, out_b, ps_ap, ps_b, n, tmp_ap, tmp_b, p0=0, p1=128):
        S.op("act", lambda E: E.activation(tmp_ap, ps_ap, AF.Ln, bias=gcol("eps", 0, p0, p1), scale=1.0 / n),
             reads=[ps_b, gcols_b], writes=[tmp_b])
        S.op("act", lambda E: E.activation(out_ap, tmp_ap, AF.Exp, scale=-0.5), reads=[tmp_b], writes=[out_b])

    def load_x():
        cf.reset()
        rot = rotf(2, 1024)
        psb = [B("ps"), B("ps")]
        for t in range(NT):
            ap, b = rot.next()
            S.dma("sp", ap, x_in[t * 128:(t + 1) * 128, :], b, writes=[b])
            for half in range(2):
                pb, ps = psb[half], psum[half]
                for cc in range(4):
                    c = half * 4 + cc
                    S.op("pe", lambda E, o=ps[:, cc * 128:(cc + 1) * 128], i=ap[:, c * 128:(c + 1) * 128]:
                         E.transpose(o, i, ident_f[:, :]), reads=[b, ident_f_b], writes=[pb])
                g = t // 4
                outap = xT[:, half * 4:half * 4 + 4, t * 128:(t + 1) * 128]
                inap = ps[:, :].rearrange("p (c n) -> p c n", c=4)
                wr = [xTb[half * 4 + cc][g] for cc in range(4)]
                if half == 0:
                    S.op("act", lambda E, o=outap, i=inap: E.copy(o, i), reads=[pb], writes=wr)
                else:
                    S.op("dve", lambda E, o=outap, i=inap: E.tensor_copy(o, i), reads=[pb], writes=wr)
        S.barrier()

    def store_x():
        cf.reset()
        rot = rotf(2, 1024)
        psb = [B("ps"), B("ps")]
        toks = []
        for t in range(NT):
            ap, b = rot.next()
            g = t // 4
            for half in range(2):
                pb, ps = psb[half], psum[half]
                for cc in range(4):
                    c = half * 4 + cc
                    S.op("pe", lambda E, o=ps[:, cc * 128:(cc + 1) * 128], i=xT[:, c, t * 128:(t + 1) * 128]:
                         E.transpose(o, i, ident_f[:, :]), reads=[xTb[c][g], ident_f_b], writes=[pb])
                if half == 0:
                    S.op("act", lambda E, o=ap[:, 0:512], i=ps[:, :]: E.copy(o, i), reads=[pb], writes=[b])
                else:
                    S.op("dve", lambda E, o=ap[:, 512:1024], i=ps[:, :]: E.tensor_copy(o, i), reads=[pb], writes=[b])
            toks.append(S.dma("sp", x_out[t * 128:(t + 1) * 128, :], ap, b, reads=[b]))
        for tk in toks:
            S._wait("sp", tk)
        S.barrier()

    def norm_prep(gname, gi0, grp, xb_ap, xb_b, rstd_ap, rstd_b, sq_ap, sq_b, ps_i, ps_b, tmp_ap, tmp_b, tokps=None):
        tsl = slice(grp * 512, (grp + 1) * 512)
        xbufs = [xTb[c][grp] for c in range(8)]
        S.op("act", lambda E: E.activation(sq_ap, xT[:, :, tsl], AF.Square), reads=xbufs, writes=[sq_b])
        ps = psum[ps_i]
        for c in range(8):
            S.op("pe", lambda E, c=c: E.matmul(ps[:, :], ones_bf, sq_ap[:, c, :], start=(c == 0), stop=(c == 7)),
                 reads=[sq_b, cbf_b], writes=[ps_b])
        rstd_from_ss(rstd_ap, rstd_b, ps[:, :], ps_b, D, tmp_ap, tmp_b)
        if tokps is not None:
            tp_ap, tp_b = tokps
            for tt in range(4):
                for c in range(8):
                    S.op("pe", lambda E, c=c, tt=tt: E.matmul(tp_ap[:, grp * 4 + tt:grp * 4 + tt + 1],
                                                              sq_ap[:, c, tt * 128:(tt + 1) * 128], ones_bf[:, 0:1],
                                                              start=(c == 0), stop=(c == 7)),
                         reads=[sq_b, cbf_b], writes=[tp_b])
        for c in range(8):
            S.op("dve", lambda E, c=c: E.tensor_scalar(xb_ap[:, c, :], xT[:, c, tsl], gcol(gname, gi0 + c), None, ALU.mult),
                 reads=[xTb[c][grp], gcols_b], writes=[xb_b])

    def ffn(wgu_d, wd_d, gname, gi0):
        for half in range(2):
            cf.reset()
            ch.reset()
            rstd = [cf.take(512) for _ in range(2)]
            rstd_b = [B() for _ in range(2)]
            tmp, tmp_b = cf.take(512), B()
            stg_gu = rotf(2, 2048, lambda a: a.rearrange("p (a c m) -> p a c m", a=2, c=8))
            stg_d = rotf(2, 1408, lambda a: a.rearrange("p (k m) -> p k m", k=11))
            a_rot, s_rot, t_rot = rotf(2, 512), rotf(2, 512), rotf(2, 512)
            xb = [ch.take(4096).rearrange("p (c n) -> p c n", c=8) for _ in range(2)]
            xb_b = [B() for _ in range(2)]
            hid = ch.take(22 * 1024).rearrange("p (k n) -> p k n", k=22)
            hid_b = [[B() for g in range(2)] for k in range(22)]
            wgu = roth(2, 2048, lambda a: a.rearrange("p (a c m) -> p a c m", a=2, c=8))
            wd = roth(2, 2816, lambda a: a.rearrange("p (k m) -> p k m", k=22))
            sq, sq_b = ch.take(4096).rearrange("p (c n) -> p c n", c=8), B()
            ps_ss_b = B("ps")
            ps_g, ps_u, ps_a = psrot([1, 2]), psrot([3, 4]), psrot([5, 6])
            for gi in range(2):
                norm_prep(gname, gi0, half * 2 + gi, xb[gi], xb_b[gi], rstd[gi], rstd_b[gi], sq, sq_b, 0, ps_ss_b, tmp, tmp_b)
            wgu_v = wgu_d.rearrange("(c p) n -> p c n", p=128)
            for j in range(22):
                st, st_b = stg_gu.next()
                S.dma("sp", st[:, 0], wgu_v[:, :, j * 128:(j + 1) * 128], st_b, writes=[st_b])
                S.dma("sp", st[:, 1], wgu_v[:, :, DFF + j * 128:DFF + (j + 1) * 128], st_b, writes=[st_b])
                wt, wt_b = wgu.next()
                cast(wt, wt_b, st, st_b)
                for gi in range(2):
                    pg, pg_b = ps_g.next()
                    pu, pu_b = ps_u.next()
                    for c in range(8):
                        S.op("pe", lambda E, c=c, pg=pg, wt=wt, gi=gi: E.matmul(pg[:, :], wt[:, 0, c, :], xb[gi][:, c, :],
                                                                                 start=(c == 0), stop=(c == 7)),
                             reads=[wt_b, xb_b[gi]], writes=[pg_b])
                    for c in range(8):
                        S.op("pe", lambda E, c=c, pu=pu, wt=wt, gi=gi: E.matmul(pu[:, :], wt[:, 1, c, :], xb[gi][:, c, :],
                                                                                 start=(c == 0), stop=(c == 7)),
                             reads=[wt_b, xb_b[gi]], writes=[pu_b])
                    a, a_b = a_rot.next()
                    s, s_b = s_rot.next()
                    t, t_b = t_rot.next()
                    S.op("dve", lambda E, a=a, pg=pg, gi=gi: E.tensor_tensor(a, pg[:, :], rstd[gi], ALU.mult),
                         reads=[pg_b, rstd_b[gi]], writes=[a_b])
                    S.op("act", lambda E, s=s, a=a: E.activation(s, a, AF.Silu), reads=[a_b], writes=[s_b])
                    S.op("dve", lambda E, t=t, pu=pu, gi=gi: E.tensor_tensor(t, pu[:, :], rstd[gi], ALU.mult),
                         reads=[pu_b, rstd_b[gi]], writes=[t_b])
                    S.op("pool", lambda E, s=s, t=t, j=j, gi=gi: E.tensor_tensor(hid[:, j, gi * 512:(gi + 1) * 512], s, t, ALU.mult),
                         reads=[s_b, t_b], writes=[hid_b[j][gi]])
            wd_v = wd_d.rearrange("(k p) n -> p k n", p=128)
            for i in range(8):
                wt, wt_b = wd.next()
                for hh in range(2):
                    st, st_b = stg_d.next()
                    S.dma("sp", st, wd_v[:, hh * 11:(hh + 1) * 11, i * 128:(i + 1) * 128], st_b, writes=[st_b])
                    cast(wt[:, hh * 11:(hh + 1) * 11, :], wt_b, st, st_b)
                for gi in range(2):
                    pa, pa_b = ps_a.next()
                    for k in range(22):
                        S.op("pe", lambda E, k=k, pa=pa, wt=wt, gi=gi: E.matmul(pa[:, :], wt[:, k, :], hid[:, k, gi * 512:(gi + 1) * 512],
                                                                                 start=(k == 0), stop=(k == 21)),
                             reads=[wt_b, hid_b[k][gi]], writes=[pa_b])
                    grp = half * 2 + gi
                    xs = xT[:, i, grp * 512:(grp + 1) * 512]
                    S.op("dve", lambda E, xs=xs, pa=pa: E.scalar_tensor_tensor(xs, pa[:, :], 0.5, xs, ALU.mult, ALU.add),
                         reads=[pa_b, xTb[i][grp]], writes=[xTb[i][grp]])
            S.barrier()

    def out_proj(mix, mix_b, w_dram, wfm, ps_o):
        wv = w_dram.rearrange("(c p) n -> p c n", p=128)
        for i in range(8):
            wt, wt_b = wfm.next()
            wload(wv[:, :, i * 128:(i + 1) * 128], 8, 128, wt, wt_b)
            for g in range(4):
                po, po_b = ps_o.next()
                for c in range(8):
                    mb = mix_b[c][g] if isinstance(mix_b[c], list) else mix_b[c]
                    S.op("pe", lambda E, c=c, po=po, wt=wt, g=g: E.matmul(po[:, :], wt[:, c, :], mix[c][:, g * 512:(g + 1) * 512],
                                                                           start=(c == 0), stop=(c == 7)),
                         reads=[wt_b, mb], writes=[po_b])
                xs = xT[:, i, g * 512:(g + 1) * 512]
                S.op("dve", lambda E, xs=xs, po=po: E.tensor_tensor(xs, po[:, :], xs, ALU.add),
                     reads=[po_b, xTb[i][g]], writes=[xTb[i][g]])

    def gelu(x_ap, x_b, out_ap, out_b, ga, gb):
        (a, a_b), (b, b_b) = ga, gb
        S.op("act", lambda E: E.activation(a, x_ap, AF.Square), reads=[x_b], writes=[a_b])
        S.op("dve", lambda E: E.tensor_scalar(a, a, 0.044715, 1.0, ALU.mult, ALU.add), reads=[a_b], writes=[a_b])
        S.op("dve", lambda E: E.tensor_tensor(a, a, x_ap, ALU.mult), reads=[a_b, x_b], writes=[a_b])
        S.op("act", lambda E: E.activation(b, a, AF.Sigmoid, scale=1.5957691216057308), reads=[a_b], writes=[b_b])
        S.op("pool", lambda E: E.tensor_tensor(out_ap, x_ap, b, ALU.mult), reads=[x_b, b_b], writes=[out_b])

    def allgather(src_ap, src_b, dst_ap, dst_b, rows, rc):
        S._deps("pool", [src_b], [dst_b])
        h = S.sem("cc")
        S.cnt.setdefault("cc", 0)
        for i in range(rows // rc):
            S.cnt["cc"] += 1
            S.prog["pool"].append(lambda E, i=i: E.collective_compute("AllGather", ALU.bypass, replica_groups=[[0, 1, 2, 3], [4, 5, 6, 7]],
                                                                      ins=[src_ap[i * rc:(i + 1) * rc, :]],
                                                                      outs=[dst_ap[i * 4 * rc:(i + 1) * 4 * rc, :]]).then_inc(h, 1))
        S._upd(("cc", S.cnt["cc"]), [src_b], [dst_b])

    def grow(rc, rk, row):
        return (row // rc) * 4 * rc + rk * rc + (row % rc)

    def phase_A0():
        cf.reset()
        ch.reset()
        rstd = [cf.take(512) for _ in range(4)]
        rstd_b = [B() for _ in range(4)]
        tmp, tmp_b = cf.take(512), B()
        rtok, rtok_b = cf.take(16), B()
        rtmp, rtmp_b = cf.take(16), B()
        wsetup()
        zt = rotf(2, 512)
        ga, gb = (cf.take(512), B()), (cf.take(512), B())
        gl = (cf.take(512), B())
        bc, bc_b = cf.take(1536), B()
        trim, trim_b = cf.take(128), B()
        st6, st6_b = cf.take(8), B()
        mv, mv_b = cf.take(4), B()
        xb = ch.take(16384).rearrange("p (c n) -> p c n", c=8)
        xb_b = [B() for _ in range(4)]
        sq, sq_b = ch.take(4096).rearrange("p (c n) -> p c n", c=8), B()
        Wv, Wv_b = ch.take(4096).rearrange("p (c n) -> p c n", c=8), B()
        Wzg, Wzg_b = ch.take(4096).rearrange("p (c n) -> p c n", c=8), B()
        wfm = roth(2, 1024, lambda a: a.rearrange("p (c n) -> p c n", c=8))
        uT = ch.take(8192).rearrange("p (c n) -> p c n", c=4)
        uT_b = [[B() for g in range(4)] for c in range(4)]
        WsT, WsT_b = ch.take(1024).rearrange("p (g n) -> p g n", g=8), B()
        ev = roth(2, 512)
        vev = roth(2, 512)
        gtok = roth(2, 512)
        osg = roth(2, 512, lambda a: a.rearrange("p (c n) -> p c n", c=4))
        ps_ss_b = B("ps")
        tokps = (psum[7][:, 0:16], B("ps"))
        ps_p = psrot([1, 2, 3])
        ps_m = psrot([4, 5])
        S.dma("sp", bc, bc0_d[:, :], bc_b, writes=[bc_b])
        S.dma("sp", trim, trim_d[:, :], trim_b, writes=[trim_b])
        for hh in range(4):
            st, st_b = W.st.next()
            S.dma("sp", st[:, 0:256], sguwT_d[:, hh * 256:(hh + 1) * 256], st_b, writes=[st_b])
            for k in range(2):
                gidx = hh * 2 + k
                S.op("dve", lambda E, st=st, k=k, gidx=gidx: E.tensor_tensor(WsT[:, gidx, :], st[:, k * 128:(k + 1) * 128], trim, ALU.mult),
                     reads=[st_b, trim_b], writes=[WsT_b])
        for g in range(4):
            norm_prep("mix", 0, g, xb[:, :, g * 512:(g + 1) * 512], xb_b[g], rstd[g], rstd_b[g], sq, sq_b, 0, ps_ss_b, tmp, tmp_b, tokps=tokps)
        rstd_from_ss(rtok, rtok_b, tokps[0], tokps[1], D, rtmp, rtmp_b)
        win = Wd["sbg_in"].rearrange("(c p) n -> p c n", p=128)
        for hh in range(2):
            wload(win[:, hh * 4:(hh + 1) * 4, 1024:1536], 4, 512, Wv[:, hh * 4:(hh + 1) * 4, :], Wv_b)
            wload(win[:, hh * 4:(hh + 1) * 4, 2048:2560], 4, 512, Wzg[:, hh * 4:(hh + 1) * 4, :], Wzg_b)
        for oc in list(range(0, 8)) + list(range(12, 16)):
            wt, wt_b = wfm.next()
            wload(win[:, :, oc * 128:(oc + 1) * 128], 8, 128, wt, wt_b)
            for g in range(4):
                pp, pp_b = ps_p.next()
                for c in range(8):
                    S.op("pe", lambda E, c=c, pp=pp, wt=wt, g=g: E.matmul(pp[:, :], wt[:, c, :], xb[:, c, g * 512:(g + 1) * 512],
                                                                           start=(c == 0), stop=(c == 7)),
                         reads=[wt_b, xb_b[g]], writes=[pp_b])
                if oc < 8:
                    e, e_b = ev.next()
                    sc = 0.125 if oc < 4 else 1.0
                    S.op("dve", lambda E, e=e, pp=pp, g=g, sc=sc: E.scalar_tensor_tensor(e, pp[:, :], sc, rstd[g], ALU.mult, ALU.mult),
                         reads=[pp_b, rstd_b[g]], writes=[e_b])
                    dst, dn = (q0_d, "q0") if oc < 4 else (k0_d, "k0")
                    r0 = (oc % 4) * 128
                    S.dma("sp", dst[r0:r0 + 128, g * 512:(g + 1) * 512], e, e_b, reads=[e_b], writes=[db[dn]])
                else:
                    z, z_b = zt.next()
                    S.op("dve", lambda E, z=z, pp=pp, g=g: E.tensor_tensor(z, pp[:, :], rstd[g], ALU.mult),
                         reads=[pp_b, rstd_b[g]], writes=[z_b])
                    gelu(z, z_b, uT[:, oc - 12, g * 512:(g + 1) * 512], uT_b[oc - 12][g], ga, gb)
        for t in range(NT):
            g = t // 4
            pp, pp_b = ps_p.next()
            for c in range(8):
                S.op("pe", lambda E, c=c, pp=pp, t=t: E.matmul(pp[:, :], xb[:, c, t * 128:(t + 1) * 128], Wv[:, c, :],
                                                               start=(c == 0), stop=(c == 7)),
                     reads=[Wv_b, xb_b[g]], writes=[pp_b])
            e, e_b = vev.next()
            S.op("dve", lambda E, e=e, pp=pp, t=t: E.tensor_scalar(e, pp[:, :], rtok[:, t:t + 1], None, ALU.mult),
                 reads=[pp_b, rtok_b], writes=[e_b])
            S.dma("sp", v0_d[t * 128:(t + 1) * 128, :], e, e_b, reads=[e_b], writes=[db["v0"]])
            pp, pp_b = ps_p.next()
            for c in range(8):
                S.op("pe", lambda E, c=c, pp=pp, t=t: E.matmul(pp[:, :], xb[:, c, t * 128:(t + 1) * 128], Wzg[:, c, :],
                                                               start=(c == 0), stop=(c == 7)),
                     reads=[Wzg_b, xb_b[g]], writes=[pp_b])
            z, z_b = zt.next()
            S.op("dve", lambda E, z=z, pp=pp, t=t: E.tensor_scalar(z, pp[:, :], rtok[:, t:t + 1], None, ALU.mult),
                 reads=[pp_b, rtok_b], writes=[z_b])
            gelu(z, z_b, gl[0], gl[1], ga, gb)
            S.op("dve", lambda E: E.bn_stats(st6[:, 0:6], gl[0]), reads=[gl[1]], writes=[st6_b])
            S.op("dve", lambda E: E.bn_aggr(mv[:, 0:2], st6[:, 0:6]), reads=[st6_b], writes=[mv_b])
            S.op("act", lambda E: E.activation(mv[:, 2:3], mv[:, 1:2], AF.Ln, bias=gcol("eps")), reads=[mv_b, gcols_b], writes=[mv_b])
            S.op("act", lambda E: E.activation(mv[:, 3:4], mv[:, 2:3], AF.Exp, scale=-0.5), reads=[mv_b], writes=[mv_b])
            S.op("dve", lambda E: E.tensor_scalar(gl[0], gl[0], mv[:, 0:1], mv[:, 3:4], ALU.subtract, ALU.mult),
                 reads=[gl[1], mv_b], writes=[gl[1]])
            S.op("dve", lambda E: E.tensor_tensor(gl[0], gl[0], bc[:, 0:512], ALU.mult), reads=[gl[1], bc_b], writes=[gl[1]])
            gt, gt_b = gtok.next()
            S.op("dve", lambda E, gt=gt: E.tensor_tensor(gt, gl[0], bc[:, 512:1024], ALU.add), reads=[gl[1], bc_b], writes=[gt_b])
            pm, pm_b = ps_m.next()
            for grp in range(8):
                hb = 64 * (grp % 2)
                cc = grp // 2
                S.op("pe", lambda E, pm=pm, gt=gt, grp=grp, hb=hb, cc=cc: E.matmul(pm[hb:hb + 64, cc * 128:(cc + 1) * 128],
                                                                                  gt[:, grp * 64:(grp + 1) * 64], WsT[:, grp, :],
                                                                                  start=True, stop=True),
                     reads=[gt_b, WsT_b], writes=[pm_b])
            z2, z2_b = zt.next()
            S.op("dve", lambda E, z2=z2, pm=pm: E.tensor_tensor(z2, pm[:, :], bc[:, 1024:1536], ALU.add), reads=[pm_b, bc_b], writes=[z2_b])
            og, og_b = osg.next()
            S.op("pool", lambda E, og=og, z2=z2, t=t: E.tensor_tensor(og, z2.rearrange("p (c n) -> p c n", c=4),
                                                                     uT[:, :, t * 128:(t + 1) * 128], ALU.mult),
                 reads=[z2_b] + [uT_b[c][g] for c in range(4)], writes=[og_b])
            S.dma("sp", osg_d.rearrange("(c p) n -> p c n", p=128)[:, :, t * 128:(t + 1) * 128], og, og_b, reads=[og_b], writes=[db["osg"]])
        S.barrier()
        if exch == "cc":
            allgather(k0_d, db["k0"], kg0_d, db["kg0"], 512, RC_K0)
            allgather(v0_d, db["v0"], vg0_d, db["vg0"], T, RC_V0)
        S.barrier()

    def key_loc(j):
        tj, mj = j // 4, j % 4
        rk = mj if tj % 2 == 0 else 3 - mj
        return rk, tj, mj

    def phase_B0():
        cf.reset()
        ch.reset()
        Rs = [(cf.take(512), B()) for _ in range(2)]
        ttr = rotf(4, 512)
        e2r = rotf(4, 512)
        wsetup()
        Kr = roth(2, 8192)
        Vc, Vc_b = ch.take(8192).rearrange("p (r t d) -> p r t d", r=4, t=16), B()
        qr = roth(2, 2048)
        Lpr = roth(4, 512)
        wvr = roth(6, 512)
        osb = ch.take(8192).rearrange("p (c n) -> p c n", c=4)
        osb_b = [[B() for g in range(4)] for c in range(4)]
        msk, msk_b = ch.take(1024).rearrange("p (v n) -> p v n", v=8), B()
        wfm = roth(2, 1024, lambda a: a.rearrange("p (c n) -> p c n", c=8))
        ps_e, ps_t, ps_o = psrot([0, 1, 2, 3]), psrot([4, 5]), psrot([6, 7])
        S.dma("sp", msk, masks_d[:, 0:1024].rearrange("p (v n) -> p v n", v=8), msk_b, writes=[msk_b])
        for c in range(4):
            Kc, Kc_b = Kr.next()
            for rk in range(4):
                S.dma("sp", Kc[:, rk * T:(rk + 1) * T], kg0_d[grow(RC_K0, rk, c * 128):grow(RC_K0, rk, c * 128) + 128, :], Kc_b,
                      reads=[db["kg0"]], writes=[Kc_b])
                for tq in range(4):
                    S.dma("sp", Vc[:, rk, tq * 4:(tq + 1) * 4, :],
                          vg0_d[grow(RC_V0, rk, tq * 512):grow(RC_V0, rk, tq * 512) + 512, c * 128:(c + 1) * 128].rearrange("(t p) d -> p t d", p=128), Vc_b,
                          reads=[db["vg0"]], writes=[Vc_b])
            qc, qc_b = qr.next()
            S.dma("sp", qc, q0_d[c * 128:(c + 1) * 128, :], qc_b, reads=[db["q0"]], writes=[qc_b])
            for u in range(4):
                js = list(range(16 * u + 15, -1, -1))
                pos_ = [ps_o.next() for _ in range(2)]
                st2q, st3q = [[], []], [[], []]
                for hh in range(2):
                    S.op("pool", lambda E: E.memset(Rs[hh][0], 0.0), writes=[Rs[hh][1]])
                for idx, j in enumerate(js):
                    rk, tj, mj = key_loc(j)
                    kcol = rk * T + tj * 128
                    diag = j >= 16 * u
                    col0, vidx = 0, 0
                    if diag:
                        m = (j - 16 * u) // 4
                        col0 = 128 * m
                        vidx = ((4 * u + m) % 2) * 4 + mj
                    for hh in range(2):
                        hb = 64 * hh
                        R, R_b = Rs[hh]
                        po, po_b = pos_[hh]
                        qsl = qc[hb:hb + 64, u * 512 + col0:(u + 1) * 512]
                        ksl = Kc[hb:hb + 64, kcol:kcol + 128]
                        pE, pE_b = ps_e.next()
                        tt, tt_b = ttr.next()
                        Lp, Lp_b = Lpr.next()
                        S.op("pe", lambda E: E.matmul(pE[:, col0:], ksl, qsl, start=True, stop=False), reads=[Kc_b, qc_b], writes=[pE_b])
                        if diag:
                            S.op("pe", lambda E: E.matmul(pE[:, col0:col0 + 128], ident_bf, msk[:, vidx, :], start=False, stop=False, skip_group_check=True),
                                 reads=[cbf_b, msk_b], writes=[pE_b])
                        S.op("act", lambda E: E.activation(tt[:, col0:], pE[:, col0:], AF.Exp), reads=[pE_b], writes=[tt_b])
                        S.op("act", lambda E: E.activation(Lp[:, col0:], tt[:, col0:], AF.Ln, bias=gcol("one")), reads=[tt_b, gcols_b], writes=[Lp_b])

                        def stage2(pE=pE, pE_b=pE_b, Lp=Lp, Lp_b=Lp_b, col0=col0, rk=rk, tj=tj, idx=idx, j=j, hh=hh, hb=hb, R=R, R_b=R_b, po=po, po_b=po_b):
                            pT, pT_b = ps_t.next()
                            e2, e2_b = e2r.next()
                            wv, wv_b = wvr.next()
                            S.op("pe", lambda E: E.matmul(pE[:, col0:], negU_bf, Lp[:, col0:], start=False, stop=True, skip_group_check=True),
                                 reads=[cbf_b, Lp_b], writes=[pE_b])
                            S.op("pe", lambda E: E.matmul(pT[:, col0:], ones_bf, Lp[:, col0:], start=True, stop=True), reads=[cbf_b, Lp_b], writes=[pT_b])
                            S.op("dve", lambda E: E.tensor_tensor(e2[:, col0:], pE[:, col0:], R[:, col0:], ALU.subtract), reads=[pE_b, R_b], writes=[e2_b])
                            S.op("act", lambda E: E.activation(wv[:, col0:], e2[:, col0:], AF.Exp), reads=[e2_b], writes=[wv_b])
                            S.op("dve", lambda E: E.tensor_tensor(R[:, col0:], pT[:, col0:], R[:, col0:], ALU.add), reads=[pT_b, R_b], writes=[R_b])

                            def stage3():
                                S.op("pe", lambda E: E.matmul(po[hb:hb + 64, col0:], Vc[:, rk, tj, hb:hb + 64], wv[:, col0:], start=(idx == 0), stop=(j == 0),
                                                              skip_group_check=True),
                                     reads=[Vc_b, wv_b], writes=[po_b])
                            st3q[hh].append(stage3)
                        st2q[hh].append(stage2)
                        if len(st2q[hh]) > 1:
                            st2q[hh].pop(0)()
                        if len(st3q[hh]) > 1:
                            st3q[hh].pop(0)()
                for hh in range(2):
                    while st2q[hh]:
                        st2q[hh].pop(0)()
                for hh in range(2):
                    while st3q[hh]:
                        st3q[hh].pop(0)()
                for hh in range(2):
                    hb = 64 * hh
                    po, po_b = pos_[hh]
                    S.op("dve", lambda E: E.tensor_copy(osb[hb:hb + 64, c, u * 512:(u + 1) * 512], po[hb:hb + 64, :]),
                         reads=[po_b], writes=[osb_b[c][u]])
        Kc, Kc_b = Kr.next()
        osgs = Kc.rearrange("p (c n) -> p c n", c=4)
        S.dma("sp", osgs, osg_d.rearrange("(c p) n -> p c n", p=128), Kc_b, reads=[db["osg"]], writes=[Kc_b])
        mix = [osb[:, c, :] for c in range(4)] + [osgs[:, c, :] for c in range(4)]
        mix_b = [osb_b[c] for c in range(4)] + [Kc_b] * 4
        out_proj(mix, mix_b, Wd["sbg_out"], wfm, psrot([0, 1, 2]))
        S.barrier()

    def phase_X(l):
        cf.reset()
        ch.reset()
        rstd1, rstd1_b = cf.take(512), B()
        tmp, tmp_b = cf.take(512), B()
        wsetup()
        mst = rotf(1, 1024)
        msmall, msmall_b = cf.take(8), B()
        qraw = rotf(1, 1024, lambda a: a.rearrange("p (c n) -> p c n", c=2))
        rq = rotf(2, 512)
        kraw = rotf(1, 512, lambda a: a.rearrange("p (c n) -> p c n", c=2))
        rk_ = rotf(2, 256)
        rcp = rotf(1, 512)
        xb = ch.take(4096).rearrange("p (c n) -> p c n", c=8)
        xb1_b = B()
        sq, sq_b = ch.take(4096).rearrange("p (c n) -> p c n", c=8), B()
        hmT, hmT_b = ch.take(2048).rearrange("p (c n) -> p c n", c=8), B()
        mrow = roth(1, 1024)
        kT, kT_b = ch.take(2048).rearrange("p (h c n) -> p h c n", h=4, c=2), B()
        Vm, Vm_b = ch.take(2048).rearrange("p (m n) -> p m n", m=2), B()
        wfm = roth(2, 1024, lambda a: a.rearrange("p (c n) -> p c n", c=8))
        Wv_ = roth(1, 4096, lambda a: a.rearrange("p (c n) -> p c n", c=8))
        qn = roth(2, 1024, lambda a: a.rearrange("p (c n) -> p c n", c=2))
        sqq = roth(2, 1024, lambda a: a.rearrange("p (c n) -> p c n", c=2))
        Pm = roth(2, 1024, lambda a: a.rearrange("p (m n) -> p m n", m=2))
        oT = ch.take(4096).rearrange("p (c n) -> p c n", c=8)
        oT_b = [B() for c in range(8)]
        ps_ss_b = B("ps")
        ps_p = psrot([1, 2])
        ps_q = psrot([3])
        ps_s2 = psrot([4, 5])
        ps_o2 = psrot([6, 7])
        for mt in range(2):
            ms, ms_b = mst.next()
            S.dma("sp", ms, mem_d[mt * 128:(mt + 1) * 128, :], ms_b, writes=[ms_b])
            mr, mr_b = mrow.next()
            S.op("act", lambda E, ms=ms, mr=mr: E.activation(mr, ms, AF.Square, accum_out=msmall[:, 0:1]), reads=[ms_b], writes=[mr_b, msmall_b])
            S.op("act", lambda E: E.activation(msmall[:, 1:2], msmall[:, 0:1], AF.Ln, bias=gcol("eps"), scale=1.0 / D),
                 reads=[msmall_b, gcols_b], writes=[msmall_b])
            S.op("act", lambda E: E.activation(msmall[:, 2:3], msmall[:, 1:2], AF.Exp, scale=-0.5), reads=[msmall_b], writes=[msmall_b])
            S.op("dve", lambda E, ms=ms: E.tensor_scalar(ms, ms, msmall[:, 2:3], None, ALU.mult), reads=[ms_b, msmall_b], writes=[ms_b])
            for half in range(2):
                pp, pp_b = ps_p.next()
                for cc in range(4):
                    c = half * 4 + cc
                    S.op("pe", lambda E, pp=pp, ms=ms, c=c, cc=cc: E.transpose(pp[:, cc * 128:(cc + 1) * 128], ms[:, c * 128:(c + 1) * 128], ident_f[:, :]),
                         reads=[ms_b, ident_f_b], writes=[pp_b])
                for cc in range(4):
                    c = half * 4 + cc
                    S.op("dve", lambda E, pp=pp, c=c, cc=cc, mt=mt: E.tensor_scalar(hmT[:, c, mt * 128:(mt + 1) * 128], pp[:, cc * 128:(cc + 1) * 128],
                                                                                     gcol("xmn", 8 * l + c), None, ALU.mult),
                         reads=[pp_b, gcols_b], writes=[hmT_b])
        wk = Wd[("xk", l)].rearrange("(c p) n -> p c n", p=128)
        for hd in range(4):
            kr_, kr_b = kraw.next()
            sk, sk_b = sqq.next()
            for dc in range(2):
                oc = hd * 2 + dc
                wt, wt_b = wfm.next()
                wload(wk[:, :, oc * 128:(oc + 1) * 128], 8, 128, wt, wt_b)
                pp, pp_b = ps_p.next()
                for c in range(8):
                    S.op("pe", lambda E, c=c, pp=pp, wt=wt: E.matmul(pp[:, 0:256], wt[:, c, :], hmT[:, c, :], start=(c == 0), stop=(c == 7)),
                         reads=[wt_b, hmT_b], writes=[pp_b])
                S.op("act", lambda E, kr_=kr_, pp=pp, dc=dc: E.copy(kr_[:, dc, :], pp[:, 0:256]), reads=[pp_b], writes=[kr_b])
                S.op("act", lambda E, sk=sk, pp=pp, dc=dc: E.activation(sk[:, dc, 0:256], pp[:, 0:256], AF.Square), reads=[pp_b], writes=[sk_b])
            pq, pq_b = ps_q.next()
            for dc in range(2):
                S.op("pe", lambda E, pq=pq, sk=sk, dc=dc: E.matmul(pq[:, 0:256], ones_bf, sk[:, dc, 0:256], start=(dc == 0), stop=(dc == 1)),
                     reads=[sk_b, cbf_b], writes=[pq_b])
            rr, rr_b = rk_.next()
            rstd_from_ss(rr, rr_b, pq[:, 0:256], pq_b, 256, tmp[:, 0:256], tmp_b)
            for dc in range(2):
                S.op("dve", lambda E, kr_=kr_, rr=rr, dc=dc, hd=hd: E.scalar_tensor_tensor(kT[:, hd, dc, :], kr_[:, dc, :], gcol("xkg", 2 * l + dc), rr,
                                                                                           ALU.mult, ALU.mult),
                     reads=[kr_b, rr_b, gcols_b], writes=[kT_b])
        wvv = Wd[("xv", l)].rearrange("(c p) n -> p c n", p=128)
        for nh in range(2):
            wt, wt_b = Wv_.next()
            for hh in range(2):
                wload(wvv[:, hh * 4:(hh + 1) * 4, nh * 512:(nh + 1) * 512], 4, 512, wt[:, hh * 4:(hh + 1) * 4, :], wt_b)
            for mt in range(2):
                pp, pp_b = ps_p.next()
                for c in range(8):
                    S.op("pe", lambda E, c=c, pp=pp, wt=wt, mt=mt: E.matmul(pp[:, :], hmT[:, c, mt * 128:(mt + 1) * 128], wt[:, c, :],
                                                                             start=(c == 0), stop=(c == 7)),
                         reads=[wt_b, hmT_b], writes=[pp_b])
                S.op("act", lambda E, pp=pp, mt=mt, nh=nh: E.copy(Vm[:, mt, nh * 512:(nh + 1) * 512], pp[:, :]), reads=[pp_b], writes=[Vm_b])
        wq = Wd[("xq", l)].rearrange("(c p) n -> p c n", p=128)
        wo = Wd[("xo", l)].rearrange("(c p) n -> p c n", p=128)
        ps_x = psrot([1, 2])
        for g in range(4):
            norm_prep("xn", 8 * l, g, xb, xb1_b, rstd1, rstd1_b, sq, sq_b, 0, ps_ss_b, tmp, tmp_b)
            for hd in range(4):
                qr_, qr_b = qraw.next()
                sk, sk_b = sqq.next()
                for dc in range(2):
                    oc = hd * 2 + dc
                    wt, wt_b = wfm.next()
                    wload(wq[:, :, oc * 128:(oc + 1) * 128], 8, 128, wt, wt_b)
                    pp, pp_b = ps_p.next()
                    for c in range(8):
                        S.op("pe", lambda E, c=c, pp=pp, wt=wt: E.matmul(pp[:, :], wt[:, c, :], xb[:, c, :], start=(c == 0), stop=(c == 7)),
                             reads=[wt_b, xb1_b], writes=[pp_b])
                    S.op("dve", lambda E, qr_=qr_, pp=pp, dc=dc: E.tensor_tensor(qr_[:, dc, :], pp[:, :], rstd1, ALU.mult),
                         reads=[pp_b, rstd1_b], writes=[qr_b])
                    S.op("act", lambda E, sk=sk, qr_=qr_, dc=dc: E.activation(sk[:, dc, :], qr_[:, dc, :], AF.Square), reads=[qr_b], writes=[sk_b])
                pq, pq_b = ps_q.next()
                for dc in range(2):
                    S.op("pe", lambda E, pq=pq, sk=sk, dc=dc: E.matmul(pq[:, :], ones_bf, sk[:, dc, :], start=(dc == 0), stop=(dc == 1)),
                         reads=[sk_b, cbf_b], writes=[pq_b])
                rr, rr_b = rq.next()
                rstd_from_ss(rr, rr_b, pq[:, :], pq_b, 256, tmp, tmp_b)
                qq, qq_b = qn.next()
                for dc in range(2):
                    S.op("dve", lambda E, qq=qq, qr_=qr_, rr=rr, dc=dc: E.scalar_tensor_tensor(qq[:, dc, :], qr_[:, dc, :], gcol("xqg", 2 * l + dc), rr,
                                                                                              ALU.mult, ALU.mult),
                         reads=[qr_b, rr_b, gcols_b], writes=[qq_b])
                pm_, pm_b = Pm.next()
                for mt in range(2):
                    pS, pS_b = ps_s2.next()
                    for dc in range(2):
                        S.op("pe", lambda E, pS=pS, qq=qq, dc=dc, mt=mt, hd=hd: E.matmul(pS[:, :], kT[:, hd, dc, mt * 128:(mt + 1) * 128], qq[:, dc, :],
                                                                                         start=(dc == 0), stop=(dc == 1)),
                             reads=[kT_b, qq_b], writes=[pS_b])
                    S.op("act", lambda E, pm_=pm_, pS=pS, mt=mt: E.activation(pm_[:, mt, :], pS[:, :], AF.Exp, scale=1.0 / 16.0),
                         reads=[pS_b], writes=[pm_b])
                pq, pq_b = ps_q.next()
                for mt in range(2):
                    S.op("pe", lambda E, pq=pq, pm_=pm_, mt=mt: E.matmul(pq[:, :], ones_bf, pm_[:, mt, :], start=(mt == 0), stop=(mt == 1)),
                         reads=[pm_b, cbf_b], writes=[pq_b])
                rc, rc_b = rcp.next()
                S.op("dve", lambda E, rc=rc, pq=pq: E.reciprocal(rc, pq[:, :]), reads=[pq_b], writes=[rc_b])
                for dc in range(2):
                    po, po_b = ps_o2.next()
                    for mt in range(2):
                        S.op("pe", lambda E, po=po, pm_=pm_, mt=mt, dc=dc, hd=hd: E.matmul(po[:, :], Vm[:, mt, hd * 256 + dc * 128:hd * 256 + (dc + 1) * 128],
                                                                                           pm_[:, mt, :], start=(mt == 0), stop=(mt == 1)),
                             reads=[Vm_b, pm_b], writes=[po_b])
                    S.op("dve", lambda E, po=po, rc=rc, hd=hd, dc=dc: E.tensor_tensor(oT[:, hd * 2 + dc, :], po[:, :], rc, ALU.mult),
                         reads=[po_b, rc_b], writes=[oT_b[hd * 2 + dc]])
            for i in range(8):
                wt, wt_b = wfm.next()
                wload(wo[:, :, i * 128:(i + 1) * 128], 8, 128, wt, wt_b)
                po, po_b = ps_x.next()
                for c in range(8):
                    S.op("pe", lambda E, c=c, po=po, wt=wt: E.matmul(po[:, :], wt[:, c, :], oT[:, c, :], start=(c == 0), stop=(c == 7)),
                         reads=[wt_b, oT_b[c]], writes=[po_b])
                xs = xT[:, i, g * 512:(g + 1) * 512]
                S.op("dve", lambda E, xs=xs, po=po: E.tensor_tensor(xs, po[:, :], xs, ALU.add), reads=[po_b, xTb[i][g]], writes=[xTb[i][g]])
        S.barrier()

    def sin_reduced(out_ap, out_b, ang_ap, ang_b, wk, wk_b, nf, nf_b, p0, p1, add=0.0):
        ni = nint[p0:p1, :]
        S.op("dve", lambda E: E.tensor_scalar(wk, ang_ap, float(add), None, ALU.add), reads=[ang_b], writes=[wk_b])
        S.op("dve", lambda E: E.tensor_scalar(ni, wk, float(1 / TWO_PI), None, ALU.mult), reads=[wk_b], writes=[nint_b])
        S.op("dve", lambda E: E.tensor_copy(nf, ni), reads=[nint_b], writes=[nf_b])
        S.op("dve", lambda E: E.scalar_tensor_tensor(wk, nf, -TWO_PI, wk, ALU.mult, ALU.add), reads=[nf_b, wk_b], writes=[wk_b])
        S.op("dve", lambda E: E.tensor_scalar(nf, wk, PI, TWO_PI, ALU.is_gt, ALU.mult), reads=[wk_b], writes=[nf_b])
        S.op("dve", lambda E: E.tensor_tensor(wk, wk, nf, ALU.subtract), reads=[nf_b, wk_b], writes=[wk_b])
        S.op("dve", lambda E: E.tensor_scalar(nf, wk, -PI, TWO_PI, ALU.is_lt, ALU.mult), reads=[wk_b], writes=[nf_b])
        S.op("dve", lambda E: E.tensor_tensor(wk, wk, nf, ALU.add), reads=[nf_b, wk_b], writes=[wk_b])
        S.op("act", lambda E: E.activation(out_ap, wk, AF.Sin), reads=[wk_b], writes=[out_b])

    class _Stop(Exception):
        pass

    def _stage(n):
        for k in range(1, 9):
            if ('a1s%d' % k) in DBG and n >= k:
                raise _Stop()

    def phase_A1():
        try:
            phase_A1_()
        except _Stop:
            S.barrier()

    def phase_A1_():
        cf.reset()
        ch.reset()
        rstd = [cf.take(512) for _ in range(4)]
        rstd_b = [B() for _ in range(4)]
        tmp, tmp_b = cf.take(512), B()
        wsetup(2, 1024)
        tabs = [cf.take(512) for _ in range(4)]
        tab_b = [B() for _ in range(4)]
        cqraw, cqraw_b = cf.take(2048).rearrange("p (c n) -> p c n", c=4), B()
        fa, fa_b = cf.take(512), B()
        fb, fb_b = cf.take(512), B()
        fc, fc_b = cf.take(512), B()
        kra, kra_b = cf.take(512), B()
        rqk = rotf(2, 512)
        xb = ch.take(16384).rearrange("p (c n) -> p c n", c=8)
        xb_b = [B() for _ in range(4)]
        sq, sq_b = ch.take(4096).rearrange("p (c n) -> p c n", c=8), B()
        Win, Win_b = ch.take(6400).rearrange("p (c n) -> p c n", c=8), B()
        Wuq, Wuq_b = ch.take(6144).rearrange("p (c n) -> p c n", c=4), B()
        Wqs, Wqs_b = ch.take(2048).rearrange("p (c n) -> p c n", c=4), B()
        Wuk, Wuk_b = ch.take(2048).rearrange("p (c n) -> p c n", c=2), B()
        Wuv, Wuv_b = ch.take(2048).rearrange("p (c n) -> p c n", c=2), B()
        cqn, cqn_b = ch.take(2048).rearrange("p (c n) -> p c n", c=4), B()
        ckvn, ckvn_b = ch.take(1024).rearrange("p (c n) -> p c n", c=2), B()
        krb, krb_b = ch.take(512), B()
        sqh = ch.take(512)
        sqh_lo, sqh_hi = B(), B()
        sqq, sqq_b = ch.take(512), B()
        qst = roth(1, 512)
        vst = roth(1, 512)
        ps_ss_b = B("ps")
        ps_p = psrot([1, 2])
        ps_a = psrot([3, 4])
        ps_b2 = psrot([5])
        ps_q = psrot([6])
        ps_r = psrot([7])

        def wl(src, kc, ncol, dst, dst_b):
            v = src.rearrange("(c p) n -> p c n", p=128)
            step = max(128, (1024 // kc) // 128 * 128) if ncol >= 128 else ncol
            for n0 in range(0, ncol, step):
                n1 = min(ncol, n0 + step)
                wload(v[:, :, n0:n1], kc, n1 - n0, dst[:, :, n0:n1], dst_b)

        wl(Wd["mla_in"], 8, 768, Win, Win_b)
        wload(Wd["mla_in"].rearrange("(c p) n -> p c n", p=128)[:, :, 768:800], 8, 32, Win[:, :, 768:800], Win_b)
        wl(Wd["mla_uq"], 4, 1536, Wuq, Wuq_b)
        wl(Wd["mla_uq_sw"], 4, 512, Wqs, Wqs_b)
        wl(Wd["mla_uk"], 2, 1024, Wuk, Wuk_b)
        wl(Wd["mla_uv"], 2, 1024, Wuv, Wuv_b)
        for g in range(4):
            norm_prep("mix", 8, g, xb[:, :, g * 512:(g + 1) * 512], xb_b[g], rstd[g], rstd_b[g], sq, sq_b, 0, ps_ss_b, tmp, tmp_b)

        P0, P1 = 64, 96
        for pass_ in ("kv", "q"):
            for g in range(1 if 'a1small' in DBG else 4):
                gs = slice(g * 512, (g + 1) * 512)
                S.dma("sp", posi[P0:P1, :], posrep_d[:, gs], posi_b, writes=[posi_b])
                S.op("dve", lambda E: E.tensor_copy(fa[P0:P1, :], posi[P0:P1, :]), reads=[posi_b], writes=[fa_b])
                S.op("dve", lambda E: E.tensor_scalar(fa[P0:P1, :], fa[P0:P1, :], gcol("invf", 0, P0, P1), None, ALU.mult),
                     reads=[fa_b, gcols_b], writes=[fa_b])
                sin_reduced(tabs[0][P0:P1, :], tab_b[0], fa[P0:P1, :], fa_b, fb[P0:P1, :], fb_b, fc[P0:P1, :], fc_b, P0, P1, add=PI / 2)
                sin_reduced(tabs[1][P0:P1, :], tab_b[1], fa[P0:P1, :], fa_b, fb[P0:P1, :], fb_b, fc[P0:P1, :], fc_b, P0, P1, add=0.0)
                S.op("dve", lambda E: E.tensor_scalar(tabs[2][P0:P1, :], tabs[0][P0:P1, :], gcol("mkg", 0, P0, P1), None, ALU.mult),
                     reads=[tab_b[0], gcols_b], writes=[tab_b[2]])
                S.op("dve", lambda E: E.tensor_scalar(tabs[3][P0:P1, :], tabs[1][P0:P1, :], gcol("sgn", 0, P0, P1), gcol("mkg_sw", 0, P0, P1), ALU.mult, ALU.mult),
                     reads=[tab_b[1], gcols_b], writes=[tab_b[3]])
                S.op("dve", lambda E: E.tensor_scalar(tabs[0][P0:P1, :], tabs[0][P0:P1, :], gcol("mqg", 0, P0, P1), None, ALU.mult),
                     reads=[tab_b[0], gcols_b], writes=[tab_b[0]])
                S.op("dve", lambda E: E.tensor_scalar(tabs[1][P0:P1, :], tabs[1][P0:P1, :], gcol("sgn", 0, P0, P1), gcol("mqg_sw", 0, P0, P1), ALU.mult, ALU.mult),
                     reads=[tab_b[1], gcols_b], writes=[tab_b[1]])

                for (n_ch, col0, dst, dst_b, gname, nfeat) in (((4, 0, cqn, cqn_b, "qlora", 512),) if pass_ == "q" else ((2, 512, ckvn, ckvn_b, "kvlora", 256),)):
                    for oc in range(n_ch):
                        pp, pp_b = ps_p.next()
                        for c in range(8):
                            S.op("pe", lambda E, c=c, pp=pp, oc=oc, col0=col0: E.matmul(pp[:, :], Win[:, c, col0 + oc * 128:col0 + (oc + 1) * 128], xb[:, c, gs],
                                                                                        start=(c == 0), stop=(c == 7)),
                                 reads=[Win_b, xb_b[g]], writes=[pp_b])
                        S.op("dve", lambda E, pp=pp, oc=oc: E.tensor_tensor(cqraw[:, oc, :], pp[:, :], rstd[g], ALU.mult),
                             reads=[pp_b, rstd_b[g]], writes=[cqraw_b])
                        S.op("act", lambda E, oc=oc: E.activation(sq[:, oc, :], cqraw[:, oc, :], AF.Square), reads=[cqraw_b], writes=[sq_b])
                    pq, pq_b = ps_q.next()
                    for oc in range(n_ch):
                        S.op("pe", lambda E, pq=pq, oc=oc, n_ch=n_ch: E.matmul(pq[:, :], ones_bf, sq[:, oc, :], start=(oc == 0), stop=(oc == n_ch - 1)),
                             reads=[sq_b, cbf_b], writes=[pq_b])
                    rr, rr_b = rqk.next()
                    rstd_from_ss(rr, rr_b, pq[:, :], pq_b, nfeat, tmp, tmp_b)
                    for oc in range(n_ch):
                        S.op("dve", lambda E, oc=oc, dst=dst, rr=rr, gname=gname: E.scalar_tensor_tensor(dst[:, oc, :], cqraw[:, oc, :], gcol(gname, oc), rr, ALU.mult, ALU.mult),
                             reads=[cqraw_b, rr_b, gcols_b], writes=[dst_b])

                if pass_ == "kv":
                    pp, pp_b = ps_p.next()
                    for c in range(8):
                        S.op("pe", lambda E, c=c, pp=pp: E.matmul(pp[0:32, :], Win[:, c, 768:800], xb[:, c, gs], start=(c == 0), stop=(c == 7)),
                             reads=[Win_b, xb_b[g]], writes=[pp_b])
                    S.op("dve", lambda E, pp=pp: E.tensor_tensor(krb[0:32, :], pp[0:32, :], rstd[g][0:32, :], ALU.mult), reads=[pp_b, rstd_b[g]], writes=[krb_b])
                    if os.environ.get("KSUB") == "1":
                        raise _Stop()
                    pa, pa_b = ps_a.next()
                    pb_, pb_b = ps_b2.next()
                    S.op("pe", lambda E, pa=pa: E.matmul(pa[P0:P1, :], ident_bf[0:32, 0:32], krb[0:32, :], start=True, stop=True), reads=[krb_b, cbf_b], writes=[pa_b])
                    S.op("pe", lambda E, pb_=pb_: E.matmul(pb_[P0:P1, :], swap_bf[0:32, 0:32], krb[0:32, :], start=True, stop=True), reads=[krb_b, cbf_b], writes=[pb_b])
                    if os.environ.get("KSUB") == "2":
                        raise _Stop()
                    S.op("act", lambda E, pa=pa: E.activation(sqh[P0:P1, :], pa[P0:P1, :], AF.Square), reads=[pa_b], writes=[sqh_hi])
                    if os.environ.get("KSUB") == "3":
                        raise _Stop()
                    S.op("act", lambda E, pa=pa: E.copy(kra[P0:P1, :], pa[P0:P1, :]), reads=[pa_b], writes=[kra_b])
                    S.op("act", lambda E, pb_=pb_: E.copy(fb[P0:P1, :], pb_[P0:P1, :]), reads=[pb_b], writes=[fb_b])
                    S.op("dve", lambda E: E.tensor_tensor(kra[P0:P1, :], kra[P0:P1, :], tabs[2][P0:P1, :], ALU.mult), reads=[kra_b, tab_b[2]], writes=[kra_b])
                    S.op("dve", lambda E: E.tensor_tensor(fb[P0:P1, :], fb[P0:P1, :], tabs[3][P0:P1, :], ALU.mult), reads=[fb_b, tab_b[3]], writes=[fb_b])
                    S.op("dve", lambda E: E.tensor_tensor(kra[P0:P1, :], kra[P0:P1, :], fb[P0:P1, :], ALU.add), reads=[kra_b, fb_b], writes=[kra_b])

                for h in range(2 if 'a1small' in DBG else 16):
                    if pass_ == "q":
                        pa, pa_b = ps_a.next()
                        pb_, pb_b = ps_b2.next()
                        for c in range(4):
                            S.op("pe", lambda E, c=c, pa=pa, h=h: E.matmul(pa[0:96, :], Wuq[:, c, h * 96:(h + 1) * 96], cqn[:, c, :], start=(c == 0), stop=(c == 3)),
                                 reads=[Wuq_b, cqn_b], writes=[pa_b])
                        for c in range(4):
                            S.op("pe", lambda E, c=c, pb_=pb_, h=h: E.matmul(pb_[P0:P1, :], Wqs[:, c, h * 32:(h + 1) * 32], cqn[:, c, :], start=(c == 0), stop=(c == 3)),
                                 reads=[Wqs_b, cqn_b], writes=[pb_b])
                        S.op("act", lambda E, pa=pa: E.activation(sqq[0:96, :], pa[0:96, :], AF.Square), reads=[pa_b], writes=[sqq_b])
                        pr, pr_b = ps_r.next()
                        S.op("pe", lambda E, pr=pr: E.matmul(pr[0:96, :], ones_bf[0:96, 0:96], sqq[0:96, :], start=True, stop=True), reads=[sqq_b, cbf_b], writes=[pr_b])
                        rr, rr_b = rqk.next()
                        rstd_from_ss(rr[0:96, :], rr_b, pr[0:96, :], pr_b, 96, tmp[0:96, :], tmp_b, 0, 96)
                        qs, qs_b = qst.next()
                        S.op("dve", lambda E, qs=qs, pa=pa, rr=rr: E.scalar_tensor_tensor(qs[0:64, :], pa[0:64, :], gcol("mqg", 0, 0, 64), rr[0:64, :], ALU.mult, ALU.mult),
                             reads=[pa_b, rr_b, gcols_b], writes=[qs_b])
                        S.op("act", lambda E, pa=pa: E.copy(fa[P0:P1, :], pa[P0:P1, :]), reads=[pa_b], writes=[fa_b])
                        S.op("act", lambda E, pb_=pb_: E.copy(fb[P0:P1, :], pb_[P0:P1, :]), reads=[pb_b], writes=[fb_b])
                        S.op("dve", lambda E: E.tensor_tensor(fa[P0:P1, :], fa[P0:P1, :], tabs[0][P0:P1, :], ALU.mult), reads=[fa_b, tab_b[0]], writes=[fa_b])
                        S.op("dve", lambda E: E.tensor_tensor(fb[P0:P1, :], fb[P0:P1, :], tabs[1][P0:P1, :], ALU.mult), reads=[fb_b, tab_b[1]], writes=[fb_b])
                        S.op("dve", lambda E: E.tensor_tensor(fa[P0:P1, :], fa[P0:P1, :], fb[P0:P1, :], ALU.add), reads=[fa_b, fb_b], writes=[fa_b])
                        S.op("dve", lambda E, qs=qs, rr=rr: E.tensor_tensor(qs[P0:P1, :], fa[P0:P1, :], rr[P0:P1, :], ALU.mult), reads=[fa_b, rr_b], writes=[qs_b])
                        S.dma("sp", q1_d[h * 96:(h + 1) * 96, gs], qs[0:96, :], qs_b, reads=[qs_b], writes=[db["q1"]])

                    if pass_ == "kv":
                        pa, pa_b = ps_a.next()
                        for c in range(2):
                            S.op("pe", lambda E, c=c, pa=pa, h=h: E.matmul(pa[0:64, :], Wuk[:, c, h * 64:(h + 1) * 64], ckvn[:, c, :], start=(c == 0), stop=(c == 1)),
                                 reads=[Wuk_b, ckvn_b], writes=[pa_b])
                        S.op("act", lambda E, pa=pa: E.activation(sqh[0:64, :], pa[0:64, :], AF.Square), reads=[pa_b], writes=[sqh_lo])
                        pr, pr_b = ps_r.next()
                        S.op("pe", lambda E, pr=pr: E.matmul(pr[0:96, :], ones_bf[0:96, 0:96], sqh[0:96, :], start=True, stop=True), reads=[sqh_lo, sqh_hi, cbf_b], writes=[pr_b])
                        rr, rr_b = rqk.next()
                        rstd_from_ss(rr[0:96, :], rr_b, pr[0:96, :], pr_b, 96, tmp[0:96, :], tmp_b, 0, 96)
                        ks, ks_b = qst.next()
                        S.op("dve", lambda E, ks=ks, pa=pa, rr=rr: E.scalar_tensor_tensor(ks[0:64, :], pa[0:64, :], gcol("mkg", 0, 0, 64), rr[0:64, :], ALU.mult, ALU.mult),
                             reads=[pa_b, rr_b, gcols_b], writes=[ks_b])
                        S.op("dve", lambda E, ks=ks, rr=rr: E.tensor_tensor(ks[P0:P1, :], kra[P0:P1, :], rr[P0:P1, :], ALU.mult), reads=[kra_b, rr_b], writes=[ks_b])
                        S.dma("sp", k1_d[h * 96:(h + 1) * 96, gs], ks[0:96, :], ks_b, reads=[ks_b], writes=[db["k1"]])

                if pass_ == "kv":
                    for tt in range(4):
                        t = g * 4 + tt
                        for half in range(2):
                            pp, pp_b = ps_p.next()
                            for c in range(2):
                                S.op("pe", lambda E, c=c, pp=pp, tt=tt, half=half: E.matmul(pp[:, :], ckvn[:, c, tt * 128:(tt + 1) * 128], Wuv[:, c, half * 512:(half + 1) * 512],
                                                                                             start=(c == 0), stop=(c == 1)),
                                     reads=[Wuv_b, ckvn_b], writes=[pp_b])
                            vs, vs_b = vst.next()
                            S.op("act", lambda E, vs=vs, pp=pp: E.copy(vs, pp[:, :]), reads=[pp_b], writes=[vs_b])
                            S.dma("sp", v1_d[t * 128:(t + 1) * 128, half * 512:(half + 1) * 512], vs, vs_b, reads=[vs_b], writes=[db["v1"]])
            if pass_ == "kv" and exch == "cc":
                allgather(k1_d, db["k1"], kg1_d, db["kg1"], 1536, RC_K1)
                allgather(v1_d, db["v1"], vg1_d, db["vg1"], T, RC_V1)
        S.barrier()

    o1_d = nc.dram_tensor("o1_d", [D, T], BF16).ap()
    o1_b = Buf("o1")

    def phase_B1():
        cf.reset()
        ch.reset()
        wsetup()
        tmpA, tmpA_b = cf.take(512), B()
        rcf, rcf_b = cf.take(512), B()
        rcs, rcs_b = cf.take(512), B()
        KK = ch.take(16384)
        Kr = Rot([(KK[:, 0:8192], B()), (KK[:, 8192:16384], B())])
        Vr = roth(2, 8192, lambda a: a.rearrange("p (r t d) -> p r t d", r=4, t=16))
        qr = roth(2, 2048)
        Pr = roth(3, 512)
        msk, msk_b = ch.take(1024).rearrange("p (v n) -> p v n", v=8), B()
        rhi, rhi_b = ch.take(512), B()
        rlo, rlo_b = ch.take(512), B()
        ost = roth(2, 512)
        wfm = roth(2, 1024, lambda a: a.rearrange("p (c n) -> p c n", c=8))
        ps_s = psrot([0, 1, 2])
        ps_o = psrot([3, 4])
        ps_r = psrot([5])
        S.dma("sp", msk, masks_d[:, 1024:2048].rearrange("p (v n) -> p v n", v=8), msk_b, writes=[msk_b])
        for (Vt, Vt_b) in Vr.slots:
            for rk in range(4):
                S.op("pool", lambda E: E.memset(Vt[:, rk, :, 64:128], 1.0), writes=[Vt_b])
        SC = float(96 ** -0.5)
        dsl = slice(64, 128)
        osl = slice(0, 64)
        def issue_loads(h):
            Kh, Kh_b = Kr.next()
            Vh, Vh_b = Vr.next()
            qh, qh_b = qr.next()
            S.dma("sp", qh[0:96, :], q1_d[h * 96:(h + 1) * 96, :], qh_b, reads=[db["q1"]], writes=[qh_b])
            for rk in range(4):
                S.dma("sp", Kh[0:96, rk * T:(rk + 1) * T], kg1_d[grow(RC_K1, rk, h * 96):grow(RC_K1, rk, h * 96) + 96, :], Kh_b,
                      reads=[db["kg1"]], writes=[Kh_b])
                for tq in range(4):
                    r0 = grow(RC_V1, rk, tq * 512)
                    S.dma("sp", Vh[:, rk, tq * 4:(tq + 1) * 4, 0:64],
                          vg1_d[r0:r0 + 512, h * 64:(h + 1) * 64].rearrange("(t p) d -> p t d", p=128), Vh_b, reads=[db["vg1"]], writes=[Vh_b])
            return Kh, Kh_b, Vh, Vh_b, qh, qh_b

        nxt = issue_loads(0)
        for h in range(16):
            Kh, Kh_b, Vh, Vh_b, qh, qh_b = nxt
            if h + 1 < 16:
                nxt = issue_loads(h + 1)
            for u in range(4):
                po, po_b = ps_o.next()
                nblk = 16 * u + 16
                pend = []
                for j in range(nblk):
                    rk, tj, mj = key_loc(j)
                    kcol = rk * T + tj * 128
                    diag = j >= 16 * u
                    col0, vidx = 0, 0
                    if diag:
                        m = (j - 16 * u) // 4
                        col0 = 128 * m
                        vidx = ((4 * u + m) % 2) * 4 + mj
                    pS, pS_b = ps_s.next()
                    S.op("pe", lambda E: E.matmul(pS[:, col0:], Kh[0:96, kcol:kcol + 128], qh[0:96, u * 512 + col0:(u + 1) * 512], start=True, stop=not diag),
                         reads=[Kh_b, qh_b], writes=[pS_b])
                    if diag:
                        S.op("pe", lambda E: E.matmul(pS[:, col0:col0 + 128], ident_bf, msk[:, vidx, :], start=False, stop=True, skip_group_check=True),
                             reads=[cbf_b, msk_b], writes=[pS_b])
                    P_, P_b = Pr.next()
                    S.op("act", lambda E: E.activation(P_[:, col0:], pS[:, col0:], AF.Exp, scale=SC), reads=[pS_b], writes=[P_b])

                    def pv(po=po, po_b=po_b, P_=P_, P_b=P_b, col0=col0, rk=rk, tj=tj, j=j, nblk=nblk, Vh=Vh, Vh_b=Vh_b):
                        S.op("pe", lambda E: E.matmul(po[:, col0:], Vh[:, rk, tj, :], P_[:, col0:], start=(j == 0), stop=(j == nblk - 1),
                                                      skip_group_check=True),
                             reads=[Vh_b, P_b], writes=[po_b])
                    if pend:
                        pend.pop()()
                    pend.append(pv)
                pend.pop()()
                S.op("act", lambda E: E.activation(tmpA[dsl, :], po[dsl, :], AF.Ln), reads=[po_b], writes=[tmpA_b])
                S.op("act", lambda E: E.activation(rcf[dsl, :], tmpA[dsl, :], AF.Exp, scale=-1.0), reads=[tmpA_b], writes=[rcf_b])
                S.op("dve", lambda E: E.tensor_copy(rhi[dsl, :], rcf[dsl, :]), reads=[rcf_b], writes=[rhi_b])
                S.op("dve", lambda E: E.tensor_tensor(rlo[dsl, :], rcf[dsl, :], rhi[dsl, :], ALU.subtract), reads=[rcf_b, rhi_b], writes=[rlo_b])
                pr, pr_b = ps_r.next()
                S.op("pe", lambda E: E.matmul(pr[osl, :], ident_bf[dsl, 64:128], rhi[dsl, :], start=True, stop=False), reads=[cbf_b, rhi_b], writes=[pr_b])
                S.op("pe", lambda E: E.matmul(pr[osl, :], ident_bf[dsl, 64:128], rlo[dsl, :], start=False, stop=True), reads=[cbf_b, rlo_b], writes=[pr_b])
                S.op("act", lambda E: E.copy(rcs[osl, :], pr[osl, :]), reads=[pr_b], writes=[rcs_b])
                os_, os_b = ost.next()
                S.op("dve", lambda E: E.tensor_tensor(os_[osl, :], po[osl, :], rcs[osl, :], ALU.mult), reads=[po_b, rcs_b], writes=[os_b])
                S.dma("pool", o1_d[h * 64:(h + 1) * 64, u * 512:(u + 1) * 512], os_[osl, :], os_b, reads=[os_b], writes=[o1_b])
        S.barrier()
        oT = KK.rearrange("p (c n) -> p c n", c=8)
        oT_b = B()
        S.dma("sp", oT, o1_d.rearrange("(c p) n -> p c n", p=128), oT_b, reads=[o1_b], writes=[oT_b])
        out_proj([oT[:, c, :] for c in range(8)], [oT_b] * 8, Wd["mla_out"], wfm, psrot([0, 1, 2]))
        S.barrier()

    load_x()
    seq = [lambda: ffn(Wd[("gu", "pre", 0)], Wd[("d", "pre", 0)], "ffn_pre", 0),
           phase_A0, phase_B0, lambda: phase_X(0),
           lambda: ffn(Wd[("gu", "post", 0)], Wd[("d", "post", 0)], "ffn_post", 0),
           lambda: ffn(Wd[("gu", "pre", 1)], Wd[("d", "pre", 1)], "ffn_pre", 16),
           phase_A1, phase_B1, lambda: phase_X(1),
           lambda: ffn(Wd[("gu", "post", 1)], Wd[("d", "post", 1)], "ffn_post", 16)]
    for i in range(lo, hi):
        seq[i]()
    for n in db:
        if db[n].lw is not None:
            S._wait("sp", db[n].lw)
    store_x()
    S.run()
    print('[build] phases', lo, hi, 'counts', dict(S.cnt), 'nsems', len(S.sems), 'maxdma', max([0] + [v for k, v in S.alltoks.items() if str(k).startswith('d')]), flush=True)
    return nc, used_inputs, outputs


def zig(t, r):
    return 4 * t + (r if t % 2 == 0 else 3 - r)


def shard_tokens(x):
    outs = []
    for c in range(NCORES):
        b, r = c // 4, c % 4
        blocks = [zig(t, r) for t in range(NT)]
        xb = x[b].reshape(64, 128, -1)
        outs.append(np.ascontiguousarray(xb[blocks].reshape(T, -1)))
    return outs


def unshard_tokens(outs, dtype=np.float32):
    full = np.zeros((2, 64, 128, D), dtype)
    for c in range(NCORES):
        b, r = c // 4, c % 4
        blocks = [zig(t, r) for t in range(NT)]
        full[b, blocks] = outs[c].reshape(NT, 128, D)
    return full.reshape(2, 8192, D)


def col(v):
    v = np.asarray(v, np.float32)
    return np.ascontiguousarray(v.reshape(-1, 128).T)


def make_masks(r):
    k = np.arange(128)[:, None]
    q = np.arange(128)[None, :]
    m = np.zeros((128, 16, 128), np.float32)
    for kind in range(2):
        tri = (k < q) if kind == 0 else (k <= q)
        for p in range(2):
            rr = r if p == 0 else 3 - r
            for mj in range(4):
                if mj < rr:
                    v = np.zeros((128, 128), np.float32)
                elif mj == rr:
                    v = np.where(tri, 0.0, NEG).astype(np.float32)
                else:
                    v = np.full((128, 128), NEG, np.float32)
                m[:, kind * 8 + p * 4 + mj, :] = v
    return m.reshape(128, 16 * 128).astype(ml_dtypes.bfloat16)


_PROG_CACHE = {}


def prepare_inputs(inp):
    f32 = np.float32
    ident = np.eye(128, dtype=f32)
    ones = np.ones((128, 128), f32)
    negU = -np.tril(np.ones((128, 128), f32))
    sw = np.zeros((128, 128), f32)
    for i in range(16):
        sw[16 + i, i] = 1.0
        sw[i, 16 + i] = 1.0
    cb = np.concatenate([ident, ones, negU, sw], axis=1).astype(ml_dtypes.bfloat16)
    gc = np.zeros((128, NGC), f32)
    for l in range(2):
        gc[:, 16 * l:16 * l + 8] = col(inp["ffn_pre_norm"][l])
        gc[:, 16 * l + 8:16 * l + 16] = col(inp["ffn_post_norm"][l])
        gc[:, 32 + 8 * l:40 + 8 * l] = col(inp["mix_norm"][l])
        gc[:, 48 + 8 * l:56 + 8 * l] = col(inp["xmem_norm"][l])
        gc[:, 64 + 8 * l:72 + 8 * l] = col(inp["xmem_mem_norm"][l])
        gc[:, 80 + 2 * l:82 + 2 * l] = col(inp["xmem_q_gain"][l])
        gc[:, 84 + 2 * l:86 + 2 * l] = col(inp["xmem_k_gain"][l])
    gc[:, 88:92] = col(inp["mla_q_lora_gain"][0])
    gc[:, 92:94] = col(inp["mla_kv_lora_gain"][0])
    perm = np.arange(96)
    perm[64:80] = np.arange(80, 96)
    perm[80:96] = np.arange(64, 80)
    gc[:96, 94] = inp["mla_q_gain"][0]
    gc[:96, 95] = inp["mla_k_gain"][0]
    gc[:96, 96] = inp["mla_q_gain"][0][perm]
    gc[:96, 97] = inp["mla_k_gain"][0][perm]
    half = 16
    invf = (f32(10000.0) ** (-np.arange(half, dtype=f32) / f32(half))).astype(f32)
    gc[64:80, 98] = invf
    gc[80:96, 98] = invf
    gc[64:80, 99] = -1.0
    gc[80:96, 99] = 1.0
    gc[:, 100] = EPS
    gc[:, 101] = np.pi / 2
    gc[:, 102] = 1.0
    bc0 = np.zeros((128, 1536), f32)
    bc0[:, 0:512] = inp["sgu_ln_gain"][0][None, :]
    bc0[:, 512:1024] = inp["sgu_ln_bias"][0][None, :]
    sb = inp["sgu_b"][0]
    for cc in range(4):
        bc0[0:64, 1024 + cc * 128:1024 + (cc + 1) * 128] = sb[2 * cc][None, :]
        bc0[64:128, 1024 + cc * 128:1024 + (cc + 1) * 128] = sb[2 * cc + 1][None, :]
    trim = (np.arange(128)[:, None] <= np.arange(128)[None, :]).astype(f32)
    sguwT = np.ascontiguousarray(inp["sgu_w"][0].transpose(2, 0, 1)).reshape(128, 8 * 128)
    common = {"consts_bf": cb, "ident_f": ident, "gcols": gc, "bc0": bc0, "trimask": trim, "sgu_wT": sguwT}
    for l in range(2):
        for which in ("pre", "post"):
            common["w_%s_gu_%d" % (which, l)] = np.ascontiguousarray(inp["ffn_%s_w_gu" % which][l])
            common["w_%s_d_%d" % (which, l)] = np.ascontiguousarray(inp["ffn_%s_w_down" % which][l])
        common["w_xq_%d" % l] = np.ascontiguousarray(inp["xmem_wq"][l])
        wkv = inp["xmem_wkv"][l].reshape(D, 4, 2, 256)
        common["w_xk_%d" % l] = np.ascontiguousarray(wkv[:, :, 0, :].reshape(D, D))
        common["w_xv_%d" % l] = np.ascontiguousarray(wkv[:, :, 1, :].reshape(D, D))
        common["w_xo_%d" % l] = np.ascontiguousarray(inp["xmem_wo"][l])
    common["w_sbg_in"] = np.ascontiguousarray(inp["sbg_w_in"][0])
    common["w_sbg_out"] = np.ascontiguousarray(inp["sbg_w_out"][0])
    common["w_mla_in"] = np.ascontiguousarray(inp["mla_w_in"][0])
    uq = inp["mla_w_uq"][0]
    common["w_mla_uq"] = np.ascontiguousarray(uq)
    uq_h = uq.reshape(512, 16, 96)
    common["w_mla_uq_sw"] = np.ascontiguousarray(np.concatenate([uq_h[:, :, 80:96], uq_h[:, :, 64:80]], axis=2).reshape(512, 512))
    ukv = inp["mla_w_ukv"][0].reshape(256, 16, 128)
    common["w_mla_uk"] = np.ascontiguousarray(ukv[:, :, 0:64].reshape(256, 1024))
    common["w_mla_uv"] = np.ascontiguousarray(ukv[:, :, 64:128].reshape(256, 1024))
    common["w_mla_out"] = np.ascontiguousarray(inp["mla_w_out"][0])
    xs = shard_tokens(inp["x"])
    pos = inp["positions"].astype(np.int32)
    in_maps = []
    for c in range(NCORES):
        b, r = c // 4, c % 4
        blocks = [zig(t, r) for t in range(NT)]
        pl = pos[b].reshape(64, 128)[blocks].reshape(T)
        m = dict(common)
        m["x_in"] = xs[c]
        m["masks"] = make_masks(r)
        m["posrep"] = np.ascontiguousarray(np.broadcast_to(pl[None, :], (32, T))).astype(np.int32)
        m["mem"] = np.ascontiguousarray(inp["mem"][b])
        in_maps.append(m)
    return in_maps


EXCH = "cc"
DEBUG_X = {}


def _run(key, in_maps):
    if key not in _PROG_CACHE:
        _PROG_CACHE[key] = build_program(*key)
    nc, used, outs = _PROG_CACHE[key]
    maps = [{k: v for k, v in m.items() if k in used} for m in in_maps]
    res = run_bass_kernel_spmd(nc, maps, core_ids=list(range(NCORES)))
    return res.results


def _gather(results, name, rc):
    out = []
    for c in range(NCORES):
        b = c // 4
        rows = results[c][name].shape[0]
        parts = []
        for i in range(rows // rc):
            for rk in range(4):
                parts.append(results[4 * b + rk][name][i * rc:(i + 1) * rc])
        out.append(np.ascontiguousarray(np.concatenate(parts, axis=0)))
    return out


def kernel(**inputs):
    inp = {k: np.asarray(v) for k, v in inputs.items()}
    in_maps = prepare_inputs(inp)
    if EXCH == "cc":
        res = _run((0, 10, "cc"), in_maps)
        return unshard_tokens([r["x_out"] for r in res])
    r1 = _run((0, 2, "host"), in_maps)
    kg0, vg0 = _gather(r1, "k0_d", RC_K0), _gather(r1, "v0_d", RC_V0)
    for c in range(NCORES):
        in_maps[c].update({"x_in": r1[c]["x_out"], "q0_d": r1[c]["q0_d"], "osg_d": r1[c]["osg_d"], "kg0_d": kg0[c], "vg0_d": vg0[c]})
    r2 = _run((2, 7, "host"), in_maps)
    DEBUG_X["p2"] = [r["x_out"] for r in r2]
    kg1, vg1 = _gather(r2, "k1_d", RC_K1), _gather(r2, "v1_d", RC_V1)
    for c in range(NCORES):
        in_maps[c].update({"x_in": r2[c]["x_out"], "q1_d": r2[c]["q1_d"], "kg1_d": kg1[c], "vg1_d": vg1[c]})
    r3 = _run((7, 10, "host"), in_maps)
    return unshard_tokens([r["x_out"] for r in r3])
```

```python
import types
import numpy as np
import ml_dtypes
import concourse.bass as bass
import concourse.mybir as mybir
from concourse.bass_utils import run_bass_kernel_spmd

F32 = mybir.dt.float32
BF16 = mybir.dt.bfloat16
I32 = mybir.dt.int32
AF = mybir.ActivationFunctionType
ALU = mybir.AluOpType

NCORES = 8
D = 1024
DFF = 2816
T = 2048
NT = 16
EPS = 1e-6
NEG = -30000.0


class Buf:
    __slots__ = ("name", "lw", "rd", "dkey", "dcnt", "mw")

    def __init__(self, name):
        self.name = name
        self.lw = None
        self.rd = {}
        self.dkey = None
        self.dcnt = 0
        self.mw = {}


ENGS = ("pe", "act", "dve", "pool", "sp")


class Sched:
    def __init__(self, nc):
        self.nc = nc
        self.prog = {e: [] for e in ENGS}
        self.cnt = {e: 0 for e in ENGS}
        self.waited = {e: {} for e in ENGS}
        self.sems = {}
        self.alltoks = {}
        self.dfree = []
        self.dcum = {}
        self.dheld = []

    def sem(self, key):
        if key not in self.sems:
            self.sems[key] = self.nc.alloc_semaphore(name="s_%s" % (str(key).replace(" ", "")))
        return self.sems[key]

    def _wait(self, eng, tok):
        key, val = tok
        if key == eng and eng in ("pe", "sp"):
            return
        if self.waited[eng].get(key, 0) >= val:
            return
        self.waited[eng][key] = val
        h = self.sem(key)
        self.prog[eng].append(lambda E, h=h, val=val: E.wait_ge(h, val))

    def _deps(self, eng, reads, writes, pwrites=()):
        for b in reads:
            if b.lw is not None:
                self._wait(eng, b.lw)
            for k, v in b.mw.items():
                self._wait(eng, (k, v))
        for b in pwrites:
            if b.lw is not None and b.lw[0] != eng:
                self._wait(eng, b.lw)
            for k, v in b.rd.items():
                if k != eng:
                    self._wait(eng, (k, v))
        for b in writes:
            if b.lw is not None and b.lw[0] != eng:
                self._wait(eng, b.lw)
            for k, v in b.mw.items():
                self._wait(eng, (k, v))
            for k, v in b.rd.items():
                if k != eng:
                    self._wait(eng, (k, v))

    def _upd(self, tok, reads, writes):
        self.alltoks[tok[0]] = max(self.alltoks.get(tok[0], 0), tok[1])
        for b in reads:
            if b.rd.get(tok[0], 0) < tok[1]:
                b.rd[tok[0]] = tok[1]
        for b in writes:
            b.lw = tok
            b.rd = {}
            b.mw = {}

    @staticmethod
    def _freeze(fn):
        if fn.__closure__:
            cells = tuple(types.CellType(c.cell_contents) for c in fn.__closure__)
            g = types.FunctionType(fn.__code__, fn.__globals__, fn.__name__, fn.__defaults__, cells)
            g.__kwdefaults__ = fn.__kwdefaults__
            return g
        return fn

    def op(self, eng, fn, reads=(), writes=()):
        fn = self._freeze(fn)
        self._deps(eng, reads, writes)
        self.cnt[eng] += 1
        tok = (eng, self.cnt[eng])
        h = self.sem(eng)
        self.prog[eng].append(lambda E, fn=fn, h=h: fn(E).then_inc(h, 1))
        self._upd(tok, reads, writes)
        return tok

    def dma(self, q, out_ap, in_ap, owner, reads=(), writes=(), pwrites=()):
        self._deps(q, reads, writes, pwrites)
        if owner.dkey is None:
            if self.dfree:
                owner.dkey = self.dfree.pop()
            else:
                owner.dkey = "q%d" % len(self.dcum)
                self.dcum[owner.dkey] = 0
            self.dheld.append(owner)
        self.dcum[owner.dkey] += 16
        tok = (owner.dkey, self.dcum[owner.dkey])
        h = self.sem(owner.dkey)
        self.prog[q].append(lambda E, o=out_ap, i=in_ap, h=h: E.dma_start(out=o, in_=i).then_inc(h, 16))
        self._upd(tok, reads, writes)
        for b in pwrites:
            if b.mw.get(tok[0], 0) < tok[1]:
                b.mw[tok[0]] = tok[1]
        return tok

    def barrier(self, release=True):
        for e in ENGS:
            for k, v in list(self.alltoks.items()):
                self._wait(e, (k, v))
        if release:
            for b in self.dheld:
                self.dfree.append(b.dkey)
                b.dkey = None
            self.dheld = []

    def run(self):
        nc = self.nc
        with nc.Block() as block:
            @block.tensor
            def _(E):
                for f in self.prog["pe"]:
                    f(E)

            @block.scalar
            def _(E):
                for f in self.prog["act"]:
                    f(E)

            @block.vector
            def _(E):
                for f in self.prog["dve"]:
                    f(E)

            @block.gpsimd
            def _(E):
                for f in self.prog["pool"]:
                    f(E)

            @block.sync
            def _(E):
                for f in self.prog["sp"]:
                    f(E)


class Rot:
    def __init__(self, slots):
        self.slots = slots
        self.i = 0

    def next(self):
        s = self.slots[self.i % len(self.slots)]
        self.i += 1
        return s


class Ctx:
    pass


def mk_tile(nc, name, shape, dt):
    return nc.alloc_sbuf_tensor(name, shape, dt)


import os
DBG = os.environ.get('KDBG', '')
RC_K0, RC_V0, RC_K1, RC_V1 = 256, 1024, 192, 512
GC = dict(ffn_pre=0, ffn_post=8, mix=32, xn=48, xmn=64, xqg=80, xkg=84, qlora=88, kvlora=92,
          mqg=94, mkg=95, mqg_sw=96, mkg_sw=97, invf=98, sgn=99, eps=100, halfpi=101, one=102, zero=103)
NGC = 104
TWO_PI = float(2 * np.pi)
PI = float(np.pi)


PHASES = ["F0pre", "A0", "B0", "X0", "F0post", "F1pre", "A1", "B1", "X1", "F1post"]


def build_program(lo=0, hi=10, exch="cc"):
    nc = bass.Bass("TRN2", target_bir_lowering=False)
    S = Sched(nc)
    outputs = ["x_out"]

    used_inputs = set()

    def din(name, shape, dt=F32):
        used_inputs.add(name)
        return nc.dram_tensor(name, list(shape), dt, kind="ExternalInput").ap()

    x_in = din("x_in", [T, D])
    x_out = nc.dram_tensor("x_out", [T, D], F32, kind="ExternalOutput").ap()
    consts_bf = din("consts_bf", [128, 4 * 128], BF16)
    ident_f_d = din("ident_f", [128, 128])
    gcols_d = din("gcols", [128, NGC])
    bc0_d = din("bc0", [128, 1536])
    trim_d = din("trimask", [128, 128])
    sguwT_d = din("sgu_wT", [128, 8 * 128])
    masks_d = din("masks", [128, 16 * 128], BF16)
    posrep_d = din("posrep", [32, T], I32)
    mem_d = din("mem", [256, D])
    WSHAPES = {"sbg_in": [D, 2560], "sbg_out": [D, D], "mla_in": [D, 800], "mla_uq": [512, 1536], "mla_uq_sw": [512, 512],
               "mla_uk": [256, 1024], "mla_uv": [256, 1024], "mla_out": [D, D]}

    class LazyW(dict):
        def __missing__(self, key):
            if isinstance(key, tuple) and key[0] in ("gu", "d"):
                name = "w_%s_%s_%d" % (key[1], key[0], key[2])
                shape = [D, 2 * DFF] if key[0] == "gu" else [DFF, D]
            elif isinstance(key, tuple):
                name = "w_%s_%d" % key
                shape = [D, D]
            else:
                name = "w_" + key
                shape = WSHAPES[key]
            self[key] = din(name, shape)
            return self[key]

    Wd = LazyW()
    def scratch(name, shape, prod, cons):
        if exch == "cc" or (lo <= prod < hi and lo <= cons < hi):
            return nc.dram_tensor(name, shape, BF16).ap()
        if lo <= prod < hi:
            outputs.append(name)
            return nc.dram_tensor(name, shape, BF16, kind="ExternalOutput").ap()
        if lo <= cons < hi:
            return din(name, shape, BF16)
        return None

    gp = 1 if exch == "cc" else -1
    q0_d = scratch("q0_d", [512, T], 1, 2)
    k0_d = scratch("k0_d", [512, T], 1, 2 if exch == "cc" else 99)
    v0_d = scratch("v0_d", [T, 512], 1, 2 if exch == "cc" else 99)
    osg_d = scratch("osg_d", [512, T], 1, 2)
    kg0_d = scratch("kg0_d", [4 * 512, T], gp, 2)
    vg0_d = scratch("vg0_d", [4 * T, 512], gp, 2)
    gp = 6 if exch == "cc" else -1
    q1_d = scratch("q1_d", [1536, T], 6, 7)
    k1_d = scratch("k1_d", [1536, T], 6, 7 if exch == "cc" else 99)
    v1_d = scratch("v1_d", [T, 1024], 6, 7 if exch == "cc" else 99)
    kg1_d = scratch("kg1_d", [4 * 1536, T], gp, 7)
    vg1_d = scratch("vg1_d", [4 * T, 1024], gp, 7)
    db = {n: Buf(n) for n in ("q0", "k0", "v0", "osg", "kg0", "vg0", "q1", "k1", "v1", "kg1", "vg1")}

    xT = nc.alloc_sbuf_tensor("xT", [128, 8, T], F32)
    xTb = [[Buf("xT_%d_%d" % (c, g)) for g in range(4)] for c in range(8)]
    cbf = nc.alloc_sbuf_tensor("cbf", [128, 4 * 128], BF16)
    cbf_b = Buf("cbf")
    ident_f = nc.alloc_sbuf_tensor("sb_ident_f", [128, 128], F32)
    ident_f_b = Buf("ident_f")
    gcols = nc.alloc_sbuf_tensor("sb_gcols", [128, NGC], F32)
    gcols_b = Buf("gcols")
    ident_bf = cbf[:, 0:128]
    ones_bf = cbf[:, 128:256]
    negU_bf = cbf[:, 256:384]
    swap_bf = cbf[:, 384:512]

    def gcol(name, i=0, p0=0, p1=128):
        return gcols[p0:p1, GC[name] + i:GC[name] + i + 1]

    S.dma("sp", cbf[:, :], consts_bf[:, :], cbf_b, writes=[cbf_b])
    S.dma("sp", ident_f[:, :], ident_f_d[:, :], ident_f_b, writes=[ident_f_b])
    S.dma("sp", gcols[:, :], gcols_d[:, :], gcols_b, writes=[gcols_b])

    posi = nc.alloc_sbuf_tensor("posi", [128, 512], I32)
    posi_b = Buf("posi")
    nint = nc.alloc_sbuf_tensor("nint", [128, 512], I32)
    nint_b = Buf("nint")
    AF32 = 11776
    ABF = 45056
    arena_f = nc.alloc_sbuf_tensor("arena_f", [128, AF32], F32)
    arena_h = nc.alloc_sbuf_tensor("arena_h", [128, ABF], BF16)
    psum = [nc.alloc_psum_tensor("ps%d" % i, [128, 512], F32) for i in range(8)]

    class Carver:
        def __init__(self, ar, size):
            self.ar, self.size, self.off = ar, size, 0

        def reset(self):
            self.off = 0

        def take(self, n):
            assert self.off + n <= self.size, ("arena overflow", self.off, n, self.size)
            a = self.ar[:, self.off:self.off + n]
            self.off += n
            return a

    cf = Carver(arena_f, AF32)
    ch = Carver(arena_h, ABF)
    nbuf = [0]

    def B(name="b"):
        nbuf[0] += 1
        return Buf("%s%d" % (name, nbuf[0]))

    def rotf(n, size, view=None):
        return Rot([((cf.take(size) if view is None else view(cf.take(size))), B()) for _ in range(n)])

    def roth(n, size, view=None):
        return Rot([((ch.take(size) if view is None else view(ch.take(size))), B()) for _ in range(n)])

    def psrot(idxs):
        return Rot([(psum[i], B("ps")) for i in idxs])

    W = Ctx()

    def wsetup(nst=2, stsize=2048):
        W.st = rotf(nst, stsize)

    castctr = [0]

    def cast(dst, dst_b, src, src_b):
        castctr[0] += 1
        if castctr[0] % 2 == 0:
            S.op("act", lambda E: E.copy(dst, src), reads=[src_b], writes=[dst_b])
        else:
            S.op("dve", lambda E: E.tensor_copy(dst, src), reads=[src_b], writes=[dst_b])

    def wload(view, kc, n, dst, dst_b, eng=None):
        assert kc * n <= 2048
        st, st_b = W.st.next()
        sv = st[:, 0:kc * n].rearrange("p (c n) -> p c n", c=kc)
        S.dma("sp", sv, view, st_b, writes=[st_b])
        cast(dst, dst_b, sv, st_b)

    def rstd_from_ss(out_ap, out_b, ps_ap, ps_b, n, tmp_ap, tmp_b, p0=0, p1=128):
        S.op("act", lambda E: E.activation(tmp_ap, ps_ap, AF.Ln, bias=gcol("eps", 0, p0, p1), scale=1.0 / n),
             reads=[ps_b, gcols_b], writes=[tmp_b])
        S.op("act", lambda E: E.activation(out_ap, tmp_ap, AF.Exp, scale=-0.5), reads=[tmp_b], writes=[out_b])

    def load_x():
        cf.reset()
        rot = rotf(2, 1024)
        psb = [B("ps"), B("ps")]
        for t in range(NT):
            ap, b = rot.next()
            S.dma("sp", ap, x_in[t * 128:(t + 1) * 128, :], b, writes=[b])
            for half in range(2):
                pb, ps = psb[half], psum[half]
                for cc in range(4):
                    c = half * 4 + cc
                    S.op("pe", lambda E, o=ps[:, cc * 128:(cc + 1) * 128], i=ap[:, c * 128:(c + 1) * 128]:
                         E.transpose(o, i, ident_f[:, :]), reads=[b, ident_f_b], writes=[pb])
                g = t // 4
                outap = xT[:, half * 4:half * 4 + 4, t * 128:(t + 1) * 128]
                inap = ps[:, :].rearrange("p (c n) -> p c n", c=4)
                wr = [xTb[half * 4 + cc][g] for cc in range(4)]
                if half == 0:
                    S.op("act", lambda E, o=outap, i=inap: E.copy(o, i), reads=[pb], writes=wr)
                else:
                    S.op("dve", lambda E, o=outap, i=inap: E.tensor_copy(o, i), reads=[pb], writes=wr)
        S.barrier()

    def store_x():
        cf.reset()
        rot = rotf(2, 1024)
        psb = [B("ps"), B("ps")]
        toks = []
        for t in range(NT):
            ap, b = rot.next()
            g = t // 4
            for half in range(2):
                pb, ps = psb[half], psum[half]
                for cc in range(4):
                    c = half * 4 + cc
                    S.op("pe", lambda E, o=ps[:, cc * 128:(cc + 1) * 128], i=xT[:, c, t * 128:(t + 1) * 128]:
                         E.transpose(o, i, ident_f[:, :]), reads=[xTb[c][g], ident_f_b], writes=[pb])
                if half == 0:
                    S.op("act", lambda E, o=ap[:, 0:512], i=ps[:, :]: E.copy(o, i), reads=[pb], writes=[b])
                else:
                    S.op("dve", lambda E, o=ap[:, 512:1024], i=ps[:, :]: E.tensor_copy(o, i), reads=[pb], writes=[b])
            toks.append(S.dma("sp", x_out[t * 128:(t + 1) * 128, :], ap, b, reads=[b]))
        for tk in toks:
            S._wait("sp", tk)
        S.barrier()

    def norm_prep(gname, gi0, grp, xb_ap, xb_b, rstd_ap, rstd_b, sq_ap, sq_b, ps_i, ps_b, tmp_ap, tmp_b, tokps=None):
        tsl = slice(grp * 512, (grp + 1) * 512)
        xbufs = [xTb[c][grp] for c in range(8)]
        S.op("act", lambda E: E.activation(sq_ap, xT[:, :, tsl], AF.Square), reads=xbufs, writes=[sq_b])
        ps = psum[ps_i]
        for c in range(8):
            S.op("pe", lambda E, c=c: E.matmul(ps[:, :], ones_bf, sq_ap[:, c, :], start=(c == 0), stop=(c == 7)),
                 reads=[sq_b, cbf_b], writes=[ps_b])
        rstd_from_ss(rstd_ap, rstd_b, ps[:, :], ps_b, D, tmp_ap, tmp_b)
        if tokps is not None:
            tp_ap, tp_b = tokps
            for tt in range(4):
                for c in range(8):
                    S.op("pe", lambda E, c=c, tt=tt: E.matmul(tp_ap[:, grp * 4 + tt:grp * 4 + tt + 1],
                                                              sq_ap[:, c, tt * 128:(tt + 1) * 128], ones_bf[:, 0:1],
                                                              start=(c == 0), stop=(c == 7)),
                         reads=[sq_b, cbf_b], writes=[tp_b])
        for c in range(8):
            S.op("dve", lambda E, c=c: E.tensor_scalar(xb_ap[:, c, :], xT[:, c, tsl], gcol(gname, gi0 + c), None, ALU.mult),
                 reads=[xTb[c][grp], gcols_b], writes=[xb_b])

    def ffn(wgu_d, wd_d, gname, gi0):
        for half in range(2):
            cf.reset()
            ch.reset()
            rstd = [cf.take(512) for _ in range(2)]
            rstd_b = [B() for _ in range(2)]
            tmp, tmp_b = cf.take(512), B()
            stg_gu = rotf(2, 2048, lambda a: a.rearrange("p (a c m) -> p a c m", a=2, c=8))
            stg_d = rotf(2, 1408, lambda a: a.rearrange("p (k m) -> p k m", k=11))
            a_rot, s_rot, t_rot = rotf(2, 512), rotf(2, 512), rotf(2, 512)
            xb = [ch.take(4096).rearrange("p (c n) -> p c n", c=8) for _ in range(2)]
            xb_b = [B() for _ in range(2)]
            hid = ch.take(22 * 1024).rearrange("p (k n) -> p k n", k=22)
            hid_b = [[B() for g in range(2)] for k in range(22)]
            wgu = roth(2, 2048, lambda a: a.rearrange("p (a c m) -> p a c m", a=2, c=8))
            wd = roth(2, 2816, lambda a: a.rearrange("p (k m) -> p k m", k=22))
            sq, sq_b = ch.take(4096).rearrange("p (c n) -> p c n", c=8), B()
            ps_ss_b = B("ps")
            ps_g, ps_u, ps_a = psrot([1, 2]), psrot([3, 4]), psrot([5, 6])
            for gi in range(2):
                norm_prep(gname, gi0, half * 2 + gi, xb[gi], xb_b[gi], rstd[gi], rstd_b[gi], sq, sq_b, 0, ps_ss_b, tmp, tmp_b)
            wgu_v = wgu_d.rearrange("(c p) n -> p c n", p=128)
            for j in range(22):
                st, st_b = stg_gu.next()
                S.dma("sp", st[:, 0], wgu_v[:, :, j * 128:(j + 1) * 128], st_b, writes=[st_b])
                S.dma("sp", st[:, 1], wgu_v[:, :, DFF + j * 128:DFF + (j + 1) * 128], st_b, writes=[st_b])
                wt, wt_b = wgu.next()
                cast(wt, wt_b, st, st_b)
                for gi in range(2):
                    pg, pg_b = ps_g.next()
                    pu, pu_b = ps_u.next()
                    for c in range(8):
                        S.op("pe", lambda E, c=c, pg=pg, wt=wt, gi=gi: E.matmul(pg[:, :], wt[:, 0, c, :], xb[gi][:, c, :],
                                                                                 start=(c == 0), stop=(c == 7)),
                             reads=[wt_b, xb_b[gi]], writes=[pg_b])
                    for c in range(8):
                        S.op("pe", lambda E, c=c, pu=pu, wt=wt, gi=gi: E.matmul(pu[:, :], wt[:, 1, c, :], xb[gi][:, c, :],
                                                                                 start=(c == 0), stop=(c == 7)),
                             reads=[wt_b, xb_b[gi]], writes=[pu_b])
                    a, a_b = a_rot.next()
                    s, s_b = s_rot.next()
                    t, t_b = t_rot.next()
                    S.op("dve", lambda E, a=a, pg=pg, gi=gi: E.tensor_tensor(a, pg[:, :], rstd[gi], ALU.mult),
                         reads=[pg_b, rstd_b[gi]], writes=[a_b])
                    S.op("act", lambda E, s=s, a=a: E.activation(s, a, AF.Silu), reads=[a_b], writes=[s_b])
                    S.op("dve", lambda E, t=t, pu=pu, gi=gi: E.tensor_tensor(t, pu[:, :], rstd[gi], ALU.mult),
                         reads=[pu_b, rstd_b[gi]], writes=[t_b])
                    S.op("pool", lambda E, s=s, t=t, j=j, gi=gi: E.tensor_tensor(hid[:, j, gi * 512:(gi + 1) * 512], s, t, ALU.mult),
                         reads=[s_b, t_b], writes=[hid_b[j][gi]])
            wd_v = wd_d.rearrange("(k p) n -> p k n", p=128)
            for i in range(8):
                wt, wt_b = wd.next()
                for hh in range(2):
                    st, st_b = stg_d.next()
                    S.dma("sp", st, wd_v[:, hh * 11:(hh + 1) * 11, i * 128:(i + 1) * 128], st_b, writes=[st_b])
                    cast(wt[:, hh * 11:(hh + 1) * 11, :], wt_b, st, st_b)
                for gi in range(2):
                    pa, pa_b = ps_a.next()
                    for k in range(22):
                        S.op("pe", lambda E, k=k, pa=pa, wt=wt, gi=gi: E.matmul(pa[:, :], wt[:, k, :], hid[:, k, gi * 512:(gi + 1) * 512],
                                                                                 start=(k == 0), stop=(k == 21)),
                             reads=[wt_b, hid_b[k][gi]], writes=[pa_b])
                    grp = half * 2 + gi
                    xs = xT[:, i, grp * 512:(grp + 1) * 512]
                    S.op("dve", lambda E, xs=xs, pa=pa: E.scalar_tensor_tensor(xs, pa[:, :], 0.5, xs, ALU.mult, ALU.add),
                         reads=[pa_b, xTb[i][grp]], writes=[xTb[i][grp]])
            S.barrier()

    def out_proj(mix, mix_b, w_dram, wfm, ps_o):
        wv = w_dram.rearrange("(c p) n -> p c n", p=128)
        for i in range(8):
            wt, wt_b = wfm.next()
            wload(wv[:, :, i * 128:(i + 1) * 128], 8, 128, wt, wt_b)
            for g in range(4):
                po, po_b = ps_o.next()
                for c in range(8):
                    mb = mix_b[c][g] if isinstance(mix_b[c], list) else mix_b[c]
                    S.op("pe", lambda E, c=c, po=po, wt=wt, g=g: E.matmul(po[:, :], wt[:, c, :], mix[c][:, g * 512:(g + 1) * 512],
                                                                           start=(c == 0), stop=(c == 7)),
                         reads=[wt_b, mb], writes=[po_b])
                xs = xT[:, i, g * 512:(g + 1) * 512]
                S.op("dve", lambda E, xs=xs, po=po: E.tensor_tensor(xs, po[:, :], xs, ALU.add),
                     reads=[po_b, xTb[i][g]], writes=[xTb[i][g]])

    def gelu(x_ap, x_b, out_ap, out_b, ga, gb):
        (a, a_b), (b, b_b) = ga, gb
        S.op("act", lambda E: E.activation(a, x_ap, AF.Square), reads=[x_b], writes=[a_b])
        S.op("dve", lambda E: E.tensor_scalar(a, a, 0.044715, 1.0, ALU.mult, ALU.add), reads=[a_b], writes=[a_b])
        S.op("dve", lambda E: E.tensor_tensor(a, a, x_ap, ALU.mult), reads=[a_b, x_b], writes=[a_b])
        S.op("act", lambda E: E.activation(b, a, AF.Sigmoid, scale=1.5957691216057308), reads=[a_b], writes=[b_b])
        S.op("pool", lambda E: E.tensor_tensor(out_ap, x_ap, b, ALU.mult), reads=[x_b, b_b], writes=[out_b])

    def allgather(src_ap, src_b, dst_ap, dst_b, rows, rc):
        S._deps("pool", [src_b], [dst_b])
        h = S.sem("cc")
        S.cnt.setdefault("cc", 0)
        for i in range(rows // rc):
            S.cnt["cc"] += 1
            S.prog["pool"].append(lambda E, i=i: E.collective_compute("AllGather", ALU.bypass, replica_groups=[[0, 1, 2, 3], [4, 5, 6, 7]],
                                                                      ins=[src_ap[i * rc:(i + 1) * rc, :]],
                                                                      outs=[dst_ap[i * 4 * rc:(i + 1) * 4 * rc, :]]).then_inc(h, 1))
        S._upd(("cc", S.cnt["cc"]), [src_b], [dst_b])

    def grow(rc, rk, row):
        return (row // rc) * 4 * rc + rk * rc + (row % rc)

    def phase_A0():
        cf.reset()
        ch.reset()
        rstd = [cf.take(512) for _ in range(4)]
        rstd_b = [B() for _ in range(4)]
        tmp, tmp_b = cf.take(512), B()
        rtok, rtok_b = cf.take(16), B()
        rtmp, rtmp_b = cf.take(16), B()
        wsetup()
        zt = rotf(2, 512)
        ga, gb = (cf.take(512), B()), (cf.take(512), B())
        gl = (cf.take(512), B())
        bc, bc_b = cf.take(1536), B()
        trim, trim_b = cf.take(128), B()
        st6, st6_b = cf.take(8), B()
        mv, mv_b = cf.take(4), B()
        xb = ch.take(16384).rearrange("p (c n) -> p c n", c=8)
        xb_b = [B() for _ in range(4)]
        sq, sq_b = ch.take(4096).rearrange("p (c n) -> p c n", c=8), B()
        Wv, Wv_b = ch.take(4096).rearrange("p (c n) -> p c n", c=8), B()
        Wzg, Wzg_b = ch.take(4096).rearrange("p (c n) -> p c n", c=8), B()
        wfm = roth(2, 1024, lambda a: a.rearrange("p (c n) -> p c n", c=8))
        uT = ch.take(8192).rearrange("p (c n) -> p c n", c=4)
        uT_b = [[B() for g in range(4)] for c in range(4)]
        WsT, WsT_b = ch.take(1024).rearrange("p (g n) -> p g n", g=8), B()
        ev = roth(2, 512)
        vev = roth(2, 512)
        gtok = roth(2, 512)
        osg = roth(2, 512, lambda a: a.rearrange("p (c n) -> p c n", c=4))
        ps_ss_b = B("ps")
        tokps = (psum[7][:, 0:16], B("ps"))
        ps_p = psrot([1, 2, 3])
        ps_m = psrot([4, 5])
        S.dma("sp", bc, bc0_d[:, :], bc_b, writes=[bc_b])
        S.dma("sp", trim, trim_d[:, :], trim_b, writes=[trim_b])
        for hh in range(4):
            st, st_b = W.st.next()
            S.dma("sp", st[:, 0:256], sguwT_d[:, hh * 256:(hh + 1) * 256], st_b, writes=[st_b])
            for k in range(2):
                gidx = hh * 2 + k
                S.op("dve", lambda E, st=st, k=k, gidx=gidx: E.tensor_tensor(WsT[:, gidx, :], st[:, k * 128:(k + 1) * 128], trim, ALU.mult),
                     reads=[st_b, trim_b], writes=[WsT_b])
        for g in range(4):
            norm_prep("mix", 0, g, xb[:, :, g * 512:(g + 1) * 512], xb_b[g], rstd[g], rstd_b[g], sq, sq_b, 0, ps_ss_b, tmp, tmp_b, tokps=tokps)
        rstd_from_ss(rtok, rtok_b, tokps[0], tokps[1], D, rtmp, rtmp_b)
        win = Wd["sbg_in"].rearrange("(c p) n -> p c n", p=128)
        for hh in range(2):
            wload(win[:, hh * 4:(hh + 1) * 4, 1024:1536], 4, 512, Wv[:, hh * 4:(hh + 1) * 4, :], Wv_b)
            wload(win[:, hh * 4:(hh + 1) * 4, 2048:2560], 4, 512, Wzg[:, hh * 4:(hh + 1) * 4, :], Wzg_b)
        for oc in list(range(0, 8)) + list(range(12, 16)):
            wt, wt_b = wfm.next()
            wload(win[:, :, oc * 128:(oc + 1) * 128], 8, 128, wt, wt_b)
            for g in range(4):
                pp, pp_b = ps_p.next()
                for c in range(8):
                    S.op("pe", lambda E, c=c, pp=pp, wt=wt, g=g: E.matmul(pp[:, :], wt[:, c, :], xb[:, c, g * 512:(g + 1) * 512],
                                                                           start=(c == 0), stop=(c == 7)),
                         reads=[wt_b, xb_b[g]], writes=[pp_b])
                if oc < 8:
                    e, e_b = ev.next()
                    sc = 0.125 if oc < 4 else 1.0
                    S.op("dve", lambda E, e=e, pp=pp, g=g, sc=sc: E.scalar_tensor_tensor(e, pp[:, :], sc, rstd[g], ALU.mult, ALU.mult),
                         reads=[pp_b, rstd_b[g]], writes=[e_b])
                    dst, dn = (q0_d, "q0") if oc < 4 else (k0_d, "k0")
                    r0 = (oc % 4) * 128
                    S.dma("sp", dst[r0:r0 + 128, g * 512:(g + 1) * 512], e, e_b, reads=[e_b], pwrites=[db[dn]])
                else:
                    z, z_b = zt.next()
                    S.op("dve", lambda E, z=z, pp=pp, g=g: E.tensor_tensor(z, pp[:, :], rstd[g], ALU.mult),
                         reads=[pp_b, rstd_b[g]], writes=[z_b])
                    gelu(z, z_b, uT[:, oc - 12, g * 512:(g + 1) * 512], uT_b[oc - 12][g], ga, gb)
        for t in range(NT):
            g = t // 4
            pp, pp_b = ps_p.next()
            for c in range(8):
                S.op("pe", lambda E, c=c, pp=pp, t=t: E.matmul(pp[:, :], xb[:, c, t * 128:(t + 1) * 128], Wv[:, c, :],
                                                               start=(c == 0), stop=(c == 7)),
                     reads=[Wv_b, xb_b[g]], writes=[pp_b])
            e, e_b = vev.next()
            S.op("dve", lambda E, e=e, pp=pp, t=t: E.tensor_scalar(e, pp[:, :], rtok[:, t:t + 1], None, ALU.mult),
                 reads=[pp_b, rtok_b], writes=[e_b])
            S.dma("sp", v0_d[t * 128:(t + 1) * 128, :], e, e_b, reads=[e_b], pwrites=[db["v0"]])
            pp, pp_b = ps_p.next()
            for c in range(8):
                S.op("pe", lambda E, c=c, pp=pp, t=t: E.matmul(pp[:, :], xb[:, c, t * 128:(t + 1) * 128], Wzg[:, c, :],
                                                               start=(c == 0), stop=(c == 7)),
                     reads=[Wzg_b, xb_b[g]], writes=[pp_b])
            z, z_b = zt.next()
            S.op("dve", lambda E, z=z, pp=pp, t=t: E.tensor_scalar(z, pp[:, :], rtok[:, t:t + 1], None, ALU.mult),
                 reads=[pp_b, rtok_b], writes=[z_b])
            gelu(z, z_b, gl[0], gl[1], ga, gb)
            S.op("dve", lambda E: E.bn_stats(st6[:, 0:6], gl[0]), reads=[gl[1]], writes=[st6_b])
            S.op("dve", lambda E: E.bn_aggr(mv[:, 0:2], st6[:, 0:6]), reads=[st6_b], writes=[mv_b])
            S.op("act", lambda E: E.activation(mv[:, 2:3], mv[:, 1:2], AF.Ln, bias=gcol("eps")), reads=[mv_b, gcols_b], writes=[mv_b])
            S.op("act", lambda E: E.activation(mv[:, 3:4], mv[:, 2:3], AF.Exp, scale=-0.5), reads=[mv_b], writes=[mv_b])
            S.op("dve", lambda E: E.tensor_scalar(gl[0], gl[0], mv[:, 0:1], mv[:, 3:4], ALU.subtract, ALU.mult),
                 reads=[gl[1], mv_b], writes=[gl[1]])
            S.op("dve", lambda E: E.tensor_tensor(gl[0], gl[0], bc[:, 0:512], ALU.mult), reads=[gl[1], bc_b], writes=[gl[1]])
            gt, gt_b = gtok.next()
            S.op("dve", lambda E, gt=gt: E.tensor_tensor(gt, gl[0], bc[:, 512:1024], ALU.add), reads=[gl[1], bc_b], writes=[gt_b])
            pm, pm_b = ps_m.next()
            for grp in range(8):
                hb = 64 * (grp % 2)
                cc = grp // 2
                S.op("pe", lambda E, pm=pm, gt=gt, grp=grp, hb=hb, cc=cc: E.matmul(pm[hb:hb + 64, cc * 128:(cc + 1) * 128],
                                                                                  gt[:, grp * 64:(grp + 1) * 64], WsT[:, grp, :],
                                                                                  start=True, stop=True),
                     reads=[gt_b, WsT_b], writes=[pm_b])
            z2, z2_b = zt.next()
            S.op("dve", lambda E, z2=z2, pm=pm: E.tensor_tensor(z2, pm[:, :], bc[:, 1024:1536], ALU.add), reads=[pm_b, bc_b], writes=[z2_b])
            og, og_b = osg.next()
            S.op("pool", lambda E, og=og, z2=z2, t=t: E.tensor_tensor(og, z2.rearrange("p (c n) -> p c n", c=4),
                                                                     uT[:, :, t * 128:(t + 1) * 128], ALU.mult),
                 reads=[z2_b] + [uT_b[c][g] for c in range(4)], writes=[og_b])
            S.dma("sp", osg_d.rearrange("(c p) n -> p c n", p=128)[:, :, t * 128:(t + 1) * 128], og, og_b, reads=[og_b], pwrites=[db["osg"]])
        S.barrier()
        if exch == "cc":
            allgather(k0_d, db["k0"], kg0_d, db["kg0"], 512, RC_K0)
            allgather(v0_d, db["v0"], vg0_d, db["vg0"], T, RC_V0)
        S.barrier()

    def key_loc(j):
        tj, mj = j // 4, j % 4
        rk = mj if tj % 2 == 0 else 3 - mj
        return rk, tj, mj

    def phase_B0():
        cf.reset()
        ch.reset()
        Rs = [(cf.take(512), B()) for _ in range(2)]
        ttr = rotf(4, 512)
        e2r = rotf(4, 512)
        wsetup()
        Kr = roth(2, 8192)
        Vc, Vc_b = ch.take(8192).rearrange("p (r t d) -> p r t d", r=4, t=16), B()
        qr = roth(2, 2048)
        Lpr = roth(4, 512)
        wvr = roth(6, 512)
        osb = ch.take(8192).rearrange("p (c n) -> p c n", c=4)
        osb_b = [[B() for g in range(4)] for c in range(4)]
        msk, msk_b = ch.take(1024).rearrange("p (v n) -> p v n", v=8), B()
        wfm = roth(2, 1024, lambda a: a.rearrange("p (c n) -> p c n", c=8))
        ps_e, ps_t, ps_o = psrot([0, 1, 2, 3]), psrot([4, 5]), psrot([6, 7])
        S.dma("sp", msk, masks_d[:, 0:1024].rearrange("p (v n) -> p v n", v=8), msk_b, writes=[msk_b])
        for c in range(4):
            Kc, Kc_b = Kr.next()
            for rk in range(4):
                S.dma("sp", Kc[:, rk * T:(rk + 1) * T], kg0_d[grow(RC_K0, rk, c * 128):grow(RC_K0, rk, c * 128) + 128, :], Kc_b,
                      reads=[db["kg0"]], writes=[Kc_b])
                for tq in range(4):
                    S.dma("sp", Vc[:, rk, tq * 4:(tq + 1) * 4, :],
                          vg0_d[grow(RC_V0, rk, tq * 512):grow(RC_V0, rk, tq * 512) + 512, c * 128:(c + 1) * 128].rearrange("(t p) d -> p t d", p=128), Vc_b,
                          reads=[db["vg0"]], writes=[Vc_b])
            qc, qc_b = qr.next()
            S.dma("sp", qc, q0_d[c * 128:(c + 1) * 128, :], qc_b, reads=[db["q0"]], writes=[qc_b])
            for u in range(4):
                js = list(range(16 * u + 15, -1, -1))
                pos_ = [ps_o.next() for _ in range(2)]
                st2q, st3q = [[], []], [[], []]
                for hh in range(2):
                    S.op("pool", lambda E: E.memset(Rs[hh][0], 0.0), writes=[Rs[hh][1]])
                for idx, j in enumerate(js):
                    rk, tj, mj = key_loc(j)
                    kcol = rk * T + tj * 128
                    diag = j >= 16 * u
                    col0, vidx = 0, 0
                    if diag:
                        m = (j - 16 * u) // 4
                        col0 = 128 * m
                        vidx = ((4 * u + m) % 2) * 4 + mj
                    for hh in range(2):
                        hb = 64 * hh
                        R, R_b = Rs[hh]
                        po, po_b = pos_[hh]
                        qsl = qc[hb:hb + 64, u * 512 + col0:(u + 1) * 512]
                        ksl = Kc[hb:hb + 64, kcol:kcol + 128]
                        pE, pE_b = ps_e.next()
                        tt, tt_b = ttr.next()
                        Lp, Lp_b = Lpr.next()
                        S.op("pe", lambda E: E.matmul(pE[:, col0:], ksl, qsl, start=True, stop=False), reads=[Kc_b, qc_b], writes=[pE_b])
                        if diag:
                            S.op("pe", lambda E: E.matmul(pE[:, col0:col0 + 128], ident_bf, msk[:, vidx, :], start=False, stop=False, skip_group_check=True),
                                 reads=[cbf_b, msk_b], writes=[pE_b])
                        S.op("act", lambda E: E.activation(tt[:, col0:], pE[:, col0:], AF.Exp), reads=[pE_b], writes=[tt_b])
                        S.op("act", lambda E: E.activation(Lp[:, col0:], tt[:, col0:], AF.Ln, bias=gcol("one")), reads=[tt_b, gcols_b], writes=[Lp_b])

                        def stage2(pE=pE, pE_b=pE_b, Lp=Lp, Lp_b=Lp_b, col0=col0, rk=rk, tj=tj, idx=idx, j=j, hh=hh, hb=hb, R=R, R_b=R_b, po=po, po_b=po_b):
                            pT, pT_b = ps_t.next()
                            e2, e2_b = e2r.next()
                            wv, wv_b = wvr.next()
                            S.op("pe", lambda E: E.matmul(pE[:, col0:], negU_bf, Lp[:, col0:], start=False, stop=True, skip_group_check=True),
                                 reads=[cbf_b, Lp_b], writes=[pE_b])
                            S.op("pe", lambda E: E.matmul(pT[:, col0:], ones_bf, Lp[:, col0:], start=True, stop=True), reads=[cbf_b, Lp_b], writes=[pT_b])
                            S.op("dve", lambda E: E.tensor_tensor(e2[:, col0:], pE[:, col0:], R[:, col0:], ALU.subtract), reads=[pE_b, R_b], writes=[e2_b])
                            S.op("act", lambda E: E.activation(wv[:, col0:], e2[:, col0:], AF.Exp), reads=[e2_b], writes=[wv_b])
                            S.op("dve", lambda E: E.tensor_tensor(R[:, col0:], pT[:, col0:], R[:, col0:], ALU.add), reads=[pT_b, R_b], writes=[R_b])

                            def stage3():
                                S.op("pe", lambda E: E.matmul(po[hb:hb + 64, col0:], Vc[:, rk, tj, hb:hb + 64], wv[:, col0:], start=(idx == 0), stop=(j == 0),
                                                              skip_group_check=True),
                                     reads=[Vc_b, wv_b], writes=[po_b])
                            st3q[hh].append(stage3)
                        st2q[hh].append(stage2)
                        if len(st2q[hh]) > 1:
                            st2q[hh].pop(0)()
                        if len(st3q[hh]) > 1:
                            st3q[hh].pop(0)()
                for hh in range(2):
                    while st2q[hh]:
                        st2q[hh].pop(0)()
                for hh in range(2):
                    while st3q[hh]:
                        st3q[hh].pop(0)()
                for hh in range(2):
                    hb = 64 * hh
                    po, po_b = pos_[hh]
                    S.op("dve", lambda E: E.tensor_copy(osb[hb:hb + 64, c, u * 512:(u + 1) * 512], po[hb:hb + 64, :]),
                         reads=[po_b], writes=[osb_b[c][u]])
        Kc, Kc_b = Kr.next()
        osgs = Kc.rearrange("p (c n) -> p c n", c=4)
        S.dma("sp", osgs, osg_d.rearrange("(c p) n -> p c n", p=128), Kc_b, reads=[db["osg"]], writes=[Kc_b])
        mix = [osb[:, c, :] for c in range(4)] + [osgs[:, c, :] for c in range(4)]
        mix_b = [osb_b[c] for c in range(4)] + [Kc_b] * 4
        out_proj(mix, mix_b, Wd["sbg_out"], wfm, psrot([0, 1, 2]))
        S.barrier()

    def phase_X(l):
        cf.reset()
        ch.reset()
        rstd1, rstd1_b = cf.take(512), B()
        tmp, tmp_b = cf.take(512), B()
        wsetup()
        mst = rotf(1, 1024)
        msmall, msmall_b = cf.take(8), B()
        qraw = rotf(1, 1024, lambda a: a.rearrange("p (c n) -> p c n", c=2))
        rq = rotf(2, 512)
        kraw = rotf(1, 512, lambda a: a.rearrange("p (c n) -> p c n", c=2))
        rk_ = rotf(2, 256)
        rcp = rotf(1, 512)
        xb = ch.take(4096).rearrange("p (c n) -> p c n", c=8)
        xb1_b = B()
        sq, sq_b = ch.take(4096).rearrange("p (c n) -> p c n", c=8), B()
        hmT, hmT_b = ch.take(2048).rearrange("p (c n) -> p c n", c=8), B()
        mrow = roth(1, 1024)
        kT, kT_b = ch.take(2048).rearrange("p (h c n) -> p h c n", h=4, c=2), B()
        Vm, Vm_b = ch.take(2048).rearrange("p (m n) -> p m n", m=2), B()
        wfm = roth(2, 1024, lambda a: a.rearrange("p (c n) -> p c n", c=8))
        Wv_ = roth(1, 4096, lambda a: a.rearrange("p (c n) -> p c n", c=8))
        qn = roth(2, 1024, lambda a: a.rearrange("p (c n) -> p c n", c=2))
        sqq = roth(2, 1024, lambda a: a.rearrange("p (c n) -> p c n", c=2))
        Pm = roth(2, 1024, lambda a: a.rearrange("p (m n) -> p m n", m=2))
        oT = ch.take(4096).rearrange("p (c n) -> p c n", c=8)
        oT_b = [B() for c in range(8)]
        ps_ss_b = B("ps")
        ps_p = psrot([1, 2])
        ps_q = psrot([3])
        ps_s2 = psrot([4, 5])
        ps_o2 = psrot([6, 7])
        for mt in range(2):
            ms, ms_b = mst.next()
            S.dma("sp", ms, mem_d[mt * 128:(mt + 1) * 128, :], ms_b, writes=[ms_b])
            mr, mr_b = mrow.next()
            S.op("act", lambda E, ms=ms, mr=mr: E.activation(mr, ms, AF.Square, accum_out=msmall[:, 0:1]), reads=[ms_b], writes=[mr_b, msmall_b])
            S.op("act", lambda E: E.activation(msmall[:, 1:2], msmall[:, 0:1], AF.Ln, bias=gcol("eps"), scale=1.0 / D),
                 reads=[msmall_b, gcols_b], writes=[msmall_b])
            S.op("act", lambda E: E.activation(msmall[:, 2:3], msmall[:, 1:2], AF.Exp, scale=-0.5), reads=[msmall_b], writes=[msmall_b])
            S.op("dve", lambda E, ms=ms: E.tensor_scalar(ms, ms, msmall[:, 2:3], None, ALU.mult), reads=[ms_b, msmall_b], writes=[ms_b])
            for half in range(2):
                pp, pp_b = ps_p.next()
                for cc in range(4):
                    c = half * 4 + cc
                    S.op("pe", lambda E, pp=pp, ms=ms, c=c, cc=cc: E.transpose(pp[:, cc * 128:(cc + 1) * 128], ms[:, c * 128:(c + 1) * 128], ident_f[:, :]),
                         reads=[ms_b, ident_f_b], writes=[pp_b])
                for cc in range(4):
                    c = half * 4 + cc
                    S.op("dve", lambda E, pp=pp, c=c, cc=cc, mt=mt: E.tensor_scalar(hmT[:, c, mt * 128:(mt + 1) * 128], pp[:, cc * 128:(cc + 1) * 128],
                                                                                     gcol("xmn", 8 * l + c), None, ALU.mult),
                         reads=[pp_b, gcols_b], writes=[hmT_b])
        wk = Wd[("xk", l)].rearrange("(c p) n -> p c n", p=128)
        for hd in range(4):
            kr_, kr_b = kraw.next()
            sk, sk_b = sqq.next()
            for dc in range(2):
                oc = hd * 2 + dc
                wt, wt_b = wfm.next()
                wload(wk[:, :, oc * 128:(oc + 1) * 128], 8, 128, wt, wt_b)
                pp, pp_b = ps_p.next()
                for c in range(8):
                    S.op("pe", lambda E, c=c, pp=pp, wt=wt: E.matmul(pp[:, 0:256], wt[:, c, :], hmT[:, c, :], start=(c == 0), stop=(c == 7)),
                         reads=[wt_b, hmT_b], writes=[pp_b])
                S.op("act", lambda E, kr_=kr_, pp=pp, dc=dc: E.copy(kr_[:, dc, :], pp[:, 0:256]), reads=[pp_b], writes=[kr_b])
                S.op("act", lambda E, sk=sk, pp=pp, dc=dc: E.activation(sk[:, dc, 0:256], pp[:, 0:256], AF.Square), reads=[pp_b], writes=[sk_b])
            pq, pq_b = ps_q.next()
            for dc in range(2):
                S.op("pe", lambda E, pq=pq, sk=sk, dc=dc: E.matmul(pq[:, 0:256], ones_bf, sk[:, dc, 0:256], start=(dc == 0), stop=(dc == 1)),
                     reads=[sk_b, cbf_b], writes=[pq_b])
            rr, rr_b = rk_.next()
            rstd_from_ss(rr, rr_b, pq[:, 0:256], pq_b, 256, tmp[:, 0:256], tmp_b)
            for dc in range(2):
                S.op("dve", lambda E, kr_=kr_, rr=rr, dc=dc, hd=hd: E.scalar_tensor_tensor(kT[:, hd, dc, :], kr_[:, dc, :], gcol("xkg", 2 * l + dc), rr,
                                                                                           ALU.mult, ALU.mult),
                     reads=[kr_b, rr_b, gcols_b], writes=[kT_b])
        wvv = Wd[("xv", l)].rearrange("(c p) n -> p c n", p=128)
        for nh in range(2):
            wt, wt_b = Wv_.next()
            for hh in range(2):
                wload(wvv[:, hh * 4:(hh + 1) * 4, nh * 512:(nh + 1) * 512], 4, 512, wt[:, hh * 4:(hh + 1) * 4, :], wt_b)
            for mt in range(2):
                pp, pp_b = ps_p.next()
                for c in range(8):
                    S.op("pe", lambda E, c=c, pp=pp, wt=wt, mt=mt: E.matmul(pp[:, :], hmT[:, c, mt * 128:(mt + 1) * 128], wt[:, c, :],
                                                                             start=(c == 0), stop=(c == 7)),
                         reads=[wt_b, hmT_b], writes=[pp_b])
                S.op("act", lambda E, pp=pp, mt=mt, nh=nh: E.copy(Vm[:, mt, nh * 512:(nh + 1) * 512], pp[:, :]), reads=[pp_b], writes=[Vm_b])
        wq = Wd[("xq", l)].rearrange("(c p) n -> p c n", p=128)
        wo = Wd[("xo", l)].rearrange("(c p) n -> p c n", p=128)
        ps_x = psrot([1, 2])
        for g in range(4):
            norm_prep("xn", 8 * l, g, xb, xb1_b, rstd1, rstd1_b, sq, sq_b, 0, ps_ss_b, tmp, tmp_b)
            for hd in range(4):
                qr_, qr_b = qraw.next()
                sk, sk_b = sqq.next()
                for dc in range(2):
                    oc = hd * 2 + dc
                    wt, wt_b = wfm.next()
                    wload(wq[:, :, oc * 128:(oc + 1) * 128], 8, 128, wt, wt_b)
                    pp, pp_b = ps_p.next()
                    for c in range(8):
                        S.op("pe", lambda E, c=c, pp=pp, wt=wt: E.matmul(pp[:, :], wt[:, c, :], xb[:, c, :], start=(c == 0), stop=(c == 7)),
                             reads=[wt_b, xb1_b], writes=[pp_b])
                    S.op("dve", lambda E, qr_=qr_, pp=pp, dc=dc: E.tensor_tensor(qr_[:, dc, :], pp[:, :], rstd1, ALU.mult),
                         reads=[pp_b, rstd1_b], writes=[qr_b])
                    S.op("act", lambda E, sk=sk, qr_=qr_, dc=dc: E.activation(sk[:, dc, :], qr_[:, dc, :], AF.Square), reads=[qr_b], writes=[sk_b])
                pq, pq_b = ps_q.next()
                for dc in range(2):
                    S.op("pe", lambda E, pq=pq, sk=sk, dc=dc: E.matmul(pq[:, :], ones_bf, sk[:, dc, :], start=(dc == 0), stop=(dc == 1)),
                         reads=[sk_b, cbf_b], writes=[pq_b])
                rr, rr_b = rq.next()
                rstd_from_ss(rr, rr_b, pq[:, :], pq_b, 256, tmp, tmp_b)
                qq, qq_b = qn.next()
                for dc in range(2):
                    S.op("dve", lambda E, qq=qq, qr_=qr_, rr=rr, dc=dc: E.scalar_tensor_tensor(qq[:, dc, :], qr_[:, dc, :], gcol("xqg", 2 * l + dc), rr,
                                                                                              ALU.mult, ALU.mult),
                         reads=[qr_b, rr_b, gcols_b], writes=[qq_b])
                pm_, pm_b = Pm.next()
                for mt in range(2):
                    pS, pS_b = ps_s2.next()
                    for dc in range(2):
                        S.op("pe", lambda E, pS=pS, qq=qq, dc=dc, mt=mt, hd=hd: E.matmul(pS[:, :], kT[:, hd, dc, mt * 128:(mt + 1) * 128], qq[:, dc, :],
                                                                                         start=(dc == 0), stop=(dc == 1)),
                             reads=[kT_b, qq_b], writes=[pS_b])
                    S.op("act", lambda E, pm_=pm_, pS=pS, mt=mt: E.activation(pm_[:, mt, :], pS[:, :], AF.Exp, scale=1.0 / 16.0),
                         reads=[pS_b], writes=[pm_b])
                pq, pq_b = ps_q.next()
                for mt in range(2):
                    S.op("pe", lambda E, pq=pq, pm_=pm_, mt=mt: E.matmul(pq[:, :], ones_bf, pm_[:, mt, :], start=(mt == 0), stop=(mt == 1)),
                         reads=[pm_b, cbf_b], writes=[pq_b])
                rc, rc_b = rcp.next()
                S.op("dve", lambda E, rc=rc, pq=pq: E.reciprocal(rc, pq[:, :]), reads=[pq_b], writes=[rc_b])
                for dc in range(2):
                    po, po_b = ps_o2.next()
                    for mt in range(2):
                        S.op("pe", lambda E, po=po, pm_=pm_, mt=mt, dc=dc, hd=hd: E.matmul(po[:, :], Vm[:, mt, hd * 256 + dc * 128:hd * 256 + (dc + 1) * 128],
                                                                                           pm_[:, mt, :], start=(mt == 0), stop=(mt == 1)),
                             reads=[Vm_b, pm_b], writes=[po_b])
                    S.op("dve", lambda E, po=po, rc=rc, hd=hd, dc=dc: E.tensor_tensor(oT[:, hd * 2 + dc, :], po[:, :], rc, ALU.mult),
                         reads=[po_b, rc_b], writes=[oT_b[hd * 2 + dc]])
            for i in range(8):
                wt, wt_b = wfm.next()
                wload(wo[:, :, i * 128:(i + 1) * 128], 8, 128, wt, wt_b)
                po, po_b = ps_x.next()
                for c in range(8):
                    S.op("pe", lambda E, c=c, po=po, wt=wt: E.matmul(po[:, :], wt[:, c, :], oT[:, c, :], start=(c == 0), stop=(c == 7)),
                         reads=[wt_b, oT_b[c]], writes=[po_b])
                xs = xT[:, i, g * 512:(g + 1) * 512]
                S.op("dve", lambda E, xs=xs, po=po: E.tensor_tensor(xs, po[:, :], xs, ALU.add), reads=[po_b, xTb[i][g]], writes=[xTb[i][g]])
        S.barrier()

    def sin_reduced(out_ap, out_b, ang_ap, ang_b, wk, wk_b, nf, nf_b, p0, p1, add=0.0):
        ni = nint[p0:p1, :]
        S.op("dve", lambda E: E.tensor_scalar(wk, ang_ap, float(add), None, ALU.add), reads=[ang_b], writes=[wk_b])
        S.op("dve", lambda E: E.tensor_scalar(ni, wk, float(1 / TWO_PI), None, ALU.mult), reads=[wk_b], writes=[nint_b])
        S.op("dve", lambda E: E.tensor_copy(nf, ni), reads=[nint_b], writes=[nf_b])
        S.op("dve", lambda E: E.scalar_tensor_tensor(wk, nf, -TWO_PI, wk, ALU.mult, ALU.add), reads=[nf_b, wk_b], writes=[wk_b])
        S.op("dve", lambda E: E.tensor_scalar(nf, wk, PI, TWO_PI, ALU.is_gt, ALU.mult), reads=[wk_b], writes=[nf_b])
        S.op("dve", lambda E: E.tensor_tensor(wk, wk, nf, ALU.subtract), reads=[nf_b, wk_b], writes=[wk_b])
        S.op("dve", lambda E: E.tensor_scalar(nf, wk, -PI, TWO_PI, ALU.is_lt, ALU.mult), reads=[wk_b], writes=[nf_b])
        S.op("dve", lambda E: E.tensor_tensor(wk, wk, nf, ALU.add), reads=[nf_b, wk_b], writes=[wk_b])
        S.op("act", lambda E: E.activation(out_ap, wk, AF.Sin), reads=[wk_b], writes=[out_b])

    class _Stop(Exception):
        pass

    def _stage(n):
        for k in range(1, 9):
            if ('a1s%d' % k) in DBG and n >= k:
                raise _Stop()

    def phase_A1():
        try:
            phase_A1_()
        except _Stop:
            S.barrier()

    def phase_A1_():
        cf.reset()
        ch.reset()
        rstd = [cf.take(512) for _ in range(4)]
        rstd_b = [B() for _ in range(4)]
        tmp, tmp_b = cf.take(512), B()
        wsetup(2, 1024)
        tabs = [cf.take(512) for _ in range(4)]
        tab_b = [B() for _ in range(4)]
        cqraw, cqraw_b = cf.take(2048).rearrange("p (c n) -> p c n", c=4), B()
        fa, fa_b = cf.take(512), B()
        fb, fb_b = cf.take(512), B()
        fc, fc_b = cf.take(512), B()
        kra, kra_b = cf.take(512), B()
        rqk = rotf(2, 512)
        xb = ch.take(16384).rearrange("p (c n) -> p c n", c=8)
        xb_b = [B() for _ in range(4)]
        sq, sq_b = ch.take(4096).rearrange("p (c n) -> p c n", c=8), B()
        Win, Win_b = ch.take(6400).rearrange("p (c n) -> p c n", c=8), B()
        Wuq, Wuq_b = ch.take(6144).rearrange("p (c n) -> p c n", c=4), B()
        Wqs, Wqs_b = ch.take(2048).rearrange("p (c n) -> p c n", c=4), B()
        Wuk, Wuk_b = ch.take(2048).rearrange("p (c n) -> p c n", c=2), B()
        Wuv, Wuv_b = ch.take(2048).rearrange("p (c n) -> p c n", c=2), B()
        cqn, cqn_b = ch.take(2048).rearrange("p (c n) -> p c n", c=4), B()
        ckvn, ckvn_b = ch.take(1024).rearrange("p (c n) -> p c n", c=2), B()
        krb, krb_b = ch.take(512), B()
        sqh = ch.take(512)
        sqh_lo, sqh_hi = B(), B()
        sqq, sqq_b = ch.take(512), B()
        qst = roth(1, 512)
        vst = roth(1, 512)
        ps_ss_b = B("ps")
        ps_p = psrot([1, 2])
        ps_a = psrot([3, 4])
        ps_b2 = psrot([5])
        ps_q = psrot([6])
        ps_r = psrot([7])

        def wl(src, kc, ncol, dst, dst_b):
            v = src.rearrange("(c p) n -> p c n", p=128)
            step = max(128, (1024 // kc) // 128 * 128) if ncol >= 128 else ncol
            for n0 in range(0, ncol, step):
                n1 = min(ncol, n0 + step)
                wload(v[:, :, n0:n1], kc, n1 - n0, dst[:, :, n0:n1], dst_b)

        wl(Wd["mla_in"], 8, 768, Win, Win_b)
        wload(Wd["mla_in"].rearrange("(c p) n -> p c n", p=128)[:, :, 768:800], 8, 32, Win[:, :, 768:800], Win_b)
        wl(Wd["mla_uq"], 4, 1536, Wuq, Wuq_b)
        wl(Wd["mla_uq_sw"], 4, 512, Wqs, Wqs_b)
        wl(Wd["mla_uk"], 2, 1024, Wuk, Wuk_b)
        wl(Wd["mla_uv"], 2, 1024, Wuv, Wuv_b)
        for g in range(4):
            norm_prep("mix", 8, g, xb[:, :, g * 512:(g + 1) * 512], xb_b[g], rstd[g], rstd_b[g], sq, sq_b, 0, ps_ss_b, tmp, tmp_b)

        P0, P1 = 64, 96
        for pass_ in ("kv", "q"):
            for g in range(1 if 'a1small' in DBG else 4):
                gs = slice(g * 512, (g + 1) * 512)
                S.dma("sp", posi[P0:P1, :], posrep_d[:, gs], posi_b, writes=[posi_b])
                S.op("dve", lambda E: E.tensor_copy(fa[P0:P1, :], posi[P0:P1, :]), reads=[posi_b], writes=[fa_b])
                S.op("dve", lambda E: E.tensor_scalar(fa[P0:P1, :], fa[P0:P1, :], gcol("invf", 0, P0, P1), None, ALU.mult),
                     reads=[fa_b, gcols_b], writes=[fa_b])
                sin_reduced(tabs[0][P0:P1, :], tab_b[0], fa[P0:P1, :], fa_b, fb[P0:P1, :], fb_b, fc[P0:P1, :], fc_b, P0, P1, add=PI / 2)
                sin_reduced(tabs[1][P0:P1, :], tab_b[1], fa[P0:P1, :], fa_b, fb[P0:P1, :], fb_b, fc[P0:P1, :], fc_b, P0, P1, add=0.0)
                S.op("dve", lambda E: E.tensor_scalar(tabs[2][P0:P1, :], tabs[0][P0:P1, :], gcol("mkg", 0, P0, P1), None, ALU.mult),
                     reads=[tab_b[0], gcols_b], writes=[tab_b[2]])
                S.op("dve", lambda E: E.tensor_scalar(tabs[3][P0:P1, :], tabs[1][P0:P1, :], gcol("sgn", 0, P0, P1), gcol("mkg_sw", 0, P0, P1), ALU.mult, ALU.mult),
                     reads=[tab_b[1], gcols_b], writes=[tab_b[3]])
                S.op("dve", lambda E: E.tensor_scalar(tabs[0][P0:P1, :], tabs[0][P0:P1, :], gcol("mqg", 0, P0, P1), None, ALU.mult),
                     reads=[tab_b[0], gcols_b], writes=[tab_b[0]])
                S.op("dve", lambda E: E.tensor_scalar(tabs[1][P0:P1, :], tabs[1][P0:P1, :], gcol("sgn", 0, P0, P1), gcol("mqg_sw", 0, P0, P1), ALU.mult, ALU.mult),
                     reads=[tab_b[1], gcols_b], writes=[tab_b[1]])

                for (n_ch, col0, dst, dst_b, gname, nfeat) in (((4, 0, cqn, cqn_b, "qlora", 512),) if pass_ == "q" else ((2, 512, ckvn, ckvn_b, "kvlora", 256),)):
                    for oc in range(n_ch):
                        pp, pp_b = ps_p.next()
                        for c in range(8):
                            S.op("pe", lambda E, c=c, pp=pp, oc=oc, col0=col0: E.matmul(pp[:, :], Win[:, c, col0 + oc * 128:col0 + (oc + 1) * 128], xb[:, c, gs],
                                                                                        start=(c == 0), stop=(c == 7)),
                                 reads=[Win_b, xb_b[g]], writes=[pp_b])
                        S.op("dve", lambda E, pp=pp, oc=oc: E.tensor_tensor(cqraw[:, oc, :], pp[:, :], rstd[g], ALU.mult),
                             reads=[pp_b, rstd_b[g]], writes=[cqraw_b])
                        S.op("act", lambda E, oc=oc: E.activation(sq[:, oc, :], cqraw[:, oc, :], AF.Square), reads=[cqraw_b], writes=[sq_b])
                    pq, pq_b = ps_q.next()
                    for oc in range(n_ch):
                        S.op("pe", lambda E, pq=pq, oc=oc, n_ch=n_ch: E.matmul(pq[:, :], ones_bf, sq[:, oc, :], start=(oc == 0), stop=(oc == n_ch - 1)),
                             reads=[sq_b, cbf_b], writes=[pq_b])
                    rr, rr_b = rqk.next()
                    rstd_from_ss(rr, rr_b, pq[:, :], pq_b, nfeat, tmp, tmp_b)
                    for oc in range(n_ch):
                        S.op("dve", lambda E, oc=oc, dst=dst, rr=rr, gname=gname: E.scalar_tensor_tensor(dst[:, oc, :], cqraw[:, oc, :], gcol(gname, oc), rr, ALU.mult, ALU.mult),
                             reads=[cqraw_b, rr_b, gcols_b], writes=[dst_b])

                if pass_ == "kv":
                    pp, pp_b = ps_p.next()
                    for c in range(8):
                        S.op("pe", lambda E, c=c, pp=pp: E.matmul(pp[0:32, :], Win[:, c, 768:800], xb[:, c, gs], start=(c == 0), stop=(c == 7)),
                             reads=[Win_b, xb_b[g]], writes=[pp_b])
                    S.op("dve", lambda E, pp=pp: E.tensor_tensor(krb[0:32, :], pp[0:32, :], rstd[g][0:32, :], ALU.mult), reads=[pp_b, rstd_b[g]], writes=[krb_b])
                    if os.environ.get("KSUB") == "1":
                        raise _Stop()
                    pa, pa_b = ps_a.next()
                    pb_, pb_b = ps_b2.next()
                    S.op("pe", lambda E, pa=pa: E.matmul(pa[P0:P1, :], ident_bf[0:32, 0:32], krb[0:32, :], start=True, stop=True), reads=[krb_b, cbf_b], writes=[pa_b])
                    S.op("pe", lambda E, pb_=pb_: E.matmul(pb_[P0:P1, :], swap_bf[0:32, 0:32], krb[0:32, :], start=True, stop=True), reads=[krb_b, cbf_b], writes=[pb_b])
                    if os.environ.get("KSUB") == "2":
                        raise _Stop()
                    S.op("act", lambda E, pa=pa: E.activation(sqh[P0:P1, :], pa[P0:P1, :], AF.Square), reads=[pa_b], writes=[sqh_hi])
                    if os.environ.get("KSUB") == "3":
                        raise _Stop()
                    S.op("act", lambda E, pa=pa: E.copy(kra[P0:P1, :], pa[P0:P1, :]), reads=[pa_b], writes=[kra_b])
                    S.op("act", lambda E, pb_=pb_: E.copy(fb[P0:P1, :], pb_[P0:P1, :]), reads=[pb_b], writes=[fb_b])
                    S.op("dve", lambda E: E.tensor_tensor(kra[P0:P1, :], kra[P0:P1, :], tabs[2][P0:P1, :], ALU.mult), reads=[kra_b, tab_b[2]], writes=[kra_b])
                    S.op("dve", lambda E: E.tensor_tensor(fb[P0:P1, :], fb[P0:P1, :], tabs[3][P0:P1, :], ALU.mult), reads=[fb_b, tab_b[3]], writes=[fb_b])
                    S.op("dve", lambda E: E.tensor_tensor(kra[P0:P1, :], kra[P0:P1, :], fb[P0:P1, :], ALU.add), reads=[kra_b, fb_b], writes=[kra_b])

                for h in range(2 if 'a1small' in DBG else 16):
                    if pass_ == "q":
                        pa, pa_b = ps_a.next()
                        pb_, pb_b = ps_b2.next()
                        for c in range(4):
                            S.op("pe", lambda E, c=c, pa=pa, h=h: E.matmul(pa[0:96, :], Wuq[:, c, h * 96:(h + 1) * 96], cqn[:, c, :], start=(c == 0), stop=(c == 3)),
                                 reads=[Wuq_b, cqn_b], writes=[pa_b])
                        for c in range(4):
                            S.op("pe", lambda E, c=c, pb_=pb_, h=h: E.matmul(pb_[P0:P1, :], Wqs[:, c, h * 32:(h + 1) * 32], cqn[:, c, :], start=(c == 0), stop=(c == 3)),
                                 reads=[Wqs_b, cqn_b], writes=[pb_b])
                        S.op("act", lambda E, pa=pa: E.activation(sqq[0:96, :], pa[0:96, :], AF.Square), reads=[pa_b], writes=[sqq_b])
                        pr, pr_b = ps_r.next()
                        S.op("pe", lambda E, pr=pr: E.matmul(pr[0:96, :], ones_bf[0:96, 0:96], sqq[0:96, :], start=True, stop=True), reads=[sqq_b, cbf_b], writes=[pr_b])
                        rr, rr_b = rqk.next()
                        rstd_from_ss(rr[0:96, :], rr_b, pr[0:96, :], pr_b, 96, tmp[0:96, :], tmp_b, 0, 96)
                        qs, qs_b = qst.next()
                        S.op("dve", lambda E, qs=qs, pa=pa, rr=rr: E.scalar_tensor_tensor(qs[0:64, :], pa[0:64, :], gcol("mqg", 0, 0, 64), rr[0:64, :], ALU.mult, ALU.mult),
                             reads=[pa_b, rr_b, gcols_b], writes=[qs_b])
                        S.op("act", lambda E, pa=pa: E.copy(fa[P0:P1, :], pa[P0:P1, :]), reads=[pa_b], writes=[fa_b])
                        S.op("act", lambda E, pb_=pb_: E.copy(fb[P0:P1, :], pb_[P0:P1, :]), reads=[pb_b], writes=[fb_b])
                        S.op("dve", lambda E: E.tensor_tensor(fa[P0:P1, :], fa[P0:P1, :], tabs[0][P0:P1, :], ALU.mult), reads=[fa_b, tab_b[0]], writes=[fa_b])
                        S.op("dve", lambda E: E.tensor_tensor(fb[P0:P1, :], fb[P0:P1, :], tabs[1][P0:P1, :], ALU.mult), reads=[fb_b, tab_b[1]], writes=[fb_b])
                        S.op("dve", lambda E: E.tensor_tensor(fa[P0:P1, :], fa[P0:P1, :], fb[P0:P1, :], ALU.add), reads=[fa_b, fb_b], writes=[fa_b])
                        S.op("dve", lambda E, qs=qs, rr=rr: E.tensor_tensor(qs[P0:P1, :], fa[P0:P1, :], rr[P0:P1, :], ALU.mult), reads=[fa_b, rr_b], writes=[qs_b])
                        S.dma("sp", q1_d[h * 96:(h + 1) * 96, gs], qs[0:96, :], qs_b, reads=[qs_b], pwrites=[db["q1"]])

                    if pass_ == "kv":
                        pa, pa_b = ps_a.next()
                        for c in range(2):
                            S.op("pe", lambda E, c=c, pa=pa, h=h: E.matmul(pa[0:64, :], Wuk[:, c, h * 64:(h + 1) * 64], ckvn[:, c, :], start=(c == 0), stop=(c == 1)),
                                 reads=[Wuk_b, ckvn_b], writes=[pa_b])
                        S.op("act", lambda E, pa=pa: E.activation(sqh[0:64, :], pa[0:64, :], AF.Square), reads=[pa_b], writes=[sqh_lo])
                        pr, pr_b = ps_r.next()
                        S.op("pe", lambda E, pr=pr: E.matmul(pr[0:96, :], ones_bf[0:96, 0:96], sqh[0:96, :], start=True, stop=True), reads=[sqh_lo, sqh_hi, cbf_b], writes=[pr_b])
                        rr, rr_b = rqk.next()
                        rstd_from_ss(rr[0:96, :], rr_b, pr[0:96, :], pr_b, 96, tmp[0:96, :], tmp_b, 0, 96)
                        ks, ks_b = qst.next()
                        S.op("dve", lambda E, ks=ks, pa=pa, rr=rr: E.scalar_tensor_tensor(ks[0:64, :], pa[0:64, :], gcol("mkg", 0, 0, 64), rr[0:64, :], ALU.mult, ALU.mult),
                             reads=[pa_b, rr_b, gcols_b], writes=[ks_b])
                        S.op("dve", lambda E, ks=ks, rr=rr: E.tensor_tensor(ks[P0:P1, :], kra[P0:P1, :], rr[P0:P1, :], ALU.mult), reads=[kra_b, rr_b], writes=[ks_b])
                        S.dma("sp", k1_d[h * 96:(h + 1) * 96, gs], ks[0:96, :], ks_b, reads=[ks_b], pwrites=[db["k1"]])

                if pass_ == "kv":
                    for tt in range(4):
                        t = g * 4 + tt
                        for half in range(2):
                            pp, pp_b = ps_p.next()
                            for c in range(2):
                                S.op("pe", lambda E, c=c, pp=pp, tt=tt, half=half: E.matmul(pp[:, :], ckvn[:, c, tt * 128:(tt + 1) * 128], Wuv[:, c, half * 512:(half + 1) * 512],
                                                                                             start=(c == 0), stop=(c == 1)),
                                     reads=[Wuv_b, ckvn_b], writes=[pp_b])
                            vs, vs_b = vst.next()
                            S.op("act", lambda E, vs=vs, pp=pp: E.copy(vs, pp[:, :]), reads=[pp_b], writes=[vs_b])
                            S.dma("sp", v1_d[t * 128:(t + 1) * 128, half * 512:(half + 1) * 512], vs, vs_b, reads=[vs_b], pwrites=[db["v1"]])
            if pass_ == "kv" and exch == "cc":
                allgather(k1_d, db["k1"], kg1_d, db["kg1"], 1536, RC_K1)
                allgather(v1_d, db["v1"], vg1_d, db["vg1"], T, RC_V1)
        S.barrier()

    o1_d = nc.dram_tensor("o1_d", [D, T], BF16).ap()
    o1_b = Buf("o1")

    def phase_B1():
        cf.reset()
        ch.reset()
        wsetup()
        tmpA, tmpA_b = cf.take(512), B()
        rcf, rcf_b = cf.take(512), B()
        rcs, rcs_b = cf.take(512), B()
        KK = ch.take(16384)
        Kr = Rot([(KK[:, 0:8192], B()), (KK[:, 8192:16384], B())])
        Vr = roth(2, 8192, lambda a: a.rearrange("p (r t d) -> p r t d", r=4, t=16))
        qr = roth(2, 2048)
        Pr = roth(3, 512)
        msk, msk_b = ch.take(1024).rearrange("p (v n) -> p v n", v=8), B()
        rhi, rhi_b = ch.take(512), B()
        rlo, rlo_b = ch.take(512), B()
        ost = roth(2, 512)
        wfm = roth(2, 1024, lambda a: a.rearrange("p (c n) -> p c n", c=8))
        ps_s = psrot([0, 1, 2])
        ps_o = psrot([3, 4])
        ps_r = psrot([5])
        S.dma("sp", msk, masks_d[:, 1024:2048].rearrange("p (v n) -> p v n", v=8), msk_b, writes=[msk_b])
        for (Vt, Vt_b) in Vr.slots:
            for rk in range(4):
                S.op("pool", lambda E: E.memset(Vt[:, rk, :, 64:128], 1.0), writes=[Vt_b])
        SC = float(96 ** -0.5)
        dsl = slice(64, 128)
        osl = slice(0, 64)
        def issue_loads(h):
            Kh, Kh_b = Kr.next()
            Vh, Vh_b = Vr.next()
            qh, qh_b = qr.next()
            S.dma("sp", qh[0:96, :], q1_d[h * 96:(h + 1) * 96, :], qh_b, reads=[db["q1"]], writes=[qh_b])
            for rk in range(4):
                S.dma("sp", Kh[0:96, rk * T:(rk + 1) * T], kg1_d[grow(RC_K1, rk, h * 96):grow(RC_K1, rk, h * 96) + 96, :], Kh_b,
                      reads=[db["kg1"]], writes=[Kh_b])
                for tq in range(4):
                    r0 = grow(RC_V1, rk, tq * 512)
                    S.dma("sp", Vh[:, rk, tq * 4:(tq + 1) * 4, 0:64],
                          vg1_d[r0:r0 + 512, h * 64:(h + 1) * 64].rearrange("(t p) d -> p t d", p=128), Vh_b, reads=[db["vg1"]], writes=[Vh_b])
            return Kh, Kh_b, Vh, Vh_b, qh, qh_b

        nxt = issue_loads(0)
        for h in range(16):
            Kh, Kh_b, Vh, Vh_b, qh, qh_b = nxt
            if h + 1 < 16:
                nxt = issue_loads(h + 1)
            for u in range(4):
                po, po_b = ps_o.next()
                nblk = 16 * u + 16
                pend = []
                for j in range(nblk):
                    rk, tj, mj = key_loc(j)
                    kcol = rk * T + tj * 128
                    diag = j >= 16 * u
                    col0, vidx = 0, 0
                    if diag:
                        m = (j - 16 * u) // 4
                        col0 = 128 * m
                        vidx = ((4 * u + m) % 2) * 4 + mj
                    pS, pS_b = ps_s.next()
                    S.op("pe", lambda E: E.matmul(pS[:, col0:], Kh[0:96, kcol:kcol + 128], qh[0:96, u * 512 + col0:(u + 1) * 512], start=True, stop=not diag),
                         reads=[Kh_b, qh_b], writes=[pS_b])
                    if diag:
                        S.op("pe", lambda E: E.matmul(pS[:, col0:col0 + 128], ident_bf, msk[:, vidx, :], start=False, stop=True, skip_group_check=True),
                             reads=[cbf_b, msk_b], writes=[pS_b])
                    P_, P_b = Pr.next()
                    S.op("act", lambda E: E.activation(P_[:, col0:], pS[:, col0:], AF.Exp, scale=SC), reads=[pS_b], writes=[P_b])

                    def pv(po=po, po_b=po_b, P_=P_, P_b=P_b, col0=col0, rk=rk, tj=tj, j=j, nblk=nblk, Vh=Vh, Vh_b=Vh_b):
                        S.op("pe", lambda E: E.matmul(po[:, col0:], Vh[:, rk, tj, :], P_[:, col0:], start=(j == 0), stop=(j == nblk - 1),
                                                      skip_group_check=True),
                             reads=[Vh_b, P_b], writes=[po_b])
                    if pend:
                        pend.pop()()
                    pend.append(pv)
                pend.pop()()
                S.op("act", lambda E: E.activation(tmpA[dsl, :], po[dsl, :], AF.Ln), reads=[po_b], writes=[tmpA_b])
                S.op("act", lambda E: E.activation(rcf[dsl, :], tmpA[dsl, :], AF.Exp, scale=-1.0), reads=[tmpA_b], writes=[rcf_b])
                S.op("dve", lambda E: E.tensor_copy(rhi[dsl, :], rcf[dsl, :]), reads=[rcf_b], writes=[rhi_b])
                S.op("dve", lambda E: E.tensor_tensor(rlo[dsl, :], rcf[dsl, :], rhi[dsl, :], ALU.subtract), reads=[rcf_b, rhi_b], writes=[rlo_b])
                pr, pr_b = ps_r.next()
                S.op("pe", lambda E: E.matmul(pr[osl, :], ident_bf[dsl, 64:128], rhi[dsl, :], start=True, stop=False), reads=[cbf_b, rhi_b], writes=[pr_b])
                S.op("pe", lambda E: E.matmul(pr[osl, :], ident_bf[dsl, 64:128], rlo[dsl, :], start=False, stop=True), reads=[cbf_b, rlo_b], writes=[pr_b])
                S.op("act", lambda E: E.copy(rcs[osl, :], pr[osl, :]), reads=[pr_b], writes=[rcs_b])
                os_, os_b = ost.next()
                S.op("dve", lambda E: E.tensor_tensor(os_[osl, :], po[osl, :], rcs[osl, :], ALU.mult), reads=[po_b, rcs_b], writes=[os_b])
                S.dma("pool", o1_d[h * 64:(h + 1) * 64, u * 512:(u + 1) * 512], os_[osl, :], os_b, reads=[os_b], pwrites=[o1_b])
        S.barrier()
        oT = KK.rearrange("p (c n) -> p c n", c=8)
        oT_b = B()
        S.dma("sp", oT, o1_d.rearrange("(c p) n -> p c n", p=128), oT_b, reads=[o1_b], writes=[oT_b])
        out_proj([oT[:, c, :] for c in range(8)], [oT_b] * 8, Wd["mla_out"], wfm, psrot([0, 1, 2]))
        S.barrier()

    load_x()
    seq = [lambda: ffn(Wd[("gu", "pre", 0)], Wd[("d", "pre", 0)], "ffn_pre", 0),
           phase_A0, phase_B0, lambda: phase_X(0),
           lambda: ffn(Wd[("gu", "post", 0)], Wd[("d", "post", 0)], "ffn_post", 0),
           lambda: ffn(Wd[("gu", "pre", 1)], Wd[("d", "pre", 1)], "ffn_pre", 16),
           phase_A1, phase_B1, lambda: phase_X(1),
           lambda: ffn(Wd[("gu", "post", 1)], Wd[("d", "post", 1)], "ffn_post", 16)]
    for i in range(lo, hi):
        seq[i]()
    for n in db:
        if db[n].lw is not None:
            S._wait("sp", db[n].lw)
        for k_, v_ in db[n].mw.items():
            S._wait("sp", (k_, v_))
    store_x()
    S.run()
    print('[build] phases', lo, hi, 'counts', dict(S.cnt), 'nsems', len(S.sems), 'maxdma', max([0] + [v for k, v in S.alltoks.items() if str(k).startswith('d')]), flush=True)
    return nc, used_inputs, outputs


def zig(t, r):
    return 4 * t + (r if t % 2 == 0 else 3 - r)


def shard_tokens(x):
    outs = []
    for c in range(NCORES):
        b, r = c // 4, c % 4
        blocks = [zig(t, r) for t in range(NT)]
        xb = x[b].reshape(64, 128, -1)
        outs.append(np.ascontiguousarray(xb[blocks].reshape(T, -1)))
    return outs


def unshard_tokens(outs, dtype=np.float32):
    full = np.zeros((2, 64, 128, D), dtype)
    for c in range(NCORES):
        b, r = c // 4, c % 4
        blocks = [zig(t, r) for t in range(NT)]
        full[b, blocks] = outs[c].reshape(NT, 128, D)
    return full.reshape(2, 8192, D)


def col(v):
    v = np.asarray(v, np.float32)
    return np.ascontiguousarray(v.reshape(-1, 128).T)


def make_masks(r):
    k = np.arange(128)[:, None]
    q = np.arange(128)[None, :]
    m = np.zeros((128, 16, 128), np.float32)
    for kind in range(2):
        tri = (k < q) if kind == 0 else (k <= q)
        for p in range(2):
            rr = r if p == 0 else 3 - r
            for mj in range(4):
                if mj < rr:
                    v = np.zeros((128, 128), np.float32)
                elif mj == rr:
                    v = np.where(tri, 0.0, NEG).astype(np.float32)
                else:
                    v = np.full((128, 128), NEG, np.float32)
                m[:, kind * 8 + p * 4 + mj, :] = v
    return m.reshape(128, 16 * 128).astype(ml_dtypes.bfloat16)


_PROG_CACHE = {}


def prepare_inputs(inp):
    f32 = np.float32
    ident = np.eye(128, dtype=f32)
    ones = np.ones((128, 128), f32)
    negU = -np.tril(np.ones((128, 128), f32))
    sw = np.zeros((128, 128), f32)
    for i in range(16):
        sw[16 + i, i] = 1.0
        sw[i, 16 + i] = 1.0
    cb = np.concatenate([ident, ones, negU, sw], axis=1).astype(ml_dtypes.bfloat16)
    gc = np.zeros((128, NGC), f32)
    for l in range(2):
        gc[:, 16 * l:16 * l + 8] = col(inp["ffn_pre_norm"][l])
        gc[:, 16 * l + 8:16 * l + 16] = col(inp["ffn_post_norm"][l])
        gc[:, 32 + 8 * l:40 + 8 * l] = col(inp["mix_norm"][l])
        gc[:, 48 + 8 * l:56 + 8 * l] = col(inp["xmem_norm"][l])
        gc[:, 64 + 8 * l:72 + 8 * l] = col(inp["xmem_mem_norm"][l])
        gc[:, 80 + 2 * l:82 + 2 * l] = col(inp["xmem_q_gain"][l])
        gc[:, 84 + 2 * l:86 + 2 * l] = col(inp["xmem_k_gain"][l])
    gc[:, 88:92] = col(inp["mla_q_lora_gain"][0])
    gc[:, 92:94] = col(inp["mla_kv_lora_gain"][0])
    perm = np.arange(96)
    perm[64:80] = np.arange(80, 96)
    perm[80:96] = np.arange(64, 80)
    gc[:96, 94] = inp["mla_q_gain"][0]
    gc[:96, 95] = inp["mla_k_gain"][0]
    gc[:96, 96] = inp["mla_q_gain"][0][perm]
    gc[:96, 97] = inp["mla_k_gain"][0][perm]
    half = 16
    invf = (f32(10000.0) ** (-np.arange(half, dtype=f32) / f32(half))).astype(f32)
    gc[64:80, 98] = invf
    gc[80:96, 98] = invf
    gc[64:80, 99] = -1.0
    gc[80:96, 99] = 1.0
    gc[:, 100] = EPS
    gc[:, 101] = np.pi / 2
    gc[:, 102] = 1.0
    bc0 = np.zeros((128, 1536), f32)
    bc0[:, 0:512] = inp["sgu_ln_gain"][0][None, :]
    bc0[:, 512:1024] = inp["sgu_ln_bias"][0][None, :]
    sb = inp["sgu_b"][0]
    for cc in range(4):
        bc0[0:64, 1024 + cc * 128:1024 + (cc + 1) * 128] = sb[2 * cc][None, :]
        bc0[64:128, 1024 + cc * 128:1024 + (cc + 1) * 128] = sb[2 * cc + 1][None, :]
    trim = (np.arange(128)[:, None] <= np.arange(128)[None, :]).astype(f32)
    sguwT = np.ascontiguousarray(inp["sgu_w"][0].transpose(2, 0, 1)).reshape(128, 8 * 128)
    common = {"consts_bf": cb, "ident_f": ident, "gcols": gc, "bc0": bc0, "trimask": trim, "sgu_wT": sguwT}
    for l in range(2):
        for which in ("pre", "post"):
            common["w_%s_gu_%d" % (which, l)] = np.ascontiguousarray(inp["ffn_%s_w_gu" % which][l])
            common["w_%s_d_%d" % (which, l)] = np.ascontiguousarray(inp["ffn_%s_w_down" % which][l])
        common["w_xq_%d" % l] = np.ascontiguousarray(inp["xmem_wq"][l])
        wkv = inp["xmem_wkv"][l].reshape(D, 4, 2, 256)
        common["w_xk_%d" % l] = np.ascontiguousarray(wkv[:, :, 0, :].reshape(D, D))
        common["w_xv_%d" % l] = np.ascontiguousarray(wkv[:, :, 1, :].reshape(D, D))
        common["w_xo_%d" % l] = np.ascontiguousarray(inp["xmem_wo"][l])
    common["w_sbg_in"] = np.ascontiguousarray(inp["sbg_w_in"][0])
    common["w_sbg_out"] = np.ascontiguousarray(inp["sbg_w_out"][0])
    common["w_mla_in"] = np.ascontiguousarray(inp["mla_w_in"][0])
    uq = inp["mla_w_uq"][0]
    common["w_mla_uq"] = np.ascontiguousarray(uq)
    uq_h = uq.reshape(512, 16, 96)
    common["w_mla_uq_sw"] = np.ascontiguousarray(np.concatenate([uq_h[:, :, 80:96], uq_h[:, :, 64:80]], axis=2).reshape(512, 512))
    ukv = inp["mla_w_ukv"][0].reshape(256, 16, 128)
    common["w_mla_uk"] = np.ascontiguousarray(ukv[:, :, 0:64].reshape(256, 1024))
    common["w_mla_uv"] = np.ascontiguousarray(ukv[:, :, 64:128].reshape(256, 1024))
    common["w_mla_out"] = np.ascontiguousarray(inp["mla_w_out"][0])
    xs = shard_tokens(inp["x"])
    pos = inp["positions"].astype(np.int32)
    in_maps = []
    for c in range(NCORES):
        b, r = c // 4, c % 4
        blocks = [zig(t, r) for t in range(NT)]
        pl = pos[b].reshape(64, 128)[blocks].reshape(T)
        m = dict(common)
        m["x_in"] = xs[c]
        m["masks"] = make_masks(r)
        m["posrep"] = np.ascontiguousarray(np.broadcast_to(pl[None, :], (32, T))).astype(np.int32)
        m["mem"] = np.ascontiguousarray(inp["mem"][b])
        in_maps.append(m)
    return in_maps


EXCH = "cc"
DEBUG_X = {}


def _run(key, in_maps):
    if key not in _PROG_CACHE:
        _PROG_CACHE[key] = build_program(*key)
    nc, used, outs = _PROG_CACHE[key]
    maps = [{k: v for k, v in m.items() if k in used} for m in in_maps]
    res = run_bass_kernel_spmd(nc, maps, core_ids=list(range(NCORES)))
    return res.results


def _gather(results, name, rc):
    out = []
    for c in range(NCORES):
        b = c // 4
        rows = results[c][name].shape[0]
        parts = []
        for i in range(rows // rc):
            for rk in range(4):
                parts.append(results[4 * b + rk][name][i * rc:(i + 1) * rc])
        out.append(np.ascontiguousarray(np.concatenate(parts, axis=0)))
    return out


def kernel(**inputs):
    inp = {k: np.asarray(v) for k, v in inputs.items()}
    in_maps = prepare_inputs(inp)
    if EXCH == "cc":
        res = _run((0, 10, "cc"), in_maps)
        return unshard_tokens([r["x_out"] for r in res])
    r1 = _run((0, 2, "host"), in_maps)
    kg0, vg0 = _gather(r1, "k0_d", RC_K0), _gather(r1, "v0_d", RC_V0)
    for c in range(NCORES):
        in_maps[c].update({"x_in": r1[c]["x_out"], "q0_d": r1[c]["q0_d"], "osg_d": r1[c]["osg_d"], "kg0_d": kg0[c], "vg0_d": vg0[c]})
    r2 = _run((2, 7, "host"), in_maps)
    DEBUG_X["p2"] = [r["x_out"] for r in r2]
    kg1, vg1 = _gather(r2, "k1_d", RC_K1), _gather(r2, "v1_d", RC_V1)
    for c in range(NCORES):
        in_maps[c].update({"x_in": r2[c]["x_out"], "q1_d": r2[c]["q1_d"], "kg1_d": kg1[c], "vg1_d": vg1[c]})
    r3 = _run((7, 10, "host"), in_maps)
    return unshard_tokens([r["x_out"] for r in r3])
```

```python
import types
import numpy as np
import ml_dtypes
import concourse.bass as bass
import concourse.mybir as mybir
from concourse.bass_utils import run_bass_kernel_spmd

F32 = mybir.dt.float32
BF16 = mybir.dt.bfloat16
I32 = mybir.dt.int32
AF = mybir.ActivationFunctionType
ALU = mybir.AluOpType

NCORES = 8
D = 1024
DFF = 2816
T = 2048
NT = 16
EPS = 1e-6
NEG = -30000.0


class Buf:
    __slots__ = ("name", "lw", "rd", "dkey", "dcnt", "mw")

    def __init__(self, name):
        self.name = name
        self.lw = None
        self.rd = {}
        self.dkey = None
        self.dcnt = 0
        self.mw = {}


ENGS = ("pe", "act", "dve", "pool", "sp")


class Sched:
    def __init__(self, nc):
        self.nc = nc
        self.prog = {e: [] for e in ENGS}
        self.cnt = {e: 0 for e in ENGS}
        self.waited = {e: {} for e in ENGS}
        self.sems = {}
        self.alltoks = {}
        self.dfree = []
        self.dcum = {}
        self.dheld = []

    def sem(self, key):
        if key not in self.sems:
            self.sems[key] = self.nc.alloc_semaphore(name="s_%s" % (str(key).replace(" ", "")))
        return self.sems[key]

    def _wait(self, eng, tok):
        key, val = tok
        if key == eng and eng in ("pe", "sp"):
            return
        if self.waited[eng].get(key, 0) >= val:
            return
        self.waited[eng][key] = val
        h = self.sem(key)
        self.prog[eng].append(lambda E, h=h, val=val: E.wait_ge(h, val))

    def _deps(self, eng, reads, writes, pwrites=()):
        for b in reads:
            if b.lw is not None:
                self._wait(eng, b.lw)
            for k, v in b.mw.items():
                self._wait(eng, (k, v))
        for b in pwrites:
            if b.lw is not None and b.lw[0] != eng:
                self._wait(eng, b.lw)
            for k, v in b.rd.items():
                if k != eng:
                    self._wait(eng, (k, v))
        for b in writes:
            if b.lw is not None and b.lw[0] != eng:
                self._wait(eng, b.lw)
            for k, v in b.mw.items():
                self._wait(eng, (k, v))
            for k, v in b.rd.items():
                if k != eng:
                    self._wait(eng, (k, v))

    def _upd(self, tok, reads, writes):
        self.alltoks[tok[0]] = max(self.alltoks.get(tok[0], 0), tok[1])
        for b in reads:
            if b.rd.get(tok[0], 0) < tok[1]:
                b.rd[tok[0]] = tok[1]
        for b in writes:
            b.lw = tok
            b.rd = {}
            b.mw = {}

    @staticmethod
    def _freeze(fn):
        if fn.__closure__:
            cells = tuple(types.CellType(c.cell_contents) for c in fn.__closure__)
            g = types.FunctionType(fn.__code__, fn.__globals__, fn.__name__, fn.__defaults__, cells)
            g.__kwdefaults__ = fn.__kwdefaults__
            return g
        return fn

    def op(self, eng, fn, reads=(), writes=()):
        fn = self._freeze(fn)
        self._deps(eng, reads, writes)
        self.cnt[eng] += 1
        tok = (eng, self.cnt[eng])
        h = self.sem(eng)
        self.prog[eng].append(lambda E, fn=fn, h=h: fn(E).then_inc(h, 1))
        self._upd(tok, reads, writes)
        return tok

    def dma(self, q, out_ap, in_ap, owner, reads=(), writes=(), pwrites=()):
        self._deps(q, reads, writes, pwrites)
        if owner.dkey is None:
            if self.dfree:
                owner.dkey = self.dfree.pop()
            else:
                owner.dkey = "q%d" % len(self.dcum)
                self.dcum[owner.dkey] = 0
            self.dheld.append(owner)
        self.dcum[owner.dkey] += 16
        tok = (owner.dkey, self.dcum[owner.dkey])
        h = self.sem(owner.dkey)
        self.prog[q].append(lambda E, o=out_ap, i=in_ap, h=h: E.dma_start(out=o, in_=i).then_inc(h, 16))
        self._upd(tok, reads, writes)
        for b in pwrites:
            if b.mw.get(tok[0], 0) < tok[1]:
                b.mw[tok[0]] = tok[1]
        return tok

    def barrier(self, release=True):
        for e in ENGS:
            for k, v in list(self.alltoks.items()):
                self._wait(e, (k, v))
        if release:
            for b in self.dheld:
                self.dfree.append(b.dkey)
                b.dkey = None
            self.dheld = []

    def run(self):
        nc = self.nc
        with nc.Block() as block:
            @block.tensor
            def _(E):
                for f in self.prog["pe"]:
                    f(E)

            @block.scalar
            def _(E):
                for f in self.prog["act"]:
                    f(E)

            @block.vector
            def _(E):
                for f in self.prog["dve"]:
                    f(E)

            @block.gpsimd
            def _(E):
                for f in self.prog["pool"]:
                    f(E)

            @block.sync
            def _(E):
                for f in self.prog["sp"]:
                    f(E)


class Rot:
    def __init__(self, slots):
        self.slots = slots
        self.i = 0

    def next(self):
        s = self.slots[self.i % len(self.slots)]
        self.i += 1
        return s


class Ctx:
    pass


def mk_tile(nc, name, shape, dt):
    return nc.alloc_sbuf_tensor(name, shape, dt)


import os
DBG = os.environ.get('KDBG', '')
RC_K0, RC_V0, RC_K1, RC_V1 = 256, 1024, 192, 512
GC = dict(ffn_pre=0, ffn_post=8, mix=32, xn=48, xmn=64, xqg=80, xkg=84, qlora=88, kvlora=92,
          mqg=94, mkg=95, mqg_sw=96, mkg_sw=97, invf=98, sgn=99, eps=100, halfpi=101, one=102, zero=103)
NGC = 104
TWO_PI = float(2 * np.pi)
PI = float(np.pi)


PHASES = ["F0pre", "A0", "B0", "X0", "F0post", "F1pre", "A1", "B1", "X1", "F1post"]


def build_program(lo=0, hi=10, exch="cc"):
    nc = bass.Bass("TRN2", target_bir_lowering=False)
    S = Sched(nc)
    outputs = ["x_out"]

    used_inputs = set()

    def din(name, shape, dt=F32):
        used_inputs.add(name)
        return nc.dram_tensor(name, list(shape), dt, kind="ExternalInput").ap()

    x_in = din("x_in", [T, D])
    x_out = nc.dram_tensor("x_out", [T, D], F32, kind="ExternalOutput").ap()
    consts_bf = din("consts_bf", [128, 4 * 128], BF16)
    ident_f_d = din("ident_f", [128, 128])
    gcols_d = din("gcols", [128, NGC])
    bc0_d = din("bc0", [128, 1536])
    trim_d = din("trimask", [128, 128])
    sguwT_d = din("sgu_wT", [128, 8 * 128])
    masks_d = din("masks", [128, 16 * 128], BF16)
    posrep_d = din("posrep", [32, T], I32)
    mem_d = din("mem", [256, D])
    WSHAPES = {"sbg_in": [D, 2560], "sbg_out": [D, D], "mla_in": [D, 800], "mla_uq": [512, 1536], "mla_uq_sw": [512, 512],
               "mla_uk": [256, 1024], "mla_uv": [256, 1024], "mla_out": [D, D]}

    class LazyW(dict):
        def __missing__(self, key):
            if isinstance(key, tuple) and key[0] in ("gu", "d"):
                name = "w_%s_%s_%d" % (key[1], key[0], key[2])
                shape = [D, 2 * DFF] if key[0] == "gu" else [DFF, D]
            elif isinstance(key, tuple):
                name = "w_%s_%d" % key
                shape = [D, D]
            else:
                name = "w_" + key
                shape = WSHAPES[key]
            self[key] = din(name, shape)
            return self[key]

    Wd = LazyW()
    def scratch(name, shape, prod, cons):
        if exch == "cc" or (lo <= prod < hi and lo <= cons < hi):
            return nc.dram_tensor(name, shape, BF16).ap()
        if lo <= prod < hi:
            outputs.append(name)
            return nc.dram_tensor(name, shape, BF16, kind="ExternalOutput").ap()
        if lo <= cons < hi:
            return din(name, shape, BF16)
        return None

    gp = 1 if exch == "cc" else -1
    q0_d = scratch("q0_d", [512, T], 1, 2)
    k0_d = scratch("k0_d", [512, T], 1, 2 if exch == "cc" else 99)
    v0_d = scratch("v0_d", [T, 512], 1, 2 if exch == "cc" else 99)
    osg_d = scratch("osg_d", [512, T], 1, 2)
    kg0_d = scratch("kg0_d", [4 * 512, T], gp, 2)
    vg0_d = scratch("vg0_d", [4 * T, 512], gp, 2)
    gp = 6 if exch == "cc" else -1
    q1_d = scratch("q1_d", [1536, T], 6, 7)
    k1_d = scratch("k1_d", [1536, T], 6, 7 if exch == "cc" else 99)
    v1_d = scratch("v1_d", [T, 1024], 6, 7 if exch == "cc" else 99)
    kg1_d = scratch("kg1_d", [4 * 1536, T], gp, 7)
    vg1_d = scratch("vg1_d", [4 * T, 1024], gp, 7)
    db = {n: Buf(n) for n in ("q0", "k0", "v0", "osg", "kg0", "vg0", "q1", "k1", "v1", "kg1", "vg1")}

    xT = nc.alloc_sbuf_tensor("xT", [128, 8, T], F32)
    xTb = [[Buf("xT_%d_%d" % (c, g)) for g in range(4)] for c in range(8)]
    cbf = nc.alloc_sbuf_tensor("cbf", [128, 4 * 128], BF16)
    cbf_b = Buf("cbf")
    ident_f = nc.alloc_sbuf_tensor("sb_ident_f", [128, 128], F32)
    ident_f_b = Buf("ident_f")
    gcols = nc.alloc_sbuf_tensor("sb_gcols", [128, NGC], F32)
    gcols_b = Buf("gcols")
    ident_bf = cbf[:, 0:128]
    ones_bf = cbf[:, 128:256]
    negU_bf = cbf[:, 256:384]
    swap_bf = cbf[:, 384:512]

    def gcol(name, i=0, p0=0, p1=128):
        return gcols[p0:p1, GC[name] + i:GC[name] + i + 1]

    S.dma("sp", cbf[:, :], consts_bf[:, :], cbf_b, writes=[cbf_b])
    S.dma("sp", ident_f[:, :], ident_f_d[:, :], ident_f_b, writes=[ident_f_b])
    S.dma("sp", gcols[:, :], gcols_d[:, :], gcols_b, writes=[gcols_b])

    posi = nc.alloc_sbuf_tensor("posi", [128, 512], I32)
    posi_b = Buf("posi")
    nint = nc.alloc_sbuf_tensor("nint", [128, 512], I32)
    nint_b = Buf("nint")
    AF32 = 11776
    ABF = 45056
    arena_f = nc.alloc_sbuf_tensor("arena_f", [128, AF32], F32)
    arena_h = nc.alloc_sbuf_tensor("arena_h", [128, ABF], BF16)
    psum = [nc.alloc_psum_tensor("ps%d" % i, [128, 512], F32) for i in range(8)]

    class Carver:
        def __init__(self, ar, size):
            self.ar, self.size, self.off = ar, size, 0

        def reset(self):
            self.off = 0

        def take(self, n):
            assert self.off + n <= self.size, ("arena overflow", self.off, n, self.size)
            a = self.ar[:, self.off:self.off + n]
            self.off += n
            return a

    cf = Carver(arena_f, AF32)
    ch = Carver(arena_h, ABF)
    nbuf = [0]

    def B(name="b"):
        nbuf[0] += 1
        return Buf("%s%d" % (name, nbuf[0]))

    def rotf(n, size, view=None):
        return Rot([((cf.take(size) if view is None else view(cf.take(size))), B()) for _ in range(n)])

    def roth(n, size, view=None):
        return Rot([((ch.take(size) if view is None else view(ch.take(size))), B()) for _ in range(n)])

    def psrot(idxs):
        return Rot([(psum[i], B("ps")) for i in idxs])

    W = Ctx()

    def wsetup(nst=2, stsize=2048):
        W.st = rotf(nst, stsize)

    castctr = [0]

    def cast(dst, dst_b, src, src_b):
        castctr[0] += 1
        if castctr[0] % 2 == 0:
            S.op("act", lambda E: E.copy(dst, src), reads=[src_b], writes=[dst_b])
        else:
            S.op("dve", lambda E: E.tensor_copy(dst, src), reads=[src_b], writes=[dst_b])

    def wload(view, kc, n, dst, dst_b, eng=None):
        assert kc * n <= 2048
        st, st_b = W.st.next()
        sv = st[:, 0:kc * n].rearrange("p (c n) -> p c n", c=kc)
        S.dma("sp", sv, view, st_b, writes=[st_b])
        cast(dst, dst_b, sv, st_b)

    def rstd_from_ss(out_ap, out_b, ps_ap, ps_b, n, tmp_ap, tmp_b, p0=0, p1=128):
        S.op("act", lambda E: E.activation(tmp_ap, ps_ap, AF.Ln, bias=gcol("eps", 0, p0, p1), scale=1.0 / n),
             reads=[ps_b, gcols_b], writes=[tmp_b])
        S.op("act", lambda E: E.activation(out_ap, tmp_ap, AF.Exp, scale=-0.5), reads=[tmp_b], writes=[out_b])

    def load_x():
        cf.reset()
        rot = rotf(2, 1024)
        psb = [B("ps"), B("ps")]
        for t in range(NT):
            ap, b = rot.next()
            S.dma("sp", ap, x_in[t * 128:(t + 1) * 128, :], b, writes=[b])
            for half in range(2):
                pb, ps = psb[half], psum[half]
                for cc in range(4):
                    c = half * 4 + cc
                    S.op("pe", lambda E, o=ps[:, cc * 128:(cc + 1) * 128], i=ap[:, c * 128:(c + 1) * 128]:
                         E.transpose(o, i, ident_f[:, :]), reads=[b, ident_f_b], writes=[pb])
                g = t // 4
                outap = xT[:, half * 4:half * 4 + 4, t * 128:(t + 1) * 128]
                inap = ps[:, :].rearrange("p (c n) -> p c n", c=4)
                wr = [xTb[half * 4 + cc][g] for cc in range(4)]
                if half == 0:
                    S.op("act", lambda E, o=outap, i=inap: E.copy(o, i), reads=[pb], writes=wr)
                else:
                    S.op("dve", lambda E, o=outap, i=inap: E.tensor_copy(o, i), reads=[pb], writes=wr)
        S.barrier()

    def store_x():
        cf.reset()
        rot = rotf(2, 1024)
        psb = [B("ps"), B("ps")]
        toks = []
        for t in range(NT):
            ap, b = rot.next()
            g = t // 4
            for half in range(2):
                pb, ps = psb[half], psum[half]
                for cc in range(4):
                    c = half * 4 + cc
                    S.op("pe", lambda E, o=ps[:, cc * 128:(cc + 1) * 128], i=xT[:, c, t * 128:(t + 1) * 128]:
                         E.transpose(o, i, ident_f[:, :]), reads=[xTb[c][g], ident_f_b], writes=[pb])
                if half == 0:
                    S.op("act", lambda E, o=ap[:, 0:512], i=ps[:, :]: E.copy(o, i), reads=[pb], writes=[b])
                else:
                    S.op("dve", lambda E, o=ap[:, 512:1024], i=ps[:, :]: E.tensor_copy(o, i), reads=[pb], writes=[b])
            toks.append(S.dma("sp", x_out[t * 128:(t + 1) * 128, :], ap, b, reads=[b]))
        for tk in toks:
            S._wait("sp", tk)
        S.barrier()

    def norm_prep(gname, gi0, grp, xb_ap, xb_b, rstd_ap, rstd_b, sq_ap, sq_b, ps_i, ps_b, tmp_ap, tmp_b, tokps=None):
        tsl = slice(grp * 512, (grp + 1) * 512)
        xbufs = [xTb[c][grp] for c in range(8)]
        S.op("act", lambda E: E.activation(sq_ap, xT[:, :, tsl], AF.Square), reads=xbufs, writes=[sq_b])
        ps = psum[ps_i]
        for c in range(8):
            S.op("pe", lambda E, c=c: E.matmul(ps[:, :], ones_bf, sq_ap[:, c, :], start=(c == 0), stop=(c == 7)),
                 reads=[sq_b, cbf_b], writes=[ps_b])
        rstd_from_ss(rstd_ap, rstd_b, ps[:, :], ps_b, D, tmp_ap, tmp_b)
        if tokps is not None:
            tp_ap, tp_b = tokps
            for tt in range(4):
                for c in range(8):
                    S.op("pe", lambda E, c=c, tt=tt: E.matmul(tp_ap[:, grp * 4 + tt:grp * 4 + tt + 1],
                                                              sq_ap[:, c, tt * 128:(tt + 1) * 128], ones_bf[:, 0:1],
                                                              start=(c == 0), stop=(c == 7)),
                         reads=[sq_b, cbf_b], writes=[tp_b])
        for c in range(8):
            S.op("dve", lambda E, c=c: E.tensor_scalar(xb_ap[:, c, :], xT[:, c, tsl], gcol(gname, gi0 + c), None, ALU.mult),
                 reads=[xTb[c][grp], gcols_b], writes=[xb_b])

    def ffn(wgu_d, wd_d, gname, gi0):
        for half in range(2):
            cf.reset()
            ch.reset()
            rstd = [cf.take(512) for _ in range(2)]
            rstd_b = [B() for _ in range(2)]
            tmp, tmp_b = cf.take(512), B()
            stg_gu = rotf(2, 2048, lambda a: a.rearrange("p (a c m) -> p a c m", a=2, c=8))
            stg_d = rotf(2, 1408, lambda a: a.rearrange("p (k m) -> p k m", k=11))
            a_rot, s_rot, t_rot = rotf(2, 512), rotf(2, 512), rotf(2, 512)
            xb = [ch.take(4096).rearrange("p (c n) -> p c n", c=8) for _ in range(2)]
            xb_b = [B() for _ in range(2)]
            hid = ch.take(22 * 1024).rearrange("p (k n) -> p k n", k=22)
            hid_b = [[B() for g in range(2)] for k in range(22)]
            wgu = roth(2, 2048, lambda a: a.rearrange("p (a c m) -> p a c m", a=2, c=8))
            wd = roth(2, 2816, lambda a: a.rearrange("p (k m) -> p k m", k=22))
            sq, sq_b = ch.take(4096).rearrange("p (c n) -> p c n", c=8), B()
            ps_ss_b = B("ps")
            ps_g, ps_u, ps_a = psrot([1, 2]), psrot([3, 4]), psrot([5, 6])
            for gi in range(2):
                norm_prep(gname, gi0, half * 2 + gi, xb[gi], xb_b[gi], rstd[gi], rstd_b[gi], sq, sq_b, 0, ps_ss_b, tmp, tmp_b)
            wgu_v = wgu_d.rearrange("(c p) n -> p c n", p=128)
            for j in range(22):
                st, st_b = stg_gu.next()
                S.dma("sp", st[:, 0], wgu_v[:, :, j * 128:(j + 1) * 128], st_b, writes=[st_b])
                S.dma("sp", st[:, 1], wgu_v[:, :, DFF + j * 128:DFF + (j + 1) * 128], st_b, writes=[st_b])
                wt, wt_b = wgu.next()
                cast(wt, wt_b, st, st_b)
                for gi in range(2):
                    pg, pg_b = ps_g.next()
                    pu, pu_b = ps_u.next()
                    for c in range(8):
                        S.op("pe", lambda E, c=c, pg=pg, wt=wt, gi=gi: E.matmul(pg[:, :], wt[:, 0, c, :], xb[gi][:, c, :],
                                                                                 start=(c == 0), stop=(c == 7)),
                             reads=[wt_b, xb_b[gi]], writes=[pg_b])
                    for c in range(8):
                        S.op("pe", lambda E, c=c, pu=pu, wt=wt, gi=gi: E.matmul(pu[:, :], wt[:, 1, c, :], xb[gi][:, c, :],
                                                                                 start=(c == 0), stop=(c == 7)),
                             reads=[wt_b, xb_b[gi]], writes=[pu_b])
                    a, a_b = a_rot.next()
                    s, s_b = s_rot.next()
                    t, t_b = t_rot.next()
                    S.op("dve", lambda E, a=a, pg=pg, gi=gi: E.tensor_tensor(a, pg[:, :], rstd[gi], ALU.mult),
                         reads=[pg_b, rstd_b[gi]], writes=[a_b])
                    S.op("act", lambda E, s=s, a=a: E.activation(s, a, AF.Silu), reads=[a_b], writes=[s_b])
                    S.op("dve", lambda E, t=t, pu=pu, gi=gi: E.tensor_tensor(t, pu[:, :], rstd[gi], ALU.mult),
                         reads=[pu_b, rstd_b[gi]], writes=[t_b])
                    S.op("pool", lambda E, s=s, t=t, j=j, gi=gi: E.tensor_tensor(hid[:, j, gi * 512:(gi + 1) * 512], s, t, ALU.mult),
                         reads=[s_b, t_b], writes=[hid_b[j][gi]])
            wd_v = wd_d.rearrange("(k p) n -> p k n", p=128)
            for i in range(8):
                wt, wt_b = wd.next()
                for hh in range(2):
                    st, st_b = stg_d.next()
                    S.dma("sp", st, wd_v[:, hh * 11:(hh + 1) * 11, i * 128:(i + 1) * 128], st_b, writes=[st_b])
                    cast(wt[:, hh * 11:(hh + 1) * 11, :], wt_b, st, st_b)
                for gi in range(2):
                    pa, pa_b = ps_a.next()
                    for k in range(22):
                        S.op("pe", lambda E, k=k, pa=pa, wt=wt, gi=gi: E.matmul(pa[:, :], wt[:, k, :], hid[:, k, gi * 512:(gi + 1) * 512],
                                                                                 start=(k == 0), stop=(k == 21)),
                             reads=[wt_b, hid_b[k][gi]], writes=[pa_b])
                    grp = half * 2 + gi
                    xs = xT[:, i, grp * 512:(grp + 1) * 512]
                    S.op("dve", lambda E, xs=xs, pa=pa: E.scalar_tensor_tensor(xs, pa[:, :], 0.5, xs, ALU.mult, ALU.add),
                         reads=[pa_b, xTb[i][grp]], writes=[xTb[i][grp]])
            S.barrier()

    def out_proj(mix, mix_b, w_dram, wfm, ps_o):
        wv = w_dram.rearrange("(c p) n -> p c n", p=128)
        for i in range(8):
            wt, wt_b = wfm.next()
            wload(wv[:, :, i * 128:(i + 1) * 128], 8, 128, wt, wt_b)
            for g in range(4):
                po, po_b = ps_o.next()
                for c in range(8):
                    mb = mix_b[c][g] if isinstance(mix_b[c], list) else mix_b[c]
                    S.op("pe", lambda E, c=c, po=po, wt=wt, g=g: E.matmul(po[:, :], wt[:, c, :], mix[c][:, g * 512:(g + 1) * 512],
                                                                           start=(c == 0), stop=(c == 7)),
                         reads=[wt_b, mb], writes=[po_b])
                xs = xT[:, i, g * 512:(g + 1) * 512]
                S.op("dve", lambda E, xs=xs, po=po: E.tensor_tensor(xs, po[:, :], xs, ALU.add),
                     reads=[po_b, xTb[i][g]], writes=[xTb[i][g]])

    def gelu(x_ap, x_b, out_ap, out_b, ga, gb):
        (a, a_b), (b, b_b) = ga, gb
        S.op("act", lambda E: E.activation(a, x_ap, AF.Square), reads=[x_b], writes=[a_b])
        S.op("dve", lambda E: E.tensor_scalar(a, a, 0.044715, 1.0, ALU.mult, ALU.add), reads=[a_b], writes=[a_b])
        S.op("dve", lambda E: E.tensor_tensor(a, a, x_ap, ALU.mult), reads=[a_b, x_b], writes=[a_b])
        S.op("act", lambda E: E.activation(b, a, AF.Sigmoid, scale=1.5957691216057308), reads=[a_b], writes=[b_b])
        S.op("pool", lambda E: E.tensor_tensor(out_ap, x_ap, b, ALU.mult), reads=[x_b, b_b], writes=[out_b])

    def allgather(src_ap, src_b, dst_ap, dst_b, rows, rc):
        S._deps("pool", [src_b], [dst_b])
        h = S.sem("cc")
        S.cnt.setdefault("cc", 0)
        for i in range(rows // rc):
            S.cnt["cc"] += 1
            S.prog["pool"].append(lambda E, i=i: E.collective_compute("AllGather", ALU.bypass, replica_groups=[[0, 1, 2, 3], [4, 5, 6, 7]],
                                                                      ins=[src_ap[i * rc:(i + 1) * rc, :]],
                                                                      outs=[dst_ap[i * 4 * rc:(i + 1) * 4 * rc, :]]).then_inc(h, 1))
        S._upd(("cc", S.cnt["cc"]), [src_b], [dst_b])

    def grow(rc, rk, row):
        return (row // rc) * 4 * rc + rk * rc + (row % rc)

    def phase_A0():
        cf.reset()
        ch.reset()
        rstd = [cf.take(512) for _ in range(4)]
        rstd_b = [B() for _ in range(4)]
        tmp, tmp_b = cf.take(512), B()
        rtok, rtok_b = cf.take(16), B()
        rtmp, rtmp_b = cf.take(16), B()
        wsetup()
        zt = rotf(2, 512)
        ga, gb = (cf.take(512), B()), (cf.take(512), B())
        gl = (cf.take(512), B())
        bc, bc_b = cf.take(1536), B()
        trim, trim_b = cf.take(128), B()
        st6, st6_b = cf.take(8), B()
        mv, mv_b = cf.take(4), B()
        xb = ch.take(16384).rearrange("p (c n) -> p c n", c=8)
        xb_b = [B() for _ in range(4)]
        sq, sq_b = ch.take(4096).rearrange("p (c n) -> p c n", c=8), B()
        Wv, Wv_b = ch.take(4096).rearrange("p (c n) -> p c n", c=8), B()
        Wzg, Wzg_b = ch.take(4096).rearrange("p (c n) -> p c n", c=8), B()
        wfm = roth(2, 1024, lambda a: a.rearrange("p (c n) -> p c n", c=8))
        uT = ch.take(8192).rearrange("p (c n) -> p c n", c=4)
        uT_b = [[B() for g in range(4)] for c in range(4)]
        WsT, WsT_b = ch.take(1024).rearrange("p (g n) -> p g n", g=8), B()
        ev = roth(2, 512)
        vev = roth(2, 512)
        gtok = roth(2, 512)
        osg = roth(2, 512, lambda a: a.rearrange("p (c n) -> p c n", c=4))
        ps_ss_b = B("ps")
        tokps = (psum[7][:, 0:16], B("ps"))
        ps_p = psrot([1, 2, 3])
        ps_m = psrot([4, 5])
        S.dma("sp", bc, bc0_d[:, :], bc_b, writes=[bc_b])
        S.dma("sp", trim, trim_d[:, :], trim_b, writes=[trim_b])
        for hh in range(4):
            st, st_b = W.st.next()
            S.dma("sp", st[:, 0:256], sguwT_d[:, hh * 256:(hh + 1) * 256], st_b, writes=[st_b])
            for k in range(2):
                gidx = hh * 2 + k
                S.op("dve", lambda E, st=st, k=k, gidx=gidx: E.tensor_tensor(WsT[:, gidx, :], st[:, k * 128:(k + 1) * 128], trim, ALU.mult),
                     reads=[st_b, trim_b], writes=[WsT_b])
        for g in range(4):
            norm_prep("mix", 0, g, xb[:, :, g * 512:(g + 1) * 512], xb_b[g], rstd[g], rstd_b[g], sq, sq_b, 0, ps_ss_b, tmp, tmp_b, tokps=tokps)
        rstd_from_ss(rtok, rtok_b, tokps[0], tokps[1], D, rtmp, rtmp_b)
        win = Wd["sbg_in"].rearrange("(c p) n -> p c n", p=128)
        for hh in range(2):
            wload(win[:, hh * 4:(hh + 1) * 4, 1024:1536], 4, 512, Wv[:, hh * 4:(hh + 1) * 4, :], Wv_b)
            wload(win[:, hh * 4:(hh + 1) * 4, 2048:2560], 4, 512, Wzg[:, hh * 4:(hh + 1) * 4, :], Wzg_b)
        for oc in list(range(0, 8)) + list(range(12, 16)):
            wt, wt_b = wfm.next()
            wload(win[:, :, oc * 128:(oc + 1) * 128], 8, 128, wt, wt_b)
            for g in range(4):
                pp, pp_b = ps_p.next()
                for c in range(8):
                    S.op("pe", lambda E, c=c, pp=pp, wt=wt, g=g: E.matmul(pp[:, :], wt[:, c, :], xb[:, c, g * 512:(g + 1) * 512],
                                                                           start=(c == 0), stop=(c == 7)),
                         reads=[wt_b, xb_b[g]], writes=[pp_b])
                if oc < 8:
                    e, e_b = ev.next()
                    sc = 0.125 if oc < 4 else 1.0
                    S.op("dve", lambda E, e=e, pp=pp, g=g, sc=sc: E.scalar_tensor_tensor(e, pp[:, :], sc, rstd[g], ALU.mult, ALU.mult),
                         reads=[pp_b, rstd_b[g]], writes=[e_b])
                    dst, dn = (q0_d, "q0") if oc < 4 else (k0_d, "k0")
                    r0 = (oc % 4) * 128
                    S.dma("sp", dst[r0:r0 + 128, g * 512:(g + 1) * 512], e, e_b, reads=[e_b], pwrites=[db[dn]])
                else:
                    z, z_b = zt.next()
                    S.op("dve", lambda E, z=z, pp=pp, g=g: E.tensor_tensor(z, pp[:, :], rstd[g], ALU.mult),
                         reads=[pp_b, rstd_b[g]], writes=[z_b])
                    gelu(z, z_b, uT[:, oc - 12, g * 512:(g + 1) * 512], uT_b[oc - 12][g], ga, gb)
        for t in range(NT):
            g = t // 4
            pp, pp_b = ps_p.next()
            for c in range(8):
                S.op("pe", lambda E, c=c, pp=pp, t=t: E.matmul(pp[:, :], xb[:, c, t * 128:(t + 1) * 128], Wv[:, c, :],
                                                               start=(c == 0), stop=(c == 7)),
                     reads=[Wv_b, xb_b[g]], writes=[pp_b])
            e, e_b = vev.next()
            S.op("dve", lambda E, e=e, pp=pp, t=t: E.tensor_scalar(e, pp[:, :], rtok[:, t:t + 1], None, ALU.mult),
                 reads=[pp_b, rtok_b], writes=[e_b])
            S.dma("sp", v0_d[t * 128:(t + 1) * 128, :], e, e_b, reads=[e_b], pwrites=[db["v0"]])
            pp, pp_b = ps_p.next()
            for c in range(8):
                S.op("pe", lambda E, c=c, pp=pp, t=t: E.matmul(pp[:, :], xb[:, c, t * 128:(t + 1) * 128], Wzg[:, c, :],
                                                               start=(c == 0), stop=(c == 7)),
                     reads=[Wzg_b, xb_b[g]], writes=[pp_b])
            z, z_b = zt.next()
            S.op("dve", lambda E, z=z, pp=pp, t=t: E.tensor_scalar(z, pp[:, :], rtok[:, t:t + 1], None, ALU.mult),
                 reads=[pp_b, rtok_b], writes=[z_b])
            gelu(z, z_b, gl[0], gl[1], ga, gb)
            S.op("dve", lambda E: E.bn_stats(st6[:, 0:6], gl[0]), reads=[gl[1]], writes=[st6_b])
            S.op("dve", lambda E: E.bn_aggr(mv[:, 0:2], st6[:, 0:6]), reads=[st6_b], writes=[mv_b])
            S.op("act", lambda E: E.activation(mv[:, 2:3], mv[:, 1:2], AF.Ln, bias=gcol("eps")), reads=[mv_b, gcols_b], writes=[mv_b])
            S.op("act", lambda E: E.activation(mv[:, 3:4], mv[:, 2:3], AF.Exp, scale=-0.5), reads=[mv_b], writes=[mv_b])
            S.op("dve", lambda E: E.tensor_scalar(gl[0], gl[0], mv[:, 0:1], mv[:, 3:4], ALU.subtract, ALU.mult),
                 reads=[gl[1], mv_b], writes=[gl[1]])
            S.op("dve", lambda E: E.tensor_tensor(gl[0], gl[0], bc[:, 0:512], ALU.mult), reads=[gl[1], bc_b], writes=[gl[1]])
            gt, gt_b = gtok.next()
            S.op("dve", lambda E, gt=gt: E.tensor_tensor(gt, gl[0], bc[:, 512:1024], ALU.add), reads=[gl[1], bc_b], writes=[gt_b])
            pm, pm_b = ps_m.next()
            for grp in range(8):
                hb = 64 * (grp % 2)
                cc = grp // 2
                S.op("pe", lambda E, pm=pm, gt=gt, grp=grp, hb=hb, cc=cc: E.matmul(pm[hb:hb + 64, cc * 128:(cc + 1) * 128],
                                                                                  gt[:, grp * 64:(grp + 1) * 64], WsT[:, grp, :],
                                                                                  start=True, stop=True),
                     reads=[gt_b, WsT_b], writes=[pm_b])
            z2, z2_b = zt.next()
            S.op("dve", lambda E, z2=z2, pm=pm: E.tensor_tensor(z2, pm[:, :], bc[:, 1024:1536], ALU.add), reads=[pm_b, bc_b], writes=[z2_b])
            og, og_b = osg.next()
            S.op("pool", lambda E, og=og, z2=z2, t=t: E.tensor_tensor(og, z2.rearrange("p (c n) -> p c n", c=4),
                                                                     uT[:, :, t * 128:(t + 1) * 128], ALU.mult),
                 reads=[z2_b] + [uT_b[c][g] for c in range(4)], writes=[og_b])
            S.dma("sp", osg_d.rearrange("(c p) n -> p c n", p=128)[:, :, t * 128:(t + 1) * 128], og, og_b, reads=[og_b], pwrites=[db["osg"]])
        S.barrier()
        if exch == "cc":
            allgather(k0_d, db["k0"], kg0_d, db["kg0"], 512, RC_K0)
            allgather(v0_d, db["v0"], vg0_d, db["vg0"], T, RC_V0)
        S.barrier()

    def key_loc(j):
        tj, mj = j // 4, j % 4
        rk = mj if tj % 2 == 0 else 3 - mj
        return rk, tj, mj

    def phase_B0():
        cf.reset()
        ch.reset()
        Rs = [(cf.take(512), B()) for _ in range(2)]
        ttr = rotf(4, 512)
        e2r = rotf(4, 512)
        wsetup()
        Kr = roth(2, 8192)
        Vc, Vc_b = ch.take(8192).rearrange("p (r t d) -> p r t d", r=4, t=16), B()
        qr = roth(2, 2048)
        Lpr = roth(4, 512)
        wvr = roth(6, 512)
        osb = ch.take(8192).rearrange("p (c n) -> p c n", c=4)
        osb_b = [[B() for g in range(4)] for c in range(4)]
        msk, msk_b = ch.take(1024).rearrange("p (v n) -> p v n", v=8), B()
        wfm = roth(2, 1024, lambda a: a.rearrange("p (c n) -> p c n", c=8))
        ps_e, ps_t, ps_o = psrot([0, 1, 2, 3]), psrot([4, 5]), psrot([6, 7])
        S.dma("sp", msk, masks_d[:, 0:1024].rearrange("p (v n) -> p v n", v=8), msk_b, writes=[msk_b])
        for c in range(4):
            Kc, Kc_b = Kr.next()
            for rk in range(4):
                S.dma("sp", Kc[:, rk * T:(rk + 1) * T], kg0_d[grow(RC_K0, rk, c * 128):grow(RC_K0, rk, c * 128) + 128, :], Kc_b,
                      reads=[db["kg0"]], writes=[Kc_b])
                for tq in range(4):
                    S.dma("sp", Vc[:, rk, tq * 4:(tq + 1) * 4, :],
                          vg0_d[grow(RC_V0, rk, tq * 512):grow(RC_V0, rk, tq * 512) + 512, c * 128:(c + 1) * 128].rearrange("(t p) d -> p t d", p=128), Vc_b,
                          reads=[db["vg0"]], writes=[Vc_b])
            qc, qc_b = qr.next()
            S.dma("sp", qc, q0_d[c * 128:(c + 1) * 128, :], qc_b, reads=[db["q0"]], writes=[qc_b])
            for u in range(4):
                js = list(range(16 * u + 15, -1, -1))
                pos_ = [ps_o.next() for _ in range(2)]
                st2q, st3q = [[], []], [[], []]
                for hh in range(2):
                    S.op("pool", lambda E: E.memset(Rs[hh][0], 0.0), writes=[Rs[hh][1]])
                for idx, j in enumerate(js):
                    rk, tj, mj = key_loc(j)
                    kcol = rk * T + tj * 128
                    diag = j >= 16 * u
                    col0, vidx = 0, 0
                    if diag:
                        m = (j - 16 * u) // 4
                        col0 = 128 * m
                        vidx = ((4 * u + m) % 2) * 4 + mj
                    for hh in range(2):
                        hb = 64 * hh
                        R, R_b = Rs[hh]
                        po, po_b = pos_[hh]
                        qsl = qc[hb:hb + 64, u * 512 + col0:(u + 1) * 512]
                        ksl = Kc[hb:hb + 64, kcol:kcol + 128]
                        pE, pE_b = ps_e.next()
                        tt, tt_b = ttr.next()
                        Lp, Lp_b = Lpr.next()
                        S.op("pe", lambda E: E.matmul(pE[:, col0:], ksl, qsl, start=True, stop=False), reads=[Kc_b, qc_b], writes=[pE_b])
                        if diag:
                            S.op("pe", lambda E: E.matmul(pE[:, col0:col0 + 128], ident_bf, msk[:, vidx, :], start=False, stop=False, skip_group_check=True),
                                 reads=[cbf_b, msk_b], writes=[pE_b])
                        S.op("act", lambda E: E.activation(tt[:, col0:], pE[:, col0:], AF.Exp), reads=[pE_b], writes=[tt_b])
                        S.op("act", lambda E: E.activation(Lp[:, col0:], tt[:, col0:], AF.Ln, bias=gcol("one")), reads=[tt_b, gcols_b], writes=[Lp_b])

                        def stage2(pE=pE, pE_b=pE_b, Lp=Lp, Lp_b=Lp_b, col0=col0, rk=rk, tj=tj, idx=idx, j=j, hh=hh, hb=hb, R=R, R_b=R_b, po=po, po_b=po_b):
                            pT, pT_b = ps_t.next()
                            e2, e2_b = e2r.next()
                            wv, wv_b = wvr.next()
                            S.op("pe", lambda E: E.matmul(pE[:, col0:], negU_bf, Lp[:, col0:], start=False, stop=True, skip_group_check=True),
                                 reads=[cbf_b, Lp_b], writes=[pE_b])
                            S.op("pe", lambda E: E.matmul(pT[:, col0:], ones_bf, Lp[:, col0:], start=True, stop=True), reads=[cbf_b, Lp_b], writes=[pT_b])
                            S.op("dve", lambda E: E.tensor_tensor(e2[:, col0:], pE[:, col0:], R[:, col0:], ALU.subtract), reads=[pE_b, R_b], writes=[e2_b])
                            S.op("act", lambda E: E.activation(wv[:, col0:], e2[:, col0:], AF.Exp), reads=[e2_b], writes=[wv_b])
                            S.op("dve", lambda E: E.tensor_tensor(R[:, col0:], pT[:, col0:], R[:, col0:], ALU.add), reads=[pT_b, R_b], writes=[R_b])

                            def stage3():
                                S.op("pe", lambda E: E.matmul(po[hb:hb + 64, col0:], Vc[:, rk, tj, hb:hb + 64], wv[:, col0:], start=(idx == 0), stop=(j == 0),
                                                              skip_group_check=True),
                                     reads=[Vc_b, wv_b], writes=[po_b])
                            st3q[hh].append(stage3)
                        st2q[hh].append(stage2)
                        if len(st2q[hh]) > 1:
                            st2q[hh].pop(0)()
                        if len(st3q[hh]) > 1:
                            st3q[hh].pop(0)()
                for hh in range(2):
                    while st2q[hh]:
                        st2q[hh].pop(0)()
                for hh in range(2):
                    while st3q[hh]:
                        st3q[hh].pop(0)()
                for hh in range(2):
                    hb = 64 * hh
                    po, po_b = pos_[hh]
                    S.op("dve", lambda E: E.tensor_copy(osb[hb:hb + 64, c, u * 512:(u + 1) * 512], po[hb:hb + 64, :]),
                         reads=[po_b], writes=[osb_b[c][u]])
        Kc, Kc_b = Kr.next()
        osgs = Kc.rearrange("p (c n) -> p c n", c=4)
        S.dma("sp", osgs, osg_d.rearrange("(c p) n -> p c n", p=128), Kc_b, reads=[db["osg"]], writes=[Kc_b])
        mix = [osb[:, c, :] for c in range(4)] + [osgs[:, c, :] for c in range(4)]
        mix_b = [osb_b[c] for c in range(4)] + [Kc_b] * 4
        out_proj(mix, mix_b, Wd["sbg_out"], wfm, psrot([0, 1, 2]))
        S.barrier()

    def phase_X(l):
        cf.reset()
        ch.reset()
        rstd1, rstd1_b = cf.take(512), B()
        tmp, tmp_b = cf.take(512), B()
        wsetup()
        mst = rotf(1, 1024)
        msmall, msmall_b = cf.take(8), B()
        qraw = rotf(1, 1024, lambda a: a.rearrange("p (c n) -> p c n", c=2))
        rq = rotf(2, 512)
        kraw = rotf(1, 512, lambda a: a.rearrange("p (c n) -> p c n", c=2))
        rk_ = rotf(2, 256)
        rcp = rotf(1, 512)
        xb = ch.take(4096).rearrange("p (c n) -> p c n", c=8)
        xb1_b = B()
        sq, sq_b = ch.take(4096).rearrange("p (c n) -> p c n", c=8), B()
        hmT, hmT_b = ch.take(2048).rearrange("p (c n) -> p c n", c=8), B()
        mrow = roth(1, 1024)
        kT, kT_b = ch.take(2048).rearrange("p (h c n) -> p h c n", h=4, c=2), B()
        Vm, Vm_b = ch.take(2048).rearrange("p (m n) -> p m n", m=2), B()
        wfm = roth(2, 1024, lambda a: a.rearrange("p (c n) -> p c n", c=8))
        Wv_ = roth(1, 4096, lambda a: a.rearrange("p (c n) -> p c n", c=8))
        qn = roth(2, 1024, lambda a: a.rearrange("p (c n) -> p c n", c=2))
        sqq = roth(2, 1024, lambda a: a.rearrange("p (c n) -> p c n", c=2))
        Pm = roth(2, 1024, lambda a: a.rearrange("p (m n) -> p m n", m=2))
        oT = ch.take(4096).rearrange("p (c n) -> p c n", c=8)
        oT_b = [B() for c in range(8)]
        ps_ss_b = B("ps")
        ps_p = psrot([1, 2])
        ps_q = psrot([3])
        ps_s2 = psrot([4, 5])
        ps_o2 = psrot([6, 7])
        for mt in range(2):
            ms, ms_b = mst.next()
            S.dma("sp", ms, mem_d[mt * 128:(mt + 1) * 128, :], ms_b, writes=[ms_b])
            mr, mr_b = mrow.next()
            S.op("act", lambda E, ms=ms, mr=mr: E.activation(mr, ms, AF.Square, accum_out=msmall[:, 0:1]), reads=[ms_b], writes=[mr_b, msmall_b])
            S.op("act", lambda E: E.activation(msmall[:, 1:2], msmall[:, 0:1], AF.Ln, bias=gcol("eps"), scale=1.0 / D),
                 reads=[msmall_b, gcols_b], writes=[msmall_b])
            S.op("act", lambda E: E.activation(msmall[:, 2:3], msmall[:, 1:2], AF.Exp, scale=-0.5), reads=[msmall_b], writes=[msmall_b])
            S.op("dve", lambda E, ms=ms: E.tensor_scalar(ms, ms, msmall[:, 2:3], None, ALU.mult), reads=[ms_b, msmall_b], writes=[ms_b])
            for half in range(2):
                pp, pp_b = ps_p.next()
                for cc in range(4):
                    c = half * 4 + cc
                    S.op("pe", lambda E, pp=pp, ms=ms, c=c, cc=cc: E.transpose(pp[:, cc * 128:(cc + 1) * 128], ms[:, c * 128:(c + 1) * 128], ident_f[:, :]),
                         reads=[ms_b, ident_f_b], writes=[pp_b])
                for cc in range(4):
                    c = half * 4 + cc
                    S.op("dve", lambda E, pp=pp, c=c, cc=cc, mt=mt: E.tensor_scalar(hmT[:, c, mt * 128:(mt + 1) * 128], pp[:, cc * 128:(cc + 1) * 128],
                                                                                     gcol("xmn", 8 * l + c), None, ALU.mult),
                         reads=[pp_b, gcols_b], writes=[hmT_b])
        wk = Wd[("xk", l)].rearrange("(c p) n -> p c n", p=128)
        for hd in range(4):
            kr_, kr_b = kraw.next()
            sk, sk_b = sqq.next()
            for dc in range(2):
                oc = hd * 2 + dc
                wt, wt_b = wfm.next()
                wload(wk[:, :, oc * 128:(oc + 1) * 128], 8, 128, wt, wt_b)
                pp, pp_b = ps_p.next()
                for c in range(8):
                    S.op("pe", lambda E, c=c, pp=pp, wt=wt: E.matmul(pp[:, 0:256], wt[:, c, :], hmT[:, c, :], start=(c == 0), stop=(c == 7)),
                         reads=[wt_b, hmT_b], writes=[pp_b])
                S.op("act", lambda E, kr_=kr_, pp=pp, dc=dc: E.copy(kr_[:, dc, :], pp[:, 0:256]), reads=[pp_b], writes=[kr_b])
                S.op("act", lambda E, sk=sk, pp=pp, dc=dc: E.activation(sk[:, dc, 0:256], pp[:, 0:256], AF.Square), reads=[pp_b], writes=[sk_b])
            pq, pq_b = ps_q.next()
            for dc in range(2):
                S.op("pe", lambda E, pq=pq, sk=sk, dc=dc: E.matmul(pq[:, 0:256], ones_bf, sk[:, dc, 0:256], start=(dc == 0), stop=(dc == 1)),
                     reads=[sk_b, cbf_b], writes=[pq_b])
            rr, rr_b = rk_.next()
            rstd_from_ss(rr, rr_b, pq[:, 0:256], pq_b, 256, tmp[:, 0:256], tmp_b)
            for dc in range(2):
                S.op("dve", lambda E, kr_=kr_, rr=rr, dc=dc, hd=hd: E.scalar_tensor_tensor(kT[:, hd, dc, :], kr_[:, dc, :], gcol("xkg", 2 * l + dc), rr,
                                                                                           ALU.mult, ALU.mult),
                     reads=[kr_b, rr_b, gcols_b], writes=[kT_b])
        wvv = Wd[("xv", l)].rearrange("(c p) n -> p c n", p=128)
        for nh in range(2):
            wt, wt_b = Wv_.next()
            for hh in range(2):
                wload(wvv[:, hh * 4:(hh + 1) * 4, nh * 512:(nh + 1) * 512], 4, 512, wt[:, hh * 4:(hh + 1) * 4, :], wt_b)
            for mt in range(2):
                pp, pp_b = ps_p.next()
                for c in range(8):
                    S.op("pe", lambda E, c=c, pp=pp, wt=wt, mt=mt: E.matmul(pp[:, :], hmT[:, c, mt * 128:(mt + 1) * 128], wt[:, c, :],
                                                                             start=(c == 0), stop=(c == 7)),
                         reads=[wt_b, hmT_b], writes=[pp_b])
                S.op("act", lambda E, pp=pp, mt=mt, nh=nh: E.copy(Vm[:, mt, nh * 512:(nh + 1) * 512], pp[:, :]), reads=[pp_b], writes=[Vm_b])
        wq = Wd[("xq", l)].rearrange("(c p) n -> p c n", p=128)
        wo = Wd[("xo", l)].rearrange("(c p) n -> p c n", p=128)
        ps_x = psrot([1, 2])
        for g in range(4):
            norm_prep("xn", 8 * l, g, xb, xb1_b, rstd1, rstd1_b, sq, sq_b, 0, ps_ss_b, tmp, tmp_b)
            for hd in range(4):
                qr_, qr_b = qraw.next()
                sk, sk_b = sqq.next()
                for dc in range(2):
                    oc = hd * 2 + dc
                    wt, wt_b = wfm.next()
                    wload(wq[:, :, oc * 128:(oc + 1) * 128], 8, 128, wt, wt_b)
                    pp, pp_b = ps_p.next()
                    for c in range(8):
                        S.op("pe", lambda E, c=c, pp=pp, wt=wt: E.matmul(pp[:, :], wt[:, c, :], xb[:, c, :], start=(c == 0), stop=(c == 7)),
                             reads=[wt_b, xb1_b], writes=[pp_b])
                    S.op("dve", lambda E, qr_=qr_, pp=pp, dc=dc: E.tensor_tensor(qr_[:, dc, :], pp[:, :], rstd1, ALU.mult),
                         reads=[pp_b, rstd1_b], writes=[qr_b])
                    S.op("act", lambda E, sk=sk, qr_=qr_, dc=dc: E.activation(sk[:, dc, :], qr_[:, dc, :], AF.Square), reads=[qr_b], writes=[sk_b])
                pq, pq_b = ps_q.next()
                for dc in range(2):
                    S.op("pe", lambda E, pq=pq, sk=sk, dc=dc: E.matmul(pq[:, :], ones_bf, sk[:, dc, :], start=(dc == 0), stop=(dc == 1)),
                         reads=[sk_b, cbf_b], writes=[pq_b])
                rr, rr_b = rq.next()
                rstd_from_ss(rr, rr_b, pq[:, :], pq_b, 256, tmp, tmp_b)
                qq, qq_b = qn.next()
                for dc in range(2):
                    S.op("dve", lambda E, qq=qq, qr_=qr_, rr=rr, dc=dc: E.scalar_tensor_tensor(qq[:, dc, :], qr_[:, dc, :], gcol("xqg", 2 * l + dc), rr,
                                                                                              ALU.mult, ALU.mult),
                         reads=[qr_b, rr_b, gcols_b], writes=[qq_b])
                pm_, pm_b = Pm.next()
                for mt in range(2):
                    pS, pS_b = ps_s2.next()
                    for dc in range(2):
                        S.op("pe", lambda E, pS=pS, qq=qq, dc=dc, mt=mt, hd=hd: E.matmul(pS[:, :], kT[:, hd, dc, mt * 128:(mt + 1) * 128], qq[:, dc, :],
                                                                                         start=(dc == 0), stop=(dc == 1)),
                             reads=[kT_b, qq_b], writes=[pS_b])
                    S.op("act", lambda E, pm_=pm_, pS=pS, mt=mt: E.activation(pm_[:, mt, :], pS[:, :], AF.Exp, scale=1.0 / 16.0),
                         reads=[pS_b], writes=[pm_b])
                pq, pq_b = ps_q.next()
                for mt in range(2):
                    S.op("pe", lambda E, pq=pq, pm_=pm_, mt=mt: E.matmul(pq[:, :], ones_bf, pm_[:, mt, :], start=(mt == 0), stop=(mt == 1)),
                         reads=[pm_b, cbf_b], writes=[pq_b])
                rc, rc_b = rcp.next()
                S.op("dve", lambda E, rc=rc, pq=pq: E.reciprocal(rc, pq[:, :]), reads=[pq_b], writes=[rc_b])
                for dc in range(2):
                    po, po_b = ps_o2.next()
                    for mt in range(2):
                        S.op("pe", lambda E, po=po, pm_=pm_, mt=mt, dc=dc, hd=hd: E.matmul(po[:, :], Vm[:, mt, hd * 256 + dc * 128:hd * 256 + (dc + 1) * 128],
                                                                                           pm_[:, mt, :], start=(mt == 0), stop=(mt == 1)),
                             reads=[Vm_b, pm_b], writes=[po_b])
                    S.op("dve", lambda E, po=po, rc=rc, hd=hd, dc=dc: E.tensor_tensor(oT[:, hd * 2 + dc, :], po[:, :], rc, ALU.mult),
                         reads=[po_b, rc_b], writes=[oT_b[hd * 2 + dc]])
            for i in range(8):
                wt, wt_b = wfm.next()
                wload(wo[:, :, i * 128:(i + 1) * 128], 8, 128, wt, wt_b)
                po, po_b = ps_x.next()
                for c in range(8):
                    S.op("pe", lambda E, c=c, po=po, wt=wt: E.matmul(po[:, :], wt[:, c, :], oT[:, c, :], start=(c == 0), stop=(c == 7)),
                         reads=[wt_b, oT_b[c]], writes=[po_b])
                xs = xT[:, i, g * 512:(g + 1) * 512]
                S.op("dve", lambda E, xs=xs, po=po: E.tensor_tensor(xs, po[:, :], xs, ALU.add), reads=[po_b, xTb[i][g]], writes=[xTb[i][g]])
        S.barrier()

    def sin_reduced(out_ap, out_b, ang_ap, ang_b, wk, wk_b, nf, nf_b, p0, p1, add=0.0):
        ni = nint[p0:p1, :]
        S.op("dve", lambda E: E.tensor_scalar(wk, ang_ap, float(add), None, ALU.add), reads=[ang_b], writes=[wk_b])
        S.op("dve", lambda E: E.tensor_scalar(ni, wk, float(1 / TWO_PI), None, ALU.mult), reads=[wk_b], writes=[nint_b])
        S.op("dve", lambda E: E.tensor_copy(nf, ni), reads=[nint_b], writes=[nf_b])
        S.op("dve", lambda E: E.scalar_tensor_tensor(wk, nf, -TWO_PI, wk, ALU.mult, ALU.add), reads=[nf_b, wk_b], writes=[wk_b])
        S.op("dve", lambda E: E.tensor_scalar(nf, wk, PI, TWO_PI, ALU.is_gt, ALU.mult), reads=[wk_b], writes=[nf_b])
        S.op("dve", lambda E: E.tensor_tensor(wk, wk, nf, ALU.subtract), reads=[nf_b, wk_b], writes=[wk_b])
        S.op("dve", lambda E: E.tensor_scalar(nf, wk, -PI, TWO_PI, ALU.is_lt, ALU.mult), reads=[wk_b], writes=[nf_b])
        S.op("dve", lambda E: E.tensor_tensor(wk, wk, nf, ALU.add), reads=[nf_b, wk_b], writes=[wk_b])
        S.op("act", lambda E: E.activation(out_ap, wk, AF.Sin), reads=[wk_b], writes=[out_b])

    class _Stop(Exception):
        pass

    def _stage(n):
        for k in range(1, 9):
            if ('a1s%d' % k) in DBG and n >= k:
                raise _Stop()

    def phase_A1():
        try:
            phase_A1_()
        except _Stop:
            S.barrier()

    def phase_A1_():
        cf.reset()
        ch.reset()
        rstd1, rstd1_b = cf.take(512), B()
        tmpr = rotf(2, 512)
        tmp, tmp_b = tmpr.slots[0]
        wsetup(2, 1024)
        tabs = [cf.take(512) for _ in range(4)]
        tab_b = [B() for _ in range(4)]
        cqraw, cqraw_b = cf.take(2048).rearrange("p (c n) -> p c n", c=4), B()
        far = rotf(2, 512)
        fbr = rotf(2, 512)
        fa, fa_b = far.slots[0]
        fb, fb_b = fbr.slots[0]
        fc, fc_b = cf.take(512), B()
        kra, kra_b = cf.take(512), B()
        rqk = rotf(2, 512)
        xb = ch.take(4096).rearrange("p (c n) -> p c n", c=8)
        xb_b = B()
        sq, sq_b = ch.take(4096).rearrange("p (c n) -> p c n", c=8), B()
        Win, Win_b = ch.take(6400).rearrange("p (c n) -> p c n", c=8), B()
        Wuq, Wuq_b = ch.take(6144).rearrange("p (c n) -> p c n", c=4), B()
        Wqs, Wqs_b = ch.take(2048).rearrange("p (c n) -> p c n", c=4), B()
        Wuk, Wuk_b = ch.take(2048).rearrange("p (c n) -> p c n", c=2), B()
        Wuv, Wuv_b = ch.take(2048).rearrange("p (c n) -> p c n", c=2), B()
        cqn, cqn_b = ch.take(2048).rearrange("p (c n) -> p c n", c=4), B()
        ckvn, ckvn_b = ch.take(1024).rearrange("p (c n) -> p c n", c=2), B()
        krb, krb_b = ch.take(512), B()
        sqh = ch.take(512)
        sqh_lo, sqh_hi = B(), B()
        sqqr = roth(2, 512)
        qst = roth(2, 512)
        vst = roth(1, 512)
        ps_ss_b = B("ps")
        ps_p = psrot([1, 2])
        ps_a = psrot([3, 4])
        ps_b2 = psrot([5])
        ps_r = psrot([6, 7])
        ps_q = ps_r

        def wl(src, kc, ncol, dst, dst_b):
            v = src.rearrange("(c p) n -> p c n", p=128)
            step = max(128, (1024 // kc) // 128 * 128) if ncol >= 128 else ncol
            for n0 in range(0, ncol, step):
                n1 = min(ncol, n0 + step)
                wload(v[:, :, n0:n1], kc, n1 - n0, dst[:, :, n0:n1], dst_b)

        wl(Wd["mla_in"], 8, 768, Win, Win_b)
        wload(Wd["mla_in"].rearrange("(c p) n -> p c n", p=128)[:, :, 768:800], 8, 32, Win[:, :, 768:800], Win_b)
        wl(Wd["mla_uq"], 4, 1536, Wuq, Wuq_b)
        wl(Wd["mla_uq_sw"], 4, 512, Wqs, Wqs_b)
        wl(Wd["mla_uk"], 2, 1024, Wuk, Wuk_b)
        wl(Wd["mla_uv"], 2, 1024, Wuv, Wuv_b)

        P0, P1 = 64, 96
        for pass_ in ("kv", "q"):
            for g in range(1 if 'a1small' in DBG else 4):
                gs = slice(g * 512, (g + 1) * 512)
                norm_prep("mix", 8, g, xb, xb_b, rstd1, rstd1_b, sq, sq_b, 0, ps_ss_b, tmp, tmp_b)
                S.dma("sp", posi[P0:P1, :], posrep_d[:, gs], posi_b, writes=[posi_b])
                S.op("dve", lambda E: E.tensor_copy(fa[P0:P1, :], posi[P0:P1, :]), reads=[posi_b], writes=[fa_b])
                S.op("dve", lambda E: E.tensor_scalar(fa[P0:P1, :], fa[P0:P1, :], gcol("invf", 0, P0, P1), None, ALU.mult),
                     reads=[fa_b, gcols_b], writes=[fa_b])
                sin_reduced(tabs[0][P0:P1, :], tab_b[0], fa[P0:P1, :], fa_b, fb[P0:P1, :], fb_b, fc[P0:P1, :], fc_b, P0, P1, add=PI / 2)
                sin_reduced(tabs[1][P0:P1, :], tab_b[1], fa[P0:P1, :], fa_b, fb[P0:P1, :], fb_b, fc[P0:P1, :], fc_b, P0, P1, add=0.0)
                S.op("dve", lambda E: E.tensor_scalar(tabs[2][P0:P1, :], tabs[0][P0:P1, :], gcol("mkg", 0, P0, P1), None, ALU.mult),
                     reads=[tab_b[0], gcols_b], writes=[tab_b[2]])
                S.op("dve", lambda E: E.tensor_scalar(tabs[3][P0:P1, :], tabs[1][P0:P1, :], gcol("sgn", 0, P0, P1), gcol("mkg_sw", 0, P0, P1), ALU.mult, ALU.mult),
                     reads=[tab_b[1], gcols_b], writes=[tab_b[3]])
                S.op("dve", lambda E: E.tensor_scalar(tabs[0][P0:P1, :], tabs[0][P0:P1, :], gcol("mqg", 0, P0, P1), None, ALU.mult),
                     reads=[tab_b[0], gcols_b], writes=[tab_b[0]])
                S.op("dve", lambda E: E.tensor_scalar(tabs[1][P0:P1, :], tabs[1][P0:P1, :], gcol("sgn", 0, P0, P1), gcol("mqg_sw", 0, P0, P1), ALU.mult, ALU.mult),
                     reads=[tab_b[1], gcols_b], writes=[tab_b[1]])

                for (n_ch, col0, dst, dst_b, gname, nfeat) in (((4, 0, cqn, cqn_b, "qlora", 512),) if pass_ == "q" else ((2, 512, ckvn, ckvn_b, "kvlora", 256),)):
                    for oc in range(n_ch):
                        pp, pp_b = ps_p.next()
                        for c in range(8):
                            S.op("pe", lambda E, c=c, pp=pp, oc=oc, col0=col0: E.matmul(pp[:, :], Win[:, c, col0 + oc * 128:col0 + (oc + 1) * 128], xb[:, c, :],
                                                                                        start=(c == 0), stop=(c == 7)),
                                 reads=[Win_b, xb_b], writes=[pp_b])
                        S.op("dve", lambda E, pp=pp, oc=oc: E.tensor_tensor(cqraw[:, oc, :], pp[:, :], rstd1, ALU.mult),
                             reads=[pp_b, rstd1_b], writes=[cqraw_b])
                        S.op("act", lambda E, oc=oc: E.activation(sq[:, oc, :], cqraw[:, oc, :], AF.Square), reads=[cqraw_b], writes=[sq_b])
                    pq, pq_b = ps_q.next()
                    for oc in range(n_ch):
                        S.op("pe", lambda E, pq=pq, oc=oc, n_ch=n_ch: E.matmul(pq[:, :], ones_bf, sq[:, oc, :], start=(oc == 0), stop=(oc == n_ch - 1)),
                             reads=[sq_b, cbf_b], writes=[pq_b])
                    rr, rr_b = rqk.next()
                    rstd_from_ss(rr, rr_b, pq[:, :], pq_b, nfeat, tmp, tmp_b)
                    for oc in range(n_ch):
                        S.op("dve", lambda E, oc=oc, dst=dst, rr=rr, gname=gname: E.scalar_tensor_tensor(dst[:, oc, :], cqraw[:, oc, :], gcol(gname, oc), rr, ALU.mult, ALU.mult),
                             reads=[cqraw_b, rr_b, gcols_b], writes=[dst_b])

                if pass_ == "kv":
                    pp, pp_b = ps_p.next()
                    for c in range(8):
                        S.op("pe", lambda E, c=c, pp=pp: E.matmul(pp[0:32, :], Win[:, c, 768:800], xb[:, c, :], start=(c == 0), stop=(c == 7)),
                             reads=[Win_b, xb_b], writes=[pp_b])
                    S.op("dve", lambda E, pp=pp: E.tensor_tensor(krb[0:32, :], pp[0:32, :], rstd1[0:32, :], ALU.mult), reads=[pp_b, rstd1_b], writes=[krb_b])
                    if os.environ.get("KSUB") == "1":
                        raise _Stop()
                    pa, pa_b = ps_a.next()
                    pb_, pb_b = ps_b2.next()
                    S.op("pe", lambda E, pa=pa: E.matmul(pa[P0:P1, :], ident_bf[0:32, 0:32], krb[0:32, :], start=True, stop=True), reads=[krb_b, cbf_b], writes=[pa_b])
                    S.op("pe", lambda E, pb_=pb_: E.matmul(pb_[P0:P1, :], swap_bf[0:32, 0:32], krb[0:32, :], start=True, stop=True), reads=[krb_b, cbf_b], writes=[pb_b])
                    if os.environ.get("KSUB") == "2":
                        raise _Stop()
                    S.op("act", lambda E, pa=pa: E.activation(sqh[P0:P1, :], pa[P0:P1, :], AF.Square), reads=[pa_b], writes=[sqh_hi])
                    if os.environ.get("KSUB") == "3":
                        raise _Stop()
                    S.op("act", lambda E, pa=pa: E.copy(kra[P0:P1, :], pa[P0:P1, :]), reads=[pa_b], writes=[kra_b])
                    S.op("act", lambda E, pb_=pb_: E.copy(fb[P0:P1, :], pb_[P0:P1, :]), reads=[pb_b], writes=[fb_b])
                    S.op("dve", lambda E: E.tensor_tensor(kra[P0:P1, :], kra[P0:P1, :], tabs[2][P0:P1, :], ALU.mult), reads=[kra_b, tab_b[2]], writes=[kra_b])
                    S.op("dve", lambda E: E.tensor_tensor(fb[P0:P1, :], fb[P0:P1, :], tabs[3][P0:P1, :], ALU.mult), reads=[fb_b, tab_b[3]], writes=[fb_b])
                    S.op("dve", lambda E: E.tensor_tensor(kra[P0:P1, :], kra[P0:P1, :], fb[P0:P1, :], ALU.add), reads=[kra_b, fb_b], writes=[kra_b])

                for h in range(2 if 'a1small' in DBG else 16):
                    if pass_ == "q":
                        pa, pa_b = ps_a.next()
                        pb_, pb_b = ps_b2.next()
                        for c in range(4):
                            S.op("pe", lambda E, c=c, pa=pa, h=h: E.matmul(pa[0:96, :], Wuq[:, c, h * 96:(h + 1) * 96], cqn[:, c, :], start=(c == 0), stop=(c == 3)),
                                 reads=[Wuq_b, cqn_b], writes=[pa_b])
                        for c in range(4):
                            S.op("pe", lambda E, c=c, pb_=pb_, h=h: E.matmul(pb_[P0:P1, :], Wqs[:, c, h * 32:(h + 1) * 32], cqn[:, c, :], start=(c == 0), stop=(c == 3)),
                                 reads=[Wqs_b, cqn_b], writes=[pb_b])
                        sqq, sqq_b = sqqr.next()
                        tq_, tq_b = tmpr.next()
                        fa, fa_b = far.next()
                        fb, fb_b = fbr.next()
                        S.op("act", lambda E, pa=pa: E.activation(sqq[0:96, :], pa[0:96, :], AF.Square), reads=[pa_b], writes=[sqq_b])
                        pr, pr_b = ps_r.next()
                        S.op("pe", lambda E, pr=pr: E.matmul(pr[0:96, :], ones_bf[0:96, 0:96], sqq[0:96, :], start=True, stop=True), reads=[sqq_b, cbf_b], writes=[pr_b])
                        rr, rr_b = rqk.next()
                        rstd_from_ss(rr[0:96, :], rr_b, pr[0:96, :], pr_b, 96, tq_[0:96, :], tq_b, 0, 96)
                        qs, qs_b = qst.next()
                        S.op("dve", lambda E, qs=qs, pa=pa, rr=rr: E.scalar_tensor_tensor(qs[0:64, :], pa[0:64, :], gcol("mqg", 0, 0, 64), rr[0:64, :], ALU.mult, ALU.mult),
                             reads=[pa_b, rr_b, gcols_b], writes=[qs_b])
                        S.op("act", lambda E, pa=pa: E.copy(fa[P0:P1, :], pa[P0:P1, :]), reads=[pa_b], writes=[fa_b])
                        S.op("act", lambda E, pb_=pb_: E.copy(fb[P0:P1, :], pb_[P0:P1, :]), reads=[pb_b], writes=[fb_b])
                        S.op("dve", lambda E: E.tensor_tensor(fa[P0:P1, :], fa[P0:P1, :], tabs[0][P0:P1, :], ALU.mult), reads=[fa_b, tab_b[0]], writes=[fa_b])
                        S.op("dve", lambda E: E.tensor_tensor(fb[P0:P1, :], fb[P0:P1, :], tabs[1][P0:P1, :], ALU.mult), reads=[fb_b, tab_b[1]], writes=[fb_b])
                        S.op("dve", lambda E: E.tensor_tensor(fa[P0:P1, :], fa[P0:P1, :], fb[P0:P1, :], ALU.add), reads=[fa_b, fb_b], writes=[fa_b])
                        S.op("dve", lambda E, qs=qs, rr=rr: E.tensor_tensor(qs[P0:P1, :], fa[P0:P1, :], rr[P0:P1, :], ALU.mult), reads=[fa_b, rr_b], writes=[qs_b])
                        S.dma("sp", q1_d[h * 96:(h + 1) * 96, gs], qs[0:96, :], qs_b, reads=[qs_b], pwrites=[db["q1"]])

                    if pass_ == "kv":
                        pa, pa_b = ps_a.next()
                        for c in range(2):
                            S.op("pe", lambda E, c=c, pa=pa, h=h: E.matmul(pa[0:64, :], Wuk[:, c, h * 64:(h + 1) * 64], ckvn[:, c, :], start=(c == 0), stop=(c == 1)),
                                 reads=[Wuk_b, ckvn_b], writes=[pa_b])
                        S.op("act", lambda E, pa=pa: E.activation(sqh[0:64, :], pa[0:64, :], AF.Square), reads=[pa_b], writes=[sqh_lo])
                        pr, pr_b = ps_r.next()
                        S.op("pe", lambda E, pr=pr: E.matmul(pr[0:96, :], ones_bf[0:96, 0:96], sqh[0:96, :], start=True, stop=True), reads=[sqh_lo, sqh_hi, cbf_b], writes=[pr_b])
                        rr, rr_b = rqk.next()
                        tq_, tq_b = tmpr.next()
                        rstd_from_ss(rr[0:96, :], rr_b, pr[0:96, :], pr_b, 96, tq_[0:96, :], tq_b, 0, 96)
                        ks, ks_b = qst.next()
                        S.op("dve", lambda E, ks=ks, pa=pa, rr=rr: E.scalar_tensor_tensor(ks[0:64, :], pa[0:64, :], gcol("mkg", 0, 0, 64), rr[0:64, :], ALU.mult, ALU.mult),
                             reads=[pa_b, rr_b, gcols_b], writes=[ks_b])
                        S.op("dve", lambda E, ks=ks, rr=rr: E.tensor_tensor(ks[P0:P1, :], kra[P0:P1, :], rr[P0:P1, :], ALU.mult), reads=[kra_b, rr_b], writes=[ks_b])
                        S.dma("sp", k1_d[h * 96:(h + 1) * 96, gs], ks[0:96, :], ks_b, reads=[ks_b], pwrites=[db["k1"]])

                if pass_ == "kv":
                    for tt in range(4):
                        t = g * 4 + tt
                        for half in range(2):
                            pp, pp_b = ps_p.next()
                            for c in range(2):
                                S.op("pe", lambda E, c=c, pp=pp, tt=tt, half=half: E.matmul(pp[:, :], ckvn[:, c, tt * 128:(tt + 1) * 128], Wuv[:, c, half * 512:(half + 1) * 512],
                                                                                             start=(c == 0), stop=(c == 1)),
                                     reads=[Wuv_b, ckvn_b], writes=[pp_b])
                            vs, vs_b = vst.next()
                            S.op("act", lambda E, vs=vs, pp=pp: E.copy(vs, pp[:, :]), reads=[pp_b], writes=[vs_b])
                            S.dma("sp", v1_d[t * 128:(t + 1) * 128, half * 512:(half + 1) * 512], vs, vs_b, reads=[vs_b], pwrites=[db["v1"]])
            if pass_ == "kv" and exch == "cc":
                allgather(k1_d, db["k1"], kg1_d, db["kg1"], 1536, RC_K1)
                allgather(v1_d, db["v1"], vg1_d, db["vg1"], T, RC_V1)
        S.barrier()

    o1_d = nc.dram_tensor("o1_d", [D, T], BF16).ap()
    o1_b = Buf("o1")

    def phase_B1():
        cf.reset()
        ch.reset()
        wsetup()
        tmpA, tmpA_b = cf.take(512), B()
        rcf, rcf_b = cf.take(512), B()
        rcs, rcs_b = cf.take(512), B()
        KK = ch.take(16384)
        Kr = Rot([(KK[:, 0:8192], B()), (KK[:, 8192:16384], B())])
        Vr = roth(2, 8192, lambda a: a.rearrange("p (r t d) -> p r t d", r=4, t=16))
        qr = roth(2, 2048)
        Pr = roth(3, 512)
        msk, msk_b = ch.take(1024).rearrange("p (v n) -> p v n", v=8), B()
        rhi, rhi_b = ch.take(512), B()
        rlo, rlo_b = ch.take(512), B()
        ost = roth(2, 512)
        wfm = roth(2, 1024, lambda a: a.rearrange("p (c n) -> p c n", c=8))
        ps_s = psrot([0, 1, 2])
        ps_o = psrot([3, 4])
        ps_r = psrot([5])
        S.dma("sp", msk, masks_d[:, 1024:2048].rearrange("p (v n) -> p v n", v=8), msk_b, writes=[msk_b])
        for (Vt, Vt_b) in Vr.slots:
            for rk in range(4):
                S.op("pool", lambda E: E.memset(Vt[:, rk, :, 64:128], 1.0), writes=[Vt_b])
        SC = float(96 ** -0.5)
        dsl = slice(64, 128)
        osl = slice(0, 64)
        def issue_loads(h):
            Kh, Kh_b = Kr.next()
            Vh, Vh_b = Vr.next()
            qh, qh_b = qr.next()
            S.dma("sp", qh[0:96, :], q1_d[h * 96:(h + 1) * 96, :], qh_b, reads=[db["q1"]], writes=[qh_b])
            for rk in range(4):
                S.dma("sp", Kh[0:96, rk * T:(rk + 1) * T], kg1_d[grow(RC_K1, rk, h * 96):grow(RC_K1, rk, h * 96) + 96, :], Kh_b,
                      reads=[db["kg1"]], writes=[Kh_b])
                for tq in range(4):
                    r0 = grow(RC_V1, rk, tq * 512)
                    S.dma("sp", Vh[:, rk, tq * 4:(tq + 1) * 4, 0:64],
                          vg1_d[r0:r0 + 512, h * 64:(h + 1) * 64].rearrange("(t p) d -> p t d", p=128), Vh_b, reads=[db["vg1"]], writes=[Vh_b])
            return Kh, Kh_b, Vh, Vh_b, qh, qh_b

        nxt = issue_loads(0)
        for h in range(16):
            Kh, Kh_b, Vh, Vh_b, qh, qh_b = nxt
            if h + 1 < 16:
                nxt = issue_loads(h + 1)
            for u in range(4):
                po, po_b = ps_o.next()
                nblk = 16 * u + 16
                pend = []
                for j in range(nblk):
                    rk, tj, mj = key_loc(j)
                    kcol = rk * T + tj * 128
                    diag = j >= 16 * u
                    col0, vidx = 0, 0
                    if diag:
                        m = (j - 16 * u) // 4
                        col0 = 128 * m
                        vidx = ((4 * u + m) % 2) * 4 + mj
                    pS, pS_b = ps_s.next()
                    S.op("pe", lambda E: E.matmul(pS[:, col0:], Kh[0:96, kcol:kcol + 128], qh[0:96, u * 512 + col0:(u + 1) * 512], start=True, stop=not diag),
                         reads=[Kh_b, qh_b], writes=[pS_b])
                    if diag:
                        S.op("pe", lambda E: E.matmul(pS[:, col0:col0 + 128], ident_bf, msk[:, vidx, :], start=False, stop=True, skip_group_check=True),
                             reads=[cbf_b, msk_b], writes=[pS_b])
                    P_, P_b = Pr.next()
                    S.op("act", lambda E: E.activation(P_[:, col0:], pS[:, col0:], AF.Exp, scale=SC), reads=[pS_b], writes=[P_b])

                    def pv(po=po, po_b=po_b, P_=P_, P_b=P_b, col0=col0, rk=rk, tj=tj, j=j, nblk=nblk, Vh=Vh, Vh_b=Vh_b):
                        S.op("pe", lambda E: E.matmul(po[:, col0:], Vh[:, rk, tj, :], P_[:, col0:], start=(j == 0), stop=(j == nblk - 1),
                                                      skip_group_check=True),
                             reads=[Vh_b, P_b], writes=[po_b])
                    if pend:
                        pend.pop()()
                    pend.append(pv)
                pend.pop()()
                S.op("act", lambda E: E.activation(tmpA[dsl, :], po[dsl, :], AF.Ln), reads=[po_b], writes=[tmpA_b])
                S.op("act", lambda E: E.activation(rcf[dsl, :], tmpA[dsl, :], AF.Exp, scale=-1.0), reads=[tmpA_b], writes=[rcf_b])
                S.op("dve", lambda E: E.tensor_copy(rhi[dsl, :], rcf[dsl, :]), reads=[rcf_b], writes=[rhi_b])
                S.op("dve", lambda E: E.tensor_tensor(rlo[dsl, :], rcf[dsl, :], rhi[dsl, :], ALU.subtract), reads=[rcf_b, rhi_b], writes=[rlo_b])
                pr, pr_b = ps_r.next()
                S.op("pe", lambda E: E.matmul(pr[osl, :], ident_bf[dsl, 64:128], rhi[dsl, :], start=True, stop=False), reads=[cbf_b, rhi_b], writes=[pr_b])
                S.op("pe", lambda E: E.matmul(pr[osl, :], ident_bf[dsl, 64:128], rlo[dsl, :], start=False, stop=True), reads=[cbf_b, rlo_b], writes=[pr_b])
                S.op("act", lambda E: E.copy(rcs[osl, :], pr[osl, :]), reads=[pr_b], writes=[rcs_b])
                os_, os_b = ost.next()
                S.op("dve", lambda E: E.tensor_tensor(os_[osl, :], po[osl, :], rcs[osl, :], ALU.mult), reads=[po_b, rcs_b], writes=[os_b])
                S.dma("pool", o1_d[h * 64:(h + 1) * 64, u * 512:(u + 1) * 512], os_[osl, :], os_b, reads=[os_b], pwrites=[o1_b])
        S.barrier()
        oT = KK.rearrange("p (c n) -> p c n", c=8)
        oT_b = B()
        S.dma("sp", oT, o1_d.rearrange("(c p) n -> p c n", p=128), oT_b, reads=[o1_b], writes=[oT_b])
        out_proj([oT[:, c, :] for c in range(8)], [oT_b] * 8, Wd["mla_out"], wfm, psrot([0, 1, 2]))
        S.barrier()

    load_x()
    seq = [lambda: ffn(Wd[("gu", "pre", 0)], Wd[("d", "pre", 0)], "ffn_pre", 0),
           phase_A0, phase_B0, lambda: phase_X(0),
           lambda: ffn(Wd[("gu", "post", 0)], Wd[("d", "post", 0)], "ffn_post", 0),
           lambda: ffn(Wd[("gu", "pre", 1)], Wd[("d", "pre", 1)], "ffn_pre", 16),
           phase_A1, phase_B1, lambda: phase_X(1),
           lambda: ffn(Wd[("gu", "post", 1)], Wd[("d", "post", 1)], "ffn_post", 16)]
    for i in range(lo, hi):
        seq[i]()
    for n in db:
        if db[n].lw is not None:
            S._wait("sp", db[n].lw)
        for k_, v_ in db[n].mw.items():
            S._wait("sp", (k_, v_))
    store_x()
    S.run()
    print('[build] phases', lo, hi, 'counts', dict(S.cnt), 'nsems', len(S.sems), 'maxdma', max([0] + [v for k, v in S.alltoks.items() if str(k).startswith('d')]), flush=True)
    return nc, used_inputs, outputs


def zig(t, r):
    return 4 * t + (r if t % 2 == 0 else 3 - r)


def shard_tokens(x):
    outs = []
    for c in range(NCORES):
        b, r = c // 4, c % 4
        blocks = [zig(t, r) for t in range(NT)]
        xb = x[b].reshape(64, 128, -1)
        outs.append(np.ascontiguousarray(xb[blocks].reshape(T, -1)))
    return outs


def unshard_tokens(outs, dtype=np.float32):
    full = np.zeros((2, 64, 128, D), dtype)
    for c in range(NCORES):
        b, r = c // 4, c % 4
        blocks = [zig(t, r) for t in range(NT)]
        full[b, blocks] = outs[c].reshape(NT, 128, D)
    return full.reshape(2, 8192, D)


def col(v):
    v = np.asarray(v, np.float32)
    return np.ascontiguousarray(v.reshape(-1, 128).T)


def make_masks(r):
    k = np.arange(128)[:, None]
    q = np.arange(128)[None, :]
    m = np.zeros((128, 16, 128), np.float32)
    for kind in range(2):
        tri = (k < q) if kind == 0 else (k <= q)
        for p in range(2):
            rr = r if p == 0 else 3 - r
            for mj in range(4):
                if mj < rr:
                    v = np.zeros((128, 128), np.float32)
                elif mj == rr:
                    v = np.where(tri, 0.0, NEG).astype(np.float32)
                else:
                    v = np.full((128, 128), NEG, np.float32)
                m[:, kind * 8 + p * 4 + mj, :] = v
    return m.reshape(128, 16 * 128).astype(ml_dtypes.bfloat16)


_PROG_CACHE = {}


def prepare_inputs(inp):
    f32 = np.float32
    ident = np.eye(128, dtype=f32)
    ones = np.ones((128, 128), f32)
    negU = -np.tril(np.ones((128, 128), f32))
    sw = np.zeros((128, 128), f32)
    for i in range(16):
        sw[16 + i, i] = 1.0
        sw[i, 16 + i] = 1.0
    cb = np.concatenate([ident, ones, negU, sw], axis=1).astype(ml_dtypes.bfloat16)
    gc = np.zeros((128, NGC), f32)
    for l in range(2):
        gc[:, 16 * l:16 * l + 8] = col(inp["ffn_pre_norm"][l])
        gc[:, 16 * l + 8:16 * l + 16] = col(inp["ffn_post_norm"][l])
        gc[:, 32 + 8 * l:40 + 8 * l] = col(inp["mix_norm"][l])
        gc[:, 48 + 8 * l:56 + 8 * l] = col(inp["xmem_norm"][l])
        gc[:, 64 + 8 * l:72 + 8 * l] = col(inp["xmem_mem_norm"][l])
        gc[:, 80 + 2 * l:82 + 2 * l] = col(inp["xmem_q_gain"][l])
        gc[:, 84 + 2 * l:86 + 2 * l] = col(inp["xmem_k_gain"][l])
    gc[:, 88:92] = col(inp["mla_q_lora_gain"][0])
    gc[:, 92:94] = col(inp["mla_kv_lora_gain"][0])
    perm = np.arange(96)
    perm[64:80] = np.arange(80, 96)
    perm[80:96] = np.arange(64, 80)
    gc[:96, 94] = inp["mla_q_gain"][0]
    gc[:96, 95] = inp["mla_k_gain"][0]
    gc[:96, 96] = inp["mla_q_gain"][0][perm]
    gc[:96, 97] = inp["mla_k_gain"][0][perm]
    half = 16
    invf = (f32(10000.0) ** (-np.arange(half, dtype=f32) / f32(half))).astype(f32)
    gc[64:80, 98] = invf
    gc[80:96, 98] = invf
    gc[64:80, 99] = -1.0
    gc[80:96, 99] = 1.0
    gc[:, 100] = EPS
    gc[:, 101] = np.pi / 2
    gc[:, 102] = 1.0
    bc0 = np.zeros((128, 1536), f32)
    bc0[:, 0:512] = inp["sgu_ln_gain"][0][None, :]
    bc0[:, 512:1024] = inp["sgu_ln_bias"][0][None, :]
    sb = inp["sgu_b"][0]
    for cc in range(4):
        bc0[0:64, 1024 + cc * 128:1024 + (cc + 1) * 128] = sb[2 * cc][None, :]
        bc0[64:128, 1024 + cc * 128:1024 + (cc + 1) * 128] = sb[2 * cc + 1][None, :]
    trim = (np.arange(128)[:, None] <= np.arange(128)[None, :]).astype(f32)
    sguwT = np.ascontiguousarray(inp["sgu_w"][0].transpose(2, 0, 1)).reshape(128, 8 * 128)
    common = {"consts_bf": cb, "ident_f": ident, "gcols": gc, "bc0": bc0, "trimask": trim, "sgu_wT": sguwT}
    for l in range(2):
        for which in ("pre", "post"):
            common["w_%s_gu_%d" % (which, l)] = np.ascontiguousarray(inp["ffn_%s_w_gu" % which][l])
            common["w_%s_d_%d" % (which, l)] = np.ascontiguousarray(inp["ffn_%s_w_down" % which][l])
        common["w_xq_%d" % l] = np.ascontiguousarray(inp["xmem_wq"][l])
        wkv = inp["xmem_wkv"][l].reshape(D, 4, 2, 256)
        common["w_xk_%d" % l] = np.ascontiguousarray(wkv[:, :, 0, :].reshape(D, D))
        common["w_xv_%d" % l] = np.ascontiguousarray(wkv[:, :, 1, :].reshape(D, D))
        common["w_xo_%d" % l] = np.ascontiguousarray(inp["xmem_wo"][l])
    common["w_sbg_in"] = np.ascontiguousarray(inp["sbg_w_in"][0])
    common["w_sbg_out"] = np.ascontiguousarray(inp["sbg_w_out"][0])
    common["w_mla_in"] = np.ascontiguousarray(inp["mla_w_in"][0])
    uq = inp["mla_w_uq"][0]
    common["w_mla_uq"] = np.ascontiguousarray(uq)
    uq_h = uq.reshape(512, 16, 96)
    common["w_mla_uq_sw"] = np.ascontiguousarray(np.concatenate([uq_h[:, :, 80:96], uq_h[:, :, 64:80]], axis=2).reshape(512, 512))
    ukv = inp["mla_w_ukv"][0].reshape(256, 16, 128)
    common["w_mla_uk"] = np.ascontiguousarray(ukv[:, :, 0:64].reshape(256, 1024))
    common["w_mla_uv"] = np.ascontiguousarray(ukv[:, :, 64:128].reshape(256, 1024))
    common["w_mla_out"] = np.ascontiguousarray(inp["mla_w_out"][0])
    xs = shard_tokens(inp["x"])
    pos = inp["positions"].astype(np.int32)
    in_maps = []
    for c in range(NCORES):
        b, r = c // 4, c % 4
        blocks = [zig(t, r) for t in range(NT)]
        pl = pos[b].reshape(64, 128)[blocks].reshape(T)
        m = dict(common)
        m["x_in"] = xs[c]
        m["masks"] = make_masks(r)
        m["posrep"] = np.ascontiguousarray(np.broadcast_to(pl[None, :], (32, T))).astype(np.int32)
        m["mem"] = np.ascontiguousarray(inp["mem"][b])
        in_maps.append(m)
    return in_maps


EXCH = "cc"
DEBUG_X = {}


def _run(key, in_maps):
    if key not in _PROG_CACHE:
        _PROG_CACHE[key] = build_program(*key)
    nc, used, outs = _PROG_CACHE[key]
    maps = [{k: v for k, v in m.items() if k in used} for m in in_maps]
    res = run_bass_kernel_spmd(nc, maps, core_ids=list(range(NCORES)))
    return res.results


def _gather(results, name, rc):
    out = []
    for c in range(NCORES):
        b = c // 4
        rows = results[c][name].shape[0]
        parts = []
        for i in range(rows // rc):
            for rk in range(4):
                parts.append(results[4 * b + rk][name][i * rc:(i + 1) * rc])
        out.append(np.ascontiguousarray(np.concatenate(parts, axis=0)))
    return out


def kernel(**inputs):
    inp = {k: np.asarray(v) for k, v in inputs.items()}
    in_maps = prepare_inputs(inp)
    if EXCH == "cc":
        res = _run((0, 10, "cc"), in_maps)
        return unshard_tokens([r["x_out"] for r in res])
    r1 = _run((0, 2, "host"), in_maps)
    kg0, vg0 = _gather(r1, "k0_d", RC_K0), _gather(r1, "v0_d", RC_V0)
    for c in range(NCORES):
        in_maps[c].update({"x_in": r1[c]["x_out"], "q0_d": r1[c]["q0_d"], "osg_d": r1[c]["osg_d"], "kg0_d": kg0[c], "vg0_d": vg0[c]})
    r2 = _run((2, 7, "host"), in_maps)
    DEBUG_X["p2"] = [r["x_out"] for r in r2]
    kg1, vg1 = _gather(r2, "k1_d", RC_K1), _gather(r2, "v1_d", RC_V1)
    for c in range(NCORES):
        in_maps[c].update({"x_in": r2[c]["x_out"], "q1_d": r2[c]["q1_d"], "kg1_d": kg1[c], "vg1_d": vg1[c]})
    r3 = _run((7, 10, "host"), in_maps)
    return unshard_tokens([r["x_out"] for r in r3])
```

```python
import types
import numpy as np
import ml_dtypes
import concourse.bass as bass
import concourse.mybir as mybir
from concourse.bass_utils import run_bass_kernel_spmd

F32 = mybir.dt.float32
BF16 = mybir.dt.bfloat16
I32 = mybir.dt.int32
AF = mybir.ActivationFunctionType
ALU = mybir.AluOpType

NCORES = 8
D = 1024
DFF = 2816
T = 2048
NT = 16
EPS = 1e-6
NEG = -30000.0


class Buf:
    __slots__ = ("name", "lw", "rd", "dkey", "dcnt", "mw")

    def __init__(self, name):
        self.name = name
        self.lw = None
        self.rd = {}
        self.dkey = None
        self.dcnt = 0
        self.mw = {}


ENGS = ("pe", "act", "dve", "pool", "sp")


class Sched:
    def __init__(self, nc):
        self.nc = nc
        self.prog = {e: [] for e in ENGS}
        self.cnt = {e: 0 for e in ENGS}
        self.waited = {e: {} for e in ENGS}
        self.sems = {}
        self.alltoks = {}
        self.dfree = []
        self.dcum = {}
        self.dheld = []

    def sem(self, key):
        if key not in self.sems:
            self.sems[key] = self.nc.alloc_semaphore(name="s_%s" % (str(key).replace(" ", "")))
        return self.sems[key]

    def _wait(self, eng, tok):
        key, val = tok
        if key == eng and eng in ("pe", "sp"):
            return
        if self.waited[eng].get(key, 0) >= val:
            return
        self.waited[eng][key] = val
        h = self.sem(key)
        self.prog[eng].append(lambda E, h=h, val=val: E.wait_ge(h, val))

    def _deps(self, eng, reads, writes, pwrites=()):
        for b in reads:
            if b.lw is not None:
                self._wait(eng, b.lw)
            for k, v in b.mw.items():
                self._wait(eng, (k, v))
        for b in pwrites:
            if b.lw is not None and b.lw[0] != eng:
                self._wait(eng, b.lw)
            for k, v in b.rd.items():
                if k != eng:
                    self._wait(eng, (k, v))
        for b in writes:
            if b.lw is not None and b.lw[0] != eng:
                self._wait(eng, b.lw)
            for k, v in b.mw.items():
                self._wait(eng, (k, v))
            for k, v in b.rd.items():
                if k != eng:
                    self._wait(eng, (k, v))

    def _upd(self, tok, reads, writes):
        self.alltoks[tok[0]] = max(self.alltoks.get(tok[0], 0), tok[1])
        for b in reads:
            if b.rd.get(tok[0], 0) < tok[1]:
                b.rd[tok[0]] = tok[1]
        for b in writes:
            b.lw = tok
            b.rd = {}
            b.mw = {}

    @staticmethod
    def _freeze(fn):
        if fn.__closure__:
            cells = tuple(types.CellType(c.cell_contents) for c in fn.__closure__)
            g = types.FunctionType(fn.__code__, fn.__globals__, fn.__name__, fn.__defaults__, cells)
            g.__kwdefaults__ = fn.__kwdefaults__
            return g
        return fn

    def op(self, eng, fn, reads=(), writes=()):
        fn = self._freeze(fn)
        self._deps(eng, reads, writes)
        self.cnt[eng] += 1
        tok = (eng, self.cnt[eng])
        h = self.sem(eng)
        self.prog[eng].append(lambda E, fn=fn, h=h: fn(E).then_inc(h, 1))
        self._upd(tok, reads, writes)
        return tok

    def dma(self, q, out_ap, in_ap, owner, reads=(), writes=(), pwrites=()):
        self._deps(q, reads, writes, pwrites)
        if owner.dkey is None:
            if self.dfree:
                owner.dkey = self.dfree.pop()
            else:
                owner.dkey = "q%d" % len(self.dcum)
                self.dcum[owner.dkey] = 0
            self.dheld.append(owner)
        self.dcum[owner.dkey] += 16
        tok = (owner.dkey, self.dcum[owner.dkey])
        h = self.sem(owner.dkey)
        self.prog[q].append(lambda E, o=out_ap, i=in_ap, h=h: E.dma_start(out=o, in_=i).then_inc(h, 16))
        self._upd(tok, reads, writes)
        for b in pwrites:
            if b.mw.get(tok[0], 0) < tok[1]:
                b.mw[tok[0]] = tok[1]
        return tok

    def barrier(self, release=True):
        for e in ENGS:
            for k, v in list(self.alltoks.items()):
                self._wait(e, (k, v))
        if release:
            for b in self.dheld:
                self.dfree.append(b.dkey)
                b.dkey = None
            self.dheld = []

    def run(self):
        nc = self.nc
        with nc.Block() as block:
            @block.tensor
            def _(E):
                for f in self.prog["pe"]:
                    f(E)

            @block.scalar
            def _(E):
                for f in self.prog["act"]:
                    f(E)

            @block.vector
            def _(E):
                for f in self.prog["dve"]:
                    f(E)

            @block.gpsimd
            def _(E):
                for f in self.prog["pool"]:
                    f(E)

            @block.sync
            def _(E):
                for f in self.prog["sp"]:
                    f(E)


class Rot:
    def __init__(self, slots):
        self.slots = slots
        self.i = 0

    def next(self):
        s = self.slots[self.i % len(self.slots)]
        self.i += 1
        return s


class Ctx:
    pass


def mk_tile(nc, name, shape, dt):
    return nc.alloc_sbuf_tensor(name, shape, dt)


import os
DBG = os.environ.get('KDBG', '')
RC_K0, RC_V0, RC_K1, RC_V1 = 256, 1024, 192, 512
GC = dict(ffn_pre=0, ffn_post=8, mix=32, xn=48, xmn=64, xqg=80, xkg=84, qlora=88, kvlora=92,
          mqg=94, mkg=95, mqg_sw=96, mkg_sw=97, invf=98, sgn=99, eps=100, halfpi=101, one=102, zero=103)
NGC = 104
TWO_PI = float(2 * np.pi)
PI = float(np.pi)


PHASES = ["F0pre", "A0", "B0", "X0", "F0post", "F1pre", "A1", "B1", "X1", "F1post"]


def build_program(lo=0, hi=10, exch="cc"):
    nc = bass.Bass("TRN2", target_bir_lowering=False)
    S = Sched(nc)
    outputs = ["x_out"]

    used_inputs = set()

    def din(name, shape, dt=F32):
        used_inputs.add(name)
        return nc.dram_tensor(name, list(shape), dt, kind="ExternalInput").ap()

    x_in = din("x_in", [T, D])
    x_out = nc.dram_tensor("x_out", [T, D], F32, kind="ExternalOutput").ap()
    consts_bf = din("consts_bf", [128, 4 * 128], BF16)
    ident_f_d = din("ident_f", [128, 128])
    gcols_d = din("gcols", [128, NGC])
    bc0_d = din("bc0", [128, 1536])
    trim_d = din("trimask", [128, 128])
    sguwT_d = din("sgu_wT", [128, 8 * 128])
    masks_d = din("masks", [128, 16 * 128], BF16)
    posrep_d = din("posrep", [32, T], I32)
    mem_d = din("mem", [256, D])
    WSHAPES = {"sbg_in": [D, 2560], "sbg_out": [D, D], "mla_in": [D, 800], "mla_uq": [512, 1536], "mla_uq_sw": [512, 512],
               "mla_uk": [256, 1024], "mla_uv": [256, 1024], "mla_out": [D, D]}

    class LazyW(dict):
        def __missing__(self, key):
            if isinstance(key, tuple) and key[0] in ("gu", "d"):
                name = "w_%s_%s_%d" % (key[1], key[0], key[2])
                shape = [D, 2 * DFF] if key[0] == "gu" else [DFF, D]
            elif isinstance(key, tuple):
                name = "w_%s_%d" % key
                shape = [D, D]
            else:
                name = "w_" + key
                shape = WSHAPES[key]
            self[key] = din(name, shape)
            return self[key]

    Wd = LazyW()
    def scratch(name, shape, prod, cons):
        if exch == "cc" or (lo <= prod < hi and lo <= cons < hi):
            return nc.dram_tensor(name, shape, BF16).ap()
        if lo <= prod < hi:
            outputs.append(name)
            return nc.dram_tensor(name, shape, BF16, kind="ExternalOutput").ap()
        if lo <= cons < hi:
            return din(name, shape, BF16)
        return None

    gp = 1 if exch == "cc" else -1
    q0_d = scratch("q0_d", [512, T], 1, 2)
    k0_d = scratch("k0_d", [512, T], 1, 2 if exch == "cc" else 99)
    v0_d = scratch("v0_d", [T, 512], 1, 2 if exch == "cc" else 99)
    osg_d = scratch("osg_d", [512, T], 1, 2)
    kg0_d = scratch("kg0_d", [4 * 512, T], gp, 2)
    vg0_d = scratch("vg0_d", [4 * T, 512], gp, 2)
    gp = 6 if exch == "cc" else -1
    q1_d = scratch("q1_d", [1536, T], 6, 7)
    k1_d = scratch("k1_d", [1536, T], 6, 7 if exch == "cc" else 99)
    v1_d = scratch("v1_d", [T, 1024], 6, 7 if exch == "cc" else 99)
    kg1_d = scratch("kg1_d", [4 * 1536, T], gp, 7)
    vg1_d = scratch("vg1_d", [4 * T, 1024], gp, 7)
    db = {n: Buf(n) for n in ("q0", "k0", "v0", "osg", "kg0", "vg0", "q1", "k1", "v1", "kg1", "vg1")}

    xT = nc.alloc_sbuf_tensor("xT", [128, 8, T], F32)
    xTb = [[Buf("xT_%d_%d" % (c, g)) for g in range(4)] for c in range(8)]
    cbf = nc.alloc_sbuf_tensor("cbf", [128, 4 * 128], BF16)
    cbf_b = Buf("cbf")
    ident_f = nc.alloc_sbuf_tensor("sb_ident_f", [128, 128], F32)
    ident_f_b = Buf("ident_f")
    gcols = nc.alloc_sbuf_tensor("sb_gcols", [128, NGC], F32)
    gcols_b = Buf("gcols")
    ident_bf = cbf[:, 0:128]
    ones_bf = cbf[:, 128:256]
    negU_bf = cbf[:, 256:384]
    swap_bf = cbf[:, 384:512]

    def gcol(name, i=0, p0=0, p1=128):
        return gcols[p0:p1, GC[name] + i:GC[name] + i + 1]

    S.dma("sp", cbf[:, :], consts_bf[:, :], cbf_b, writes=[cbf_b])
    S.dma("sp", ident_f[:, :], ident_f_d[:, :], ident_f_b, writes=[ident_f_b])
    S.dma("sp", gcols[:, :], gcols_d[:, :], gcols_b, writes=[gcols_b])

    posi = nc.alloc_sbuf_tensor("posi", [128, 512], I32)
    posi_b = Buf("posi")
    nint = nc.alloc_sbuf_tensor("nint", [128, 512], I32)
    nint_b = Buf("nint")
    AF32 = 11776
    ABF = 45056
    arena_f = nc.alloc_sbuf_tensor("arena_f", [128, AF32], F32)
    arena_h = nc.alloc_sbuf_tensor("arena_h", [128, ABF], BF16)
    psum = [nc.alloc_psum_tensor("ps%d" % i, [128, 512], F32) for i in range(8)]

    class Carver:
        def __init__(self, ar, size):
            self.ar, self.size, self.off = ar, size, 0

        def reset(self):
            self.off = 0

        def take(self, n):
            assert self.off + n <= self.size, ("arena overflow", self.off, n, self.size)
            a = self.ar[:, self.off:self.off + n]
            self.off += n
            return a

    cf = Carver(arena_f, AF32)
    ch = Carver(arena_h, ABF)
    nbuf = [0]

    def B(name="b"):
        nbuf[0] += 1
        return Buf("%s%d" % (name, nbuf[0]))

    def rotf(n, size, view=None):
        return Rot([((cf.take(size) if view is None else view(cf.take(size))), B()) for _ in range(n)])

    def roth(n, size, view=None):
        return Rot([((ch.take(size) if view is None else view(ch.take(size))), B()) for _ in range(n)])

    def psrot(idxs):
        return Rot([(psum[i], B("ps")) for i in idxs])

    W = Ctx()

    def wsetup(nst=2, stsize=2048):
        W.st = rotf(nst, stsize)

    castctr = [0]

    def cast(dst, dst_b, src, src_b):
        castctr[0] += 1
        if castctr[0] % 2 == 0:
            S.op("act", lambda E: E.copy(dst, src), reads=[src_b], writes=[dst_b])
        else:
            S.op("dve", lambda E: E.tensor_copy(dst, src), reads=[src_b], writes=[dst_b])

    def wload(view, kc, n, dst, dst_b, eng=None):
        assert kc * n <= 2048
        st, st_b = W.st.next()
        sv = st[:, 0:kc * n].rearrange("p (c n) -> p c n", c=kc)
        S.dma("sp", sv, view, st_b, writes=[st_b])
        cast(dst, dst_b, sv, st_b)

    def rstd_from_ss(out_ap, out_b, ps_ap, ps_b, n, tmp_ap, tmp_b, p0=0, p1=128):
        S.op("act", lambda E: E.activation(tmp_ap, ps_ap, AF.Ln, bias=gcol("eps", 0, p0, p1), scale=1.0 / n),
             reads=[ps_b, gcols_b], writes=[tmp_b])
        S.op("act", lambda E: E.activation(out_ap, tmp_ap, AF.Exp, scale=-0.5), reads=[tmp_b], writes=[out_b])

    def load_x():
        cf.reset()
        rot = rotf(2, 1024)
        psb = [B("ps"), B("ps")]
        for t in range(NT):
            ap, b = rot.next()
            S.dma("sp", ap, x_in[t * 128:(t + 1) * 128, :], b, writes=[b])
            for half in range(2):
                pb, ps = psb[half], psum[half]
                for cc in range(4):
                    c = half * 4 + cc
                    S.op("pe", lambda E, o=ps[:, cc * 128:(cc + 1) * 128], i=ap[:, c * 128:(c + 1) * 128]:
                         E.transpose(o, i, ident_f[:, :]), reads=[b, ident_f_b], writes=[pb])
                g = t // 4
                outap = xT[:, half * 4:half * 4 + 4, t * 128:(t + 1) * 128]
                inap = ps[:, :].rearrange("p (c n) -> p c n", c=4)
                wr = [xTb[half * 4 + cc][g] for cc in range(4)]
                if half == 0:
                    S.op("act", lambda E, o=outap, i=inap: E.copy(o, i), reads=[pb], writes=wr)
                else:
                    S.op("dve", lambda E, o=outap, i=inap: E.tensor_copy(o, i), reads=[pb], writes=wr)
        S.barrier()

    def store_x():
        cf.reset()
        rot = rotf(2, 1024)
        psb = [B("ps"), B("ps")]
        toks = []
        for t in range(NT):
            ap, b = rot.next()
            g = t // 4
            for half in range(2):
                pb, ps = psb[half], psum[half]
                for cc in range(4):
                    c = half * 4 + cc
                    S.op("pe", lambda E, o=ps[:, cc * 128:(cc + 1) * 128], i=xT[:, c, t * 128:(t + 1) * 128]:
                         E.transpose(o, i, ident_f[:, :]), reads=[xTb[c][g], ident_f_b], writes=[pb])
                if half == 0:
                    S.op("act", lambda E, o=ap[:, 0:512], i=ps[:, :]: E.copy(o, i), reads=[pb], writes=[b])
                else:
                    S.op("dve", lambda E, o=ap[:, 512:1024], i=ps[:, :]: E.tensor_copy(o, i), reads=[pb], writes=[b])
            toks.append(S.dma("sp", x_out[t * 128:(t + 1) * 128, :], ap, b, reads=[b]))
        for tk in toks:
            S._wait("sp", tk)
        S.barrier()

    def norm_prep(gname, gi0, grp, xb_ap, xb_b, rstd_ap, rstd_b, sq_ap, sq_b, ps_i, ps_b, tmp_ap, tmp_b, tokps=None):
        tsl = slice(grp * 512, (grp + 1) * 512)
        xbufs = [xTb[c][grp] for c in range(8)]
        S.op("act", lambda E: E.activation(sq_ap, xT[:, :, tsl], AF.Square), reads=xbufs, writes=[sq_b])
        ps = psum[ps_i]
        for c in range(8):
            S.op("pe", lambda E, c=c: E.matmul(ps[:, :], ones_bf, sq_ap[:, c, :], start=(c == 0), stop=(c == 7)),
                 reads=[sq_b, cbf_b], writes=[ps_b])
        rstd_from_ss(rstd_ap, rstd_b, ps[:, :], ps_b, D, tmp_ap, tmp_b)
        if tokps is not None:
            tp_ap, tp_b = tokps
            for tt in range(4):
                for c in range(8):
                    S.op("pe", lambda E, c=c, tt=tt: E.matmul(tp_ap[:, grp * 4 + tt:grp * 4 + tt + 1],
                                                              sq_ap[:, c, tt * 128:(tt + 1) * 128], ones_bf[:, 0:1],
                                                              start=(c == 0), stop=(c == 7)),
                         reads=[sq_b, cbf_b], writes=[tp_b])
        for c in range(8):
            S.op("dve", lambda E, c=c: E.tensor_scalar(xb_ap[:, c, :], xT[:, c, tsl], gcol(gname, gi0 + c), None, ALU.mult),
                 reads=[xTb[c][grp], gcols_b], writes=[xb_b])

    def ffn(wgu_d, wd_d, gname, gi0):
        cf.reset()
        ch.reset()
        rstd = [cf.take(512) for _ in range(2)]
        rstd_b = [B() for _ in range(2)]
        tmp, tmp_b = cf.take(512), B()
        stg_gu = rotf(2, 2048, lambda a: a.rearrange("p (a c m) -> p a c m", a=2, c=8))
        stg_d = rotf(2, 1408, lambda a: a.rearrange("p (k m) -> p k m", k=11))
        a_rot, s_rot, t_rot = rotf(2, 512), rotf(2, 512), rotf(2, 512)
        xb = [ch.take(4096).rearrange("p (c n) -> p c n", c=8) for _ in range(2)]
        xb_b = [B() for _ in range(2)]
        hid = ch.take(22 * 1024).rearrange("p (k n) -> p k n", k=22)
        hid_b = [[B() for g in range(2)] for k in range(22)]
        wgu = roth(2, 2048, lambda a: a.rearrange("p (a c m) -> p a c m", a=2, c=8))
        wd = roth(2, 2816, lambda a: a.rearrange("p (k m) -> p k m", k=22))
        sq, sq_b = ch.take(4096).rearrange("p (c n) -> p c n", c=8), B()
        ps_ss_b = B("ps")
        ps_g, ps_u, ps_a = psrot([1, 2]), psrot([3, 4]), psrot([5, 6])
        for half in range(2):
            for gi in range(2):
                norm_prep(gname, gi0, half * 2 + gi, xb[gi], xb_b[gi], rstd[gi], rstd_b[gi], sq, sq_b, 0, ps_ss_b, tmp, tmp_b)
            wgu_v = wgu_d.rearrange("(c p) n -> p c n", p=128)
            for j in range(22):
                st, st_b = stg_gu.next()
                S.dma("sp", st[:, 0], wgu_v[:, :, j * 128:(j + 1) * 128], st_b, writes=[st_b])
                S.dma("sp", st[:, 1], wgu_v[:, :, DFF + j * 128:DFF + (j + 1) * 128], st_b, writes=[st_b])
                wt, wt_b = wgu.next()
                cast(wt, wt_b, st, st_b)
                for gi in range(2):
                    pg, pg_b = ps_g.next()
                    pu, pu_b = ps_u.next()
                    for c in range(8):
                        S.op("pe", lambda E, c=c, pg=pg, wt=wt, gi=gi: E.matmul(pg[:, :], wt[:, 0, c, :], xb[gi][:, c, :],
                                                                                 start=(c == 0), stop=(c == 7)),
                             reads=[wt_b, xb_b[gi]], writes=[pg_b])
                    for c in range(8):
                        S.op("pe", lambda E, c=c, pu=pu, wt=wt, gi=gi: E.matmul(pu[:, :], wt[:, 1, c, :], xb[gi][:, c, :],
                                                                                 start=(c == 0), stop=(c == 7)),
                             reads=[wt_b, xb_b[gi]], writes=[pu_b])
                    a, a_b = a_rot.next()
                    s, s_b = s_rot.next()
                    t, t_b = t_rot.next()
                    S.op("dve", lambda E, a=a, pg=pg, gi=gi: E.tensor_tensor(a, pg[:, :], rstd[gi], ALU.mult),
                         reads=[pg_b, rstd_b[gi]], writes=[a_b])
                    S.op("act", lambda E, s=s, a=a: E.activation(s, a, AF.Silu), reads=[a_b], writes=[s_b])
                    S.op("dve", lambda E, t=t, pu=pu, gi=gi: E.tensor_tensor(t, pu[:, :], rstd[gi], ALU.mult),
                         reads=[pu_b, rstd_b[gi]], writes=[t_b])
                    S.op("pool", lambda E, s=s, t=t, j=j, gi=gi: E.tensor_tensor(hid[:, j, gi * 512:(gi + 1) * 512], s, t, ALU.mult),
                         reads=[s_b, t_b], writes=[hid_b[j][gi]])
            wd_v = wd_d.rearrange("(k p) n -> p k n", p=128)
            for i in range(8):
                wt, wt_b = wd.next()
                for hh in range(2):
                    st, st_b = stg_d.next()
                    S.dma("sp", st, wd_v[:, hh * 11:(hh + 1) * 11, i * 128:(i + 1) * 128], st_b, writes=[st_b])
                    cast(wt[:, hh * 11:(hh + 1) * 11, :], wt_b, st, st_b)
                for gi in range(2):
                    pa, pa_b = ps_a.next()
                    for k in range(22):
                        S.op("pe", lambda E, k=k, pa=pa, wt=wt, gi=gi: E.matmul(pa[:, :], wt[:, k, :], hid[:, k, gi * 512:(gi + 1) * 512],
                                                                                 start=(k == 0), stop=(k == 21)),
                             reads=[wt_b, hid_b[k][gi]], writes=[pa_b])
                    grp = half * 2 + gi
                    xs = xT[:, i, grp * 512:(grp + 1) * 512]
                    S.op("dve", lambda E, xs=xs, pa=pa: E.scalar_tensor_tensor(xs, pa[:, :], 0.5, xs, ALU.mult, ALU.add),
                         reads=[pa_b, xTb[i][grp]], writes=[xTb[i][grp]])
        S.barrier()

    def out_proj(mix, mix_b, w_dram, wfm, ps_o):
        wv = w_dram.rearrange("(c p) n -> p c n", p=128)
        for i in range(8):
            wt, wt_b = wfm.next()
            wload(wv[:, :, i * 128:(i + 1) * 128], 8, 128, wt, wt_b)
            for g in range(4):
                po, po_b = ps_o.next()
                for c in range(8):
                    mb = mix_b[c][g] if isinstance(mix_b[c], list) else mix_b[c]
                    S.op("pe", lambda E, c=c, po=po, wt=wt, g=g: E.matmul(po[:, :], wt[:, c, :], mix[c][:, g * 512:(g + 1) * 512],
                                                                           start=(c == 0), stop=(c == 7)),
                         reads=[wt_b, mb], writes=[po_b])
                xs = xT[:, i, g * 512:(g + 1) * 512]
                S.op("dve", lambda E, xs=xs, po=po: E.tensor_tensor(xs, po[:, :], xs, ALU.add),
                     reads=[po_b, xTb[i][g]], writes=[xTb[i][g]])

    def gelu(x_ap, x_b, out_ap, out_b, ga, gb):
        (a, a_b), (b, b_b) = ga, gb
        S.op("act", lambda E: E.activation(a, x_ap, AF.Square), reads=[x_b], writes=[a_b])
        S.op("dve", lambda E: E.tensor_scalar(a, a, 0.044715, 1.0, ALU.mult, ALU.add), reads=[a_b], writes=[a_b])
        S.op("dve", lambda E: E.tensor_tensor(a, a, x_ap, ALU.mult), reads=[a_b, x_b], writes=[a_b])
        S.op("act", lambda E: E.activation(b, a, AF.Sigmoid, scale=1.5957691216057308), reads=[a_b], writes=[b_b])
        S.op("pool", lambda E: E.tensor_tensor(out_ap, x_ap, b, ALU.mult), reads=[x_b, b_b], writes=[out_b])

    def allgather(src_ap, src_b, dst_ap, dst_b, rows, rc):
        S._deps("pool", [src_b], [dst_b])
        h = S.sem("cc")
        S.cnt.setdefault("cc", 0)
        for i in range(rows // rc):
            S.cnt["cc"] += 1
            S.prog["pool"].append(lambda E, i=i: E.collective_compute("AllGather", ALU.bypass, replica_groups=[[0, 1, 2, 3], [4, 5, 6, 7]],
                                                                      ins=[src_ap[i * rc:(i + 1) * rc, :]],
                                                                      outs=[dst_ap[i * 4 * rc:(i + 1) * 4 * rc, :]]).then_inc(h, 1))
        S._upd(("cc", S.cnt["cc"]), [src_b], [dst_b])

    def grow(rc, rk, row):
        return (row // rc) * 4 * rc + rk * rc + (row % rc)

    def phase_A0():
        cf.reset()
        ch.reset()
        rstd = [cf.take(512) for _ in range(4)]
        rstd_b = [B() for _ in range(4)]
        tmp, tmp_b = cf.take(512), B()
        rtok, rtok_b = cf.take(16), B()
        rtmp, rtmp_b = cf.take(16), B()
        wsetup()
        zt = rotf(2, 512)
        ga, gb = (cf.take(512), B()), (cf.take(512), B())
        gl = (cf.take(512), B())
        bc, bc_b = cf.take(1536), B()
        trim, trim_b = cf.take(128), B()
        st6, st6_b = cf.take(8), B()
        mv, mv_b = cf.take(4), B()
        xb = ch.take(16384).rearrange("p (c n) -> p c n", c=8)
        xb_b = [B() for _ in range(4)]
        sq, sq_b = ch.take(4096).rearrange("p (c n) -> p c n", c=8), B()
        Wv, Wv_b = ch.take(4096).rearrange("p (c n) -> p c n", c=8), B()
        Wzg, Wzg_b = ch.take(4096).rearrange("p (c n) -> p c n", c=8), B()
        wfm = roth(2, 1024, lambda a: a.rearrange("p (c n) -> p c n", c=8))
        uT = ch.take(8192).rearrange("p (c n) -> p c n", c=4)
        uT_b = [[B() for g in range(4)] for c in range(4)]
        WsT, WsT_b = ch.take(1024).rearrange("p (g n) -> p g n", g=8), B()
        ev = roth(2, 512)
        vev = roth(2, 512)
        gtok = roth(2, 512)
        osg = roth(2, 512, lambda a: a.rearrange("p (c n) -> p c n", c=4))
        ps_ss_b = B("ps")
        tokps = (psum[7][:, 0:16], B("ps"))
        ps_p = psrot([1, 2, 3])
        ps_m = psrot([4, 5])
        S.dma("sp", bc, bc0_d[:, :], bc_b, writes=[bc_b])
        S.dma("sp", trim, trim_d[:, :], trim_b, writes=[trim_b])
        for hh in range(4):
            st, st_b = W.st.next()
            S.dma("sp", st[:, 0:256], sguwT_d[:, hh * 256:(hh + 1) * 256], st_b, writes=[st_b])
            for k in range(2):
                gidx = hh * 2 + k
                S.op("dve", lambda E, st=st, k=k, gidx=gidx: E.tensor_tensor(WsT[:, gidx, :], st[:, k * 128:(k + 1) * 128], trim, ALU.mult),
                     reads=[st_b, trim_b], writes=[WsT_b])
        for g in range(4):
            norm_prep("mix", 0, g, xb[:, :, g * 512:(g + 1) * 512], xb_b[g], rstd[g], rstd_b[g], sq, sq_b, 0, ps_ss_b, tmp, tmp_b, tokps=tokps)
        rstd_from_ss(rtok, rtok_b, tokps[0], tokps[1], D, rtmp, rtmp_b)
        win = Wd["sbg_in"].rearrange("(c p) n -> p c n", p=128)
        for hh in range(2):
            wload(win[:, hh * 4:(hh + 1) * 4, 1024:1536], 4, 512, Wv[:, hh * 4:(hh + 1) * 4, :], Wv_b)
            wload(win[:, hh * 4:(hh + 1) * 4, 2048:2560], 4, 512, Wzg[:, hh * 4:(hh + 1) * 4, :], Wzg_b)
        for oc in list(range(0, 8)) + list(range(12, 16)):
            wt, wt_b = wfm.next()
            wload(win[:, :, oc * 128:(oc + 1) * 128], 8, 128, wt, wt_b)
            for g in range(4):
                pp, pp_b = ps_p.next()
                for c in range(8):
                    S.op("pe", lambda E, c=c, pp=pp, wt=wt, g=g: E.matmul(pp[:, :], wt[:, c, :], xb[:, c, g * 512:(g + 1) * 512],
                                                                           start=(c == 0), stop=(c == 7)),
                         reads=[wt_b, xb_b[g]], writes=[pp_b])
                if oc < 8:
                    e, e_b = ev.next()
                    sc = 0.125 if oc < 4 else 1.0
                    S.op("dve", lambda E, e=e, pp=pp, g=g, sc=sc: E.scalar_tensor_tensor(e, pp[:, :], sc, rstd[g], ALU.mult, ALU.mult),
                         reads=[pp_b, rstd_b[g]], writes=[e_b])
                    dst, dn = (q0_d, "q0") if oc < 4 else (k0_d, "k0")
                    r0 = (oc % 4) * 128
                    S.dma("sp", dst[r0:r0 + 128, g * 512:(g + 1) * 512], e, e_b, reads=[e_b], pwrites=[db[dn]])
                else:
                    z, z_b = zt.next()
                    S.op("dve", lambda E, z=z, pp=pp, g=g: E.tensor_tensor(z, pp[:, :], rstd[g], ALU.mult),
                         reads=[pp_b, rstd_b[g]], writes=[z_b])
                    gelu(z, z_b, uT[:, oc - 12, g * 512:(g + 1) * 512], uT_b[oc - 12][g], ga, gb)
        for t in range(NT):
            g = t // 4
            pp, pp_b = ps_p.next()
            for c in range(8):
                S.op("pe", lambda E, c=c, pp=pp, t=t: E.matmul(pp[:, :], xb[:, c, t * 128:(t + 1) * 128], Wv[:, c, :],
                                                               start=(c == 0), stop=(c == 7)),
                     reads=[Wv_b, xb_b[g]], writes=[pp_b])
            e, e_b = vev.next()
            S.op("dve", lambda E, e=e, pp=pp, t=t: E.tensor_scalar(e, pp[:, :], rtok[:, t:t + 1], None, ALU.mult),
                 reads=[pp_b, rtok_b], writes=[e_b])
            S.dma("sp", v0_d[t * 128:(t + 1) * 128, :], e, e_b, reads=[e_b], pwrites=[db["v0"]])
            pp, pp_b = ps_p.next()
            for c in range(8):
                S.op("pe", lambda E, c=c, pp=pp, t=t: E.matmul(pp[:, :], xb[:, c, t * 128:(t + 1) * 128], Wzg[:, c, :],
                                                               start=(c == 0), stop=(c == 7)),
                     reads=[Wzg_b, xb_b[g]], writes=[pp_b])
            z, z_b = zt.next()
            S.op("dve", lambda E, z=z, pp=pp, t=t: E.tensor_scalar(z, pp[:, :], rtok[:, t:t + 1], None, ALU.mult),
                 reads=[pp_b, rtok_b], writes=[z_b])
            gelu(z, z_b, gl[0], gl[1], ga, gb)
            S.op("dve", lambda E: E.bn_stats(st6[:, 0:6], gl[0]), reads=[gl[1]], writes=[st6_b])
            S.op("dve", lambda E: E.bn_aggr(mv[:, 0:2], st6[:, 0:6]), reads=[st6_b], writes=[mv_b])
            S.op("act", lambda E: E.activation(mv[:, 2:3], mv[:, 1:2], AF.Ln, bias=gcol("eps")), reads=[mv_b, gcols_b], writes=[mv_b])
            S.op("act", lambda E: E.activation(mv[:, 3:4], mv[:, 2:3], AF.Exp, scale=-0.5), reads=[mv_b], writes=[mv_b])
            S.op("dve", lambda E: E.tensor_scalar(gl[0], gl[0], mv[:, 0:1], mv[:, 3:4], ALU.subtract, ALU.mult),
                 reads=[gl[1], mv_b], writes=[gl[1]])
            S.op("dve", lambda E: E.tensor_tensor(gl[0], gl[0], bc[:, 0:512], ALU.mult), reads=[gl[1], bc_b], writes=[gl[1]])
            gt, gt_b = gtok.next()
            S.op("dve", lambda E, gt=gt: E.tensor_tensor(gt, gl[0], bc[:, 512:1024], ALU.add), reads=[gl[1], bc_b], writes=[gt_b])
            pm, pm_b = ps_m.next()
            for grp in range(8):
                hb = 64 * (grp % 2)
                cc = grp // 2
                S.op("pe", lambda E, pm=pm, gt=gt, grp=grp, hb=hb, cc=cc: E.matmul(pm[hb:hb + 64, cc * 128:(cc + 1) * 128],
                                                                                  gt[:, grp * 64:(grp + 1) * 64], WsT[:, grp, :],
                                                                                  start=True, stop=True),
                     reads=[gt_b, WsT_b], writes=[pm_b])
            z2, z2_b = zt.next()
            S.op("dve", lambda E, z2=z2, pm=pm: E.tensor_tensor(z2, pm[:, :], bc[:, 1024:1536], ALU.add), reads=[pm_b, bc_b], writes=[z2_b])
            og, og_b = osg.next()
            S.op("pool", lambda E, og=og, z2=z2, t=t: E.tensor_tensor(og, z2.rearrange("p (c n) -> p c n", c=4),
                                                                     uT[:, :, t * 128:(t + 1) * 128], ALU.mult),
                 reads=[z2_b] + [uT_b[c][g] for c in range(4)], writes=[og_b])
            S.dma("sp", osg_d.rearrange("(c p) n -> p c n", p=128)[:, :, t * 128:(t + 1) * 128], og, og_b, reads=[og_b], pwrites=[db["osg"]])
        S.barrier()
        if exch == "cc":
            allgather(k0_d, db["k0"], kg0_d, db["kg0"], 512, RC_K0)
            allgather(v0_d, db["v0"], vg0_d, db["vg0"], T, RC_V0)
        S.barrier()

    def key_loc(j):
        tj, mj = j // 4, j % 4
        rk = mj if tj % 2 == 0 else 3 - mj
        return rk, tj, mj

    def phase_B0():
        cf.reset()
        ch.reset()
        Rs = [(cf.take(512), B()) for _ in range(2)]
        ttr = rotf(4, 512)
        e2r = rotf(4, 512)
        wsetup()
        Kr = roth(2, 8192)
        Vc, Vc_b = ch.take(8192).rearrange("p (r t d) -> p r t d", r=4, t=16), B()
        qr = roth(2, 2048)
        Lpr = roth(4, 512)
        wvr = roth(6, 512)
        osb = ch.take(8192).rearrange("p (c n) -> p c n", c=4)
        osb_b = [[B() for g in range(4)] for c in range(4)]
        msk, msk_b = ch.take(1024).rearrange("p (v n) -> p v n", v=8), B()
        wfm = roth(2, 1024, lambda a: a.rearrange("p (c n) -> p c n", c=8))
        ps_e, ps_t, ps_o = psrot([0, 1, 2, 3]), psrot([4, 5]), psrot([6, 7])
        S.dma("sp", msk, masks_d[:, 0:1024].rearrange("p (v n) -> p v n", v=8), msk_b, writes=[msk_b])
        for c in range(4):
            Kc, Kc_b = Kr.next()
            for rk in range(4):
                S.dma("sp", Kc[:, rk * T:(rk + 1) * T], kg0_d[grow(RC_K0, rk, c * 128):grow(RC_K0, rk, c * 128) + 128, :], Kc_b,
                      reads=[db["kg0"]], writes=[Kc_b])
                for tq in range(4):
                    S.dma("sp", Vc[:, rk, tq * 4:(tq + 1) * 4, :],
                          vg0_d[grow(RC_V0, rk, tq * 512):grow(RC_V0, rk, tq * 512) + 512, c * 128:(c + 1) * 128].rearrange("(t p) d -> p t d", p=128), Vc_b,
                          reads=[db["vg0"]], writes=[Vc_b])
            qc, qc_b = qr.next()
            S.dma("sp", qc, q0_d[c * 128:(c + 1) * 128, :], qc_b, reads=[db["q0"]], writes=[qc_b])
            for u in range(4):
                js = list(range(16 * u + 15, -1, -1))
                pos_ = [ps_o.next() for _ in range(2)]
                st2q, st3q = [[], []], [[], []]
                for hh in range(2):
                    S.op("pool", lambda E: E.memset(Rs[hh][0], 0.0), writes=[Rs[hh][1]])
                for idx, j in enumerate(js):
                    rk, tj, mj = key_loc(j)
                    kcol = rk * T + tj * 128
                    diag = j >= 16 * u
                    col0, vidx = 0, 0
                    if diag:
                        m = (j - 16 * u) // 4
                        col0 = 128 * m
                        vidx = ((4 * u + m) % 2) * 4 + mj
                    for hh in range(2):
                        hb = 64 * hh
                        R, R_b = Rs[hh]
                        po, po_b = pos_[hh]
                        qsl = qc[hb:hb + 64, u * 512 + col0:(u + 1) * 512]
                        ksl = Kc[hb:hb + 64, kcol:kcol + 128]
                        pE, pE_b = ps_e.next()
                        tt, tt_b = ttr.next()
                        Lp, Lp_b = Lpr.next()
                        S.op("pe", lambda E: E.matmul(pE[:, col0:], ksl, qsl, start=True, stop=False), reads=[Kc_b, qc_b], writes=[pE_b])
                        if diag:
                            S.op("pe", lambda E: E.matmul(pE[:, col0:col0 + 128], ident_bf, msk[:, vidx, :], start=False, stop=False, skip_group_check=True),
                                 reads=[cbf_b, msk_b], writes=[pE_b])
                        S.op("act", lambda E: E.activation(tt[:, col0:], pE[:, col0:], AF.Exp), reads=[pE_b], writes=[tt_b])
                        S.op("act", lambda E: E.activation(Lp[:, col0:], tt[:, col0:], AF.Ln, bias=gcol("one")), reads=[tt_b, gcols_b], writes=[Lp_b])

                        def stage2(pE=pE, pE_b=pE_b, Lp=Lp, Lp_b=Lp_b, col0=col0, rk=rk, tj=tj, idx=idx, j=j, hh=hh, hb=hb, R=R, R_b=R_b, po=po, po_b=po_b):
                            pT, pT_b = ps_t.next()
                            e2, e2_b = e2r.next()
                            wv, wv_b = wvr.next()
                            S.op("pe", lambda E: E.matmul(pE[:, col0:], negU_bf, Lp[:, col0:], start=False, stop=True, skip_group_check=True),
                                 reads=[cbf_b, Lp_b], writes=[pE_b])
                            S.op("pe", lambda E: E.matmul(pT[:, col0:], ones_bf, Lp[:, col0:], start=True, stop=True), reads=[cbf_b, Lp_b], writes=[pT_b])
                            S.op("dve", lambda E: E.tensor_tensor(e2[:, col0:], pE[:, col0:], R[:, col0:], ALU.subtract), reads=[pE_b, R_b], writes=[e2_b])
                            S.op("act", lambda E: E.activation(wv[:, col0:], e2[:, col0:], AF.Exp), reads=[e2_b], writes=[wv_b])
                            S.op("dve", lambda E: E.tensor_tensor(R[:, col0:], pT[:, col0:], R[:, col0:], ALU.add), reads=[pT_b, R_b], writes=[R_b])

                            def stage3():
                                S.op("pe", lambda E: E.matmul(po[hb:hb + 64, col0:], Vc[:, rk, tj, hb:hb + 64], wv[:, col0:], start=(idx == 0), stop=(j == 0),
                                                              skip_group_check=True),
                                     reads=[Vc_b, wv_b], writes=[po_b])
                            st3q[hh].append(stage3)
                        st2q[hh].append(stage2)
                        if len(st2q[hh]) > 1:
                            st2q[hh].pop(0)()
                        if len(st3q[hh]) > 1:
                            st3q[hh].pop(0)()
                for hh in range(2):
                    while st2q[hh]:
                        st2q[hh].pop(0)()
                for hh in range(2):
                    while st3q[hh]:
                        st3q[hh].pop(0)()
                for hh in range(2):
                    hb = 64 * hh
                    po, po_b = pos_[hh]
                    S.op("dve", lambda E: E.tensor_copy(osb[hb:hb + 64, c, u * 512:(u + 1) * 512], po[hb:hb + 64, :]),
                         reads=[po_b], writes=[osb_b[c][u]])
        Kc, Kc_b = Kr.next()
        osgs = Kc.rearrange("p (c n) -> p c n", c=4)
        S.dma("sp", osgs, osg_d.rearrange("(c p) n -> p c n", p=128), Kc_b, reads=[db["osg"]], writes=[Kc_b])
        mix = [osb[:, c, :] for c in range(4)] + [osgs[:, c, :] for c in range(4)]
        mix_b = [osb_b[c] for c in range(4)] + [Kc_b] * 4
        out_proj(mix, mix_b, Wd["sbg_out"], wfm, psrot([0, 1, 2]))
        S.barrier()

    def phase_X(l):
        cf.reset()
        ch.reset()
        rstd1, rstd1_b = cf.take(512), B()
        tmp, tmp_b = cf.take(512), B()
        wsetup()
        mst = rotf(1, 1024)
        msmall, msmall_b = cf.take(8), B()
        qraw = rotf(2, 1024, lambda a: a.rearrange("p (c n) -> p c n", c=2))
        rq = rotf(2, 512)
        kraw = rotf(1, 512, lambda a: a.rearrange("p (c n) -> p c n", c=2))
        rk_ = rotf(2, 256)
        rcp = rotf(2, 512)
        xb = ch.take(4096).rearrange("p (c n) -> p c n", c=8)
        xb1_b = B()
        sq, sq_b = ch.take(4096).rearrange("p (c n) -> p c n", c=8), B()
        hmT, hmT_b = ch.take(2048).rearrange("p (c n) -> p c n", c=8), B()
        mrow = roth(1, 1024)
        kT, kT_b = ch.take(2048).rearrange("p (h c n) -> p h c n", h=4, c=2), B()
        Vm, Vm_b = ch.take(2048).rearrange("p (m n) -> p m n", m=2), B()
        wfm = roth(2, 1024, lambda a: a.rearrange("p (c n) -> p c n", c=8))
        Wv_ = roth(1, 4096, lambda a: a.rearrange("p (c n) -> p c n", c=8))
        qn = roth(2, 1024, lambda a: a.rearrange("p (c n) -> p c n", c=2))
        sqq = roth(2, 1024, lambda a: a.rearrange("p (c n) -> p c n", c=2))
        Pm = roth(2, 1024, lambda a: a.rearrange("p (m n) -> p m n", m=2))
        oT = ch.take(4096).rearrange("p (c n) -> p c n", c=8)
        oT_b = [B() for c in range(8)]
        ps_ss_b = B("ps")
        ps_p = psrot([1, 2])
        ps_q = psrot([3])
        ps_s2 = psrot([4, 5])
        ps_o2 = psrot([6, 7])
        for mt in range(2):
            ms, ms_b = mst.next()
            S.dma("sp", ms, mem_d[mt * 128:(mt + 1) * 128, :], ms_b, writes=[ms_b])
            mr, mr_b = mrow.next()
            S.op("act", lambda E, ms=ms, mr=mr: E.activation(mr, ms, AF.Square, accum_out=msmall[:, 0:1]), reads=[ms_b], writes=[mr_b, msmall_b])
            S.op("act", lambda E: E.activation(msmall[:, 1:2], msmall[:, 0:1], AF.Ln, bias=gcol("eps"), scale=1.0 / D),
                 reads=[msmall_b, gcols_b], writes=[msmall_b])
            S.op("act", lambda E: E.activation(msmall[:, 2:3], msmall[:, 1:2], AF.Exp, scale=-0.5), reads=[msmall_b], writes=[msmall_b])
            S.op("dve", lambda E, ms=ms: E.tensor_scalar(ms, ms, msmall[:, 2:3], None, ALU.mult), reads=[ms_b, msmall_b], writes=[ms_b])
            for half in range(2):
                pp, pp_b = ps_p.next()
                for cc in range(4):
                    c = half * 4 + cc
                    S.op("pe", lambda E, pp=pp, ms=ms, c=c, cc=cc: E.transpose(pp[:, cc * 128:(cc + 1) * 128], ms[:, c * 128:(c + 1) * 128], ident_f[:, :]),
                         reads=[ms_b, ident_f_b], writes=[pp_b])
                for cc in range(4):
                    c = half * 4 + cc
                    S.op("dve", lambda E, pp=pp, c=c, cc=cc, mt=mt: E.tensor_scalar(hmT[:, c, mt * 128:(mt + 1) * 128], pp[:, cc * 128:(cc + 1) * 128],
                                                                                     gcol("xmn", 8 * l + c), None, ALU.mult),
                         reads=[pp_b, gcols_b], writes=[hmT_b])
        wk = Wd[("xk", l)].rearrange("(c p) n -> p c n", p=128)
        for hd in range(4):
            kr_, kr_b = kraw.next()
            sk, sk_b = sqq.next()
            for dc in range(2):
                oc = hd * 2 + dc
                wt, wt_b = wfm.next()
                wload(wk[:, :, oc * 128:(oc + 1) * 128], 8, 128, wt, wt_b)
                pp, pp_b = ps_p.next()
                for c in range(8):
                    S.op("pe", lambda E, c=c, pp=pp, wt=wt: E.matmul(pp[:, 0:256], wt[:, c, :], hmT[:, c, :], start=(c == 0), stop=(c == 7)),
                         reads=[wt_b, hmT_b], writes=[pp_b])
                S.op("act", lambda E, kr_=kr_, pp=pp, dc=dc: E.copy(kr_[:, dc, :], pp[:, 0:256]), reads=[pp_b], writes=[kr_b])
                S.op("act", lambda E, sk=sk, pp=pp, dc=dc: E.activation(sk[:, dc, 0:256], pp[:, 0:256], AF.Square), reads=[pp_b], writes=[sk_b])
            pq, pq_b = ps_q.next()
            for dc in range(2):
                S.op("pe", lambda E, pq=pq, sk=sk, dc=dc: E.matmul(pq[:, 0:256], ones_bf, sk[:, dc, 0:256], start=(dc == 0), stop=(dc == 1)),
                     reads=[sk_b, cbf_b], writes=[pq_b])
            rr, rr_b = rk_.next()
            rstd_from_ss(rr, rr_b, pq[:, 0:256], pq_b, 256, tmp[:, 0:256], tmp_b)
            for dc in range(2):
                S.op("dve", lambda E, kr_=kr_, rr=rr, dc=dc, hd=hd: E.scalar_tensor_tensor(kT[:, hd, dc, :], kr_[:, dc, :], gcol("xkg", 2 * l + dc), rr,
                                                                                           ALU.mult, ALU.mult),
                     reads=[kr_b, rr_b, gcols_b], writes=[kT_b])
        wvv = Wd[("xv", l)].rearrange("(c p) n -> p c n", p=128)
        for nh in range(2):
            wt, wt_b = Wv_.next()
            for hh in range(2):
                wload(wvv[:, hh * 4:(hh + 1) * 4, nh * 512:(nh + 1) * 512], 4, 512, wt[:, hh * 4:(hh + 1) * 4, :], wt_b)
            for mt in range(2):
                pp, pp_b = ps_p.next()
                for c in range(8):
                    S.op("pe", lambda E, c=c, pp=pp, wt=wt, mt=mt: E.matmul(pp[:, :], hmT[:, c, mt * 128:(mt + 1) * 128], wt[:, c, :],
                                                                             start=(c == 0), stop=(c == 7)),
                         reads=[wt_b, hmT_b], writes=[pp_b])
                S.op("act", lambda E, pp=pp, mt=mt, nh=nh: E.copy(Vm[:, mt, nh * 512:(nh + 1) * 512], pp[:, :]), reads=[pp_b], writes=[Vm_b])
        wq = Wd[("xq", l)].rearrange("(c p) n -> p c n", p=128)
        wo = Wd[("xo", l)].rearrange("(c p) n -> p c n", p=128)
        ps_x = psrot([1, 2])
        for g in range(4):
            norm_prep("xn", 8 * l, g, xb, xb1_b, rstd1, rstd1_b, sq, sq_b, 0, ps_ss_b, tmp, tmp_b)
            for hd in range(4):
                qr_, qr_b = qraw.next()
                sk, sk_b = sqq.next()
                for dc in range(2):
                    oc = hd * 2 + dc
                    wt, wt_b = wfm.next()
                    wload(wq[:, :, oc * 128:(oc + 1) * 128], 8, 128, wt, wt_b)
                    pp, pp_b = ps_p.next()
                    for c in range(8):
                        S.op("pe", lambda E, c=c, pp=pp, wt=wt: E.matmul(pp[:, :], wt[:, c, :], xb[:, c, :], start=(c == 0), stop=(c == 7)),
                             reads=[wt_b, xb1_b], writes=[pp_b])
                    S.op("dve", lambda E, qr_=qr_, pp=pp, dc=dc: E.tensor_tensor(qr_[:, dc, :], pp[:, :], rstd1, ALU.mult),
                         reads=[pp_b, rstd1_b], writes=[qr_b])
                    S.op("act", lambda E, sk=sk, qr_=qr_, dc=dc: E.activation(sk[:, dc, :], qr_[:, dc, :], AF.Square), reads=[qr_b], writes=[sk_b])
                pq, pq_b = ps_q.next()
                for dc in range(2):
                    S.op("pe", lambda E, pq=pq, sk=sk, dc=dc: E.matmul(pq[:, :], ones_bf, sk[:, dc, :], start=(dc == 0), stop=(dc == 1)),
                         reads=[sk_b, cbf_b], writes=[pq_b])
                rr, rr_b = rq.next()
                rstd_from_ss(rr, rr_b, pq[:, :], pq_b, 256, tmp, tmp_b)
                qq, qq_b = qn.next()
                for dc in range(2):
                    S.op("dve", lambda E, qq=qq, qr_=qr_, rr=rr, dc=dc: E.scalar_tensor_tensor(qq[:, dc, :], qr_[:, dc, :], gcol("xqg", 2 * l + dc), rr,
                                                                                              ALU.mult, ALU.mult),
                         reads=[qr_b, rr_b, gcols_b], writes=[qq_b])
                pm_, pm_b = Pm.next()
                for mt in range(2):
                    pS, pS_b = ps_s2.next()
                    for dc in range(2):
                        S.op("pe", lambda E, pS=pS, qq=qq, dc=dc, mt=mt, hd=hd: E.matmul(pS[:, :], kT[:, hd, dc, mt * 128:(mt + 1) * 128], qq[:, dc, :],
                                                                                         start=(dc == 0), stop=(dc == 1)),
                             reads=[kT_b, qq_b], writes=[pS_b])
                    S.op("act", lambda E, pm_=pm_, pS=pS, mt=mt: E.activation(pm_[:, mt, :], pS[:, :], AF.Exp, scale=1.0 / 16.0),
                         reads=[pS_b], writes=[pm_b])
                pq, pq_b = ps_q.next()
                for mt in range(2):
                    S.op("pe", lambda E, pq=pq, pm_=pm_, mt=mt: E.matmul(pq[:, :], ones_bf, pm_[:, mt, :], start=(mt == 0), stop=(mt == 1)),
                         reads=[pm_b, cbf_b], writes=[pq_b])
                rc, rc_b = rcp.next()
                S.op("dve", lambda E, rc=rc, pq=pq: E.reciprocal(rc, pq[:, :]), reads=[pq_b], writes=[rc_b])
                for dc in range(2):
                    po, po_b = ps_o2.next()
                    for mt in range(2):
                        S.op("pe", lambda E, po=po, pm_=pm_, mt=mt, dc=dc, hd=hd: E.matmul(po[:, :], Vm[:, mt, hd * 256 + dc * 128:hd * 256 + (dc + 1) * 128],
                                                                                           pm_[:, mt, :], start=(mt == 0), stop=(mt == 1)),
                             reads=[Vm_b, pm_b], writes=[po_b])
                    S.op("dve", lambda E, po=po, rc=rc, hd=hd, dc=dc: E.tensor_tensor(oT[:, hd * 2 + dc, :], po[:, :], rc, ALU.mult),
                         reads=[po_b, rc_b], writes=[oT_b[hd * 2 + dc]])
            for i in range(8):
                wt, wt_b = wfm.next()
                wload(wo[:, :, i * 128:(i + 1) * 128], 8, 128, wt, wt_b)
                po, po_b = ps_x.next()
                for c in range(8):
                    S.op("pe", lambda E, c=c, po=po, wt=wt: E.matmul(po[:, :], wt[:, c, :], oT[:, c, :], start=(c == 0), stop=(c == 7)),
                         reads=[wt_b, oT_b[c]], writes=[po_b])
                xs = xT[:, i, g * 512:(g + 1) * 512]
                S.op("dve", lambda E, xs=xs, po=po: E.tensor_tensor(xs, po[:, :], xs, ALU.add), reads=[po_b, xTb[i][g]], writes=[xTb[i][g]])
        S.barrier()

    def sin_reduced(out_ap, out_b, ang_ap, ang_b, wk, wk_b, nf, nf_b, p0, p1, add=0.0):
        ni = nint[p0:p1, :]
        S.op("dve", lambda E: E.tensor_scalar(wk, ang_ap, float(add), None, ALU.add), reads=[ang_b], writes=[wk_b])
        S.op("dve", lambda E: E.tensor_scalar(ni, wk, float(1 / TWO_PI), None, ALU.mult), reads=[wk_b], writes=[nint_b])
        S.op("dve", lambda E: E.tensor_copy(nf, ni), reads=[nint_b], writes=[nf_b])
        S.op("dve", lambda E: E.scalar_tensor_tensor(wk, nf, -TWO_PI, wk, ALU.mult, ALU.add), reads=[nf_b, wk_b], writes=[wk_b])
        S.op("dve", lambda E: E.tensor_scalar(nf, wk, PI, TWO_PI, ALU.is_gt, ALU.mult), reads=[wk_b], writes=[nf_b])
        S.op("dve", lambda E: E.tensor_tensor(wk, wk, nf, ALU.subtract), reads=[nf_b, wk_b], writes=[wk_b])
        S.op("dve", lambda E: E.tensor_scalar(nf, wk, -PI, TWO_PI, ALU.is_lt, ALU.mult), reads=[wk_b], writes=[nf_b])
        S.op("dve", lambda E: E.tensor_tensor(wk, wk, nf, ALU.add), reads=[nf_b, wk_b], writes=[wk_b])
        S.op("act", lambda E: E.activation(out_ap, wk, AF.Sin), reads=[wk_b], writes=[out_b])

    class _Stop(Exception):
        pass

    def _stage(n):
        for k in range(1, 9):
            if ('a1s%d' % k) in DBG and n >= k:
                raise _Stop()

    def phase_A1():
        try:
            phase_A1_()
        except _Stop:
            S.barrier()

    def phase_A1_():
        cf.reset()
        ch.reset()
        rstd1, rstd1_b = cf.take(512), B()
        tmpr = rotf(2, 512)
        tmp, tmp_b = tmpr.slots[0]
        wsetup(2, 1024)
        tabs = [cf.take(512) for _ in range(4)]
        tab_b = [B() for _ in range(4)]
        cqraw, cqraw_b = cf.take(2048).rearrange("p (c n) -> p c n", c=4), B()
        far = rotf(2, 512)
        fbr = rotf(2, 512)
        fa, fa_b = far.slots[0]
        fb, fb_b = fbr.slots[0]
        fc, fc_b = cf.take(512), B()
        kra, kra_b = cf.take(512), B()
        rqk = rotf(2, 512)
        xb = ch.take(4096).rearrange("p (c n) -> p c n", c=8)
        xb_b = B()
        sq, sq_b = ch.take(4096).rearrange("p (c n) -> p c n", c=8), B()
        Win, Win_b = ch.take(6400).rearrange("p (c n) -> p c n", c=8), B()
        Wuq, Wuq_b = ch.take(6144).rearrange("p (c n) -> p c n", c=4), B()
        Wqs, Wqs_b = ch.take(2048).rearrange("p (c n) -> p c n", c=4), B()
        Wuk, Wuk_b = ch.take(2048).rearrange("p (c n) -> p c n", c=2), B()
        Wuv, Wuv_b = ch.take(2048).rearrange("p (c n) -> p c n", c=2), B()
        cqn, cqn_b = ch.take(2048).rearrange("p (c n) -> p c n", c=4), B()
        ckvn, ckvn_b = ch.take(1024).rearrange("p (c n) -> p c n", c=2), B()
        krb, krb_b = ch.take(512), B()
        sqh = ch.take(512)
        sqh_lo, sqh_hi = B(), B()
        sqqr = roth(2, 512)
        qst = roth(2, 512)
        vst = roth(1, 512)
        ps_ss_b = B("ps")
        ps_p = psrot([1, 2])
        ps_a = psrot([3, 4])
        ps_b2 = psrot([5])
        ps_r = psrot([6, 7])
        ps_q = ps_r

        def wl(src, kc, ncol, dst, dst_b):
            v = src.rearrange("(c p) n -> p c n", p=128)
            step = max(128, (1024 // kc) // 128 * 128) if ncol >= 128 else ncol
            for n0 in range(0, ncol, step):
                n1 = min(ncol, n0 + step)
                wload(v[:, :, n0:n1], kc, n1 - n0, dst[:, :, n0:n1], dst_b)

        wl(Wd["mla_in"], 8, 768, Win, Win_b)
        wload(Wd["mla_in"].rearrange("(c p) n -> p c n", p=128)[:, :, 768:800], 8, 32, Win[:, :, 768:800], Win_b)
        wl(Wd["mla_uq"], 4, 1536, Wuq, Wuq_b)
        wl(Wd["mla_uq_sw"], 4, 512, Wqs, Wqs_b)
        wl(Wd["mla_uk"], 2, 1024, Wuk, Wuk_b)
        wl(Wd["mla_uv"], 2, 1024, Wuv, Wuv_b)

        P0, P1 = 64, 96
        for pass_ in ("kv", "q"):
            for g in range(1 if 'a1small' in DBG else 4):
                gs = slice(g * 512, (g + 1) * 512)
                norm_prep("mix", 8, g, xb, xb_b, rstd1, rstd1_b, sq, sq_b, 0, ps_ss_b, tmp, tmp_b)
                S.dma("sp", posi[P0:P1, :], posrep_d[:, gs], posi_b, writes=[posi_b])
                S.op("dve", lambda E: E.tensor_copy(fa[P0:P1, :], posi[P0:P1, :]), reads=[posi_b], writes=[fa_b])
                S.op("dve", lambda E: E.tensor_scalar(fa[P0:P1, :], fa[P0:P1, :], gcol("invf", 0, P0, P1), None, ALU.mult),
                     reads=[fa_b, gcols_b], writes=[fa_b])
                sin_reduced(tabs[0][P0:P1, :], tab_b[0], fa[P0:P1, :], fa_b, fb[P0:P1, :], fb_b, fc[P0:P1, :], fc_b, P0, P1, add=PI / 2)
                sin_reduced(tabs[1][P0:P1, :], tab_b[1], fa[P0:P1, :], fa_b, fb[P0:P1, :], fb_b, fc[P0:P1, :], fc_b, P0, P1, add=0.0)
                S.op("dve", lambda E: E.tensor_scalar(tabs[2][P0:P1, :], tabs[0][P0:P1, :], gcol("mkg", 0, P0, P1), None, ALU.mult),
                     reads=[tab_b[0], gcols_b], writes=[tab_b[2]])
                S.op("dve", lambda E: E.tensor_scalar(tabs[3][P0:P1, :], tabs[1][P0:P1, :], gcol("sgn", 0, P0, P1), gcol("mkg_sw", 0, P0, P1), ALU.mult, ALU.mult),
                     reads=[tab_b[1], gcols_b], writes=[tab_b[3]])
                S.op("dve", lambda E: E.tensor_scalar(tabs[0][P0:P1, :], tabs[0][P0:P1, :], gcol("mqg", 0, P0, P1), None, ALU.mult),
                     reads=[tab_b[0], gcols_b], writes=[tab_b[0]])
                S.op("dve", lambda E: E.tensor_scalar(tabs[1][P0:P1, :], tabs[1][P0:P1, :], gcol("sgn", 0, P0, P1), gcol("mqg_sw", 0, P0, P1), ALU.mult, ALU.mult),
                     reads=[tab_b[1], gcols_b], writes=[tab_b[1]])

                for (n_ch, col0, dst, dst_b, gname, nfeat) in (((4, 0, cqn, cqn_b, "qlora", 512),) if pass_ == "q" else ((2, 512, ckvn, ckvn_b, "kvlora", 256),)):
                    for oc in range(n_ch):
                        pp, pp_b = ps_p.next()
                        for c in range(8):
                            S.op("pe", lambda E, c=c, pp=pp, oc=oc, col0=col0: E.matmul(pp[:, :], Win[:, c, col0 + oc * 128:col0 + (oc + 1) * 128], xb[:, c, :],
                                                                                        start=(c == 0), stop=(c == 7)),
                                 reads=[Win_b, xb_b], writes=[pp_b])
                        S.op("dve", lambda E, pp=pp, oc=oc: E.tensor_tensor(cqraw[:, oc, :], pp[:, :], rstd1, ALU.mult),
                             reads=[pp_b, rstd1_b], writes=[cqraw_b])
                        S.op("act", lambda E, oc=oc: E.activation(sq[:, oc, :], cqraw[:, oc, :], AF.Square), reads=[cqraw_b], writes=[sq_b])
                    pq, pq_b = ps_q.next()
                    for oc in range(n_ch):
                        S.op("pe", lambda E, pq=pq, oc=oc, n_ch=n_ch: E.matmul(pq[:, :], ones_bf, sq[:, oc, :], start=(oc == 0), stop=(oc == n_ch - 1)),
                             reads=[sq_b, cbf_b], writes=[pq_b])
                    rr, rr_b = rqk.next()
                    rstd_from_ss(rr, rr_b, pq[:, :], pq_b, nfeat, tmp, tmp_b)
                    for oc in range(n_ch):
                        S.op("dve", lambda E, oc=oc, dst=dst, rr=rr, gname=gname: E.scalar_tensor_tensor(dst[:, oc, :], cqraw[:, oc, :], gcol(gname, oc), rr, ALU.mult, ALU.mult),
                             reads=[cqraw_b, rr_b, gcols_b], writes=[dst_b])

                if pass_ == "kv":
                    pp, pp_b = ps_p.next()
                    for c in range(8):
                        S.op("pe", lambda E, c=c, pp=pp: E.matmul(pp[0:32, :], Win[:, c, 768:800], xb[:, c, :], start=(c == 0), stop=(c == 7)),
                             reads=[Win_b, xb_b], writes=[pp_b])
                    S.op("dve", lambda E, pp=pp: E.tensor_tensor(krb[0:32, :], pp[0:32, :], rstd1[0:32, :], ALU.mult), reads=[pp_b, rstd1_b], writes=[krb_b])
                    if os.environ.get("KSUB") == "1":
                        raise _Stop()
                    pa, pa_b = ps_a.next()
                    pb_, pb_b = ps_b2.next()
                    S.op("pe", lambda E, pa=pa: E.matmul(pa[P0:P1, :], ident_bf[0:32, 0:32], krb[0:32, :], start=True, stop=True), reads=[krb_b, cbf_b], writes=[pa_b])
                    S.op("pe", lambda E, pb_=pb_: E.matmul(pb_[P0:P1, :], swap_bf[0:32, 0:32], krb[0:32, :], start=True, stop=True), reads=[krb_b, cbf_b], writes=[pb_b])
                    if os.environ.get("KSUB") == "2":
                        raise _Stop()
                    S.op("act", lambda E, pa=pa: E.activation(sqh[P0:P1, :], pa[P0:P1, :], AF.Square), reads=[pa_b], writes=[sqh_hi])
                    if os.environ.get("KSUB") == "3":
                        raise _Stop()
                    S.op("act", lambda E, pa=pa: E.copy(kra[P0:P1, :], pa[P0:P1, :]), reads=[pa_b], writes=[kra_b])
                    S.op("act", lambda E, pb_=pb_: E.copy(fb[P0:P1, :], pb_[P0:P1, :]), reads=[pb_b], writes=[fb_b])
                    S.op("dve", lambda E: E.tensor_tensor(kra[P0:P1, :], kra[P0:P1, :], tabs[2][P0:P1, :], ALU.mult), reads=[kra_b, tab_b[2]], writes=[kra_b])
                    S.op("dve", lambda E: E.tensor_tensor(fb[P0:P1, :], fb[P0:P1, :], tabs[3][P0:P1, :], ALU.mult), reads=[fb_b, tab_b[3]], writes=[fb_b])
                    S.op("dve", lambda E: E.tensor_tensor(kra[P0:P1, :], kra[P0:P1, :], fb[P0:P1, :], ALU.add), reads=[kra_b, fb_b], writes=[kra_b])

                for h in range(2 if 'a1small' in DBG else 16):
                    if pass_ == "q":
                        pa, pa_b = ps_a.next()
                        pb_, pb_b = ps_b2.next()
                        for c in range(4):
                            S.op("pe", lambda E, c=c, pa=pa, h=h: E.matmul(pa[0:96, :], Wuq[:, c, h * 96:(h + 1) * 96], cqn[:, c, :], start=(c == 0), stop=(c == 3)),
                                 reads=[Wuq_b, cqn_b], writes=[pa_b])
                        for c in range(4):
                            S.op("pe", lambda E, c=c, pb_=pb_, h=h: E.matmul(pb_[P0:P1, :], Wqs[:, c, h * 32:(h + 1) * 32], cqn[:, c, :], start=(c == 0), stop=(c == 3)),
                                 reads=[Wqs_b, cqn_b], writes=[pb_b])
                        sqq, sqq_b = sqqr.next()
                        tq_, tq_b = tmpr.next()
                        fa, fa_b = far.next()
                        fb, fb_b = fbr.next()
                        S.op("act", lambda E, pa=pa: E.activation(sqq[0:96, :], pa[0:96, :], AF.Square), reads=[pa_b], writes=[sqq_b])
                        pr, pr_b = ps_r.next()
                        S.op("pe", lambda E, pr=pr: E.matmul(pr[0:96, :], ones_bf[0:96, 0:96], sqq[0:96, :], start=True, stop=True), reads=[sqq_b, cbf_b], writes=[pr_b])
                        rr, rr_b = rqk.next()
                        rstd_from_ss(rr[0:96, :], rr_b, pr[0:96, :], pr_b, 96, tq_[0:96, :], tq_b, 0, 96)
                        qs, qs_b = qst.next()
                        S.op("dve", lambda E, qs=qs, pa=pa, rr=rr: E.scalar_tensor_tensor(qs[0:64, :], pa[0:64, :], gcol("mqg", 0, 0, 64), rr[0:64, :], ALU.mult, ALU.mult),
                             reads=[pa_b, rr_b, gcols_b], writes=[qs_b])
                        S.op("act", lambda E, pa=pa: E.copy(fa[P0:P1, :], pa[P0:P1, :]), reads=[pa_b], writes=[fa_b])
                        S.op("act", lambda E, pb_=pb_: E.copy(fb[P0:P1, :], pb_[P0:P1, :]), reads=[pb_b], writes=[fb_b])
                        S.op("dve", lambda E: E.tensor_tensor(fa[P0:P1, :], fa[P0:P1, :], tabs[0][P0:P1, :], ALU.mult), reads=[fa_b, tab_b[0]], writes=[fa_b])
                        S.op("dve", lambda E: E.tensor_tensor(fb[P0:P1, :], fb[P0:P1, :], tabs[1][P0:P1, :], ALU.mult), reads=[fb_b, tab_b[1]], writes=[fb_b])
                        S.op("dve", lambda E: E.tensor_tensor(fa[P0:P1, :], fa[P0:P1, :], fb[P0:P1, :], ALU.add), reads=[fa_b, fb_b], writes=[fa_b])
                        S.op("dve", lambda E, qs=qs, rr=rr: E.tensor_tensor(qs[P0:P1, :], fa[P0:P1, :], rr[P0:P1, :], ALU.mult), reads=[fa_b, rr_b], writes=[qs_b])
                        S.dma("sp", q1_d[h * 96:(h + 1) * 96, gs], qs[0:96, :], qs_b, reads=[qs_b], pwrites=[db["q1"]])

                    if pass_ == "kv":
                        pa, pa_b = ps_a.next()
                        for c in range(2):
                            S.op("pe", lambda E, c=c, pa=pa, h=h: E.matmul(pa[0:64, :], Wuk[:, c, h * 64:(h + 1) * 64], ckvn[:, c, :], start=(c == 0), stop=(c == 1)),
                                 reads=[Wuk_b, ckvn_b], writes=[pa_b])
                        S.op("act", lambda E, pa=pa: E.activation(sqh[0:64, :], pa[0:64, :], AF.Square), reads=[pa_b], writes=[sqh_lo])
                        pr, pr_b = ps_r.next()
                        S.op("pe", lambda E, pr=pr: E.matmul(pr[0:96, :], ones_bf[0:96, 0:96], sqh[0:96, :], start=True, stop=True), reads=[sqh_lo, sqh_hi, cbf_b], writes=[pr_b])
                        rr, rr_b = rqk.next()
                        tq_, tq_b = tmpr.next()
                        rstd_from_ss(rr[0:96, :], rr_b, pr[0:96, :], pr_b, 96, tq_[0:96, :], tq_b, 0, 96)
                        ks, ks_b = qst.next()
                        S.op("dve", lambda E, ks=ks, pa=pa, rr=rr: E.scalar_tensor_tensor(ks[0:64, :], pa[0:64, :], gcol("mkg", 0, 0, 64), rr[0:64, :], ALU.mult, ALU.mult),
                             reads=[pa_b, rr_b, gcols_b], writes=[ks_b])
                        S.op("dve", lambda E, ks=ks, rr=rr: E.tensor_tensor(ks[P0:P1, :], kra[P0:P1, :], rr[P0:P1, :], ALU.mult), reads=[kra_b, rr_b], writes=[ks_b])
                        S.dma("sp", k1_d[h * 96:(h + 1) * 96, gs], ks[0:96, :], ks_b, reads=[ks_b], pwrites=[db["k1"]])

                if pass_ == "kv":
                    for tt in range(4):
                        t = g * 4 + tt
                        for half in range(2):
                            pp, pp_b = ps_p.next()
                            for c in range(2):
                                S.op("pe", lambda E, c=c, pp=pp, tt=tt, half=half: E.matmul(pp[:, :], ckvn[:, c, tt * 128:(tt + 1) * 128], Wuv[:, c, half * 512:(half + 1) * 512],
                                                                                             start=(c == 0), stop=(c == 1)),
                                     reads=[Wuv_b, ckvn_b], writes=[pp_b])
                            vs, vs_b = vst.next()
                            S.op("act", lambda E, vs=vs, pp=pp: E.copy(vs, pp[:, :]), reads=[pp_b], writes=[vs_b])
                            S.dma("sp", v1_d[t * 128:(t + 1) * 128, half * 512:(half + 1) * 512], vs, vs_b, reads=[vs_b], pwrites=[db["v1"]])
            if pass_ == "kv" and exch == "cc":
                allgather(k1_d, db["k1"], kg1_d, db["kg1"], 1536, RC_K1)
                allgather(v1_d, db["v1"], vg1_d, db["vg1"], T, RC_V1)
        S.barrier()

    o1_d = nc.dram_tensor("o1_d", [D, T], BF16).ap()
    o1_b = Buf("o1")

    def phase_B1():
        cf.reset()
        ch.reset()
        wsetup()
        tmpA, tmpA_b = cf.take(512), B()
        rcf, rcf_b = cf.take(512), B()
        rcs, rcs_b = cf.take(512), B()
        KK = ch.take(16384)
        Kr = Rot([(KK[:, 0:8192], B()), (KK[:, 8192:16384], B())])
        Vr = roth(2, 8192, lambda a: a.rearrange("p (r t d) -> p r t d", r=4, t=16))
        qr = roth(2, 2048)
        Pr = roth(3, 512)
        msk, msk_b = ch.take(1024).rearrange("p (v n) -> p v n", v=8), B()
        rhi, rhi_b = ch.take(512), B()
        rlo, rlo_b = ch.take(512), B()
        ost = roth(2, 512)
        wfm = roth(2, 1024, lambda a: a.rearrange("p (c n) -> p c n", c=8))
        ps_s = psrot([0, 1, 2])
        ps_o = psrot([3, 4])
        ps_r = psrot([5])
        S.dma("sp", msk, masks_d[:, 1024:2048].rearrange("p (v n) -> p v n", v=8), msk_b, writes=[msk_b])
        for (Vt, Vt_b) in Vr.slots:
            for rk in range(4):
                S.op("pool", lambda E: E.memset(Vt[:, rk, :, 64:128], 1.0), writes=[Vt_b])
        SC = float(96 ** -0.5)
        dsl = slice(64, 128)
        osl = slice(0, 64)
        def issue_loads(h):
            Kh, Kh_b = Kr.next()
            Vh, Vh_b = Vr.next()
            qh, qh_b = qr.next()
            S.dma("sp", qh[0:96, :], q1_d[h * 96:(h + 1) * 96, :], qh_b, reads=[db["q1"]], writes=[qh_b])
            for rk in range(4):
                S.dma("sp", Kh[0:96, rk * T:(rk + 1) * T], kg1_d[grow(RC_K1, rk, h * 96):grow(RC_K1, rk, h * 96) + 96, :], Kh_b,
                      reads=[db["kg1"]], writes=[Kh_b])
                for tq in range(4):
                    r0 = grow(RC_V1, rk, tq * 512)
                    S.dma("sp", Vh[:, rk, tq * 4:(tq + 1) * 4, 0:64],
                          vg1_d[r0:r0 + 512, h * 64:(h + 1) * 64].rearrange("(t p) d -> p t d", p=128), Vh_b, reads=[db["vg1"]], writes=[Vh_b])
            return Kh, Kh_b, Vh, Vh_b, qh, qh_b

        nxt = issue_loads(0)
        for h in range(16):
            Kh, Kh_b, Vh, Vh_b, qh, qh_b = nxt
            if h + 1 < 16:
                nxt = issue_loads(h + 1)
            for u in range(4):
                po, po_b = ps_o.next()
                nblk = 16 * u + 16
                pend = []
                for j in range(nblk):
                    rk, tj, mj = key_loc(j)
                    kcol = rk * T + tj * 128
                    diag = j >= 16 * u
                    col0, vidx = 0, 0
                    if diag:
                        m = (j - 16 * u) // 4
                        col0 = 128 * m
                        vidx = ((4 * u + m) % 2) * 4 + mj
                    pS, pS_b = ps_s.next()
                    S.op("pe", lambda E: E.matmul(pS[:, col0:], Kh[0:96, kcol:kcol + 128], qh[0:96, u * 512 + col0:(u + 1) * 512], start=True, stop=not diag),
                         reads=[Kh_b, qh_b], writes=[pS_b])
                    if diag:
                        S.op("pe", lambda E: E.matmul(pS[:, col0:col0 + 128], ident_bf, msk[:, vidx, :], start=False, stop=True, skip_group_check=True),
                             reads=[cbf_b, msk_b], writes=[pS_b])
                    P_, P_b = Pr.next()
                    S.op("act", lambda E: E.activation(P_[:, col0:], pS[:, col0:], AF.Exp, scale=SC), reads=[pS_b], writes=[P_b])

                    def pv(po=po, po_b=po_b, P_=P_, P_b=P_b, col0=col0, rk=rk, tj=tj, j=j, nblk=nblk, Vh=Vh, Vh_b=Vh_b):
                        S.op("pe", lambda E: E.matmul(po[:, col0:], Vh[:, rk, tj, :], P_[:, col0:], start=(j == 0), stop=(j == nblk - 1),
                                                      skip_group_check=True),
                             reads=[Vh_b, P_b], writes=[po_b])
                    if pend:
                        pend.pop()()
                    pend.append(pv)
                pend.pop()()
                S.op("act", lambda E: E.activation(tmpA[dsl, :], po[dsl, :], AF.Ln), reads=[po_b], writes=[tmpA_b])
                S.op("act", lambda E: E.activation(rcf[dsl, :], tmpA[dsl, :], AF.Exp, scale=-1.0), reads=[tmpA_b], writes=[rcf_b])
                S.op("dve", lambda E: E.tensor_copy(rhi[dsl, :], rcf[dsl, :]), reads=[rcf_b], writes=[rhi_b])
                S.op("dve", lambda E: E.tensor_tensor(rlo[dsl, :], rcf[dsl, :], rhi[dsl, :], ALU.subtract), reads=[rcf_b, rhi_b], writes=[rlo_b])
                pr, pr_b = ps_r.next()
                S.op("pe", lambda E: E.matmul(pr[osl, :], ident_bf[dsl, 64:128], rhi[dsl, :], start=True, stop=False), reads=[cbf_b, rhi_b], writes=[pr_b])
                S.op("pe", lambda E: E.matmul(pr[osl, :], ident_bf[dsl, 64:128], rlo[dsl, :], start=False, stop=True), reads=[cbf_b, rlo_b], writes=[pr_b])
                S.op("act", lambda E: E.copy(rcs[osl, :], pr[osl, :]), reads=[pr_b], writes=[rcs_b])
                os_, os_b = ost.next()
                S.op("dve", lambda E: E.tensor_tensor(os_[osl, :], po[osl, :], rcs[osl, :], ALU.mult), reads=[po_b, rcs_b], writes=[os_b])
                S.dma("pool", o1_d[h * 64:(h + 1) * 64, u * 512:(u + 1) * 512], os_[osl, :], os_b, reads=[os_b], pwrites=[o1_b])
        S.barrier()
        oT = KK.rearrange("p (c n) -> p c n", c=8)
        oT_b = B()
        S.dma("sp", oT, o1_d.rearrange("(c p) n -> p c n", p=128), oT_b, reads=[o1_b], writes=[oT_b])
        out_proj([oT[:, c, :] for c in range(8)], [oT_b] * 8, Wd["mla_out"], wfm, psrot([0, 1, 2]))
        S.barrier()

    load_x()
    seq = [lambda: ffn(Wd[("gu", "pre", 0)], Wd[("d", "pre", 0)], "ffn_pre", 0),
           phase_A0, phase_B0, lambda: phase_X(0),
           lambda: ffn(Wd[("gu", "post", 0)], Wd[("d", "post", 0)], "ffn_post", 0),
           lambda: ffn(Wd[("gu", "pre", 1)], Wd[("d", "pre", 1)], "ffn_pre", 16),
           phase_A1, phase_B1, lambda: phase_X(1),
           lambda: ffn(Wd[("gu", "post", 1)], Wd[("d", "post", 1)], "ffn_post", 16)]
    for i in range(lo, hi):
        seq[i]()
    for n in db:
        if db[n].lw is not None:
            S._wait("sp", db[n].lw)
        for k_, v_ in db[n].mw.items():
            S._wait("sp", (k_, v_))
    store_x()
    S.run()
    print('[build] phases', lo, hi, 'counts', dict(S.cnt), 'nsems', len(S.sems), 'maxdma', max([0] + [v for k, v in S.alltoks.items() if str(k).startswith('d')]), flush=True)
    return nc, used_inputs, outputs


def zig(t, r):
    return 4 * t + (r if t % 2 == 0 else 3 - r)


def shard_tokens(x):
    outs = []
    for c in range(NCORES):
        b, r = c // 4, c % 4
        blocks = [zig(t, r) for t in range(NT)]
        xb = x[b].reshape(64, 128, -1)
        outs.append(np.ascontiguousarray(xb[blocks].reshape(T, -1)))
    return outs


def unshard_tokens(outs, dtype=np.float32):
    full = np.zeros((2, 64, 128, D), dtype)
    for c in range(NCORES):
        b, r = c // 4, c % 4
        blocks = [zig(t, r) for t in range(NT)]
        full[b, blocks] = outs[c].reshape(NT, 128, D)
    return full.reshape(2, 8192, D)


def col(v):
    v = np.asarray(v, np.float32)
    return np.ascontiguousarray(v.reshape(-1, 128).T)


def make_masks(r):
    k = np.arange(128)[:, None]
    q = np.arange(128)[None, :]
    m = np.zeros((128, 16, 128), np.float32)
    for kind in range(2):
        tri = (k < q) if kind == 0 else (k <= q)
        for p in range(2):
            rr = r if p == 0 else 3 - r
            for mj in range(4):
                if mj < rr:
                    v = np.zeros((128, 128), np.float32)
                elif mj == rr:
                    v = np.where(tri, 0.0, NEG).astype(np.float32)
                else:
                    v = np.full((128, 128), NEG, np.float32)
                m[:, kind * 8 + p * 4 + mj, :] = v
    return m.reshape(128, 16 * 128).astype(ml_dtypes.bfloat16)


_PROG_CACHE = {}


def prepare_inputs(inp):
    f32 = np.float32
    ident = np.eye(128, dtype=f32)
    ones = np.ones((128, 128), f32)
    negU = -np.tril(np.ones((128, 128), f32))
    sw = np.zeros((128, 128), f32)
    for i in range(16):
        sw[16 + i, i] = 1.0
        sw[i, 16 + i] = 1.0
    cb = np.concatenate([ident, ones, negU, sw], axis=1).astype(ml_dtypes.bfloat16)
    gc = np.zeros((128, NGC), f32)
    for l in range(2):
        gc[:, 16 * l:16 * l + 8] = col(inp["ffn_pre_norm"][l])
        gc[:, 16 * l + 8:16 * l + 16] = col(inp["ffn_post_norm"][l])
        gc[:, 32 + 8 * l:40 + 8 * l] = col(inp["mix_norm"][l])
        gc[:, 48 + 8 * l:56 + 8 * l] = col(inp["xmem_norm"][l])
        gc[:, 64 + 8 * l:72 + 8 * l] = col(inp["xmem_mem_norm"][l])
        gc[:, 80 + 2 * l:82 + 2 * l] = col(inp["xmem_q_gain"][l])
        gc[:, 84 + 2 * l:86 + 2 * l] = col(inp["xmem_k_gain"][l])
    gc[:, 88:92] = col(inp["mla_q_lora_gain"][0])
    gc[:, 92:94] = col(inp["mla_kv_lora_gain"][0])
    perm = np.arange(96)
    perm[64:80] = np.arange(80, 96)
    perm[80:96] = np.arange(64, 80)
    gc[:96, 94] = inp["mla_q_gain"][0]
    gc[:96, 95] = inp["mla_k_gain"][0]
    gc[:96, 96] = inp["mla_q_gain"][0][perm]
    gc[:96, 97] = inp["mla_k_gain"][0][perm]
    half = 16
    invf = (f32(10000.0) ** (-np.arange(half, dtype=f32) / f32(half))).astype(f32)
    gc[64:80, 98] = invf
    gc[80:96, 98] = invf
    gc[64:80, 99] = -1.0
    gc[80:96, 99] = 1.0
    gc[:, 100] = EPS
    gc[:, 101] = np.pi / 2
    gc[:, 102] = 1.0
    bc0 = np.zeros((128, 1536), f32)
    bc0[:, 0:512] = inp["sgu_ln_gain"][0][None, :]
    bc0[:, 512:1024] = inp["sgu_ln_bias"][0][None, :]
    sb = inp["sgu_b"][0]
    for cc in range(4):
        bc0[0:64, 1024 + cc * 128:1024 + (cc + 1) * 128] = sb[2 * cc][None, :]
        bc0[64:128, 1024 + cc * 128:1024 + (cc + 1) * 128] = sb[2 * cc + 1][None, :]
    trim = (np.arange(128)[:, None] <= np.arange(128)[None, :]).astype(f32)
    sguwT = np.ascontiguousarray(inp["sgu_w"][0].transpose(2, 0, 1)).reshape(128, 8 * 128)
    common = {"consts_bf": cb, "ident_f": ident, "gcols": gc, "bc0": bc0, "trimask": trim, "sgu_wT": sguwT}
    for l in range(2):
        for which in ("pre", "post"):
            common["w_%s_gu_%d" % (which, l)] = np.ascontiguousarray(inp["ffn_%s_w_gu" % which][l])
            common["w_%s_d_%d" % (which, l)] = np.ascontiguousarray(inp["ffn_%s_w_down" % which][l])
        common["w_xq_%d" % l] = np.ascontiguousarray(inp["xmem_wq"][l])
        wkv = inp["xmem_wkv"][l].reshape(D, 4, 2, 256)
        common["w_xk_%d" % l] = np.ascontiguousarray(wkv[:, :, 0, :].reshape(D, D))
        common["w_xv_%d" % l] = np.ascontiguousarray(wkv[:, :, 1, :].reshape(D, D))
        common["w_xo_%d" % l] = np.ascontiguousarray(inp["xmem_wo"][l])
    common["w_sbg_in"] = np.ascontiguousarray(inp["sbg_w_in"][0])
    common["w_sbg_out"] = np.ascontiguousarray(inp["sbg_w_out"][0])
    common["w_mla_in"] = np.ascontiguousarray(inp["mla_w_in"][0])
    uq = inp["mla_w_uq"][0]
    common["w_mla_uq"] = np.ascontiguousarray(uq)
    uq_h = uq.reshape(512, 16, 96)
    common["w_mla_uq_sw"] = np.ascontiguousarray(np.concatenate([uq_h[:, :, 80:96], uq_h[:, :, 64:80]], axis=2).reshape(512, 512))
    ukv = inp["mla_w_ukv"][0].reshape(256, 16, 128)
    common["w_mla_uk"] = np.ascontiguousarray(ukv[:, :, 0:64].reshape(256, 1024))
    common["w_mla_uv"] = np.ascontiguousarray(ukv[:, :, 64:128].reshape(256, 1024))
    common["w_mla_out"] = np.ascontiguousarray(inp["mla_w_out"][0])
    xs = shard_tokens(inp["x"])
    pos = inp["positions"].astype(np.int32)
    in_maps = []
    for c in range(NCORES):
        b, r = c // 4, c % 4
        blocks = [zig(t, r) for t in range(NT)]
        pl = pos[b].reshape(64, 128)[blocks].reshape(T)
        m = dict(common)
        m["x_in"] = xs[c]
        m["masks"] = make_masks(r)
        m["posrep"] = np.ascontiguousarray(np.broadcast_to(pl[None, :], (32, T))).astype(np.int32)
        m["mem"] = np.ascontiguousarray(inp["mem"][b])
        in_maps.append(m)
    return in_maps


EXCH = "cc"
DEBUG_X = {}


def _run(key, in_maps):
    if key not in _PROG_CACHE:
        _PROG_CACHE[key] = build_program(*key)
    nc, used, outs = _PROG_CACHE[key]
    maps = [{k: v for k, v in m.items() if k in used} for m in in_maps]
    res = run_bass_kernel_spmd(nc, maps, core_ids=list(range(NCORES)))
    return res.results


def _gather(results, name, rc):
    out = []
    for c in range(NCORES):
        b = c // 4
        rows = results[c][name].shape[0]
        parts = []
        for i in range(rows // rc):
            for rk in range(4):
                parts.append(results[4 * b + rk][name][i * rc:(i + 1) * rc])
        out.append(np.ascontiguousarray(np.concatenate(parts, axis=0)))
    return out


def kernel(**inputs):
    inp = {k: np.asarray(v) for k, v in inputs.items()}
    in_maps = prepare_inputs(inp)
    if EXCH == "cc":
        res = _run((0, 10, "cc"), in_maps)
        return unshard_tokens([r["x_out"] for r in res])
    r1 = _run((0, 2, "host"), in_maps)
    kg0, vg0 = _gather(r1, "k0_d", RC_K0), _gather(r1, "v0_d", RC_V0)
    for c in range(NCORES):
        in_maps[c].update({"x_in": r1[c]["x_out"], "q0_d": r1[c]["q0_d"], "osg_d": r1[c]["osg_d"], "kg0_d": kg0[c], "vg0_d": vg0[c]})
    r2 = _run((2, 7, "host"), in_maps)
    DEBUG_X["p2"] = [r["x_out"] for r in r2]
    kg1, vg1 = _gather(r2, "k1_d", RC_K1), _gather(r2, "v1_d", RC_V1)
    for c in range(NCORES):
        in_maps[c].update({"x_in": r2[c]["x_out"], "q1_d": r2[c]["q1_d"], "kg1_d": kg1[c], "vg1_d": vg1[c]})
    r3 = _run((7, 10, "host"), in_maps)
    return unshard_tokens([r["x_out"] for r in r3])
```

```python
import types
import numpy as np
import ml_dtypes
import concourse.bass as bass
import concourse.mybir as mybir
from concourse.bass_utils import run_bass_kernel_spmd

F32 = mybir.dt.float32
BF16 = mybir.dt.bfloat16
I32 = mybir.dt.int32
AF = mybir.ActivationFunctionType
ALU = mybir.AluOpType

NCORES = 8
D = 1024
DFF = 2816
T = 2048
NT = 16
EPS = 1e-6
NEG = -30000.0


class Buf:
    __slots__ = ("name", "lw", "rd", "dkey", "dcnt", "mw")

    def __init__(self, name):
        self.name = name
        self.lw = None
        self.rd = {}
        self.dkey = None
        self.dcnt = 0
        self.mw = {}


ENGS = ("pe", "act", "dve", "pool", "sp")


class Sched:
    def __init__(self, nc):
        self.nc = nc
        self.prog = {e: [] for e in ENGS}
        self.cnt = {e: 0 for e in ENGS}
        self.waited = {e: {} for e in ENGS}
        self.sems = {}
        self.alltoks = {}
        self.dfree = []
        self.dcum = {}
        self.dheld = []

    def sem(self, key):
        if key not in self.sems:
            self.sems[key] = self.nc.alloc_semaphore(name="s_%s" % (str(key).replace(" ", "")))
        return self.sems[key]

    def _wait(self, eng, tok):
        key, val = tok
        if key == eng and eng in ("pe", "sp"):
            return
        if self.waited[eng].get(key, 0) >= val:
            return
        self.waited[eng][key] = val
        h = self.sem(key)
        self.prog[eng].append(lambda E, h=h, val=val: E.wait_ge(h, val))

    def _deps(self, eng, reads, writes, pwrites=()):
        for b in reads:
            if b.lw is not None:
                self._wait(eng, b.lw)
            for k, v in b.mw.items():
                self._wait(eng, (k, v))
        for b in pwrites:
            if b.lw is not None and b.lw[0] != eng:
                self._wait(eng, b.lw)
            for k, v in b.rd.items():
                if k != eng:
                    self._wait(eng, (k, v))
        for b in writes:
            if b.lw is not None and b.lw[0] != eng:
                self._wait(eng, b.lw)
            for k, v in b.mw.items():
                self._wait(eng, (k, v))
            for k, v in b.rd.items():
                if k != eng:
                    self._wait(eng, (k, v))

    def _upd(self, tok, reads, writes):
        self.alltoks[tok[0]] = max(self.alltoks.get(tok[0], 0), tok[1])
        for b in reads:
            if b.rd.get(tok[0], 0) < tok[1]:
                b.rd[tok[0]] = tok[1]
        for b in writes:
            b.lw = tok
            b.rd = {}
            b.mw = {}

    @staticmethod
    def _freeze(fn):
        if fn.__closure__:
            cells = tuple(types.CellType(c.cell_contents) for c in fn.__closure__)
            g = types.FunctionType(fn.__code__, fn.__globals__, fn.__name__, fn.__defaults__, cells)
            g.__kwdefaults__ = fn.__kwdefaults__
            return g
        return fn

    def op(self, eng, fn, reads=(), writes=()):
        fn = self._freeze(fn)
        self._deps(eng, reads, writes)
        self.cnt[eng] += 1
        tok = (eng, self.cnt[eng])
        h = self.sem(eng)
        self.prog[eng].append(lambda E, fn=fn, h=h: fn(E).then_inc(h, 1))
        self._upd(tok, reads, writes)
        return tok

    def dma(self, q, out_ap, in_ap, owner, reads=(), writes=(), pwrites=()):
        self._deps(q, reads, writes, pwrites)
        if owner.dkey is None:
            if self.dfree:
                owner.dkey = self.dfree.pop()
            else:
                owner.dkey = "q%d" % len(self.dcum)
                self.dcum[owner.dkey] = 0
            self.dheld.append(owner)
        self.dcum[owner.dkey] += 16
        tok = (owner.dkey, self.dcum[owner.dkey])
        h = self.sem(owner.dkey)
        self.prog[q].append(lambda E, o=out_ap, i=in_ap, h=h: E.dma_start(out=o, in_=i).then_inc(h, 16))
        self._upd(tok, reads, writes)
        for b in pwrites:
            if b.mw.get(tok[0], 0) < tok[1]:
                b.mw[tok[0]] = tok[1]
        return tok

    def barrier(self, release=True):
        for e in ENGS:
            for k, v in list(self.alltoks.items()):
                self._wait(e, (k, v))
        if release:
            for b in self.dheld:
                self.dfree.append(b.dkey)
                b.dkey = None
            self.dheld = []

    def run(self):
        nc = self.nc
        with nc.Block() as block:
            @block.tensor
            def _(E):
                for f in self.prog["pe"]:
                    f(E)

            @block.scalar
            def _(E):
                for f in self.prog["act"]:
                    f(E)

            @block.vector
            def _(E):
                for f in self.prog["dve"]:
                    f(E)

            @block.gpsimd
            def _(E):
                for f in self.prog["pool"]:
                    f(E)

            @block.sync
            def _(E):
                for f in self.prog["sp"]:
                    f(E)


class Rot:
    def __init__(self, slots):
        self.slots = slots
        self.i = 0

    def next(self):
        s = self.slots[self.i % len(self.slots)]
        self.i += 1
        return s


class Ctx:
    pass


def mk_tile(nc, name, shape, dt):
    return nc.alloc_sbuf_tensor(name, shape, dt)


import os
DBG = os.environ.get('KDBG', '')
RC_K0, RC_V0, RC_K1, RC_V1 = 256, 1024, 192, 512
GC = dict(ffn_pre=0, ffn_post=8, mix=32, xn=48, xmn=64, xqg=80, xkg=84, qlora=88, kvlora=92,
          mqg=94, mkg=95, mqg_sw=96, mkg_sw=97, invf=98, sgn=99, eps=100, halfpi=101, one=102, zero=103)
NGC = 104
TWO_PI = float(2 * np.pi)
PI = float(np.pi)


PHASES = ["F0pre", "A0", "B0", "X0", "F0post", "F1pre", "A1", "B1", "X1", "F1post"]


def build_program(lo=0, hi=10, exch="cc"):
    nc = bass.Bass("TRN2", target_bir_lowering=False)
    S = Sched(nc)
    outputs = ["x_out"]

    used_inputs = set()

    def din(name, shape, dt=F32):
        used_inputs.add(name)
        return nc.dram_tensor(name, list(shape), dt, kind="ExternalInput").ap()

    x_in = din("x_in", [T, D])
    x_out = nc.dram_tensor("x_out", [T, D], F32, kind="ExternalOutput").ap()
    consts_bf = din("consts_bf", [128, 4 * 128], BF16)
    ident_f_d = din("ident_f", [128, 128])
    gcols_d = din("gcols", [128, NGC])
    bc0_d = din("bc0", [128, 1536])
    trim_d = din("trimask", [128, 128])
    sguwT_d = din("sgu_wT", [128, 8 * 128])
    masks_d = din("masks", [128, 16 * 128], BF16)
    posrep_d = din("posrep", [32, T], I32)
    mem_d = din("mem", [256, D])
    WSHAPES = {"sbg_in": [D, 2560], "sbg_out": [D, D], "mla_in": [D, 800], "mla_uq": [512, 1536], "mla_uq_sw": [512, 512],
               "mla_uk": [256, 1024], "mla_uv": [256, 1024], "mla_out": [D, D]}

    class LazyW(dict):
        def __missing__(self, key):
            if isinstance(key, tuple) and key[0] in ("gu", "d"):
                name = "w_%s_%s_%d" % (key[1], key[0], key[2])
                shape = [D, 2 * DFF] if key[0] == "gu" else [DFF, D]
            elif isinstance(key, tuple):
                name = "w_%s_%d" % key
                shape = [D, D]
            else:
                name = "w_" + key
                shape = WSHAPES[key]
            self[key] = din(name, shape)
            return self[key]

    Wd = LazyW()
    def scratch(name, shape, prod, cons):
        if exch == "cc" or (lo <= prod < hi and lo <= cons < hi):
            return nc.dram_tensor(name, shape, BF16).ap()
        if lo <= prod < hi:
            outputs.append(name)
            return nc.dram_tensor(name, shape, BF16, kind="ExternalOutput").ap()
        if lo <= cons < hi:
            return din(name, shape, BF16)
        return None

    gp = 1 if exch == "cc" else -1
    q0_d = scratch("q0_d", [512, T], 1, 2)
    k0_d = scratch("k0_d", [512, T], 1, 2 if exch == "cc" else 99)
    v0_d = scratch("v0_d", [T, 512], 1, 2 if exch == "cc" else 99)
    osg_d = scratch("osg_d", [512, T], 1, 2)
    kg0_d = scratch("kg0_d", [4 * 512, T], gp, 2)
    vg0_d = scratch("vg0_d", [4 * T, 512], gp, 2)
    gp = 6 if exch == "cc" else -1
    q1_d = scratch("q1_d", [1536, T], 6, 7)
    k1_d = scratch("k1_d", [1536, T], 6, 7 if exch == "cc" else 99)
    v1_d = scratch("v1_d", [T, 1024], 6, 7 if exch == "cc" else 99)
    kg1_d = scratch("kg1_d", [4 * 1536, T], gp, 7)
    vg1_d = scratch("vg1_d", [4 * T, 1024], gp, 7)
    db = {n: Buf(n) for n in ("q0", "k0", "v0", "osg", "kg0", "vg0", "q1", "k1", "v1", "kg1", "vg1")}

    xT = nc.alloc_sbuf_tensor("xT", [128, 8, T], F32)
    xTb = [[Buf("xT_%d_%d" % (c, g)) for g in range(4)] for c in range(8)]
    cbf = nc.alloc_sbuf_tensor("cbf", [128, 4 * 128], BF16)
    cbf_b = Buf("cbf")
    ident_f = nc.alloc_sbuf_tensor("sb_ident_f", [128, 128], F32)
    ident_f_b = Buf("ident_f")
    gcols = nc.alloc_sbuf_tensor("sb_gcols", [128, NGC], F32)
    gcols_b = Buf("gcols")
    ident_bf = cbf[:, 0:128]
    ones_bf = cbf[:, 128:256]
    negU_bf = cbf[:, 256:384]
    swap_bf = cbf[:, 384:512]

    def gcol(name, i=0, p0=0, p1=128):
        return gcols[p0:p1, GC[name] + i:GC[name] + i + 1]

    S.dma("sp", cbf[:, :], consts_bf[:, :], cbf_b, writes=[cbf_b])
    S.dma("sp", ident_f[:, :], ident_f_d[:, :], ident_f_b, writes=[ident_f_b])
    S.dma("sp", gcols[:, :], gcols_d[:, :], gcols_b, writes=[gcols_b])

    posi = nc.alloc_sbuf_tensor("posi", [128, 512], I32)
    posi_b = Buf("posi")
    nint = nc.alloc_sbuf_tensor("nint", [128, 512], I32)
    nint_b = Buf("nint")
    AF32 = 11776
    ABF = 45056
    arena_f = nc.alloc_sbuf_tensor("arena_f", [128, AF32], F32)
    arena_h = nc.alloc_sbuf_tensor("arena_h", [128, ABF], BF16)
    psum = [nc.alloc_psum_tensor("ps%d" % i, [128, 512], F32) for i in range(8)]

    class Carver:
        def __init__(self, ar, size):
            self.ar, self.size, self.off = ar, size, 0

        def reset(self):
            self.off = 0

        def take(self, n):
            assert self.off + n <= self.size, ("arena overflow", self.off, n, self.size)
            a = self.ar[:, self.off:self.off + n]
            self.off += n
            return a

    cf = Carver(arena_f, AF32)
    ch = Carver(arena_h, ABF)
    nbuf = [0]

    def B(name="b"):
        nbuf[0] += 1
        return Buf("%s%d" % (name, nbuf[0]))

    def rotf(n, size, view=None):
        return Rot([((cf.take(size) if view is None else view(cf.take(size))), B()) for _ in range(n)])

    def roth(n, size, view=None):
        return Rot([((ch.take(size) if view is None else view(ch.take(size))), B()) for _ in range(n)])

    def psrot(idxs):
        return Rot([(psum[i], B("ps")) for i in idxs])

    W = Ctx()

    def wsetup(nst=2, stsize=2048):
        W.st = rotf(nst, stsize)

    castctr = [0]

    def cast(dst, dst_b, src, src_b):
        castctr[0] += 1
        if castctr[0] % 2 == 0:
            S.op("act", lambda E: E.copy(dst, src), reads=[src_b], writes=[dst_b])
        else:
            S.op("dve", lambda E: E.tensor_copy(dst, src), reads=[src_b], writes=[dst_b])

    def wload(view, kc, n, dst, dst_b, eng=None):
        assert kc * n <= 2048
        st, st_b = W.st.next()
        sv = st[:, 0:kc * n].rearrange("p (c n) -> p c n", c=kc)
        S.dma("sp", sv, view, st_b, writes=[st_b])
        cast(dst, dst_b, sv, st_b)

    def rstd_from_ss(out_ap, out_b, ps_ap, ps_b, n, tmp_ap, tmp_b, p0=0, p1=128):
        S.op("act", lambda E: E.activation(tmp_ap, ps_ap, AF.Ln, bias=gcol("eps", 0, p0, p1), scale=1.0 / n),
             reads=[ps_b, gcols_b], writes=[tmp_b])
        S.op("act", lambda E: E.activation(out_ap, tmp_ap, AF.Exp, scale=-0.5), reads=[tmp_b], writes=[out_b])

    def load_x():
        cf.reset()
        rot = rotf(2, 1024)
        psb = [B("ps"), B("ps")]
        for t in range(NT):
            ap, b = rot.next()
            S.dma("sp", ap, x_in[t * 128:(t + 1) * 128, :], b, writes=[b])
            for half in range(2):
                pb, ps = psb[half], psum[half]
                for cc in range(4):
                    c = half * 4 + cc
                    S.op("pe", lambda E, o=ps[:, cc * 128:(cc + 1) * 128], i=ap[:, c * 128:(c + 1) * 128]:
                         E.transpose(o, i, ident_f[:, :]), reads=[b, ident_f_b], writes=[pb])
                g = t // 4
                outap = xT[:, half * 4:half * 4 + 4, t * 128:(t + 1) * 128]
                inap = ps[:, :].rearrange("p (c n) -> p c n", c=4)
                wr = [xTb[half * 4 + cc][g] for cc in range(4)]
                if half == 0:
                    S.op("act", lambda E, o=outap, i=inap: E.copy(o, i), reads=[pb], writes=wr)
                else:
                    S.op("dve", lambda E, o=outap, i=inap: E.tensor_copy(o, i), reads=[pb], writes=wr)
        S.barrier()

    def store_x():
        cf.reset()
        rot = rotf(2, 1024)
        psb = [B("ps"), B("ps")]
        toks = []
        for t in range(NT):
            ap, b = rot.next()
            g = t // 4
            for half in range(2):
                pb, ps = psb[half], psum[half]
                for cc in range(4):
                    c = half * 4 + cc
                    S.op("pe", lambda E, o=ps[:, cc * 128:(cc + 1) * 128], i=xT[:, c, t * 128:(t + 1) * 128]:
                         E.transpose(o, i, ident_f[:, :]), reads=[xTb[c][g], ident_f_b], writes=[pb])
                if half == 0:
                    S.op("act", lambda E, o=ap[:, 0:512], i=ps[:, :]: E.copy(o, i), reads=[pb], writes=[b])
                else:
                    S.op("dve", lambda E, o=ap[:, 512:1024], i=ps[:, :]: E.tensor_copy(o, i), reads=[pb], writes=[b])
            toks.append(S.dma("sp", x_out[t * 128:(t + 1) * 128, :], ap, b, reads=[b]))
        for tk in toks:
            S._wait("sp", tk)
        S.barrier()

    def norm_prep(gname, gi0, grp, xb_ap, xb_b, rstd_ap, rstd_b, sq_ap, sq_b, ps_i, ps_b, tmp_ap, tmp_b, tokps=None):
        tsl = slice(grp * 512, (grp + 1) * 512)
        xbufs = [xTb[c][grp] for c in range(8)]
        S.op("act", lambda E: E.activation(sq_ap, xT[:, :, tsl], AF.Square), reads=xbufs, writes=[sq_b])
        ps = psum[ps_i]
        for c in range(8):
            S.op("pe", lambda E, c=c: E.matmul(ps[:, :], ones_bf, sq_ap[:, c, :], start=(c == 0), stop=(c == 7)),
                 reads=[sq_b, cbf_b], writes=[ps_b])
        rstd_from_ss(rstd_ap, rstd_b, ps[:, :], ps_b, D, tmp_ap, tmp_b)
        if tokps is not None:
            tp_ap, tp_b = tokps
            for tt in range(4):
                for c in range(8):
                    S.op("pe", lambda E, c=c, tt=tt: E.matmul(tp_ap[:, grp * 4 + tt:grp * 4 + tt + 1],
                                                              sq_ap[:, c, tt * 128:(tt + 1) * 128], ones_bf[:, 0:1],
                                                              start=(c == 0), stop=(c == 7)),
                         reads=[sq_b, cbf_b], writes=[tp_b])
        for c in range(8):
            S.op("dve", lambda E, c=c: E.tensor_scalar(xb_ap[:, c, :], xT[:, c, tsl], gcol(gname, gi0 + c), None, ALU.mult),
                 reads=[xTb[c][grp], gcols_b], writes=[xb_b])

    def ffn(wgu_d, wd_d, gname, gi0):
        cf.reset()
        ch.reset()
        rstd = [cf.take(512) for _ in range(2)]
        rstd_b = [B() for _ in range(2)]
        tmp, tmp_b = cf.take(512), B()
        stg_gu = rotf(2, 2048, lambda a: a.rearrange("p (a c m) -> p a c m", a=2, c=8))
        stg_d = rotf(2, 1408, lambda a: a.rearrange("p (k m) -> p k m", k=11))
        a_rot, s_rot, t_rot = rotf(2, 512), rotf(2, 512), rotf(2, 512)
        xb = [ch.take(4096).rearrange("p (c n) -> p c n", c=8) for _ in range(2)]
        xb_b = [B() for _ in range(2)]
        hid = ch.take(22 * 1024).rearrange("p (k n) -> p k n", k=22)
        hid_b = [[B() for g in range(2)] for k in range(22)]
        wgu = roth(2, 2048, lambda a: a.rearrange("p (a c m) -> p a c m", a=2, c=8))
        wd = roth(2, 2816, lambda a: a.rearrange("p (k m) -> p k m", k=22))
        sq, sq_b = ch.take(4096).rearrange("p (c n) -> p c n", c=8), B()
        ps_ss_b = B("ps")
        ps_g, ps_u, ps_a = psrot([1, 2]), psrot([3, 4]), psrot([5, 6])
        for half in range(2):
            for gi in range(2):
                norm_prep(gname, gi0, half * 2 + gi, xb[gi], xb_b[gi], rstd[gi], rstd_b[gi], sq, sq_b, 0, ps_ss_b, tmp, tmp_b)
            wgu_v = wgu_d.rearrange("(c p) n -> p c n", p=128)
            for j in range(22):
                st, st_b = stg_gu.next()
                S.dma("sp", st[:, 0], wgu_v[:, :, j * 128:(j + 1) * 128], st_b, writes=[st_b])
                S.dma("sp", st[:, 1], wgu_v[:, :, DFF + j * 128:DFF + (j + 1) * 128], st_b, writes=[st_b])
                wt, wt_b = wgu.next()
                cast(wt, wt_b, st, st_b)
                for gi in range(2):
                    pg, pg_b = ps_g.next()
                    pu, pu_b = ps_u.next()
                    for c in range(8):
                        S.op("pe", lambda E, c=c, pg=pg, wt=wt, gi=gi: E.matmul(pg[:, :], wt[:, 0, c, :], xb[gi][:, c, :],
                                                                                 start=(c == 0), stop=(c == 7)),
                             reads=[wt_b, xb_b[gi]], writes=[pg_b])
                    for c in range(8):
                        S.op("pe", lambda E, c=c, pu=pu, wt=wt, gi=gi: E.matmul(pu[:, :], wt[:, 1, c, :], xb[gi][:, c, :],
                                                                                 start=(c == 0), stop=(c == 7)),
                             reads=[wt_b, xb_b[gi]], writes=[pu_b])
                    a, a_b = a_rot.next()
                    s, s_b = s_rot.next()
                    t, t_b = t_rot.next()
                    S.op("dve", lambda E, a=a, pg=pg, gi=gi: E.tensor_tensor(a, pg[:, :], rstd[gi], ALU.mult),
                         reads=[pg_b, rstd_b[gi]], writes=[a_b])
                    S.op("act", lambda E, s=s, a=a: E.activation(s, a, AF.Silu), reads=[a_b], writes=[s_b])
                    S.op("dve", lambda E, t=t, pu=pu, gi=gi: E.tensor_tensor(t, pu[:, :], rstd[gi], ALU.mult),
                         reads=[pu_b, rstd_b[gi]], writes=[t_b])
                    S.op("pool", lambda E, s=s, t=t, j=j, gi=gi: E.tensor_tensor(hid[:, j, gi * 512:(gi + 1) * 512], s, t, ALU.mult),
                         reads=[s_b, t_b], writes=[hid_b[j][gi]])
            wd_v = wd_d.rearrange("(k p) n -> p k n", p=128)
            for i in range(8):
                wt, wt_b = wd.next()
                for hh in range(2):
                    st, st_b = stg_d.next()
                    S.dma("sp", st, wd_v[:, hh * 11:(hh + 1) * 11, i * 128:(i + 1) * 128], st_b, writes=[st_b])
                    cast(wt[:, hh * 11:(hh + 1) * 11, :], wt_b, st, st_b)
                for gi in range(2):
                    pa, pa_b = ps_a.next()
                    for k in range(22):
                        S.op("pe", lambda E, k=k, pa=pa, wt=wt, gi=gi: E.matmul(pa[:, :], wt[:, k, :], hid[:, k, gi * 512:(gi + 1) * 512],
                                                                                 start=(k == 0), stop=(k == 21)),
                             reads=[wt_b, hid_b[k][gi]], writes=[pa_b])
                    grp = half * 2 + gi
                    xs = xT[:, i, grp * 512:(grp + 1) * 512]
                    S.op("dve", lambda E, xs=xs, pa=pa: E.scalar_tensor_tensor(xs, pa[:, :], 0.5, xs, ALU.mult, ALU.add),
                         reads=[pa_b, xTb[i][grp]], writes=[xTb[i][grp]])
        S.barrier()

    def out_proj(mix, mix_b, w_dram, wfm, ps_o):
        wv = w_dram.rearrange("(c p) n -> p c n", p=128)
        for i in range(8):
            wt, wt_b = wfm.next()
            wload(wv[:, :, i * 128:(i + 1) * 128], 8, 128, wt, wt_b)
            for g in range(4):
                po, po_b = ps_o.next()
                for c in range(8):
                    mb = mix_b[c][g] if isinstance(mix_b[c], list) else mix_b[c]
                    S.op("pe", lambda E, c=c, po=po, wt=wt, g=g: E.matmul(po[:, :], wt[:, c, :], mix[c][:, g * 512:(g + 1) * 512],
                                                                           start=(c == 0), stop=(c == 7)),
                         reads=[wt_b, mb], writes=[po_b])
                xs = xT[:, i, g * 512:(g + 1) * 512]
                S.op("dve", lambda E, xs=xs, po=po: E.tensor_tensor(xs, po[:, :], xs, ALU.add),
                     reads=[po_b, xTb[i][g]], writes=[xTb[i][g]])

    def gelu(x_ap, x_b, out_ap, out_b, ga, gb):
        (a, a_b), (b, b_b) = ga, gb
        S.op("act", lambda E: E.activation(a, x_ap, AF.Square), reads=[x_b], writes=[a_b])
        S.op("dve", lambda E: E.tensor_scalar(a, a, 0.044715, 1.0, ALU.mult, ALU.add), reads=[a_b], writes=[a_b])
        S.op("dve", lambda E: E.tensor_tensor(a, a, x_ap, ALU.mult), reads=[a_b, x_b], writes=[a_b])
        S.op("act", lambda E: E.activation(b, a, AF.Sigmoid, scale=1.5957691216057308), reads=[a_b], writes=[b_b])
        S.op("pool", lambda E: E.tensor_tensor(out_ap, x_ap, b, ALU.mult), reads=[x_b, b_b], writes=[out_b])

    def allgather(src_ap, src_b, dst_ap, dst_b, rows, rc):
        S._deps("pool", [src_b], [dst_b])
        h = S.sem("cc")
        S.cnt.setdefault("cc", 0)
        for i in range(rows // rc):
            S.cnt["cc"] += 1
            S.prog["pool"].append(lambda E, i=i: E.collective_compute("AllGather", ALU.bypass, replica_groups=[[0, 1, 2, 3], [4, 5, 6, 7]],
                                                                      ins=[src_ap[i * rc:(i + 1) * rc, :]],
                                                                      outs=[dst_ap[i * 4 * rc:(i + 1) * 4 * rc, :]]).then_inc(h, 1))
        S._upd(("cc", S.cnt["cc"]), [src_b], [dst_b])

    def grow(rc, rk, row):
        return (row // rc) * 4 * rc + rk * rc + (row % rc)

    def phase_A0():
        cf.reset()
        ch.reset()
        rstd = [cf.take(512) for _ in range(4)]
        rstd_b = [B() for _ in range(4)]
        tmp, tmp_b = cf.take(512), B()
        rtok, rtok_b = cf.take(16), B()
        rtmp, rtmp_b = cf.take(16), B()
        wsetup()
        zt = rotf(2, 512)
        ga, gb = (cf.take(512), B()), (cf.take(512), B())
        gl = (cf.take(512), B())
        bc, bc_b = cf.take(1536), B()
        trim, trim_b = cf.take(128), B()
        st6, st6_b = cf.take(8), B()
        mv, mv_b = cf.take(4), B()
        xb = ch.take(16384).rearrange("p (c n) -> p c n", c=8)
        xb_b = [B() for _ in range(4)]
        sq, sq_b = ch.take(4096).rearrange("p (c n) -> p c n", c=8), B()
        Wv, Wv_b = ch.take(4096).rearrange("p (c n) -> p c n", c=8), B()
        Wzg, Wzg_b = ch.take(4096).rearrange("p (c n) -> p c n", c=8), B()
        wfm = roth(2, 1024, lambda a: a.rearrange("p (c n) -> p c n", c=8))
        uT = ch.take(8192).rearrange("p (c n) -> p c n", c=4)
        uT_b = [[B() for g in range(4)] for c in range(4)]
        WsT, WsT_b = ch.take(1024).rearrange("p (g n) -> p g n", g=8), B()
        ev = roth(2, 512)
        vev = roth(2, 512)
        gtok = roth(2, 512)
        osg = roth(2, 512, lambda a: a.rearrange("p (c n) -> p c n", c=4))
        ps_ss_b = B("ps")
        tokps = (psum[7][:, 0:16], B("ps"))
        ps_p = psrot([1, 2, 3])
        ps_m = psrot([4, 5])
        S.dma("sp", bc, bc0_d[:, :], bc_b, writes=[bc_b])
        S.dma("sp", trim, trim_d[:, :], trim_b, writes=[trim_b])
        for hh in range(4):
            st, st_b = W.st.next()
            S.dma("sp", st[:, 0:256], sguwT_d[:, hh * 256:(hh + 1) * 256], st_b, writes=[st_b])
            for k in range(2):
                gidx = hh * 2 + k
                S.op("dve", lambda E, st=st, k=k, gidx=gidx: E.tensor_tensor(WsT[:, gidx, :], st[:, k * 128:(k + 1) * 128], trim, ALU.mult),
                     reads=[st_b, trim_b], writes=[WsT_b])
        for g in range(4):
            norm_prep("mix", 0, g, xb[:, :, g * 512:(g + 1) * 512], xb_b[g], rstd[g], rstd_b[g], sq, sq_b, 0, ps_ss_b, tmp, tmp_b, tokps=tokps)
        rstd_from_ss(rtok, rtok_b, tokps[0], tokps[1], D, rtmp, rtmp_b)
        win = Wd["sbg_in"].rearrange("(c p) n -> p c n", p=128)
        for hh in range(2):
            wload(win[:, hh * 4:(hh + 1) * 4, 1024:1536], 4, 512, Wv[:, hh * 4:(hh + 1) * 4, :], Wv_b)
            wload(win[:, hh * 4:(hh + 1) * 4, 2048:2560], 4, 512, Wzg[:, hh * 4:(hh + 1) * 4, :], Wzg_b)
        for oc in list(range(0, 8)) + list(range(12, 16)):
            wt, wt_b = wfm.next()
            wload(win[:, :, oc * 128:(oc + 1) * 128], 8, 128, wt, wt_b)
            for g in range(4):
                pp, pp_b = ps_p.next()
                for c in range(8):
                    S.op("pe", lambda E, c=c, pp=pp, wt=wt, g=g: E.matmul(pp[:, :], wt[:, c, :], xb[:, c, g * 512:(g + 1) * 512],
                                                                           start=(c == 0), stop=(c == 7)),
                         reads=[wt_b, xb_b[g]], writes=[pp_b])
                if oc < 8:
                    e, e_b = ev.next()
                    sc = 0.125 if oc < 4 else 1.0
                    S.op("dve", lambda E, e=e, pp=pp, g=g, sc=sc: E.scalar_tensor_tensor(e, pp[:, :], sc, rstd[g], ALU.mult, ALU.mult),
                         reads=[pp_b, rstd_b[g]], writes=[e_b])
                    dst, dn = (q0_d, "q0") if oc < 4 else (k0_d, "k0")
                    r0 = (oc % 4) * 128
                    S.dma("sp", dst[r0:r0 + 128, g * 512:(g + 1) * 512], e, e_b, reads=[e_b], pwrites=[db[dn]])
                else:
                    z, z_b = zt.next()
                    S.op("dve", lambda E, z=z, pp=pp, g=g: E.tensor_tensor(z, pp[:, :], rstd[g], ALU.mult),
                         reads=[pp_b, rstd_b[g]], writes=[z_b])
                    gelu(z, z_b, uT[:, oc - 12, g * 512:(g + 1) * 512], uT_b[oc - 12][g], ga, gb)
        for t in range(NT):
            g = t // 4
            pp, pp_b = ps_p.next()
            for c in range(8):
                S.op("pe", lambda E, c=c, pp=pp, t=t: E.matmul(pp[:, :], xb[:, c, t * 128:(t + 1) * 128], Wv[:, c, :],
                                                               start=(c == 0), stop=(c == 7)),
                     reads=[Wv_b, xb_b[g]], writes=[pp_b])
            e, e_b = vev.next()
            S.op("dve", lambda E, e=e, pp=pp, t=t: E.tensor_scalar(e, pp[:, :], rtok[:, t:t + 1], None, ALU.mult),
                 reads=[pp_b, rtok_b], writes=[e_b])
            S.dma("sp", v0_d[t * 128:(t + 1) * 128, :], e, e_b, reads=[e_b], pwrites=[db["v0"]])
            pp, pp_b = ps_p.next()
            for c in range(8):
                S.op("pe", lambda E, c=c, pp=pp, t=t: E.matmul(pp[:, :], xb[:, c, t * 128:(t + 1) * 128], Wzg[:, c, :],
                                                               start=(c == 0), stop=(c == 7)),
                     reads=[Wzg_b, xb_b[g]], writes=[pp_b])
            z, z_b = zt.next()
            S.op("dve", lambda E, z=z, pp=pp, t=t: E.tensor_scalar(z, pp[:, :], rtok[:, t:t + 1], None, ALU.mult),
                 reads=[pp_b, rtok_b], writes=[z_b])
            gelu(z, z_b, gl[0], gl[1], ga, gb)
            S.op("dve", lambda E: E.bn_stats(st6[:, 0:6], gl[0]), reads=[gl[1]], writes=[st6_b])
            S.op("dve", lambda E: E.bn_aggr(mv[:, 0:2], st6[:, 0:6]), reads=[st6_b], writes=[mv_b])
            S.op("act", lambda E: E.activation(mv[:, 2:3], mv[:, 1:2], AF.Ln, bias=gcol("eps")), reads=[mv_b, gcols_b], writes=[mv_b])
            S.op("act", lambda E: E.activation(mv[:, 3:4], mv[:, 2:3], AF.Exp, scale=-0.5), reads=[mv_b], writes=[mv_b])
            S.op("dve", lambda E: E.tensor_scalar(gl[0], gl[0], mv[:, 0:1], mv[:, 3:4], ALU.subtract, ALU.mult),
                 reads=[gl[1], mv_b], writes=[gl[1]])
            S.op("dve", lambda E: E.tensor_tensor(gl[0], gl[0], bc[:, 0:512], ALU.mult), reads=[gl[1], bc_b], writes=[gl[1]])
            gt, gt_b = gtok.next()
            S.op("dve", lambda E, gt=gt: E.tensor_tensor(gt, gl[0], bc[:, 512:1024], ALU.add), reads=[gl[1], bc_b], writes=[gt_b])
            pm, pm_b = ps_m.next()
            for grp in range(8):
                hb = 64 * (grp % 2)
                cc = grp // 2
                S.op("pe", lambda E, pm=pm, gt=gt, grp=grp, hb=hb, cc=cc: E.matmul(pm[hb:hb + 64, cc * 128:(cc + 1) * 128],
                                                                                  gt[:, grp * 64:(grp + 1) * 64], WsT[:, grp, :],
                                                                                  start=True, stop=True),
                     reads=[gt_b, WsT_b], writes=[pm_b])
            z2, z2_b = zt.next()
            S.op("dve", lambda E, z2=z2, pm=pm: E.tensor_tensor(z2, pm[:, :], bc[:, 1024:1536], ALU.add), reads=[pm_b, bc_b], writes=[z2_b])
            og, og_b = osg.next()
            S.op("pool", lambda E, og=og, z2=z2, t=t: E.tensor_tensor(og, z2.rearrange("p (c n) -> p c n", c=4),
                                                                     uT[:, :, t * 128:(t + 1) * 128], ALU.mult),
                 reads=[z2_b] + [uT_b[c][g] for c in range(4)], writes=[og_b])
            S.dma("sp", osg_d.rearrange("(c p) n -> p c n", p=128)[:, :, t * 128:(t + 1) * 128], og, og_b, reads=[og_b], pwrites=[db["osg"]])
        S.barrier()
        if exch == "cc":
            allgather(k0_d, db["k0"], kg0_d, db["kg0"], 512, RC_K0)
            allgather(v0_d, db["v0"], vg0_d, db["vg0"], T, RC_V0)
        S.barrier()

    def key_loc(j):
        tj, mj = j // 4, j % 4
        rk = mj if tj % 2 == 0 else 3 - mj
        return rk, tj, mj

    def phase_B0():
        cf.reset()
        ch.reset()
        Rs = [(cf.take(512), B()) for _ in range(2)]
        ttr = rotf(4, 512)
        e2r = rotf(4, 512)
        wsetup()
        Kr = roth(2, 8192)
        Vc, Vc_b = ch.take(8192).rearrange("p (r t d) -> p r t d", r=4, t=16), B()
        qr = roth(2, 2048)
        Lpr = roth(4, 512)
        wvr = roth(6, 512)
        osb = ch.take(8192).rearrange("p (c n) -> p c n", c=4)
        osb_b = [[B() for g in range(4)] for c in range(4)]
        msk, msk_b = ch.take(1024).rearrange("p (v n) -> p v n", v=8), B()
        wfm = roth(2, 1024, lambda a: a.rearrange("p (c n) -> p c n", c=8))
        ps_e, ps_t, ps_o = psrot([0, 1, 2, 3]), psrot([4, 5]), psrot([6, 7])
        S.dma("sp", msk, masks_d[:, 0:1024].rearrange("p (v n) -> p v n", v=8), msk_b, writes=[msk_b])
        for c in range(4):
            Kc, Kc_b = Kr.next()
            for rk in range(4):
                S.dma("sp", Kc[:, rk * T:(rk + 1) * T], kg0_d[grow(RC_K0, rk, c * 128):grow(RC_K0, rk, c * 128) + 128, :], Kc_b,
                      reads=[db["kg0"]], writes=[Kc_b])
                for tq in range(4):
                    S.dma("sp", Vc[:, rk, tq * 4:(tq + 1) * 4, :],
                          vg0_d[grow(RC_V0, rk, tq * 512):grow(RC_V0, rk, tq * 512) + 512, c * 128:(c + 1) * 128].rearrange("(t p) d -> p t d", p=128), Vc_b,
                          reads=[db["vg0"]], writes=[Vc_b])
            qc, qc_b = qr.next()
            S.dma("sp", qc, q0_d[c * 128:(c + 1) * 128, :], qc_b, reads=[db["q0"]], writes=[qc_b])
            for u in range(4):
                js = list(range(16 * u + 15, -1, -1))
                pos_ = [ps_o.next() for _ in range(2)]
                st2q, st3q = [[], []], [[], []]
                for hh in range(2):
                    S.op("pool", lambda E: E.memset(Rs[hh][0], 0.0), writes=[Rs[hh][1]])
                for idx, j in enumerate(js):
                    rk, tj, mj = key_loc(j)
                    kcol = rk * T + tj * 128
                    diag = j >= 16 * u
                    col0, vidx = 0, 0
                    if diag:
                        m = (j - 16 * u) // 4
                        col0 = 128 * m
                        vidx = ((4 * u + m) % 2) * 4 + mj
                    for hh in range(2):
                        hb = 64 * hh
                        R, R_b = Rs[hh]
                        po, po_b = pos_[hh]
                        qsl = qc[hb:hb + 64, u * 512 + col0:(u + 1) * 512]
                        ksl = Kc[hb:hb + 64, kcol:kcol + 128]
                        pE, pE_b = ps_e.next()
                        tt, tt_b = ttr.next()
                        Lp, Lp_b = Lpr.next()
                        S.op("pe", lambda E: E.matmul(pE[:, col0:], ksl, qsl, start=True, stop=False), reads=[Kc_b, qc_b], writes=[pE_b])
                        if diag:
                            S.op("pe", lambda E: E.matmul(pE[:, col0:col0 + 128], ident_bf, msk[:, vidx, :], start=False, stop=False, skip_group_check=True),
                                 reads=[cbf_b, msk_b], writes=[pE_b])
                        S.op("act", lambda E: E.activation(tt[:, col0:], pE[:, col0:], AF.Exp), reads=[pE_b], writes=[tt_b])
                        S.op("act", lambda E: E.activation(Lp[:, col0:], tt[:, col0:], AF.Ln, bias=gcol("one")), reads=[tt_b, gcols_b], writes=[Lp_b])

                        def stage2(pE=pE, pE_b=pE_b, Lp=Lp, Lp_b=Lp_b, col0=col0, rk=rk, tj=tj, idx=idx, j=j, hh=hh, hb=hb, R=R, R_b=R_b, po=po, po_b=po_b):
                            pT, pT_b = ps_t.next()
                            e2, e2_b = e2r.next()
                            wv, wv_b = wvr.next()
                            S.op("pe", lambda E: E.matmul(pE[:, col0:], negU_bf, Lp[:, col0:], start=False, stop=True, skip_group_check=True),
                                 reads=[cbf_b, Lp_b], writes=[pE_b])
                            S.op("pe", lambda E: E.matmul(pT[:, col0:], ones_bf, Lp[:, col0:], start=True, stop=True), reads=[cbf_b, Lp_b], writes=[pT_b])
                            S.op("dve", lambda E: E.tensor_tensor(e2[:, col0:], pE[:, col0:], R[:, col0:], ALU.subtract), reads=[pE_b, R_b], writes=[e2_b])
                            S.op("act", lambda E: E.activation(wv[:, col0:], e2[:, col0:], AF.Exp), reads=[e2_b], writes=[wv_b])
                            S.op("dve", lambda E: E.tensor_tensor(R[:, col0:], pT[:, col0:], R[:, col0:], ALU.add), reads=[pT_b, R_b], writes=[R_b])

                            def stage3():
                                S.op("pe", lambda E: E.matmul(po[hb:hb + 64, col0:], Vc[:, rk, tj, hb:hb + 64], wv[:, col0:], start=(idx == 0), stop=(j == 0),
                                                              skip_group_check=True),
                                     reads=[Vc_b, wv_b], writes=[po_b])
                            st3q[hh].append(stage3)
                        st2q[hh].append(stage2)
                        if len(st2q[hh]) > 1:
                            st2q[hh].pop(0)()
                        if len(st3q[hh]) > 1:
                            st3q[hh].pop(0)()
                for hh in range(2):
                    while st2q[hh]:
                        st2q[hh].pop(0)()
                for hh in range(2):
                    while st3q[hh]:
                        st3q[hh].pop(0)()
                for hh in range(2):
                    hb = 64 * hh
                    po, po_b = pos_[hh]
                    S.op("dve", lambda E: E.tensor_copy(osb[hb:hb + 64, c, u * 512:(u + 1) * 512], po[hb:hb + 64, :]),
                         reads=[po_b], writes=[osb_b[c][u]])
        Kc, Kc_b = Kr.next()
        osgs = Kc.rearrange("p (c n) -> p c n", c=4)
        S.dma("sp", osgs, osg_d.rearrange("(c p) n -> p c n", p=128), Kc_b, reads=[db["osg"]], writes=[Kc_b])
        mix = [osb[:, c, :] for c in range(4)] + [osgs[:, c, :] for c in range(4)]
        mix_b = [osb_b[c] for c in range(4)] + [Kc_b] * 4
        out_proj(mix, mix_b, Wd["sbg_out"], wfm, psrot([0, 1, 2]))
        S.barrier()

    def phase_X(l):
        cf.reset()
        ch.reset()
        rstd1, rstd1_b = cf.take(512), B()
        tmp, tmp_b = cf.take(512), B()
        wsetup()
        mst = rotf(1, 1024)
        msmall, msmall_b = cf.take(8), B()
        qraw = rotf(2, 1024, lambda a: a.rearrange("p (c n) -> p c n", c=2))
        rq = rotf(2, 512)
        kraw = rotf(1, 512, lambda a: a.rearrange("p (c n) -> p c n", c=2))
        rk_ = rotf(2, 256)
        rcp = rotf(2, 512)
        xb = ch.take(4096).rearrange("p (c n) -> p c n", c=8)
        xb1_b = B()
        sq, sq_b = ch.take(4096).rearrange("p (c n) -> p c n", c=8), B()
        hmT, hmT_b = ch.take(2048).rearrange("p (c n) -> p c n", c=8), B()
        mrow = roth(1, 1024)
        kT, kT_b = ch.take(2048).rearrange("p (h c n) -> p h c n", h=4, c=2), B()
        Vm, Vm_b = ch.take(2048).rearrange("p (m n) -> p m n", m=2), B()
        wfm = roth(2, 1024, lambda a: a.rearrange("p (c n) -> p c n", c=8))
        Wv_ = roth(1, 4096, lambda a: a.rearrange("p (c n) -> p c n", c=8))
        qn = roth(2, 1024, lambda a: a.rearrange("p (c n) -> p c n", c=2))
        sqq = roth(2, 1024, lambda a: a.rearrange("p (c n) -> p c n", c=2))
        Pm = roth(2, 1024, lambda a: a.rearrange("p (m n) -> p m n", m=2))
        oT = ch.take(4096).rearrange("p (c n) -> p c n", c=8)
        oT_b = [B() for c in range(8)]
        ps_ss_b = B("ps")
        ps_p = psrot([1, 2])
        ps_q = psrot([3])
        ps_s2 = psrot([4, 5])
        ps_o2 = psrot([6, 7])
        for mt in range(2):
            ms, ms_b = mst.next()
            S.dma("sp", ms, mem_d[mt * 128:(mt + 1) * 128, :], ms_b, writes=[ms_b])
            mr, mr_b = mrow.next()
            S.op("act", lambda E, ms=ms, mr=mr: E.activation(mr, ms, AF.Square, accum_out=msmall[:, 0:1]), reads=[ms_b], writes=[mr_b, msmall_b])
            S.op("act", lambda E: E.activation(msmall[:, 1:2], msmall[:, 0:1], AF.Ln, bias=gcol("eps"), scale=1.0 / D),
                 reads=[msmall_b, gcols_b], writes=[msmall_b])
            S.op("act", lambda E: E.activation(msmall[:, 2:3], msmall[:, 1:2], AF.Exp, scale=-0.5), reads=[msmall_b], writes=[msmall_b])
            S.op("dve", lambda E, ms=ms: E.tensor_scalar(ms, ms, msmall[:, 2:3], None, ALU.mult), reads=[ms_b, msmall_b], writes=[ms_b])
            for half in range(2):
                pp, pp_b = ps_p.next()
                for cc in range(4):
                    c = half * 4 + cc
                    S.op("pe", lambda E, pp=pp, ms=ms, c=c, cc=cc: E.transpose(pp[:, cc * 128:(cc + 1) * 128], ms[:, c * 128:(c + 1) * 128], ident_f[:, :]),
                         reads=[ms_b, ident_f_b], writes=[pp_b])
                for cc in range(4):
                    c = half * 4 + cc
                    S.op("dve", lambda E, pp=pp, c=c, cc=cc, mt=mt: E.tensor_scalar(hmT[:, c, mt * 128:(mt + 1) * 128], pp[:, cc * 128:(cc + 1) * 128],
                                                                                     gcol("xmn", 8 * l + c), None, ALU.mult),
                         reads=[pp_b, gcols_b], writes=[hmT_b])
        wk = Wd[("xk", l)].rearrange("(c p) n -> p c n", p=128)
        for hd in range(4):
            kr_, kr_b = kraw.next()
            sk, sk_b = sqq.next()
            for dc in range(2):
                oc = hd * 2 + dc
                wt, wt_b = wfm.next()
                wload(wk[:, :, oc * 128:(oc + 1) * 128], 8, 128, wt, wt_b)
                pp, pp_b = ps_p.next()
                for c in range(8):
                    S.op("pe", lambda E, c=c, pp=pp, wt=wt: E.matmul(pp[:, 0:256], wt[:, c, :], hmT[:, c, :], start=(c == 0), stop=(c == 7)),
                         reads=[wt_b, hmT_b], writes=[pp_b])
                S.op("act", lambda E, kr_=kr_, pp=pp, dc=dc: E.copy(kr_[:, dc, :], pp[:, 0:256]), reads=[pp_b], writes=[kr_b])
                S.op("act", lambda E, sk=sk, pp=pp, dc=dc: E.activation(sk[:, dc, 0:256], pp[:, 0:256], AF.Square), reads=[pp_b], writes=[sk_b])
            pq, pq_b = ps_q.next()
            for dc in range(2):
                S.op("pe", lambda E, pq=pq, sk=sk, dc=dc: E.matmul(pq[:, 0:256], ones_bf, sk[:, dc, 0:256], start=(dc == 0), stop=(dc == 1)),
                     reads=[sk_b, cbf_b], writes=[pq_b])
            rr, rr_b = rk_.next()
            rstd_from_ss(rr, rr_b, pq[:, 0:256], pq_b, 256, tmp[:, 0:256], tmp_b)
            for dc in range(2):
                S.op("dve", lambda E, kr_=kr_, rr=rr, dc=dc, hd=hd: E.scalar_tensor_tensor(kT[:, hd, dc, :], kr_[:, dc, :], gcol("xkg", 2 * l + dc), rr,
                                                                                           ALU.mult, ALU.mult),
                     reads=[kr_b, rr_b, gcols_b], writes=[kT_b])
        wvv = Wd[("xv", l)].rearrange("(c p) n -> p c n", p=128)
        for nh in range(2):
            wt, wt_b = Wv_.next()
            for hh in range(2):
                wload(wvv[:, hh * 4:(hh + 1) * 4, nh * 512:(nh + 1) * 512], 4, 512, wt[:, hh * 4:(hh + 1) * 4, :], wt_b)
            for mt in range(2):
                pp, pp_b = ps_p.next()
                for c in range(8):
                    S.op("pe", lambda E, c=c, pp=pp, wt=wt, mt=mt: E.matmul(pp[:, :], hmT[:, c, mt * 128:(mt + 1) * 128], wt[:, c, :],
                                                                             start=(c == 0), stop=(c == 7)),
                         reads=[wt_b, hmT_b], writes=[pp_b])
                S.op("act", lambda E, pp=pp, mt=mt, nh=nh: E.copy(Vm[:, mt, nh * 512:(nh + 1) * 512], pp[:, :]), reads=[pp_b], writes=[Vm_b])
        wq = Wd[("xq", l)].rearrange("(c p) n -> p c n", p=128)
        wo = Wd[("xo", l)].rearrange("(c p) n -> p c n", p=128)
        ps_x = psrot([1, 2])
        for g in range(4):
            norm_prep("xn", 8 * l, g, xb, xb1_b, rstd1, rstd1_b, sq, sq_b, 0, ps_ss_b, tmp, tmp_b)
            for hd in range(4):
                qr_, qr_b = qraw.next()
                sk, sk_b = sqq.next()
                for dc in range(2):
                    oc = hd * 2 + dc
                    wt, wt_b = wfm.next()
                    wload(wq[:, :, oc * 128:(oc + 1) * 128], 8, 128, wt, wt_b)
                    pp, pp_b = ps_p.next()
                    for c in range(8):
                        S.op("pe", lambda E, c=c, pp=pp, wt=wt: E.matmul(pp[:, :], wt[:, c, :], xb[:, c, :], start=(c == 0), stop=(c == 7)),
                             reads=[wt_b, xb1_b], writes=[pp_b])
                    S.op("dve", lambda E, qr_=qr_, pp=pp, dc=dc: E.tensor_tensor(qr_[:, dc, :], pp[:, :], rstd1, ALU.mult),
                         reads=[pp_b, rstd1_b], writes=[qr_b])
                    S.op("act", lambda E, sk=sk, qr_=qr_, dc=dc: E.activation(sk[:, dc, :], qr_[:, dc, :], AF.Square), reads=[qr_b], writes=[sk_b])
                pq, pq_b = ps_q.next()
                for dc in range(2):
                    S.op("pe", lambda E, pq=pq, sk=sk, dc=dc: E.matmul(pq[:, :], ones_bf, sk[:, dc, :], start=(dc == 0), stop=(dc == 1)),
                         reads=[sk_b, cbf_b], writes=[pq_b])
                rr, rr_b = rq.next()
                rstd_from_ss(rr, rr_b, pq[:, :], pq_b, 256, tmp, tmp_b)
                qq, qq_b = qn.next()
                for dc in range(2):
                    S.op("dve", lambda E, qq=qq, qr_=qr_, rr=rr, dc=dc: E.scalar_tensor_tensor(qq[:, dc, :], qr_[:, dc, :], gcol("xqg", 2 * l + dc), rr,
                                                                                              ALU.mult, ALU.mult),
                         reads=[qr_b, rr_b, gcols_b], writes=[qq_b])
                pm_, pm_b = Pm.next()
                for mt in range(2):
                    pS, pS_b = ps_s2.next()
                    for dc in range(2):
                        S.op("pe", lambda E, pS=pS, qq=qq, dc=dc, mt=mt, hd=hd: E.matmul(pS[:, :], kT[:, hd, dc, mt * 128:(mt + 1) * 128], qq[:, dc, :],
                                                                                         start=(dc == 0), stop=(dc == 1)),
                             reads=[kT_b, qq_b], writes=[pS_b])
                    S.op("act", lambda E, pm_=pm_, pS=pS, mt=mt: E.activation(pm_[:, mt, :], pS[:, :], AF.Exp, scale=1.0 / 16.0),
                         reads=[pS_b], writes=[pm_b])
                pq, pq_b = ps_q.next()
                for mt in range(2):
                    S.op("pe", lambda E, pq=pq, pm_=pm_, mt=mt: E.matmul(pq[:, :], ones_bf, pm_[:, mt, :], start=(mt == 0), stop=(mt == 1)),
                         reads=[pm_b, cbf_b], writes=[pq_b])
                rc, rc_b = rcp.next()
                S.op("dve", lambda E, rc=rc, pq=pq: E.reciprocal(rc, pq[:, :]), reads=[pq_b], writes=[rc_b])
                for dc in range(2):
                    po, po_b = ps_o2.next()
                    for mt in range(2):
                        S.op("pe", lambda E, po=po, pm_=pm_, mt=mt, dc=dc, hd=hd: E.matmul(po[:, :], Vm[:, mt, hd * 256 + dc * 128:hd * 256 + (dc + 1) * 128],
                                                                                           pm_[:, mt, :], start=(mt == 0), stop=(mt == 1)),
                             reads=[Vm_b, pm_b], writes=[po_b])
                    S.op("dve", lambda E, po=po, rc=rc, hd=hd, dc=dc: E.tensor_tensor(oT[:, hd * 2 + dc, :], po[:, :], rc, ALU.mult),
                         reads=[po_b, rc_b], writes=[oT_b[hd * 2 + dc]])
            for i in range(8):
                wt, wt_b = wfm.next()
                wload(wo[:, :, i * 128:(i + 1) * 128], 8, 128, wt, wt_b)
                po, po_b = ps_x.next()
                for c in range(8):
                    S.op("pe", lambda E, c=c, po=po, wt=wt: E.matmul(po[:, :], wt[:, c, :], oT[:, c, :], start=(c == 0), stop=(c == 7)),
                         reads=[wt_b, oT_b[c]], writes=[po_b])
                xs = xT[:, i, g * 512:(g + 1) * 512]
                S.op("dve", lambda E, xs=xs, po=po: E.tensor_tensor(xs, po[:, :], xs, ALU.add), reads=[po_b, xTb[i][g]], writes=[xTb[i][g]])
        S.barrier()

    def sin_reduced(out_ap, out_b, ang_ap, ang_b, wk, wk_b, nf, nf_b, p0, p1, add=0.0):
        ni = nint[p0:p1, :]
        S.op("dve", lambda E: E.tensor_scalar(wk, ang_ap, float(add), None, ALU.add), reads=[ang_b], writes=[wk_b])
        S.op("dve", lambda E: E.tensor_scalar(ni, wk, float(1 / TWO_PI), None, ALU.mult), reads=[wk_b], writes=[nint_b])
        S.op("dve", lambda E: E.tensor_copy(nf, ni), reads=[nint_b], writes=[nf_b])
        S.op("dve", lambda E: E.scalar_tensor_tensor(wk, nf, -TWO_PI, wk, ALU.mult, ALU.add), reads=[nf_b, wk_b], writes=[wk_b])
        S.op("dve", lambda E: E.tensor_scalar(nf, wk, PI, TWO_PI, ALU.is_gt, ALU.mult), reads=[wk_b], writes=[nf_b])
        S.op("dve", lambda E: E.tensor_tensor(wk, wk, nf, ALU.subtract), reads=[nf_b, wk_b], writes=[wk_b])
        S.op("dve", lambda E: E.tensor_scalar(nf, wk, -PI, TWO_PI, ALU.is_lt, ALU.mult), reads=[wk_b], writes=[nf_b])
        S.op("dve", lambda E: E.tensor_tensor(wk, wk, nf, ALU.add), reads=[nf_b, wk_b], writes=[wk_b])
        S.op("act", lambda E: E.activation(out_ap, wk, AF.Sin), reads=[wk_b], writes=[out_b])

    class _Stop(Exception):
        pass

    def _stage(n):
        for k in range(1, 9):
            if ('a1s%d' % k) in DBG and n >= k:
                raise _Stop()

    def phase_A1():
        try:
            phase_A1_()
        except _Stop:
            S.barrier()

    def phase_A1_():
        cf.reset()
        ch.reset()
        rstd1, rstd1_b = cf.take(512), B()
        tmpr = rotf(2, 512)
        tmp, tmp_b = tmpr.slots[0]
        wsetup(2, 1024)
        tabs = [cf.take(512) for _ in range(4)]
        tab_b = [B() for _ in range(4)]
        cqraw, cqraw_b = cf.take(2048).rearrange("p (c n) -> p c n", c=4), B()
        far = rotf(2, 512)
        fbr = rotf(2, 512)
        fa, fa_b = far.slots[0]
        fb, fb_b = fbr.slots[0]
        fc, fc_b = cf.take(512), B()
        kra, kra_b = cf.take(512), B()
        rqk = rotf(2, 512)
        xb = ch.take(4096).rearrange("p (c n) -> p c n", c=8)
        xb_b = B()
        sq, sq_b = ch.take(4096).rearrange("p (c n) -> p c n", c=8), B()
        Win, Win_b = ch.take(6400).rearrange("p (c n) -> p c n", c=8), B()
        Wuq, Wuq_b = ch.take(6144).rearrange("p (c n) -> p c n", c=4), B()
        Wqs, Wqs_b = ch.take(2048).rearrange("p (c n) -> p c n", c=4), B()
        Wuk, Wuk_b = ch.take(2048).rearrange("p (c n) -> p c n", c=2), B()
        Wuv, Wuv_b = ch.take(2048).rearrange("p (c n) -> p c n", c=2), B()
        cqn, cqn_b = ch.take(2048).rearrange("p (c n) -> p c n", c=4), B()
        ckvn, ckvn_b = ch.take(1024).rearrange("p (c n) -> p c n", c=2), B()
        krb, krb_b = ch.take(512), B()
        sqh = ch.take(512)
        sqh_lo, sqh_hi = B(), B()
        sqqr = roth(2, 512)
        qst = roth(2, 512)
        vst = roth(1, 512)
        ps_ss_b = B("ps")
        ps_p = psrot([1, 2])
        ps_a = psrot([3, 4])
        ps_b2 = psrot([5])
        ps_r = psrot([6, 7])
        ps_q = ps_r

        def wl(src, kc, ncol, dst, dst_b):
            v = src.rearrange("(c p) n -> p c n", p=128)
            step = max(128, (1024 // kc) // 128 * 128) if ncol >= 128 else ncol
            for n0 in range(0, ncol, step):
                n1 = min(ncol, n0 + step)
                wload(v[:, :, n0:n1], kc, n1 - n0, dst[:, :, n0:n1], dst_b)

        wl(Wd["mla_in"], 8, 768, Win, Win_b)
        wload(Wd["mla_in"].rearrange("(c p) n -> p c n", p=128)[:, :, 768:800], 8, 32, Win[:, :, 768:800], Win_b)
        wl(Wd["mla_uq"], 4, 1536, Wuq, Wuq_b)
        wl(Wd["mla_uq_sw"], 4, 512, Wqs, Wqs_b)
        wl(Wd["mla_uk"], 2, 1024, Wuk, Wuk_b)
        wl(Wd["mla_uv"], 2, 1024, Wuv, Wuv_b)

        P0, P1 = 64, 96
        for pass_ in ("kv", "q"):
            for g in range(1 if 'a1small' in DBG else 4):
                gs = slice(g * 512, (g + 1) * 512)
                norm_prep("mix", 8, g, xb, xb_b, rstd1, rstd1_b, sq, sq_b, 0, ps_ss_b, tmp, tmp_b)
                S.dma("sp", posi[P0:P1, :], posrep_d[:, gs], posi_b, writes=[posi_b])
                S.op("dve", lambda E: E.tensor_copy(fa[P0:P1, :], posi[P0:P1, :]), reads=[posi_b], writes=[fa_b])
                S.op("dve", lambda E: E.tensor_scalar(fa[P0:P1, :], fa[P0:P1, :], gcol("invf", 0, P0, P1), None, ALU.mult),
                     reads=[fa_b, gcols_b], writes=[fa_b])
                sin_reduced(tabs[0][P0:P1, :], tab_b[0], fa[P0:P1, :], fa_b, fb[P0:P1, :], fb_b, fc[P0:P1, :], fc_b, P0, P1, add=PI / 2)
                sin_reduced(tabs[1][P0:P1, :], tab_b[1], fa[P0:P1, :], fa_b, fb[P0:P1, :], fb_b, fc[P0:P1, :], fc_b, P0, P1, add=0.0)
                S.op("dve", lambda E: E.tensor_scalar(tabs[2][P0:P1, :], tabs[0][P0:P1, :], gcol("mkg", 0, P0, P1), None, ALU.mult),
                     reads=[tab_b[0], gcols_b], writes=[tab_b[2]])
                S.op("dve", lambda E: E.tensor_scalar(tabs[3][P0:P1, :], tabs[1][P0:P1, :], gcol("sgn", 0, P0, P1), gcol("mkg_sw", 0, P0, P1), ALU.mult, ALU.mult),
                     reads=[tab_b[1], gcols_b], writes=[tab_b[3]])
                S.op("dve", lambda E: E.tensor_scalar(tabs[0][P0:P1, :], tabs[0][P0:P1, :], gcol("mqg", 0, P0, P1), None, ALU.mult),
                     reads=[tab_b[0], gcols_b], writes=[tab_b[0]])
                S.op("dve", lambda E: E.tensor_scalar(tabs[1][P0:P1, :], tabs[1][P0:P1, :], gcol("sgn", 0, P0, P1), gcol("mqg_sw", 0, P0, P1), ALU.mult, ALU.mult),
                     reads=[tab_b[1], gcols_b], writes=[tab_b[1]])

                for (n_ch, col0, dst, dst_b, gname, nfeat) in (((4, 0, cqn, cqn_b, "qlora", 512),) if pass_ == "q" else ((2, 512, ckvn, ckvn_b, "kvlora", 256),)):
                    for oc in range(n_ch):
                        pp, pp_b = ps_p.next()
                        for c in range(8):
                            S.op("pe", lambda E, c=c, pp=pp, oc=oc, col0=col0: E.matmul(pp[:, :], Win[:, c, col0 + oc * 128:col0 + (oc + 1) * 128], xb[:, c, :],
                                                                                        start=(c == 0), stop=(c == 7)),
                                 reads=[Win_b, xb_b], writes=[pp_b])
                        S.op("dve", lambda E, pp=pp, oc=oc: E.tensor_tensor(cqraw[:, oc, :], pp[:, :], rstd1, ALU.mult),
                             reads=[pp_b, rstd1_b], writes=[cqraw_b])
                        S.op("act", lambda E, oc=oc: E.activation(sq[:, oc, :], cqraw[:, oc, :], AF.Square), reads=[cqraw_b], writes=[sq_b])
                    pq, pq_b = ps_q.next()
                    for oc in range(n_ch):
                        S.op("pe", lambda E, pq=pq, oc=oc, n_ch=n_ch: E.matmul(pq[:, :], ones_bf, sq[:, oc, :], start=(oc == 0), stop=(oc == n_ch - 1)),
                             reads=[sq_b, cbf_b], writes=[pq_b])
                    rr, rr_b = rqk.next()
                    rstd_from_ss(rr, rr_b, pq[:, :], pq_b, nfeat, tmp, tmp_b)
                    for oc in range(n_ch):
                        S.op("dve", lambda E, oc=oc, dst=dst, rr=rr, gname=gname: E.scalar_tensor_tensor(dst[:, oc, :], cqraw[:, oc, :], gcol(gname, oc), rr, ALU.mult, ALU.mult),
                             reads=[cqraw_b, rr_b, gcols_b], writes=[dst_b])

                if pass_ == "kv":
                    pp, pp_b = ps_p.next()
                    for c in range(8):
                        S.op("pe", lambda E, c=c, pp=pp: E.matmul(pp[0:32, :], Win[:, c, 768:800], xb[:, c, :], start=(c == 0), stop=(c == 7)),
                             reads=[Win_b, xb_b], writes=[pp_b])
                    S.op("dve", lambda E, pp=pp: E.tensor_tensor(krb[0:32, :], pp[0:32, :], rstd1[0:32, :], ALU.mult), reads=[pp_b, rstd1_b], writes=[krb_b])
                    if os.environ.get("KSUB") == "1":
                        raise _Stop()
                    pa, pa_b = ps_a.next()
                    pb_, pb_b = ps_b2.next()
                    S.op("pe", lambda E, pa=pa: E.matmul(pa[P0:P1, :], ident_bf[0:32, 0:32], krb[0:32, :], start=True, stop=True), reads=[krb_b, cbf_b], writes=[pa_b])
                    S.op("pe", lambda E, pb_=pb_: E.matmul(pb_[P0:P1, :], swap_bf[0:32, 0:32], krb[0:32, :], start=True, stop=True), reads=[krb_b, cbf_b], writes=[pb_b])
                    if os.environ.get("KSUB") == "2":
                        raise _Stop()
                    S.op("act", lambda E, pa=pa: E.activation(sqh[P0:P1, :], pa[P0:P1, :], AF.Square), reads=[pa_b], writes=[sqh_hi])
                    if os.environ.get("KSUB") == "3":
                        raise _Stop()
                    S.op("act", lambda E, pa=pa: E.copy(kra[P0:P1, :], pa[P0:P1, :]), reads=[pa_b], writes=[kra_b])
                    S.op("act", lambda E, pb_=pb_: E.copy(fb[P0:P1, :], pb_[P0:P1, :]), reads=[pb_b], writes=[fb_b])
                    S.op("dve", lambda E: E.tensor_tensor(kra[P0:P1, :], kra[P0:P1, :], tabs[2][P0:P1, :], ALU.mult), reads=[kra_b, tab_b[2]], writes=[kra_b])
                    S.op("dve", lambda E: E.tensor_tensor(fb[P0:P1, :], fb[P0:P1, :], tabs[3][P0:P1, :], ALU.mult), reads=[fb_b, tab_b[3]], writes=[fb_b])
                    S.op("dve", lambda E: E.tensor_tensor(kra[P0:P1, :], kra[P0:P1, :], fb[P0:P1, :], ALU.add), reads=[kra_b, fb_b], writes=[kra_b])

                for h in range(2 if 'a1small' in DBG else 16):
                    if pass_ == "q":
                        pa, pa_b = ps_a.next()
                        pb_, pb_b = ps_b2.next()
                        for c in range(4):
                            S.op("pe", lambda E, c=c, pa=pa, h=h: E.matmul(pa[0:96, :], Wuq[:, c, h * 96:(h + 1) * 96], cqn[:, c, :], start=(c == 0), stop=(c == 3)),
                                 reads=[Wuq_b, cqn_b], writes=[pa_b])
                        for c in range(4):
                            S.op("pe", lambda E, c=c, pb_=pb_, h=h: E.matmul(pb_[P0:P1, :], Wqs[:, c, h * 32:(h + 1) * 32], cqn[:, c, :], start=(c == 0), stop=(c == 3)),
                                 reads=[Wqs_b, cqn_b], writes=[pb_b])
                        sqq, sqq_b = sqqr.next()
                        tq_, tq_b = tmpr.next()
                        fa, fa_b = far.next()
                        fb, fb_b = fbr.next()
                        S.op("act", lambda E, pa=pa: E.activation(sqq[0:96, :], pa[0:96, :], AF.Square), reads=[pa_b], writes=[sqq_b])
                        pr, pr_b = ps_r.next()
                        S.op("pe", lambda E, pr=pr: E.matmul(pr[0:96, :], ones_bf[0:96, 0:96], sqq[0:96, :], start=True, stop=True), reads=[sqq_b, cbf_b], writes=[pr_b])
                        rr, rr_b = rqk.next()
                        rstd_from_ss(rr[0:96, :], rr_b, pr[0:96, :], pr_b, 96, tq_[0:96, :], tq_b, 0, 96)
                        qs, qs_b = qst.next()
                        S.op("dve", lambda E, qs=qs, pa=pa, rr=rr: E.scalar_tensor_tensor(qs[0:64, :], pa[0:64, :], gcol("mqg", 0, 0, 64), rr[0:64, :], ALU.mult, ALU.mult),
                             reads=[pa_b, rr_b, gcols_b], writes=[qs_b])
                        S.op("act", lambda E, pa=pa: E.copy(fa[P0:P1, :], pa[P0:P1, :]), reads=[pa_b], writes=[fa_b])
                        S.op("act", lambda E, pb_=pb_: E.copy(fb[P0:P1, :], pb_[P0:P1, :]), reads=[pb_b], writes=[fb_b])
                        S.op("dve", lambda E: E.tensor_tensor(fa[P0:P1, :], fa[P0:P1, :], tabs[0][P0:P1, :], ALU.mult), reads=[fa_b, tab_b[0]], writes=[fa_b])
                        S.op("dve", lambda E: E.tensor_tensor(fb[P0:P1, :], fb[P0:P1, :], tabs[1][P0:P1, :], ALU.mult), reads=[fb_b, tab_b[1]], writes=[fb_b])
                        S.op("dve", lambda E: E.tensor_tensor(fa[P0:P1, :], fa[P0:P1, :], fb[P0:P1, :], ALU.add), reads=[fa_b, fb_b], writes=[fa_b])
                        S.op("dve", lambda E, qs=qs, rr=rr: E.tensor_tensor(qs[P0:P1, :], fa[P0:P1, :], rr[P0:P1, :], ALU.mult), reads=[fa_b, rr_b], writes=[qs_b])
                        S.dma("sp", q1_d[h * 96:(h + 1) * 96, gs], qs[0:96, :], qs_b, reads=[qs_b], pwrites=[db["q1"]])

                    if pass_ == "kv":
                        pa, pa_b = ps_a.next()
                        for c in range(2):
                            S.op("pe", lambda E, c=c, pa=pa, h=h: E.matmul(pa[0:64, :], Wuk[:, c, h * 64:(h + 1) * 64], ckvn[:, c, :], start=(c == 0), stop=(c == 1)),
                                 reads=[Wuk_b, ckvn_b], writes=[pa_b])
                        S.op("act", lambda E, pa=pa: E.activation(sqh[0:64, :], pa[0:64, :], AF.Square), reads=[pa_b], writes=[sqh_lo])
                        pr, pr_b = ps_r.next()
                        S.op("pe", lambda E, pr=pr: E.matmul(pr[0:96, :], ones_bf[0:96, 0:96], sqh[0:96, :], start=True, stop=True), reads=[sqh_lo, sqh_hi, cbf_b], writes=[pr_b])
                        rr, rr_b = rqk.next()
                        tq_, tq_b = tmpr.next()
                        rstd_from_ss(rr[0:96, :], rr_b, pr[0:96, :], pr_b, 96, tq_[0:96, :], tq_b, 0, 96)
                        ks, ks_b = qst.next()
                        S.op("dve", lambda E, ks=ks, pa=pa, rr=rr: E.scalar_tensor_tensor(ks[0:64, :], pa[0:64, :], gcol("mkg", 0, 0, 64), rr[0:64, :], ALU.mult, ALU.mult),
                             reads=[pa_b, rr_b, gcols_b], writes=[ks_b])
                        S.op("dve", lambda E, ks=ks, rr=rr: E.tensor_tensor(ks[P0:P1, :], kra[P0:P1, :], rr[P0:P1, :], ALU.mult), reads=[kra_b, rr_b], writes=[ks_b])
                        S.dma("sp", k1_d[h * 96:(h + 1) * 96, gs], ks[0:96, :], ks_b, reads=[ks_b], pwrites=[db["k1"]])

                if pass_ == "kv":
                    for tt in range(4):
                        t = g * 4 + tt
                        for half in range(2):
                            pp, pp_b = ps_p.next()
                            for c in range(2):
                                S.op("pe", lambda E, c=c, pp=pp, tt=tt, half=half: E.matmul(pp[:, :], ckvn[:, c, tt * 128:(tt + 1) * 128], Wuv[:, c, half * 512:(half + 1) * 512],
                                                                                             start=(c == 0), stop=(c == 1)),
                                     reads=[Wuv_b, ckvn_b], writes=[pp_b])
                            vs, vs_b = vst.next()
                            S.op("act", lambda E, vs=vs, pp=pp: E.copy(vs, pp[:, :]), reads=[pp_b], writes=[vs_b])
                            S.dma("sp", v1_d[t * 128:(t + 1) * 128, half * 512:(half + 1) * 512], vs, vs_b, reads=[vs_b], pwrites=[db["v1"]])
            if pass_ == "kv" and exch == "cc":
                allgather(k1_d, db["k1"], kg1_d, db["kg1"], 1536, RC_K1)
                allgather(v1_d, db["v1"], vg1_d, db["vg1"], T, RC_V1)
        S.barrier()

    o1_d = nc.dram_tensor("o1_d", [D, T], BF16).ap()
    o1_b = Buf("o1")

    def phase_B1():
        cf.reset()
        ch.reset()
        wsetup()
        tmpA, tmpA_b = cf.take(512), B()
        rcf, rcf_b = cf.take(512), B()
        rcs, rcs_b = cf.take(512), B()
        KK = ch.take(16384)
        Kr = Rot([(KK[:, 0:8192], B()), (KK[:, 8192:16384], B())])
        Vr = roth(2, 8192, lambda a: a.rearrange("p (r t d) -> p r t d", r=4, t=16))
        qr = roth(2, 2048)
        Pr = roth(5, 512)
        msk, msk_b = ch.take(1024).rearrange("p (v n) -> p v n", v=8), B()
        rhi, rhi_b = ch.take(512), B()
        rlo, rlo_b = ch.take(512), B()
        ost = roth(2, 512)
        wfm = roth(2, 1024, lambda a: a.rearrange("p (c n) -> p c n", c=8))
        ps_s = psrot([0, 1, 2, 6, 7])
        ps_o = psrot([3, 4])
        ps_r = psrot([5])
        S.dma("sp", msk, masks_d[:, 1024:2048].rearrange("p (v n) -> p v n", v=8), msk_b, writes=[msk_b])
        for (Vt, Vt_b) in Vr.slots:
            for rk in range(4):
                S.op("pool", lambda E: E.memset(Vt[:, rk, :, 64:128], 1.0), writes=[Vt_b])
        SC = float(96 ** -0.5)
        dsl = slice(64, 128)
        osl = slice(0, 64)
        def issue_loads(h):
            Kh, Kh_b = Kr.next()
            Vh, Vh_b = Vr.next()
            qh, qh_b = qr.next()
            S.dma("sp", qh[0:96, :], q1_d[h * 96:(h + 1) * 96, :], qh_b, reads=[db["q1"]], writes=[qh_b])
            for rk in range(4):
                S.dma("sp", Kh[0:96, rk * T:(rk + 1) * T], kg1_d[grow(RC_K1, rk, h * 96):grow(RC_K1, rk, h * 96) + 96, :], Kh_b,
                      reads=[db["kg1"]], writes=[Kh_b])
                for tq in range(4):
                    r0 = grow(RC_V1, rk, tq * 512)
                    S.dma("sp", Vh[:, rk, tq * 4:(tq + 1) * 4, 0:64],
                          vg1_d[r0:r0 + 512, h * 64:(h + 1) * 64].rearrange("(t p) d -> p t d", p=128), Vh_b, reads=[db["vg1"]], writes=[Vh_b])
            return Kh, Kh_b, Vh, Vh_b, qh, qh_b

        nxt = issue_loads(0)
        for h in range(16):
            Kh, Kh_b, Vh, Vh_b, qh, qh_b = nxt
            if h + 1 < 16:
                nxt = issue_loads(h + 1)
            for u in range(4):
                po, po_b = ps_o.next()
                nblk = 16 * u + 16
                pend = []
                for j in range(nblk):
                    rk, tj, mj = key_loc(j)
                    kcol = rk * T + tj * 128
                    diag = j >= 16 * u
                    col0, vidx = 0, 0
                    if diag:
                        m = (j - 16 * u) // 4
                        col0 = 128 * m
                        vidx = ((4 * u + m) % 2) * 4 + mj
                    pS, pS_b = ps_s.next()
                    S.op("pe", lambda E: E.matmul(pS[:, col0:], Kh[0:96, kcol:kcol + 128], qh[0:96, u * 512 + col0:(u + 1) * 512], start=True, stop=not diag),
                         reads=[Kh_b, qh_b], writes=[pS_b])
                    if diag:
                        S.op("pe", lambda E: E.matmul(pS[:, col0:col0 + 128], ident_bf, msk[:, vidx, :], start=False, stop=True, skip_group_check=True),
                             reads=[cbf_b, msk_b], writes=[pS_b])
                    P_, P_b = Pr.next()
                    S.op("act", lambda E: E.activation(P_[:, col0:], pS[:, col0:], AF.Exp, scale=SC), reads=[pS_b], writes=[P_b])

                    def pv(po=po, po_b=po_b, P_=P_, P_b=P_b, col0=col0, rk=rk, tj=tj, j=j, nblk=nblk, Vh=Vh, Vh_b=Vh_b):
                        S.op("pe", lambda E: E.matmul(po[:, col0:], Vh[:, rk, tj, :], P_[:, col0:], start=(j == 0), stop=(j == nblk - 1),
                                                      skip_group_check=True),
                             reads=[Vh_b, P_b], writes=[po_b])
                    pend.append(pv)
                    if len(pend) > 2:
                        pend.pop(0)()
                while pend:
                    pend.pop(0)()
                S.op("act", lambda E: E.activation(tmpA[dsl, :], po[dsl, :], AF.Ln), reads=[po_b], writes=[tmpA_b])
                S.op("act", lambda E: E.activation(rcf[dsl, :], tmpA[dsl, :], AF.Exp, scale=-1.0), reads=[tmpA_b], writes=[rcf_b])
                S.op("dve", lambda E: E.tensor_copy(rhi[dsl, :], rcf[dsl, :]), reads=[rcf_b], writes=[rhi_b])
                S.op("dve", lambda E: E.tensor_tensor(rlo[dsl, :], rcf[dsl, :], rhi[dsl, :], ALU.subtract), reads=[rcf_b, rhi_b], writes=[rlo_b])
                pr, pr_b = ps_r.next()
                S.op("pe", lambda E: E.matmul(pr[osl, :], ident_bf[dsl, 64:128], rhi[dsl, :], start=True, stop=False), reads=[cbf_b, rhi_b], writes=[pr_b])
                S.op("pe", lambda E: E.matmul(pr[osl, :], ident_bf[dsl, 64:128], rlo[dsl, :], start=False, stop=True), reads=[cbf_b, rlo_b], writes=[pr_b])
                S.op("act", lambda E: E.copy(rcs[osl, :], pr[osl, :]), reads=[pr_b], writes=[rcs_b])
                os_, os_b = ost.next()
                S.op("dve", lambda E: E.tensor_tensor(os_[osl, :], po[osl, :], rcs[osl, :], ALU.mult), reads=[po_b, rcs_b], writes=[os_b])
                S.dma("pool", o1_d[h * 64:(h + 1) * 64, u * 512:(u + 1) * 512], os_[osl, :], os_b, reads=[os_b], pwrites=[o1_b])
        S.barrier()
        oT = KK.rearrange("p (c n) -> p c n", c=8)
        oT_b = B()
        S.dma("sp", oT, o1_d.rearrange("(c p) n -> p c n", p=128), oT_b, reads=[o1_b], writes=[oT_b])
        out_proj([oT[:, c, :] for c in range(8)], [oT_b] * 8, Wd["mla_out"], wfm, psrot([0, 1, 2]))
        S.barrier()

    load_x()
    seq = [lambda: ffn(Wd[("gu", "pre", 0)], Wd[("d", "pre", 0)], "ffn_pre", 0),
           phase_A0, phase_B0, lambda: phase_X(0),
           lambda: ffn(Wd[("gu", "post", 0)], Wd[("d", "post", 0)], "ffn_post", 0),
           lambda: ffn(Wd[("gu", "pre", 1)], Wd[("d", "pre", 1)], "ffn_pre", 16),
           phase_A1, phase_B1, lambda: phase_X(1),
           lambda: ffn(Wd[("gu", "post", 1)], Wd[("d", "post", 1)], "ffn_post", 16)]
    for i in range(lo, hi):
        seq[i]()
    for n in db:
        if db[n].lw is not None:
            S._wait("sp", db[n].lw)
        for k_, v_ in db[n].mw.items():
            S._wait("sp", (k_, v_))
    store_x()
    S.run()
    print('[build] phases', lo, hi, 'counts', dict(S.cnt), 'nsems', len(S.sems), 'maxdma', max([0] + [v for k, v in S.alltoks.items() if str(k).startswith('d')]), flush=True)
    return nc, used_inputs, outputs


def zig(t, r):
    return 4 * t + (r if t % 2 == 0 else 3 - r)


def shard_tokens(x):
    outs = []
    for c in range(NCORES):
        b, r = c // 4, c % 4
        blocks = [zig(t, r) for t in range(NT)]
        xb = x[b].reshape(64, 128, -1)
        outs.append(np.ascontiguousarray(xb[blocks].reshape(T, -1)))
    return outs


def unshard_tokens(outs, dtype=np.float32):
    full = np.zeros((2, 64, 128, D), dtype)
    for c in range(NCORES):
        b, r = c // 4, c % 4
        blocks = [zig(t, r) for t in range(NT)]
        full[b, blocks] = outs[c].reshape(NT, 128, D)
    return full.reshape(2, 8192, D)


def col(v):
    v = np.asarray(v, np.float32)
    return np.ascontiguousarray(v.reshape(-1, 128).T)


def make_masks(r):
    k = np.arange(128)[:, None]
    q = np.arange(128)[None, :]
    m = np.zeros((128, 16, 128), np.float32)
    for kind in range(2):
        tri = (k < q) if kind == 0 else (k <= q)
        for p in range(2):
            rr = r if p == 0 else 3 - r
            for mj in range(4):
                if mj < rr:
                    v = np.zeros((128, 128), np.float32)
                elif mj == rr:
                    v = np.where(tri, 0.0, NEG).astype(np.float32)
                else:
                    v = np.full((128, 128), NEG, np.float32)
                m[:, kind * 8 + p * 4 + mj, :] = v
    return m.reshape(128, 16 * 128).astype(ml_dtypes.bfloat16)


_PROG_CACHE = {}


def prepare_inputs(inp):
    f32 = np.float32
    ident = np.eye(128, dtype=f32)
    ones = np.ones((128, 128), f32)
    negU = -np.tril(np.ones((128, 128), f32))
    sw = np.zeros((128, 128), f32)
    for i in range(16):
        sw[16 + i, i] = 1.0
        sw[i, 16 + i] = 1.0
    cb = np.concatenate([ident, ones, negU, sw], axis=1).astype(ml_dtypes.bfloat16)
    gc = np.zeros((128, NGC), f32)
    for l in range(2):
        gc[:, 16 * l:16 * l + 8] = col(inp["ffn_pre_norm"][l])
        gc[:, 16 * l + 8:16 * l + 16] = col(inp["ffn_post_norm"][l])
        gc[:, 32 + 8 * l:40 + 8 * l] = col(inp["mix_norm"][l])
        gc[:, 48 + 8 * l:56 + 8 * l] = col(inp["xmem_norm"][l])
        gc[:, 64 + 8 * l:72 + 8 * l] = col(inp["xmem_mem_norm"][l])
        gc[:, 80 + 2 * l:82 + 2 * l] = col(inp["xmem_q_gain"][l])
        gc[:, 84 + 2 * l:86 + 2 * l] = col(inp["xmem_k_gain"][l])
    gc[:, 88:92] = col(inp["mla_q_lora_gain"][0])
    gc[:, 92:94] = col(inp["mla_kv_lora_gain"][0])
    perm = np.arange(96)
    perm[64:80] = np.arange(80, 96)
    perm[80:96] = np.arange(64, 80)
    gc[:96, 94] = inp["mla_q_gain"][0]
    gc[:96, 95] = inp["mla_k_gain"][0]
    gc[:96, 96] = inp["mla_q_gain"][0][perm]
    gc[:96, 97] = inp["mla_k_gain"][0][perm]
    half = 16
    invf = (f32(10000.0) ** (-np.arange(half, dtype=f32) / f32(half))).astype(f32)
    gc[64:80, 98] = invf
    gc[80:96, 98] = invf
    gc[64:80, 99] = -1.0
    gc[80:96, 99] = 1.0
    gc[:, 100] = EPS
    gc[:, 101] = np.pi / 2
    gc[:, 102] = 1.0
    bc0 = np.zeros((128, 1536), f32)
    bc0[:, 0:512] = inp["sgu_ln_gain"][0][None, :]
    bc0[:, 512:1024] = inp["sgu_ln_bias"][0][None, :]
    sb = inp["sgu_b"][0]
    for cc in range(4):
        bc0[0:64, 1024 + cc * 128:1024 + (cc + 1) * 128] = sb[2 * cc][None, :]
        bc0[64:128, 1024 + cc * 128:1024 + (cc + 1) * 128] = sb[2 * cc + 1][None, :]
    trim = (np.arange(128)[:, None] <= np.arange(128)[None, :]).astype(f32)
    sguwT = np.ascontiguousarray(inp["sgu_w"][0].transpose(2, 0, 1)).reshape(128, 8 * 128)
    common = {"consts_bf": cb, "ident_f": ident, "gcols": gc, "bc0": bc0, "trimask": trim, "sgu_wT": sguwT}
    for l in range(2):
        for which in ("pre", "post"):
            common["w_%s_gu_%d" % (which, l)] = np.ascontiguousarray(inp["ffn_%s_w_gu" % which][l])
            common["w_%s_d_%d" % (which, l)] = np.ascontiguousarray(inp["ffn_%s_w_down" % which][l])
        common["w_xq_%d" % l] = np.ascontiguousarray(inp["xmem_wq"][l])
        wkv = inp["xmem_wkv"][l].reshape(D, 4, 2, 256)
        common["w_xk_%d" % l] = np.ascontiguousarray(wkv[:, :, 0, :].reshape(D, D))
        common["w_xv_%d" % l] = np.ascontiguousarray(wkv[:, :, 1, :].reshape(D, D))
        common["w_xo_%d" % l] = np.ascontiguousarray(inp["xmem_wo"][l])
    common["w_sbg_in"] = np.ascontiguousarray(inp["sbg_w_in"][0])
    common["w_sbg_out"] = np.ascontiguousarray(inp["sbg_w_out"][0])
    common["w_mla_in"] = np.ascontiguousarray(inp["mla_w_in"][0])
    uq = inp["mla_w_uq"][0]
    common["w_mla_uq"] = np.ascontiguousarray(uq)
    uq_h = uq.reshape(512, 16, 96)
    common["w_mla_uq_sw"] = np.ascontiguousarray(np.concatenate([uq_h[:, :, 80:96], uq_h[:, :, 64:80]], axis=2).reshape(512, 512))
    ukv = inp["mla_w_ukv"][0].reshape(256, 16, 128)
    common["w_mla_uk"] = np.ascontiguousarray(ukv[:, :, 0:64].reshape(256, 1024))
    common["w_mla_uv"] = np.ascontiguousarray(ukv[:, :, 64:128].reshape(256, 1024))
    common["w_mla_out"] = np.ascontiguousarray(inp["mla_w_out"][0])
    xs = shard_tokens(inp["x"])
    pos = inp["positions"].astype(np.int32)
    in_maps = []
    for c in range(NCORES):
        b, r = c // 4, c % 4
        blocks = [zig(t, r) for t in range(NT)]
        pl = pos[b].reshape(64, 128)[blocks].reshape(T)
        m = dict(common)
        m["x_in"] = xs[c]
        m["masks"] = make_masks(r)
        m["posrep"] = np.ascontiguousarray(np.broadcast_to(pl[None, :], (32, T))).astype(np.int32)
        m["mem"] = np.ascontiguousarray(inp["mem"][b])
        in_maps.append(m)
    return in_maps


EXCH = "cc"
DEBUG_X = {}


def _run(key, in_maps):
    if key not in _PROG_CACHE:
        _PROG_CACHE[key] = build_program(*key)
    nc, used, outs = _PROG_CACHE[key]
    maps = [{k: v for k, v in m.items() if k in used} for m in in_maps]
    res = run_bass_kernel_spmd(nc, maps, core_ids=list(range(NCORES)))
    return res.results


def _gather(results, name, rc):
    out = []
    for c in range(NCORES):
        b = c // 4
        rows = results[c][name].shape[0]
        parts = []
        for i in range(rows // rc):
            for rk in range(4):
                parts.append(results[4 * b + rk][name][i * rc:(i + 1) * rc])
        out.append(np.ascontiguousarray(np.concatenate(parts, axis=0)))
    return out


def kernel(**inputs):
    inp = {k: np.asarray(v) for k, v in inputs.items()}
    in_maps = prepare_inputs(inp)
    if EXCH == "cc":
        res = _run((0, 10, "cc"), in_maps)
        return unshard_tokens([r["x_out"] for r in res])
    r1 = _run((0, 2, "host"), in_maps)
    kg0, vg0 = _gather(r1, "k0_d", RC_K0), _gather(r1, "v0_d", RC_V0)
    for c in range(NCORES):
        in_maps[c].update({"x_in": r1[c]["x_out"], "q0_d": r1[c]["q0_d"], "osg_d": r1[c]["osg_d"], "kg0_d": kg0[c], "vg0_d": vg0[c]})
    r2 = _run((2, 7, "host"), in_maps)
    DEBUG_X["p2"] = [r["x_out"] for r in r2]
    kg1, vg1 = _gather(r2, "k1_d", RC_K1), _gather(r2, "v1_d", RC_V1)
    for c in range(NCORES):
        in_maps[c].update({"x_in": r2[c]["x_out"], "q1_d": r2[c]["q1_d"], "kg1_d": kg1[c], "vg1_d": vg1[c]})
    r3 = _run((7, 10, "host"), in_maps)
    return unshard_tokens([r["x_out"] for r in r3])
```

```python
import types
import numpy as np
import ml_dtypes
import concourse.bass as bass
import concourse.mybir as mybir
from concourse.bass_utils import run_bass_kernel_spmd

F32 = mybir.dt.float32
BF16 = mybir.dt.bfloat16
I32 = mybir.dt.int32
AF = mybir.ActivationFunctionType
ALU = mybir.AluOpType

NCORES = 8
D = 1024
DFF = 2816
T = 2048
NT = 16
EPS = 1e-6
NEG = -30000.0


class Buf:
    __slots__ = ("name", "lw", "rd", "dkey", "dcnt", "mw")

    def __init__(self, name):
        self.name = name
        self.lw = None
        self.rd = {}
        self.dkey = None
        self.dcnt = 0
        self.mw = {}


ENGS = ("pe", "act", "dve", "pool", "sp")


class Sched:
    def __init__(self, nc):
        self.nc = nc
        self.prog = {e: [] for e in ENGS}
        self.cnt = {e: 0 for e in ENGS}
        self.waited = {e: {} for e in ENGS}
        self.sems = {}
        self.alltoks = {}
        self.dfree = []
        self.dcum = {}
        self.dheld = []

    def sem(self, key):
        if key not in self.sems:
            self.sems[key] = self.nc.alloc_semaphore(name="s_%s" % (str(key).replace(" ", "")))
        return self.sems[key]

    def _wait(self, eng, tok):
        key, val = tok
        if key == eng and eng in ("pe", "sp"):
            return
        if self.waited[eng].get(key, 0) >= val:
            return
        self.waited[eng][key] = val
        h = self.sem(key)
        self.prog[eng].append(lambda E, h=h, val=val: E.wait_ge(h, val))

    def _deps(self, eng, reads, writes, pwrites=()):
        for b in reads:
            if b.lw is not None:
                self._wait(eng, b.lw)
            for k, v in b.mw.items():
                self._wait(eng, (k, v))
        for b in pwrites:
            if b.lw is not None and b.lw[0] != eng:
                self._wait(eng, b.lw)
            for k, v in b.rd.items():
                if k != eng:
                    self._wait(eng, (k, v))
        for b in writes:
            if b.lw is not None and b.lw[0] != eng:
                self._wait(eng, b.lw)
            for k, v in b.mw.items():
                self._wait(eng, (k, v))
            for k, v in b.rd.items():
                if k != eng:
                    self._wait(eng, (k, v))

    def _upd(self, tok, reads, writes):
        self.alltoks[tok[0]] = max(self.alltoks.get(tok[0], 0), tok[1])
        for b in reads:
            if b.rd.get(tok[0], 0) < tok[1]:
                b.rd[tok[0]] = tok[1]
        for b in writes:
            b.lw = tok
            b.rd = {}
            b.mw = {}

    @staticmethod
    def _freeze(fn):
        if fn.__closure__:
            cells = tuple(types.CellType(c.cell_contents) for c in fn.__closure__)
            g = types.FunctionType(fn.__code__, fn.__globals__, fn.__name__, fn.__defaults__, cells)
            g.__kwdefaults__ = fn.__kwdefaults__
            return g
        return fn

    def op(self, eng, fn, reads=(), writes=()):
        fn = self._freeze(fn)
        self._deps(eng, reads, writes)
        self.cnt[eng] += 1
        tok = (eng, self.cnt[eng])
        h = self.sem(eng)
        self.prog[eng].append(lambda E, fn=fn, h=h: fn(E).then_inc(h, 1))
        self._upd(tok, reads, writes)
        return tok

    def dma(self, q, out_ap, in_ap, owner, reads=(), writes=(), pwrites=()):
        self._deps(q, reads, writes, pwrites)
        if owner.dkey is None:
            if self.dfree:
                owner.dkey = self.dfree.pop()
            else:
                owner.dkey = "q%d" % len(self.dcum)
                self.dcum[owner.dkey] = 0
            self.dheld.append(owner)
        self.dcum[owner.dkey] += 16
        tok = (owner.dkey, self.dcum[owner.dkey])
        h = self.sem(owner.dkey)
        self.prog[q].append(lambda E, o=out_ap, i=in_ap, h=h: E.dma_start(out=o, in_=i).then_inc(h, 16))
        self._upd(tok, reads, writes)
        for b in pwrites:
            if b.mw.get(tok[0], 0) < tok[1]:
                b.mw[tok[0]] = tok[1]
        return tok

    def barrier(self, release=True):
        for e in ENGS:
            for k, v in list(self.alltoks.items()):
                self._wait(e, (k, v))
        if release:
            for b in self.dheld:
                self.dfree.append(b.dkey)
                b.dkey = None
            self.dheld = []

    def run(self):
        nc = self.nc
        with nc.Block() as block:
            @block.tensor
            def _(E):
                for f in self.prog["pe"]:
                    f(E)

            @block.scalar
            def _(E):
                for f in self.prog["act"]:
                    f(E)

            @block.vector
            def _(E):
                for f in self.prog["dve"]:
                    f(E)

            @block.gpsimd
            def _(E):
                for f in self.prog["pool"]:
                    f(E)

            @block.sync
            def _(E):
                for f in self.prog["sp"]:
                    f(E)


class Rot:
    def __init__(self, slots):
        self.slots = slots
        self.i = 0

    def next(self):
        s = self.slots[self.i % len(self.slots)]
        self.i += 1
        return s


class Ctx:
    pass


def mk_tile(nc, name, shape, dt):
    return nc.alloc_sbuf_tensor(name, shape, dt)


import os
DBG = os.environ.get('KDBG', '')
RC_K0, RC_V0, RC_K1, RC_V1 = 256, 1024, 192, 512
GC = dict(ffn_pre=0, ffn_post=8, mix=32, xn=48, xmn=64, xqg=80, xkg=84, qlora=88, kvlora=92,
          mqg=94, mkg=95, mqg_sw=96, mkg_sw=97, invf=98, sgn=99, eps=100, halfpi=101, one=102, zero=103)
NGC = 104
TWO_PI = float(2 * np.pi)
PI = float(np.pi)


PHASES = ["F0pre", "A0", "B0", "X0", "F0post", "F1pre", "A1", "B1", "X1", "F1post"]


def build_program(lo=0, hi=10, exch="cc"):
    nc = bass.Bass("TRN2", target_bir_lowering=False)
    S = Sched(nc)
    outputs = ["x_out"]

    used_inputs = set()

    def din(name, shape, dt=F32):
        used_inputs.add(name)
        return nc.dram_tensor(name, list(shape), dt, kind="ExternalInput").ap()

    x_in = din("x_in", [T, D])
    x_out = nc.dram_tensor("x_out", [T, D], F32, kind="ExternalOutput").ap()
    consts_bf = din("consts_bf", [128, 4 * 128], BF16)
    ident_f_d = din("ident_f", [128, 128])
    gcols_d = din("gcols", [128, NGC])
    bc0_d = din("bc0", [128, 1536])
    trim_d = din("trimask", [128, 128])
    sguwT_d = din("sgu_wT", [128, 8 * 128])
    masks_d = din("masks", [128, 16 * 128], BF16)
    posrep_d = din("posrep", [32, T], I32)
    mem_d = din("mem", [256, D])
    WSHAPES = {"sbg_in": [D, 2560], "sbg_out": [D, D], "mla_in": [D, 800], "mla_uq": [512, 1536], "mla_uq_sw": [512, 512],
               "mla_uk": [256, 1024], "mla_uv": [256, 1024], "mla_out": [D, D]}

    class LazyW(dict):
        def __missing__(self, key):
            if isinstance(key, tuple) and key[0] in ("gu", "d"):
                name = "w_%s_%s_%d" % (key[1], key[0], key[2])
                shape = [D, 2 * DFF] if key[0] == "gu" else [DFF, D]
            elif isinstance(key, tuple):
                name = "w_%s_%d" % key
                shape = [D, D]
            else:
                name = "w_" + key
                shape = WSHAPES[key]
            self[key] = din(name, shape)
            return self[key]

    Wd = LazyW()
    def scratch(name, shape, prod, cons):
        if exch == "cc" or (lo <= prod < hi and lo <= cons < hi):
            return nc.dram_tensor(name, shape, BF16).ap()
        if lo <= prod < hi:
            outputs.append(name)
            return nc.dram_tensor(name, shape, BF16, kind="ExternalOutput").ap()
        if lo <= cons < hi:
            return din(name, shape, BF16)
        return None

    gp = 1 if exch == "cc" else -1
    q0_d = scratch("q0_d", [512, T], 1, 2)
    k0_d = scratch("k0_d", [512, T], 1, 2 if exch == "cc" else 99)
    v0_d = scratch("v0_d", [T, 512], 1, 2 if exch == "cc" else 99)
    osg_d = scratch("osg_d", [512, T], 1, 2)
    kg0_d = scratch("kg0_d", [4 * 512, T], gp, 2)
    vg0_d = scratch("vg0_d", [4 * T, 512], gp, 2)
    gp = 6 if exch == "cc" else -1
    q1_d = scratch("q1_d", [1536, T], 6, 7)
    k1_d = scratch("k1_d", [1536, T], 6, 7 if exch == "cc" else 99)
    v1_d = scratch("v1_d", [T, 1024], 6, 7 if exch == "cc" else 99)
    kg1_d = scratch("kg1_d", [4 * 1536, T], gp, 7)
    vg1_d = scratch("vg1_d", [4 * T, 1024], gp, 7)
    db = {n: Buf(n) for n in ("q0", "k0", "v0", "osg", "kg0", "vg0", "q1", "k1", "v1", "kg1", "vg1")}

    xT = nc.alloc_sbuf_tensor("xT", [128, 8, T], F32)
    xTb = [[Buf("xT_%d_%d" % (c, g)) for g in range(4)] for c in range(8)]
    cbf = nc.alloc_sbuf_tensor("cbf", [128, 4 * 128], BF16)
    cbf_b = Buf("cbf")
    ident_f = nc.alloc_sbuf_tensor("sb_ident_f", [128, 128], F32)
    ident_f_b = Buf("ident_f")
    gcols = nc.alloc_sbuf_tensor("sb_gcols", [128, NGC], F32)
    gcols_b = Buf("gcols")
    ident_bf = cbf[:, 0:128]
    ones_bf = cbf[:, 128:256]
    negU_bf = cbf[:, 256:384]
    swap_bf = cbf[:, 384:512]

    def gcol(name, i=0, p0=0, p1=128):
        return gcols[p0:p1, GC[name] + i:GC[name] + i + 1]

    S.dma("sp", cbf[:, :], consts_bf[:, :], cbf_b, writes=[cbf_b])
    S.dma("sp", ident_f[:, :], ident_f_d[:, :], ident_f_b, writes=[ident_f_b])
    S.dma("sp", gcols[:, :], gcols_d[:, :], gcols_b, writes=[gcols_b])

    posi = nc.alloc_sbuf_tensor("posi", [128, 512], I32)
    posi_b = Buf("posi")
    nint = nc.alloc_sbuf_tensor("nint", [128, 512], I32)
    nint_b = Buf("nint")
    AF32 = 11776
    ABF = 45056
    arena_f = nc.alloc_sbuf_tensor("arena_f", [128, AF32], F32)
    arena_h = nc.alloc_sbuf_tensor("arena_h", [128, ABF], BF16)
    psum = [nc.alloc_psum_tensor("ps%d" % i, [128, 512], F32) for i in range(8)]

    class Carver:
        def __init__(self, ar, size):
            self.ar, self.size, self.off = ar, size, 0

        def reset(self):
            self.off = 0

        def take(self, n):
            assert self.off + n <= self.size, ("arena overflow", self.off, n, self.size)
            a = self.ar[:, self.off:self.off + n]
            self.off += n
            return a

    cf = Carver(arena_f, AF32)
    ch = Carver(arena_h, ABF)
    nbuf = [0]

    def B(name="b"):
        nbuf[0] += 1
        return Buf("%s%d" % (name, nbuf[0]))

    def rotf(n, size, view=None):
        return Rot([((cf.take(size) if view is None else view(cf.take(size))), B()) for _ in range(n)])

    def roth(n, size, view=None):
        return Rot([((ch.take(size) if view is None else view(ch.take(size))), B()) for _ in range(n)])

    def psrot(idxs):
        return Rot([(psum[i], B("ps")) for i in idxs])

    W = Ctx()

    def wsetup(nst=2, stsize=2048):
        W.st = rotf(nst, stsize)

    castctr = [0]

    def cast(dst, dst_b, src, src_b):
        castctr[0] += 1
        if castctr[0] % 2 == 0:
            S.op("act", lambda E: E.copy(dst, src), reads=[src_b], writes=[dst_b])
        else:
            S.op("dve", lambda E: E.tensor_copy(dst, src), reads=[src_b], writes=[dst_b])

    def wload(view, kc, n, dst, dst_b, eng=None):
        assert kc * n <= 2048
        st, st_b = W.st.next()
        sv = st[:, 0:kc * n].rearrange("p (c n) -> p c n", c=kc)
        S.dma("sp", sv, view, st_b, writes=[st_b])
        cast(dst, dst_b, sv, st_b)

    def rstd_from_ss(out_ap, out_b, ps_ap, ps_b, n, tmp_ap, tmp_b, p0=0, p1=128):
        S.op("act", lambda E: E.activation(tmp_ap, ps_ap, AF.Ln, bias=gcol("eps", 0, p0, p1), scale=1.0 / n),
             reads=[ps_b, gcols_b], writes=[tmp_b])
        S.op("act", lambda E: E.activation(out_ap, tmp_ap, AF.Exp, scale=-0.5), reads=[tmp_b], writes=[out_b])

    def load_x():
        cf.reset()
        rot = rotf(2, 1024)
        psb = [B("ps"), B("ps")]
        for t in range(NT):
            ap, b = rot.next()
            S.dma("sp", ap, x_in[t * 128:(t + 1) * 128, :], b, writes=[b])
            for half in range(2):
                pb, ps = psb[half], psum[half]
                for cc in range(4):
                    c = half * 4 + cc
                    S.op("pe", lambda E, o=ps[:, cc * 128:(cc + 1) * 128], i=ap[:, c * 128:(c + 1) * 128]:
                         E.transpose(o, i, ident_f[:, :]), reads=[b, ident_f_b], writes=[pb])
                g = t // 4
                outap = xT[:, half * 4:half * 4 + 4, t * 128:(t + 1) * 128]
                inap = ps[:, :].rearrange("p (c n) -> p c n", c=4)
                wr = [xTb[half * 4 + cc][g] for cc in range(4)]
                if half == 0:
                    S.op("act", lambda E, o=outap, i=inap: E.copy(o, i), reads=[pb], writes=wr)
                else:
                    S.op("dve", lambda E, o=outap, i=inap: E.tensor_copy(o, i), reads=[pb], writes=wr)
        S.barrier()

    def store_x():
        cf.reset()
        rot = rotf(2, 1024)
        psb = [B("ps"), B("ps")]
        toks = []
        for t in range(NT):
            ap, b = rot.next()
            g = t // 4
            for half in range(2):
                pb, ps = psb[half], psum[half]
                for cc in range(4):
                    c = half * 4 + cc
                    S.op("pe", lambda E, o=ps[:, cc * 128:(cc + 1) * 128], i=xT[:, c, t * 128:(t + 1) * 128]:
                         E.transpose(o, i, ident_f[:, :]), reads=[xTb[c][g], ident_f_b], writes=[pb])
                if half == 0:
                    S.op("act", lambda E, o=ap[:, 0:512], i=ps[:, :]: E.copy(o, i), reads=[pb], writes=[b])
                else:
                    S.op("dve", lambda E, o=ap[:, 512:1024], i=ps[:, :]: E.tensor_copy(o, i), reads=[pb], writes=[b])
            toks.append(S.dma("sp", x_out[t * 128:(t + 1) * 128, :], ap, b, reads=[b]))
        for tk in toks:
            S._wait("sp", tk)
        S.barrier()

    def norm_prep(gname, gi0, grp, xb_ap, xb_b, rstd_ap, rstd_b, sq_ap, sq_b, ps_i, ps_b, tmp_ap, tmp_b, tokps=None):
        tsl = slice(grp * 512, (grp + 1) * 512)
        xbufs = [xTb[c][grp] for c in range(8)]
        S.op("act", lambda E: E.activation(sq_ap, xT[:, :, tsl], AF.Square), reads=xbufs, writes=[sq_b])
        ps = psum[ps_i]
        for c in range(8):
            S.op("pe", lambda E, c=c: E.matmul(ps[:, :], ones_bf, sq_ap[:, c, :], start=(c == 0), stop=(c == 7)),
                 reads=[sq_b, cbf_b], writes=[ps_b])
        rstd_from_ss(rstd_ap, rstd_b, ps[:, :], ps_b, D, tmp_ap, tmp_b)
        if tokps is not None:
            tp_ap, tp_b = tokps
            for tt in range(4):
                for c in range(8):
                    S.op("pe", lambda E, c=c, tt=tt: E.matmul(tp_ap[:, grp * 4 + tt:grp * 4 + tt + 1],
                                                              sq_ap[:, c, tt * 128:(tt + 1) * 128], ones_bf[:, 0:1],
                                                              start=(c == 0), stop=(c == 7)),
                         reads=[sq_b, cbf_b], writes=[tp_b])
        for c in range(8):
            S.op("dve", lambda E, c=c: E.tensor_scalar(xb_ap[:, c, :], xT[:, c, tsl], gcol(gname, gi0 + c), None, ALU.mult),
                 reads=[xTb[c][grp], gcols_b], writes=[xb_b])

    def ffn(wgu_d, wd_d, gname, gi0):
        cf.reset()
        ch.reset()
        rstd = [cf.take(512) for _ in range(2)]
        rstd_b = [B() for _ in range(2)]
        tmp, tmp_b = cf.take(512), B()
        stg_gu = rotf(2, 2048, lambda a: a.rearrange("p (a c m) -> p a c m", a=2, c=8))
        stg_d = rotf(2, 1408, lambda a: a.rearrange("p (k m) -> p k m", k=11))
        a_rot, s_rot, t_rot = rotf(2, 512), rotf(2, 512), rotf(2, 512)
        xb = [ch.take(4096).rearrange("p (c n) -> p c n", c=8) for _ in range(2)]
        xb_b = [B() for _ in range(2)]
        hid = ch.take(22 * 1024).rearrange("p (k n) -> p k n", k=22)
        hid_b = [[B() for g in range(2)] for k in range(22)]
        wgu = roth(2, 2048, lambda a: a.rearrange("p (a c m) -> p a c m", a=2, c=8))
        wd = roth(2, 2816, lambda a: a.rearrange("p (k m) -> p k m", k=22))
        sq, sq_b = ch.take(4096).rearrange("p (c n) -> p c n", c=8), B()
        ps_ss_b = B("ps")
        ps_g, ps_u, ps_a = psrot([1, 2]), psrot([3, 4]), psrot([5, 6])
        for half in range(2):
            for gi in range(2):
                norm_prep(gname, gi0, half * 2 + gi, xb[gi], xb_b[gi], rstd[gi], rstd_b[gi], sq, sq_b, 0, ps_ss_b, tmp, tmp_b)
            wgu_v = wgu_d.rearrange("(c p) n -> p c n", p=128)
            for j in range(22):
                st, st_b = stg_gu.next()
                S.dma("sp", st[:, 0], wgu_v[:, :, j * 128:(j + 1) * 128], st_b, writes=[st_b])
                S.dma("sp", st[:, 1], wgu_v[:, :, DFF + j * 128:DFF + (j + 1) * 128], st_b, writes=[st_b])
                wt, wt_b = wgu.next()
                cast(wt, wt_b, st, st_b)
                for gi in range(2):
                    pg, pg_b = ps_g.next()
                    pu, pu_b = ps_u.next()
                    for c in range(8):
                        S.op("pe", lambda E, c=c, pg=pg, wt=wt, gi=gi: E.matmul(pg[:, :], wt[:, 0, c, :], xb[gi][:, c, :],
                                                                                 start=(c == 0), stop=(c == 7)),
                             reads=[wt_b, xb_b[gi]], writes=[pg_b])
                    for c in range(8):
                        S.op("pe", lambda E, c=c, pu=pu, wt=wt, gi=gi: E.matmul(pu[:, :], wt[:, 1, c, :], xb[gi][:, c, :],
                                                                                 start=(c == 0), stop=(c == 7)),
                             reads=[wt_b, xb_b[gi]], writes=[pu_b])
                    a, a_b = a_rot.next()
                    s, s_b = s_rot.next()
                    t, t_b = t_rot.next()
                    S.op("dve", lambda E, a=a, pg=pg, gi=gi: E.tensor_tensor(a, pg[:, :], rstd[gi], ALU.mult),
                         reads=[pg_b, rstd_b[gi]], writes=[a_b])
                    S.op("act", lambda E, s=s, a=a: E.activation(s, a, AF.Silu), reads=[a_b], writes=[s_b])
                    S.op("dve", lambda E, t=t, pu=pu, gi=gi: E.tensor_tensor(t, pu[:, :], rstd[gi], ALU.mult),
                         reads=[pu_b, rstd_b[gi]], writes=[t_b])
                    S.op("pool", lambda E, s=s, t=t, j=j, gi=gi: E.tensor_tensor(hid[:, j, gi * 512:(gi + 1) * 512], s, t, ALU.mult),
                         reads=[s_b, t_b], writes=[hid_b[j][gi]])
            wd_v = wd_d.rearrange("(k p) n -> p k n", p=128)
            for i in range(8):
                wt, wt_b = wd.next()
                for hh in range(2):
                    st, st_b = stg_d.next()
                    S.dma("sp", st, wd_v[:, hh * 11:(hh + 1) * 11, i * 128:(i + 1) * 128], st_b, writes=[st_b])
                    cast(wt[:, hh * 11:(hh + 1) * 11, :], wt_b, st, st_b)
                for gi in range(2):
                    pa, pa_b = ps_a.next()
                    for k in range(22):
                        S.op("pe", lambda E, k=k, pa=pa, wt=wt, gi=gi: E.matmul(pa[:, :], wt[:, k, :], hid[:, k, gi * 512:(gi + 1) * 512],
                                                                                 start=(k == 0), stop=(k == 21)),
                             reads=[wt_b, hid_b[k][gi]], writes=[pa_b])
                    grp = half * 2 + gi
                    xs = xT[:, i, grp * 512:(grp + 1) * 512]
                    S.op("dve", lambda E, xs=xs, pa=pa: E.scalar_tensor_tensor(xs, pa[:, :], 0.5, xs, ALU.mult, ALU.add),
                         reads=[pa_b, xTb[i][grp]], writes=[xTb[i][grp]])
        S.barrier()

    def out_proj(mix, mix_b, w_dram, wfm, ps_o):
        wv = w_dram.rearrange("(c p) n -> p c n", p=128)
        for i in range(8):
            wt, wt_b = wfm.next()
            wload(wv[:, :, i * 128:(i + 1) * 128], 8, 128, wt, wt_b)
            for g in range(4):
                po, po_b = ps_o.next()
                for c in range(8):
                    mb = mix_b[c][g] if isinstance(mix_b[c], list) else mix_b[c]
                    S.op("pe", lambda E, c=c, po=po, wt=wt, g=g: E.matmul(po[:, :], wt[:, c, :], mix[c][:, g * 512:(g + 1) * 512],
                                                                           start=(c == 0), stop=(c == 7)),
                         reads=[wt_b, mb], writes=[po_b])
                xs = xT[:, i, g * 512:(g + 1) * 512]
                S.op("dve", lambda E, xs=xs, po=po: E.tensor_tensor(xs, po[:, :], xs, ALU.add),
                     reads=[po_b, xTb[i][g]], writes=[xTb[i][g]])

    def gelu(x_ap, x_b, out_ap, out_b, ga, gb):
        (a, a_b), (b, b_b) = ga, gb
        S.op("act", lambda E: E.activation(a, x_ap, AF.Square), reads=[x_b], writes=[a_b])
        S.op("dve", lambda E: E.tensor_scalar(a, a, 0.044715, 1.0, ALU.mult, ALU.add), reads=[a_b], writes=[a_b])
        S.op("dve", lambda E: E.tensor_tensor(a, a, x_ap, ALU.mult), reads=[a_b, x_b], writes=[a_b])
        S.op("act", lambda E: E.activation(b, a, AF.Sigmoid, scale=1.5957691216057308), reads=[a_b], writes=[b_b])
        S.op("pool", lambda E: E.tensor_tensor(out_ap, x_ap, b, ALU.mult), reads=[x_b, b_b], writes=[out_b])

    def allgather(src_ap, src_b, dst_ap, dst_b, rows, rc):
        S._deps("pool", [src_b], [dst_b])
        h = S.sem("cc")
        S.cnt.setdefault("cc", 0)
        for i in range(rows // rc):
            S.cnt["cc"] += 1
            S.prog["pool"].append(lambda E, i=i: E.collective_compute("AllGather", ALU.bypass, replica_groups=[[0, 1, 2, 3], [4, 5, 6, 7]],
                                                                      ins=[src_ap[i * rc:(i + 1) * rc, :]],
                                                                      outs=[dst_ap[i * 4 * rc:(i + 1) * 4 * rc, :]]).then_inc(h, 1))
        S._upd(("cc", S.cnt["cc"]), [src_b], [dst_b])

    def grow(rc, rk, row):
        return (row // rc) * 4 * rc + rk * rc + (row % rc)

    def phase_A0():
        cf.reset()
        ch.reset()
        rstd = [cf.take(512) for _ in range(4)]
        rstd_b = [B() for _ in range(4)]
        tmp, tmp_b = cf.take(512), B()
        rtok, rtok_b = cf.take(16), B()
        rtmp, rtmp_b = cf.take(16), B()
        wsetup()
        zt = rotf(2, 512)
        ga, gb = (cf.take(512), B()), (cf.take(512), B())
        gl = (cf.take(512), B())
        bc, bc_b = cf.take(1536), B()
        trim, trim_b = cf.take(128), B()
        st6, st6_b = cf.take(8), B()
        mv, mv_b = cf.take(4), B()
        xb = ch.take(16384).rearrange("p (c n) -> p c n", c=8)
        xb_b = [B() for _ in range(4)]
        sq, sq_b = ch.take(4096).rearrange("p (c n) -> p c n", c=8), B()
        Wv, Wv_b = ch.take(4096).rearrange("p (c n) -> p c n", c=8), B()
        Wzg, Wzg_b = ch.take(4096).rearrange("p (c n) -> p c n", c=8), B()
        wfm = roth(2, 1024, lambda a: a.rearrange("p (c n) -> p c n", c=8))
        uT = ch.take(8192).rearrange("p (c n) -> p c n", c=4)
        uT_b = [[B() for g in range(4)] for c in range(4)]
        WsT, WsT_b = ch.take(1024).rearrange("p (g n) -> p g n", g=8), B()
        ev = roth(2, 512)
        vev = roth(2, 512)
        gtok = roth(2, 512)
        osg = roth(2, 512, lambda a: a.rearrange("p (c n) -> p c n", c=4))
        ps_ss_b = B("ps")
        tokps = (psum[7][:, 0:16], B("ps"))
        ps_p = psrot([1, 2, 3])
        ps_m = psrot([4, 5])
        S.dma("sp", bc, bc0_d[:, :], bc_b, writes=[bc_b])
        S.dma("sp", trim, trim_d[:, :], trim_b, writes=[trim_b])
        for hh in range(4):
            st, st_b = W.st.next()
            S.dma("sp", st[:, 0:256], sguwT_d[:, hh * 256:(hh + 1) * 256], st_b, writes=[st_b])
            for k in range(2):
                gidx = hh * 2 + k
                S.op("dve", lambda E, st=st, k=k, gidx=gidx: E.tensor_tensor(WsT[:, gidx, :], st[:, k * 128:(k + 1) * 128], trim, ALU.mult),
                     reads=[st_b, trim_b], writes=[WsT_b])
        for g in range(4):
            norm_prep("mix", 0, g, xb[:, :, g * 512:(g + 1) * 512], xb_b[g], rstd[g], rstd_b[g], sq, sq_b, 0, ps_ss_b, tmp, tmp_b, tokps=tokps)
        rstd_from_ss(rtok, rtok_b, tokps[0], tokps[1], D, rtmp, rtmp_b)
        win = Wd["sbg_in"].rearrange("(c p) n -> p c n", p=128)
        for hh in range(2):
            wload(win[:, hh * 4:(hh + 1) * 4, 1024:1536], 4, 512, Wv[:, hh * 4:(hh + 1) * 4, :], Wv_b)
            wload(win[:, hh * 4:(hh + 1) * 4, 2048:2560], 4, 512, Wzg[:, hh * 4:(hh + 1) * 4, :], Wzg_b)
        for oc in list(range(0, 8)) + list(range(12, 16)):
            wt, wt_b = wfm.next()
            wload(win[:, :, oc * 128:(oc + 1) * 128], 8, 128, wt, wt_b)
            for g in range(4):
                pp, pp_b = ps_p.next()
                for c in range(8):
                    S.op("pe", lambda E, c=c, pp=pp, wt=wt, g=g: E.matmul(pp[:, :], wt[:, c, :], xb[:, c, g * 512:(g + 1) * 512],
                                                                           start=(c == 0), stop=(c == 7)),
                         reads=[wt_b, xb_b[g]], writes=[pp_b])
                if oc < 8:
                    e, e_b = ev.next()
                    sc = 0.125 if oc < 4 else 1.0
                    S.op("dve", lambda E, e=e, pp=pp, g=g, sc=sc: E.scalar_tensor_tensor(e, pp[:, :], sc, rstd[g], ALU.mult, ALU.mult),
                         reads=[pp_b, rstd_b[g]], writes=[e_b])
                    dst, dn = (q0_d, "q0") if oc < 4 else (k0_d, "k0")
                    r0 = (oc % 4) * 128
                    S.dma("sp", dst[r0:r0 + 128, g * 512:(g + 1) * 512], e, e_b, reads=[e_b], pwrites=[db[dn]])
                else:
                    z, z_b = zt.next()
                    S.op("dve", lambda E, z=z, pp=pp, g=g: E.tensor_tensor(z, pp[:, :], rstd[g], ALU.mult),
                         reads=[pp_b, rstd_b[g]], writes=[z_b])
                    gelu(z, z_b, uT[:, oc - 12, g * 512:(g + 1) * 512], uT_b[oc - 12][g], ga, gb)
        for t in range(NT):
            g = t // 4
            pp, pp_b = ps_p.next()
            for c in range(8):
                S.op("pe", lambda E, c=c, pp=pp, t=t: E.matmul(pp[:, :], xb[:, c, t * 128:(t + 1) * 128], Wv[:, c, :],
                                                               start=(c == 0), stop=(c == 7)),
                     reads=[Wv_b, xb_b[g]], writes=[pp_b])
            e, e_b = vev.next()
            S.op("dve", lambda E, e=e, pp=pp, t=t: E.tensor_scalar(e, pp[:, :], rtok[:, t:t + 1], None, ALU.mult),
                 reads=[pp_b, rtok_b], writes=[e_b])
            S.dma("sp", v0_d[t * 128:(t + 1) * 128, :], e, e_b, reads=[e_b], pwrites=[db["v0"]])
            pp, pp_b = ps_p.next()
            for c in range(8):
                S.op("pe", lambda E, c=c, pp=pp, t=t: E.matmul(pp[:, :], xb[:, c, t * 128:(t + 1) * 128], Wzg[:, c, :],
                                                               start=(c == 0), stop=(c == 7)),
                     reads=[Wzg_b, xb_b[g]], writes=[pp_b])
            z, z_b = zt.next()
            S.op("dve", lambda E, z=z, pp=pp, t=t: E.tensor_scalar(z, pp[:, :], rtok[:, t:t + 1], None, ALU.mult),
                 reads=[pp_b, rtok_b], writes=[z_b])
            gelu(z, z_b, gl[0], gl[1], ga, gb)
            S.op("dve", lambda E: E.bn_stats(st6[:, 0:6], gl[0]), reads=[gl[1]], writes=[st6_b])
            S.op("dve", lambda E: E.bn_aggr(mv[:, 0:2], st6[:, 0:6]), reads=[st6_b], writes=[mv_b])
            S.op("act", lambda E: E.activation(mv[:, 2:3], mv[:, 1:2], AF.Ln, bias=gcol("eps")), reads=[mv_b, gcols_b], writes=[mv_b])
            S.op("act", lambda E: E.activation(mv[:, 3:4], mv[:, 2:3], AF.Exp, scale=-0.5), reads=[mv_b], writes=[mv_b])
            S.op("dve", lambda E: E.tensor_scalar(gl[0], gl[0], mv[:, 0:1], mv[:, 3:4], ALU.subtract, ALU.mult),
                 reads=[gl[1], mv_b], writes=[gl[1]])
            S.op("dve", lambda E: E.tensor_tensor(gl[0], gl[0], bc[:, 0:512], ALU.mult), reads=[gl[1], bc_b], writes=[gl[1]])
            gt, gt_b = gtok.next()
            S.op("dve", lambda E, gt=gt: E.tensor_tensor(gt, gl[0], bc[:, 512:1024], ALU.add), reads=[gl[1], bc_b], writes=[gt_b])
            pm, pm_b = ps_m.next()
            for grp in range(8):
                hb = 64 * (grp % 2)
                cc = grp // 2
                S.op("pe", lambda E, pm=pm, gt=gt, grp=grp, hb=hb, cc=cc: E.matmul(pm[hb:hb + 64, cc * 128:(cc + 1) * 128],
                                                                                  gt[:, grp * 64:(grp + 1) * 64], WsT[:, grp, :],
                                                                                  start=True, stop=True),
                     reads=[gt_b, WsT_b], writes=[pm_b])
            z2, z2_b = zt.next()
            S.op("dve", lambda E, z2=z2, pm=pm: E.tensor_tensor(z2, pm[:, :], bc[:, 1024:1536], ALU.add), reads=[pm_b, bc_b], writes=[z2_b])
            og, og_b = osg.next()
            S.op("pool", lambda E, og=og, z2=z2, t=t: E.tensor_tensor(og, z2.rearrange("p (c n) -> p c n", c=4),
                                                                     uT[:, :, t * 128:(t + 1) * 128], ALU.mult),
                 reads=[z2_b] + [uT_b[c][g] for c in range(4)], writes=[og_b])
            S.dma("sp", osg_d.rearrange("(c p) n -> p c n", p=128)[:, :, t * 128:(t + 1) * 128], og, og_b, reads=[og_b], pwrites=[db["osg"]])
        S.barrier()
        if exch == "cc":
            allgather(k0_d, db["k0"], kg0_d, db["kg0"], 512, RC_K0)
            allgather(v0_d, db["v0"], vg0_d, db["vg0"], T, RC_V0)
        S.barrier()

    def key_loc(j):
        tj, mj = j // 4, j % 4
        rk = mj if tj % 2 == 0 else 3 - mj
        return rk, tj, mj

    def phase_B0():
        cf.reset()
        ch.reset()
        Rs = [(cf.take(512), B()) for _ in range(2)]
        ttr = rotf(4, 512)
        e2r = rotf(4, 512)
        wsetup()
        Kr = roth(2, 8192)
        Vc, Vc_b = ch.take(8192).rearrange("p (r t d) -> p r t d", r=4, t=16), B()
        qr = roth(2, 2048)
        Lpr = roth(4, 512)
        wvr = roth(6, 512)
        osb = ch.take(8192).rearrange("p (c n) -> p c n", c=4)
        osb_b = [[B() for g in range(4)] for c in range(4)]
        msk, msk_b = ch.take(1024).rearrange("p (v n) -> p v n", v=8), B()
        wfm = roth(2, 1024, lambda a: a.rearrange("p (c n) -> p c n", c=8))
        ps_e, ps_t, ps_o = psrot([0, 1, 2, 3]), psrot([4, 5]), psrot([6, 7])
        S.dma("sp", msk, masks_d[:, 0:1024].rearrange("p (v n) -> p v n", v=8), msk_b, writes=[msk_b])
        for c in range(4):
            Kc, Kc_b = Kr.next()
            for rk in range(4):
                S.dma("sp", Kc[:, rk * T:(rk + 1) * T], kg0_d[grow(RC_K0, rk, c * 128):grow(RC_K0, rk, c * 128) + 128, :], Kc_b,
                      reads=[db["kg0"]], writes=[Kc_b])
                for tq in range(4):
                    S.dma("sp", Vc[:, rk, tq * 4:(tq + 1) * 4, :],
                          vg0_d[grow(RC_V0, rk, tq * 512):grow(RC_V0, rk, tq * 512) + 512, c * 128:(c + 1) * 128].rearrange("(t p) d -> p t d", p=128), Vc_b,
                          reads=[db["vg0"]], writes=[Vc_b])
            qc, qc_b = qr.next()
            S.dma("sp", qc, q0_d[c * 128:(c + 1) * 128, :], qc_b, reads=[db["q0"]], writes=[qc_b])
            for u in range(4):
                js = list(range(16 * u + 15, -1, -1))
                pos_ = [ps_o.next() for _ in range(2)]
                st2q, st3q = [[], []], [[], []]
                for hh in range(2):
                    S.op("pool", lambda E: E.memset(Rs[hh][0], 0.0), writes=[Rs[hh][1]])
                for idx, j in enumerate(js):
                    rk, tj, mj = key_loc(j)
                    kcol = rk * T + tj * 128
                    diag = j >= 16 * u
                    col0, vidx = 0, 0
                    if diag:
                        m = (j - 16 * u) // 4
                        col0 = 128 * m
                        vidx = ((4 * u + m) % 2) * 4 + mj
                    for hh in range(2):
                        hb = 64 * hh
                        R, R_b = Rs[hh]
                        po, po_b = pos_[hh]
                        qsl = qc[hb:hb + 64, u * 512 + col0:(u + 1) * 512]
                        ksl = Kc[hb:hb + 64, kcol:kcol + 128]
                        pE, pE_b = ps_e.next()
                        tt, tt_b = ttr.next()
                        Lp, Lp_b = Lpr.next()
                        S.op("pe", lambda E: E.matmul(pE[:, col0:], ksl, qsl, start=True, stop=False), reads=[Kc_b, qc_b], writes=[pE_b])
                        if diag:
                            S.op("pe", lambda E: E.matmul(pE[:, col0:col0 + 128], ident_bf, msk[:, vidx, :], start=False, stop=False, skip_group_check=True),
                                 reads=[cbf_b, msk_b], writes=[pE_b])
                        S.op("act", lambda E: E.activation(tt[:, col0:], pE[:, col0:], AF.Exp), reads=[pE_b], writes=[tt_b])
                        S.op("act", lambda E: E.activation(Lp[:, col0:], tt[:, col0:], AF.Ln, bias=gcol("one")), reads=[tt_b, gcols_b], writes=[Lp_b])

                        def stage2(pE=pE, pE_b=pE_b, Lp=Lp, Lp_b=Lp_b, col0=col0, rk=rk, tj=tj, idx=idx, j=j, hh=hh, hb=hb, R=R, R_b=R_b, po=po, po_b=po_b):
                            pT, pT_b = ps_t.next()
                            e2, e2_b = e2r.next()
                            wv, wv_b = wvr.next()
                            S.op("pe", lambda E: E.matmul(pE[:, col0:], negU_bf, Lp[:, col0:], start=False, stop=True, skip_group_check=True),
                                 reads=[cbf_b, Lp_b], writes=[pE_b])
                            S.op("pe", lambda E: E.matmul(pT[:, col0:], ones_bf, Lp[:, col0:], start=True, stop=True), reads=[cbf_b, Lp_b], writes=[pT_b])
                            S.op("dve", lambda E: E.tensor_tensor(e2[:, col0:], pE[:, col0:], R[:, col0:], ALU.subtract), reads=[pE_b, R_b], writes=[e2_b])
                            S.op("act", lambda E: E.activation(wv[:, col0:], e2[:, col0:], AF.Exp), reads=[e2_b], writes=[wv_b])
                            S.op("dve", lambda E: E.tensor_tensor(R[:, col0:], pT[:, col0:], R[:, col0:], ALU.add), reads=[pT_b, R_b], writes=[R_b])

                            def stage3():
                                S.op("pe", lambda E: E.matmul(po[hb:hb + 64, col0:], Vc[:, rk, tj, hb:hb + 64], wv[:, col0:], start=(idx == 0), stop=(j == 0),
                                                              skip_group_check=True),
                                     reads=[Vc_b, wv_b], writes=[po_b])
                            st3q[hh].append(stage3)
                        st2q[hh].append(stage2)
                        if len(st2q[hh]) > 1:
                            st2q[hh].pop(0)()
                        if len(st3q[hh]) > 1:
                            st3q[hh].pop(0)()
                for hh in range(2):
                    while st2q[hh]:
                        st2q[hh].pop(0)()
                for hh in range(2):
                    while st3q[hh]:
                        st3q[hh].pop(0)()
                for hh in range(2):
                    hb = 64 * hh
                    po, po_b = pos_[hh]
                    S.op("dve", lambda E: E.tensor_copy(osb[hb:hb + 64, c, u * 512:(u + 1) * 512], po[hb:hb + 64, :]),
                         reads=[po_b], writes=[osb_b[c][u]])
        Kc, Kc_b = Kr.next()
        osgs = Kc.rearrange("p (c n) -> p c n", c=4)
        S.dma("sp", osgs, osg_d.rearrange("(c p) n -> p c n", p=128), Kc_b, reads=[db["osg"]], writes=[Kc_b])
        mix = [osb[:, c, :] for c in range(4)] + [osgs[:, c, :] for c in range(4)]
        mix_b = [osb_b[c] for c in range(4)] + [Kc_b] * 4
        out_proj(mix, mix_b, Wd["sbg_out"], wfm, psrot([0, 1, 2]))
        S.barrier()

    def phase_X(l):
        cf.reset()
        ch.reset()
        rstd1, rstd1_b = cf.take(512), B()
        tmp, tmp_b = cf.take(512), B()
        wsetup()
        mst = rotf(1, 1024)
        msmall, msmall_b = cf.take(8), B()
        qraw = rotf(2, 1024, lambda a: a.rearrange("p (c n) -> p c n", c=2))
        rq = rotf(2, 512)
        kraw = rotf(1, 512, lambda a: a.rearrange("p (c n) -> p c n", c=2))
        rk_ = rotf(2, 256)
        rcp = rotf(2, 512)
        xb = ch.take(4096).rearrange("p (c n) -> p c n", c=8)
        xb1_b = B()
        sq, sq_b = ch.take(4096).rearrange("p (c n) -> p c n", c=8), B()
        hmT, hmT_b = ch.take(2048).rearrange("p (c n) -> p c n", c=8), B()
        mrow = roth(1, 1024)
        kT, kT_b = ch.take(2048).rearrange("p (h c n) -> p h c n", h=4, c=2), B()
        Vm, Vm_b = ch.take(2048).rearrange("p (m n) -> p m n", m=2), B()
        wfm = roth(2, 1024, lambda a: a.rearrange("p (c n) -> p c n", c=8))
        Wv_ = roth(1, 4096, lambda a: a.rearrange("p (c n) -> p c n", c=8))
        qn = roth(2, 1024, lambda a: a.rearrange("p (c n) -> p c n", c=2))
        sqq = roth(2, 1024, lambda a: a.rearrange("p (c n) -> p c n", c=2))
        Pm = roth(2, 1024, lambda a: a.rearrange("p (m n) -> p m n", m=2))
        oT = ch.take(4096).rearrange("p (c n) -> p c n", c=8)
        oT_b = [B() for c in range(8)]
        ps_ss_b = B("ps")
        ps_p = psrot([1, 2])
        ps_q = psrot([3])
        ps_s2 = psrot([4, 5])
        ps_o2 = psrot([6, 7])
        for mt in range(2):
            ms, ms_b = mst.next()
            S.dma("sp", ms, mem_d[mt * 128:(mt + 1) * 128, :], ms_b, writes=[ms_b])
            mr, mr_b = mrow.next()
            S.op("act", lambda E, ms=ms, mr=mr: E.activation(mr, ms, AF.Square, accum_out=msmall[:, 0:1]), reads=[ms_b], writes=[mr_b, msmall_b])
            S.op("act", lambda E: E.activation(msmall[:, 1:2], msmall[:, 0:1], AF.Ln, bias=gcol("eps"), scale=1.0 / D),
                 reads=[msmall_b, gcols_b], writes=[msmall_b])
            S.op("act", lambda E: E.activation(msmall[:, 2:3], msmall[:, 1:2], AF.Exp, scale=-0.5), reads=[msmall_b], writes=[msmall_b])
            S.op("dve", lambda E, ms=ms: E.tensor_scalar(ms, ms, msmall[:, 2:3], None, ALU.mult), reads=[ms_b, msmall_b], writes=[ms_b])
            for half in range(2):
                pp, pp_b = ps_p.next()
                for cc in range(4):
                    c = half * 4 + cc
                    S.op("pe", lambda E, pp=pp, ms=ms, c=c, cc=cc: E.transpose(pp[:, cc * 128:(cc + 1) * 128], ms[:, c * 128:(c + 1) * 128], ident_f[:, :]),
                         reads=[ms_b, ident_f_b], writes=[pp_b])
                for cc in range(4):
                    c = half * 4 + cc
                    S.op("dve", lambda E, pp=pp, c=c, cc=cc, mt=mt: E.tensor_scalar(hmT[:, c, mt * 128:(mt + 1) * 128], pp[:, cc * 128:(cc + 1) * 128],
                                                                                     gcol("xmn", 8 * l + c), None, ALU.mult),
                         reads=[pp_b, gcols_b], writes=[hmT_b])
        wk = Wd[("xk", l)].rearrange("(c p) n -> p c n", p=128)
        for hd in range(4):
            kr_, kr_b = kraw.next()
            sk, sk_b = sqq.next()
            for dc in range(2):
                oc = hd * 2 + dc
                wt, wt_b = wfm.next()
                wload(wk[:, :, oc * 128:(oc + 1) * 128], 8, 128, wt, wt_b)
                pp, pp_b = ps_p.next()
                for c in range(8):
                    S.op("pe", lambda E, c=c, pp=pp, wt=wt: E.matmul(pp[:, 0:256], wt[:, c, :], hmT[:, c, :], start=(c == 0), stop=(c == 7)),
                         reads=[wt_b, hmT_b], writes=[pp_b])
                S.op("act", lambda E, kr_=kr_, pp=pp, dc=dc: E.copy(kr_[:, dc, :], pp[:, 0:256]), reads=[pp_b], writes=[kr_b])
                S.op("act", lambda E, sk=sk, pp=pp, dc=dc: E.activation(sk[:, dc, 0:256], pp[:, 0:256], AF.Square), reads=[pp_b], writes=[sk_b])
            pq, pq_b = ps_q.next()
            for dc in range(2):
                S.op("pe", lambda E, pq=pq, sk=sk, dc=dc: E.matmul(pq[:, 0:256], ones_bf, sk[:, dc, 0:256], start=(dc == 0), stop=(dc == 1)),
                     reads=[sk_b, cbf_b], writes=[pq_b])
            rr, rr_b = rk_.next()
            rstd_from_ss(rr, rr_b, pq[:, 0:256], pq_b, 256, tmp[:, 0:256], tmp_b)
            for dc in range(2):
                S.op("dve", lambda E, kr_=kr_, rr=rr, dc=dc, hd=hd: E.scalar_tensor_tensor(kT[:, hd, dc, :], kr_[:, dc, :], gcol("xkg", 2 * l + dc), rr,
                                                                                           ALU.mult, ALU.mult),
                     reads=[kr_b, rr_b, gcols_b], writes=[kT_b])
        wvv = Wd[("xv", l)].rearrange("(c p) n -> p c n", p=128)
        for nh in range(2):
            wt, wt_b = Wv_.next()
            for hh in range(2):
                wload(wvv[:, hh * 4:(hh + 1) * 4, nh * 512:(nh + 1) * 512], 4, 512, wt[:, hh * 4:(hh + 1) * 4, :], wt_b)
            for mt in range(2):
                pp, pp_b = ps_p.next()
                for c in range(8):
                    S.op("pe", lambda E, c=c, pp=pp, wt=wt, mt=mt: E.matmul(pp[:, :], hmT[:, c, mt * 128:(mt + 1) * 128], wt[:, c, :],
                                                                             start=(c == 0), stop=(c == 7)),
                         reads=[wt_b, hmT_b], writes=[pp_b])
                S.op("act", lambda E, pp=pp, mt=mt, nh=nh: E.copy(Vm[:, mt, nh * 512:(nh + 1) * 512], pp[:, :]), reads=[pp_b], writes=[Vm_b])
        wq = Wd[("xq", l)].rearrange("(c p) n -> p c n", p=128)
        wo = Wd[("xo", l)].rearrange("(c p) n -> p c n", p=128)
        ps_x = psrot([1, 2])
        for g in range(4):
            norm_prep("xn", 8 * l, g, xb, xb1_b, rstd1, rstd1_b, sq, sq_b, 0, ps_ss_b, tmp, tmp_b)
            for hd in range(4):
                qr_, qr_b = qraw.next()
                sk, sk_b = sqq.next()
                for dc in range(2):
                    oc = hd * 2 + dc
                    wt, wt_b = wfm.next()
                    wload(wq[:, :, oc * 128:(oc + 1) * 128], 8, 128, wt, wt_b)
                    pp, pp_b = ps_p.next()
                    for c in range(8):
                        S.op("pe", lambda E, c=c, pp=pp, wt=wt: E.matmul(pp[:, :], wt[:, c, :], xb[:, c, :], start=(c == 0), stop=(c == 7)),
                             reads=[wt_b, xb1_b], writes=[pp_b])
                    S.op("dve", lambda E, qr_=qr_, pp=pp, dc=dc: E.tensor_tensor(qr_[:, dc, :], pp[:, :], rstd1, ALU.mult),
                         reads=[pp_b, rstd1_b], writes=[qr_b])
                    S.op("act", lambda E, sk=sk, qr_=qr_, dc=dc: E.activation(sk[:, dc, :], qr_[:, dc, :], AF.Square), reads=[qr_b], writes=[sk_b])
                pq, pq_b = ps_q.next()
                for dc in range(2):
                    S.op("pe", lambda E, pq=pq, sk=sk, dc=dc: E.matmul(pq[:, :], ones_bf, sk[:, dc, :], start=(dc == 0), stop=(dc == 1)),
                         reads=[sk_b, cbf_b], writes=[pq_b])
                rr, rr_b = rq.next()
                rstd_from_ss(rr, rr_b, pq[:, :], pq_b, 256, tmp, tmp_b)
                qq, qq_b = qn.next()
                for dc in range(2):
                    S.op("dve", lambda E, qq=qq, qr_=qr_, rr=rr, dc=dc: E.scalar_tensor_tensor(qq[:, dc, :], qr_[:, dc, :], gcol("xqg", 2 * l + dc), rr,
                                                                                              ALU.mult, ALU.mult),
                         reads=[qr_b, rr_b, gcols_b], writes=[qq_b])
                pm_, pm_b = Pm.next()
                for mt in range(2):
                    pS, pS_b = ps_s2.next()
                    for dc in range(2):
                        S.op("pe", lambda E, pS=pS, qq=qq, dc=dc, mt=mt, hd=hd: E.matmul(pS[:, :], kT[:, hd, dc, mt * 128:(mt + 1) * 128], qq[:, dc, :],
                                                                                         start=(dc == 0), stop=(dc == 1)),
                             reads=[kT_b, qq_b], writes=[pS_b])
                    S.op("act", lambda E, pm_=pm_, pS=pS, mt=mt: E.activation(pm_[:, mt, :], pS[:, :], AF.Exp, scale=1.0 / 16.0),
                         reads=[pS_b], writes=[pm_b])
                pq, pq_b = ps_q.next()
                for mt in range(2):
                    S.op("pe", lambda E, pq=pq, pm_=pm_, mt=mt: E.matmul(pq[:, :], ones_bf, pm_[:, mt, :], start=(mt == 0), stop=(mt == 1)),
                         reads=[pm_b, cbf_b], writes=[pq_b])
                rc, rc_b = rcp.next()
                S.op("dve", lambda E, rc=rc, pq=pq: E.reciprocal(rc, pq[:, :]), reads=[pq_b], writes=[rc_b])
                for dc in range(2):
                    po, po_b = ps_o2.next()
                    for mt in range(2):
                        S.op("pe", lambda E, po=po, pm_=pm_, mt=mt, dc=dc, hd=hd: E.matmul(po[:, :], Vm[:, mt, hd * 256 + dc * 128:hd * 256 + (dc + 1) * 128],
                                                                                           pm_[:, mt, :], start=(mt == 0), stop=(mt == 1)),
                             reads=[Vm_b, pm_b], writes=[po_b])
                    S.op("dve", lambda E, po=po, rc=rc, hd=hd, dc=dc: E.tensor_tensor(oT[:, hd * 2 + dc, :], po[:, :], rc, ALU.mult),
                         reads=[po_b, rc_b], writes=[oT_b[hd * 2 + dc]])
            for i in range(8):
                wt, wt_b = wfm.next()
                wload(wo[:, :, i * 128:(i + 1) * 128], 8, 128, wt, wt_b)
                po, po_b = ps_x.next()
                for c in range(8):
                    S.op("pe", lambda E, c=c, po=po, wt=wt: E.matmul(po[:, :], wt[:, c, :], oT[:, c, :], start=(c == 0), stop=(c == 7)),
                         reads=[wt_b, oT_b[c]], writes=[po_b])
                xs = xT[:, i, g * 512:(g + 1) * 512]
                S.op("dve", lambda E, xs=xs, po=po: E.tensor_tensor(xs, po[:, :], xs, ALU.add), reads=[po_b, xTb[i][g]], writes=[xTb[i][g]])
        S.barrier()

    def sin_reduced(out_ap, out_b, ang_ap, ang_b, wk, wk_b, nf, nf_b, p0, p1, add=0.0):
        ni = nint[p0:p1, :]
        S.op("dve", lambda E: E.tensor_scalar(wk, ang_ap, float(add), None, ALU.add), reads=[ang_b], writes=[wk_b])
        S.op("dve", lambda E: E.tensor_scalar(ni, wk, float(1 / TWO_PI), None, ALU.mult), reads=[wk_b], writes=[nint_b])
        S.op("dve", lambda E: E.tensor_copy(nf, ni), reads=[nint_b], writes=[nf_b])
        S.op("dve", lambda E: E.scalar_tensor_tensor(wk, nf, -TWO_PI, wk, ALU.mult, ALU.add), reads=[nf_b, wk_b], writes=[wk_b])
        S.op("dve", lambda E: E.tensor_scalar(nf, wk, PI, TWO_PI, ALU.is_gt, ALU.mult), reads=[wk_b], writes=[nf_b])
        S.op("dve", lambda E: E.tensor_tensor(wk, wk, nf, ALU.subtract), reads=[nf_b, wk_b], writes=[wk_b])
        S.op("dve", lambda E: E.tensor_scalar(nf, wk, -PI, TWO_PI, ALU.is_lt, ALU.mult), reads=[wk_b], writes=[nf_b])
        S.op("dve", lambda E: E.tensor_tensor(wk, wk, nf, ALU.add), reads=[nf_b, wk_b], writes=[wk_b])
        S.op("act", lambda E: E.activation(out_ap, wk, AF.Sin), reads=[wk_b], writes=[out_b])

    class _Stop(Exception):
        pass

    def _stage(n):
        for k in range(1, 9):
            if ('a1s%d' % k) in DBG and n >= k:
                raise _Stop()

    def phase_A1():
        try:
            phase_A1_()
        except _Stop:
            S.barrier()

    def phase_A1_():
        cf.reset()
        ch.reset()
        rstd1, rstd1_b = cf.take(512), B()
        tmpr = rotf(2, 512)
        tmp, tmp_b = tmpr.slots[0]
        wsetup(2, 1024)
        tabs = [cf.take(512) for _ in range(4)]
        tab_b = [B() for _ in range(4)]
        cqraw, cqraw_b = cf.take(2048).rearrange("p (c n) -> p c n", c=4), B()
        far = rotf(2, 512)
        fbr = rotf(2, 512)
        fa, fa_b = far.slots[0]
        fb, fb_b = fbr.slots[0]
        fc, fc_b = cf.take(512), B()
        kra, kra_b = cf.take(512), B()
        rqk = rotf(2, 512)
        xb = ch.take(4096).rearrange("p (c n) -> p c n", c=8)
        xb_b = B()
        sq, sq_b = ch.take(4096).rearrange("p (c n) -> p c n", c=8), B()
        Win, Win_b = ch.take(6400).rearrange("p (c n) -> p c n", c=8), B()
        Wuq, Wuq_b = ch.take(6144).rearrange("p (c n) -> p c n", c=4), B()
        Wqs, Wqs_b = ch.take(2048).rearrange("p (c n) -> p c n", c=4), B()
        Wuk, Wuk_b = ch.take(2048).rearrange("p (c n) -> p c n", c=2), B()
        Wuv, Wuv_b = ch.take(2048).rearrange("p (c n) -> p c n", c=2), B()
        cqn, cqn_b = ch.take(2048).rearrange("p (c n) -> p c n", c=4), B()
        ckvn, ckvn_b = ch.take(1024).rearrange("p (c n) -> p c n", c=2), B()
        krb, krb_b = ch.take(512), B()
        sqh = ch.take(512)
        sqh_lo, sqh_hi = B(), B()
        sqqr = roth(2, 512)
        qst = roth(2, 512)
        vst = roth(1, 512)
        ps_ss_b = B("ps")
        ps_p = psrot([1, 2])
        ps_a = psrot([3, 4])
        ps_b2 = psrot([5])
        ps_r = psrot([6, 7])
        ps_q = ps_r

        def wl(src, kc, ncol, dst, dst_b):
            v = src.rearrange("(c p) n -> p c n", p=128)
            step = max(128, (1024 // kc) // 128 * 128) if ncol >= 128 else ncol
            for n0 in range(0, ncol, step):
                n1 = min(ncol, n0 + step)
                wload(v[:, :, n0:n1], kc, n1 - n0, dst[:, :, n0:n1], dst_b)

        wl(Wd["mla_in"], 8, 768, Win, Win_b)
        wload(Wd["mla_in"].rearrange("(c p) n -> p c n", p=128)[:, :, 768:800], 8, 32, Win[:, :, 768:800], Win_b)
        wl(Wd["mla_uq"], 4, 1536, Wuq, Wuq_b)
        wl(Wd["mla_uq_sw"], 4, 512, Wqs, Wqs_b)
        wl(Wd["mla_uk"], 2, 1024, Wuk, Wuk_b)
        wl(Wd["mla_uv"], 2, 1024, Wuv, Wuv_b)

        P0, P1 = 64, 96
        for pass_ in ("kv", "q"):
            for g in range(1 if 'a1small' in DBG else 4):
                gs = slice(g * 512, (g + 1) * 512)
                norm_prep("mix", 8, g, xb, xb_b, rstd1, rstd1_b, sq, sq_b, 0, ps_ss_b, tmp, tmp_b)
                S.dma("sp", posi[P0:P1, :], posrep_d[:, gs], posi_b, writes=[posi_b])
                S.op("dve", lambda E: E.tensor_copy(fa[P0:P1, :], posi[P0:P1, :]), reads=[posi_b], writes=[fa_b])
                S.op("dve", lambda E: E.tensor_scalar(fa[P0:P1, :], fa[P0:P1, :], gcol("invf", 0, P0, P1), None, ALU.mult),
                     reads=[fa_b, gcols_b], writes=[fa_b])
                sin_reduced(tabs[0][P0:P1, :], tab_b[0], fa[P0:P1, :], fa_b, fb[P0:P1, :], fb_b, fc[P0:P1, :], fc_b, P0, P1, add=PI / 2)
                sin_reduced(tabs[1][P0:P1, :], tab_b[1], fa[P0:P1, :], fa_b, fb[P0:P1, :], fb_b, fc[P0:P1, :], fc_b, P0, P1, add=0.0)
                S.op("dve", lambda E: E.tensor_scalar(tabs[2][P0:P1, :], tabs[0][P0:P1, :], gcol("mkg", 0, P0, P1), None, ALU.mult),
                     reads=[tab_b[0], gcols_b], writes=[tab_b[2]])
                S.op("dve", lambda E: E.tensor_scalar(tabs[3][P0:P1, :], tabs[1][P0:P1, :], gcol("sgn", 0, P0, P1), gcol("mkg_sw", 0, P0, P1), ALU.mult, ALU.mult),
                     reads=[tab_b[1], gcols_b], writes=[tab_b[3]])
                S.op("dve", lambda E: E.tensor_scalar(tabs[0][P0:P1, :], tabs[0][P0:P1, :], gcol("mqg", 0, P0, P1), None, ALU.mult),
                     reads=[tab_b[0], gcols_b], writes=[tab_b[0]])
                S.op("dve", lambda E: E.tensor_scalar(tabs[1][P0:P1, :], tabs[1][P0:P1, :], gcol("sgn", 0, P0, P1), gcol("mqg_sw", 0, P0, P1), ALU.mult, ALU.mult),
                     reads=[tab_b[1], gcols_b], writes=[tab_b[1]])

                for (n_ch, col0, dst, dst_b, gname, nfeat) in (((4, 0, cqn, cqn_b, "qlora", 512),) if pass_ == "q" else ((2, 512, ckvn, ckvn_b, "kvlora", 256),)):
                    for oc in range(n_ch):
                        pp, pp_b = ps_p.next()
                        for c in range(8):
                            S.op("pe", lambda E, c=c, pp=pp, oc=oc, col0=col0: E.matmul(pp[:, :], Win[:, c, col0 + oc * 128:col0 + (oc + 1) * 128], xb[:, c, :],
                                                                                        start=(c == 0), stop=(c == 7)),
                                 reads=[Win_b, xb_b], writes=[pp_b])
                        S.op("dve", lambda E, pp=pp, oc=oc: E.tensor_tensor(cqraw[:, oc, :], pp[:, :], rstd1, ALU.mult),
                             reads=[pp_b, rstd1_b], writes=[cqraw_b])
                        S.op("act", lambda E, oc=oc: E.activation(sq[:, oc, :], cqraw[:, oc, :], AF.Square), reads=[cqraw_b], writes=[sq_b])
                    pq, pq_b = ps_q.next()
                    for oc in range(n_ch):
                        S.op("pe", lambda E, pq=pq, oc=oc, n_ch=n_ch: E.matmul(pq[:, :], ones_bf, sq[:, oc, :], start=(oc == 0), stop=(oc == n_ch - 1)),
                             reads=[sq_b, cbf_b], writes=[pq_b])
                    rr, rr_b = rqk.next()
                    rstd_from_ss(rr, rr_b, pq[:, :], pq_b, nfeat, tmp, tmp_b)
                    for oc in range(n_ch):
                        S.op("dve", lambda E, oc=oc, dst=dst, rr=rr, gname=gname: E.scalar_tensor_tensor(dst[:, oc, :], cqraw[:, oc, :], gcol(gname, oc), rr, ALU.mult, ALU.mult),
                             reads=[cqraw_b, rr_b, gcols_b], writes=[dst_b])

                if pass_ == "kv":
                    pp, pp_b = ps_p.next()
                    for c in range(8):
                        S.op("pe", lambda E, c=c, pp=pp: E.matmul(pp[0:32, :], Win[:, c, 768:800], xb[:, c, :], start=(c == 0), stop=(c == 7)),
                             reads=[Win_b, xb_b], writes=[pp_b])
                    S.op("dve", lambda E, pp=pp: E.tensor_tensor(krb[0:32, :], pp[0:32, :], rstd1[0:32, :], ALU.mult), reads=[pp_b, rstd1_b], writes=[krb_b])
                    if os.environ.get("KSUB") == "1":
                        raise _Stop()
                    pa, pa_b = ps_a.next()
                    pb_, pb_b = ps_b2.next()
                    S.op("pe", lambda E, pa=pa: E.matmul(pa[P0:P1, :], ident_bf[0:32, 0:32], krb[0:32, :], start=True, stop=True), reads=[krb_b, cbf_b], writes=[pa_b])
                    S.op("pe", lambda E, pb_=pb_: E.matmul(pb_[P0:P1, :], swap_bf[0:32, 0:32], krb[0:32, :], start=True, stop=True), reads=[krb_b, cbf_b], writes=[pb_b])
                    if os.environ.get("KSUB") == "2":
                        raise _Stop()
                    S.op("act", lambda E, pa=pa: E.activation(sqh[P0:P1, :], pa[P0:P1, :], AF.Square), reads=[pa_b], writes=[sqh_hi])
                    if os.environ.get("KSUB") == "3":
                        raise _Stop()
                    S.op("act", lambda E, pa=pa: E.copy(kra[P0:P1, :], pa[P0:P1, :]), reads=[pa_b], writes=[kra_b])
                    S.op("act", lambda E, pb_=pb_: E.copy(fb[P0:P1, :], pb_[P0:P1, :]), reads=[pb_b], writes=[fb_b])
                    S.op("dve", lambda E: E.tensor_tensor(kra[P0:P1, :], kra[P0:P1, :], tabs[2][P0:P1, :], ALU.mult), reads=[kra_b, tab_b[2]], writes=[kra_b])
                    S.op("dve", lambda E: E.tensor_tensor(fb[P0:P1, :], fb[P0:P1, :], tabs[3][P0:P1, :], ALU.mult), reads=[fb_b, tab_b[3]], writes=[fb_b])
                    S.op("dve", lambda E: E.tensor_tensor(kra[P0:P1, :], kra[P0:P1, :], fb[P0:P1, :], ALU.add), reads=[kra_b, fb_b], writes=[kra_b])

                for h in range(2 if 'a1small' in DBG else 16):
                    if pass_ == "q":
                        pa, pa_b = ps_a.next()
                        pb_, pb_b = ps_b2.next()
                        for c in range(4):
                            S.op("pe", lambda E, c=c, pa=pa, h=h: E.matmul(pa[0:96, :], Wuq[:, c, h * 96:(h + 1) * 96], cqn[:, c, :], start=(c == 0), stop=(c == 3)),
                                 reads=[Wuq_b, cqn_b], writes=[pa_b])
                        for c in range(4):
                            S.op("pe", lambda E, c=c, pb_=pb_, h=h: E.matmul(pb_[P0:P1, :], Wqs[:, c, h * 32:(h + 1) * 32], cqn[:, c, :], start=(c == 0), stop=(c == 3)),
                                 reads=[Wqs_b, cqn_b], writes=[pb_b])
                        sqq, sqq_b = sqqr.next()
                        tq_, tq_b = tmpr.next()
                        fa, fa_b = far.next()
                        fb, fb_b = fbr.next()
                        S.op("act", lambda E, pa=pa: E.activation(sqq[0:96, :], pa[0:96, :], AF.Square), reads=[pa_b], writes=[sqq_b])
                        pr, pr_b = ps_r.next()
                        S.op("pe", lambda E, pr=pr: E.matmul(pr[0:96, :], ones_bf[0:96, 0:96], sqq[0:96, :], start=True, stop=True), reads=[sqq_b, cbf_b], writes=[pr_b])
                        rr, rr_b = rqk.next()
                        rstd_from_ss(rr[0:96, :], rr_b, pr[0:96, :], pr_b, 96, tq_[0:96, :], tq_b, 0, 96)
                        qs, qs_b = qst.next()
                        S.op("dve", lambda E, qs=qs, pa=pa, rr=rr: E.scalar_tensor_tensor(qs[0:64, :], pa[0:64, :], gcol("mqg", 0, 0, 64), rr[0:64, :], ALU.mult, ALU.mult),
                             reads=[pa_b, rr_b, gcols_b], writes=[qs_b])
                        S.op("act", lambda E, pa=pa: E.copy(fa[P0:P1, :], pa[P0:P1, :]), reads=[pa_b], writes=[fa_b])
                        S.op("act", lambda E, pb_=pb_: E.copy(fb[P0:P1, :], pb_[P0:P1, :]), reads=[pb_b], writes=[fb_b])
                        S.op("dve", lambda E: E.tensor_tensor(fa[P0:P1, :], fa[P0:P1, :], tabs[0][P0:P1, :], ALU.mult), reads=[fa_b, tab_b[0]], writes=[fa_b])
                        S.op("dve", lambda E: E.tensor_tensor(fb[P0:P1, :], fb[P0:P1, :], tabs[1][P0:P1, :], ALU.mult), reads=[fb_b, tab_b[1]], writes=[fb_b])
                        S.op("dve", lambda E: E.tensor_tensor(fa[P0:P1, :], fa[P0:P1, :], fb[P0:P1, :], ALU.add), reads=[fa_b, fb_b], writes=[fa_b])
                        S.op("dve", lambda E, qs=qs, rr=rr: E.tensor_tensor(qs[P0:P1, :], fa[P0:P1, :], rr[P0:P1, :], ALU.mult), reads=[fa_b, rr_b], writes=[qs_b])
                        S.dma("sp", q1_d[h * 96:(h + 1) * 96, gs], qs[0:96, :], qs_b, reads=[qs_b], pwrites=[db["q1"]])

                    if pass_ == "kv":
                        pa, pa_b = ps_a.next()
                        for c in range(2):
                            S.op("pe", lambda E, c=c, pa=pa, h=h: E.matmul(pa[0:64, :], Wuk[:, c, h * 64:(h + 1) * 64], ckvn[:, c, :], start=(c == 0), stop=(c == 1)),
                                 reads=[Wuk_b, ckvn_b], writes=[pa_b])
                        S.op("act", lambda E, pa=pa: E.activation(sqh[0:64, :], pa[0:64, :], AF.Square), reads=[pa_b], writes=[sqh_lo])
                        pr, pr_b = ps_r.next()
                        S.op("pe", lambda E, pr=pr: E.matmul(pr[0:96, :], ones_bf[0:96, 0:96], sqh[0:96, :], start=True, stop=True), reads=[sqh_lo, sqh_hi, cbf_b], writes=[pr_b])
                        rr, rr_b = rqk.next()
                        tq_, tq_b = tmpr.next()
                        rstd_from_ss(rr[0:96, :], rr_b, pr[0:96, :], pr_b, 96, tq_[0:96, :], tq_b, 0, 96)
                        ks, ks_b = qst.next()
                        S.op("dve", lambda E, ks=ks, pa=pa, rr=rr: E.scalar_tensor_tensor(ks[0:64, :], pa[0:64, :], gcol("mkg", 0, 0, 64), rr[0:64, :], ALU.mult, ALU.mult),
                             reads=[pa_b, rr_b, gcols_b], writes=[ks_b])
                        S.op("dve", lambda E, ks=ks, rr=rr: E.tensor_tensor(ks[P0:P1, :], kra[P0:P1, :], rr[P0:P1, :], ALU.mult), reads=[kra_b, rr_b], writes=[ks_b])
                        S.dma("sp", k1_d[h * 96:(h + 1) * 96, gs], ks[0:96, :], ks_b, reads=[ks_b], pwrites=[db["k1"]])

                if pass_ == "kv":
                    for tt in range(4):
                        t = g * 4 + tt
                        for half in range(2):
                            pp, pp_b = ps_p.next()
                            for c in range(2):
                                S.op("pe", lambda E, c=c, pp=pp, tt=tt, half=half: E.matmul(pp[:, :], ckvn[:, c, tt * 128:(tt + 1) * 128], Wuv[:, c, half * 512:(half + 1) * 512],
                                                                                             start=(c == 0), stop=(c == 1)),
                                     reads=[Wuv_b, ckvn_b], writes=[pp_b])
                            vs, vs_b = vst.next()
                            S.op("act", lambda E, vs=vs, pp=pp: E.copy(vs, pp[:, :]), reads=[pp_b], writes=[vs_b])
                            S.dma("sp", v1_d[t * 128:(t + 1) * 128, half * 512:(half + 1) * 512], vs, vs_b, reads=[vs_b], pwrites=[db["v1"]])
            if pass_ == "kv" and exch == "cc":
                allgather(k1_d, db["k1"], kg1_d, db["kg1"], 1536, RC_K1)
                allgather(v1_d, db["v1"], vg1_d, db["vg1"], T, RC_V1)
        S.barrier()

    o1_d = nc.dram_tensor("o1_d", [D, T], BF16).ap()
    o1_b = Buf("o1")

    def phase_B1():
        cf.reset()
        ch.reset()
        wsetup()
        tmpA, tmpA_b = cf.take(512), B()
        rcf, rcf_b = cf.take(512), B()
        rcs, rcs_b = cf.take(512), B()
        KK = ch.take(16384)
        Kr = Rot([(KK[:, 0:8192], B()), (KK[:, 8192:16384], B())])
        Vr = roth(2, 8192, lambda a: a.rearrange("p (r t d) -> p r t d", r=4, t=16))
        qr = roth(2, 2048)
        Pr = roth(5, 512)
        msk, msk_b = ch.take(1024).rearrange("p (v n) -> p v n", v=8), B()
        rhi, rhi_b = ch.take(512), B()
        rlo, rlo_b = ch.take(512), B()
        ost = roth(2, 512)
        wfm = roth(2, 1024, lambda a: a.rearrange("p (c n) -> p c n", c=8))
        ps_s = psrot([0, 1, 2, 6, 7])
        ps_o = psrot([3, 4])
        ps_r = psrot([5])
        S.dma("sp", msk, masks_d[:, 1024:2048].rearrange("p (v n) -> p v n", v=8), msk_b, writes=[msk_b])
        for (Vt, Vt_b) in Vr.slots:
            for rk in range(4):
                S.op("pool", lambda E: E.memset(Vt[:, rk, :, 64:128], 1.0), writes=[Vt_b])
        SC = float(96 ** -0.5)
        dsl = slice(64, 128)
        osl = slice(0, 64)
        def issue_loads(h):
            Kh, Kh_b = Kr.next()
            Vh, Vh_b = Vr.next()
            qh, qh_b = qr.next()
            S.dma("sp", qh[0:96, :], q1_d[h * 96:(h + 1) * 96, :], qh_b, reads=[db["q1"]], writes=[qh_b])
            for rk in range(4):
                S.dma("sp", Kh[0:96, rk * T:(rk + 1) * T], kg1_d[grow(RC_K1, rk, h * 96):grow(RC_K1, rk, h * 96) + 96, :], Kh_b,
                      reads=[db["kg1"]], writes=[Kh_b])
                for tq in range(4):
                    r0 = grow(RC_V1, rk, tq * 512)
                    S.dma("sp", Vh[:, rk, tq * 4:(tq + 1) * 4, 0:64],
                          vg1_d[r0:r0 + 512, h * 64:(h + 1) * 64].rearrange("(t p) d -> p t d", p=128), Vh_b, reads=[db["vg1"]], writes=[Vh_b])
            return Kh, Kh_b, Vh, Vh_b, qh, qh_b

        epi_q = []
        nxt = issue_loads(0)
        for h in range(16):
            Kh, Kh_b, Vh, Vh_b, qh, qh_b = nxt
            if h + 1 < 16:
                nxt = issue_loads(h + 1)
            for u in range(4):
                po, po_b = ps_o.next()
                nblk = 16 * u + 16
                pend = []
                for j in range(nblk):
                    rk, tj, mj = key_loc(j)
                    kcol = rk * T + tj * 128
                    diag = j >= 16 * u
                    col0, vidx = 0, 0
                    if diag:
                        m = (j - 16 * u) // 4
                        col0 = 128 * m
                        vidx = ((4 * u + m) % 2) * 4 + mj
                    pS, pS_b = ps_s.next()
                    S.op("pe", lambda E: E.matmul(pS[:, col0:], Kh[0:96, kcol:kcol + 128], qh[0:96, u * 512 + col0:(u + 1) * 512], start=True, stop=not diag),
                         reads=[Kh_b, qh_b], writes=[pS_b])
                    if diag:
                        S.op("pe", lambda E: E.matmul(pS[:, col0:col0 + 128], ident_bf, msk[:, vidx, :], start=False, stop=True, skip_group_check=True),
                             reads=[cbf_b, msk_b], writes=[pS_b])
                    P_, P_b = Pr.next()
                    S.op("act", lambda E: E.activation(P_[:, col0:], pS[:, col0:], AF.Exp, scale=SC), reads=[pS_b], writes=[P_b])

                    def pv(po=po, po_b=po_b, P_=P_, P_b=P_b, col0=col0, rk=rk, tj=tj, j=j, nblk=nblk, Vh=Vh, Vh_b=Vh_b):
                        S.op("pe", lambda E: E.matmul(po[:, col0:], Vh[:, rk, tj, :], P_[:, col0:], start=(j == 0), stop=(j == nblk - 1),
                                                      skip_group_check=True),
                             reads=[Vh_b, P_b], writes=[po_b])
                    pend.append(pv)
                    if len(pend) > 2:
                        pend.pop(0)()
                    if j == 3 and epi_q:
                        epi_q.pop(0)()
                while pend:
                    pend.pop(0)()
                def epilogue(po=po, po_b=po_b, h=h, u=u):
                    S.op("act", lambda E: E.activation(tmpA[dsl, :], po[dsl, :], AF.Ln), reads=[po_b], writes=[tmpA_b])
                    S.op("act", lambda E: E.activation(rcf[dsl, :], tmpA[dsl, :], AF.Exp, scale=-1.0), reads=[tmpA_b], writes=[rcf_b])
                    S.op("dve", lambda E: E.tensor_copy(rhi[dsl, :], rcf[dsl, :]), reads=[rcf_b], writes=[rhi_b])
                    S.op("dve", lambda E: E.tensor_tensor(rlo[dsl, :], rcf[dsl, :], rhi[dsl, :], ALU.subtract), reads=[rcf_b, rhi_b], writes=[rlo_b])
                    pr, pr_b = ps_r.next()
                    S.op("pe", lambda E: E.matmul(pr[osl, :], ident_bf[dsl, 64:128], rhi[dsl, :], start=True, stop=False), reads=[cbf_b, rhi_b], writes=[pr_b])
                    S.op("pe", lambda E: E.matmul(pr[osl, :], ident_bf[dsl, 64:128], rlo[dsl, :], start=False, stop=True), reads=[cbf_b, rlo_b], writes=[pr_b])
                    S.op("act", lambda E: E.copy(rcs[osl, :], pr[osl, :]), reads=[pr_b], writes=[rcs_b])
                    os_, os_b = ost.next()
                    S.op("dve", lambda E: E.tensor_tensor(os_[osl, :], po[osl, :], rcs[osl, :], ALU.mult), reads=[po_b, rcs_b], writes=[os_b])
                    S.dma("pool", o1_d[h * 64:(h + 1) * 64, u * 512:(u + 1) * 512], os_[osl, :], os_b, reads=[os_b], pwrites=[o1_b])
                if epi_q:
                    epi_q.pop(0)()
                epi_q.append(epilogue)
        while epi_q:
            epi_q.pop(0)()
        S.barrier()
        oT = KK.rearrange("p (c n) -> p c n", c=8)
        oT_b = B()
        S.dma("sp", oT, o1_d.rearrange("(c p) n -> p c n", p=128), oT_b, reads=[o1_b], writes=[oT_b])
        out_proj([oT[:, c, :] for c in range(8)], [oT_b] * 8, Wd["mla_out"], wfm, psrot([0, 1, 2]))
        S.barrier()

    load_x()
    seq = [lambda: ffn(Wd[("gu", "pre", 0)], Wd[("d", "pre", 0)], "ffn_pre", 0),
           phase_A0, phase_B0, lambda: phase_X(0),
           lambda: ffn(Wd[("gu", "post", 0)], Wd[("d", "post", 0)], "ffn_post", 0),
           lambda: ffn(Wd[("gu", "pre", 1)], Wd[("d", "pre", 1)], "ffn_pre", 16),
           phase_A1, phase_B1, lambda: phase_X(1),
           lambda: ffn(Wd[("gu", "post", 1)], Wd[("d", "post", 1)], "ffn_post", 16)]
    for i in range(lo, hi):
        seq[i]()
    for n in db:
        if db[n].lw is not None:
            S._wait("sp", db[n].lw)
        for k_, v_ in db[n].mw.items():
            S._wait("sp", (k_, v_))
    store_x()
    S.run()
    print('[build] phases', lo, hi, 'counts', dict(S.cnt), 'nsems', len(S.sems), 'maxdma', max([0] + [v for k, v in S.alltoks.items() if str(k).startswith('d')]), flush=True)
    return nc, used_inputs, outputs


def zig(t, r):
    return 4 * t + (r if t % 2 == 0 else 3 - r)


def shard_tokens(x):
    outs = []
    for c in range(NCORES):
        b, r = c // 4, c % 4
        blocks = [zig(t, r) for t in range(NT)]
        xb = x[b].reshape(64, 128, -1)
        outs.append(np.ascontiguousarray(xb[blocks].reshape(T, -1)))
    return outs


def unshard_tokens(outs, dtype=np.float32):
    full = np.zeros((2, 64, 128, D), dtype)
    for c in range(NCORES):
        b, r = c // 4, c % 4
        blocks = [zig(t, r) for t in range(NT)]
        full[b, blocks] = outs[c].reshape(NT, 128, D)
    return full.reshape(2, 8192, D)


def col(v):
    v = np.asarray(v, np.float32)
    return np.ascontiguousarray(v.reshape(-1, 128).T)


def make_masks(r):
    k = np.arange(128)[:, None]
    q = np.arange(128)[None, :]
    m = np.zeros((128, 16, 128), np.float32)
    for kind in range(2):
        tri = (k < q) if kind == 0 else (k <= q)
        for p in range(2):
            rr = r if p == 0 else 3 - r
            for mj in range(4):
                if mj < rr:
                    v = np.zeros((128, 128), np.float32)
                elif mj == rr:
                    v = np.where(tri, 0.0, NEG).astype(np.float32)
                else:
                    v = np.full((128, 128), NEG, np.float32)
                m[:, kind * 8 + p * 4 + mj, :] = v
    return m.reshape(128, 16 * 128).astype(ml_dtypes.bfloat16)


_PROG_CACHE = {}


def prepare_inputs(inp):
    f32 = np.float32
    ident = np.eye(128, dtype=f32)
    ones = np.ones((128, 128), f32)
    negU = -np.tril(np.ones((128, 128), f32))
    sw = np.zeros((128, 128), f32)
    for i in range(16):
        sw[16 + i, i] = 1.0
        sw[i, 16 + i] = 1.0
    cb = np.concatenate([ident, ones, negU, sw], axis=1).astype(ml_dtypes.bfloat16)
    gc = np.zeros((128, NGC), f32)
    for l in range(2):
        gc[:, 16 * l:16 * l + 8] = col(inp["ffn_pre_norm"][l])
        gc[:, 16 * l + 8:16 * l + 16] = col(inp["ffn_post_norm"][l])
        gc[:, 32 + 8 * l:40 + 8 * l] = col(inp["mix_norm"][l])
        gc[:, 48 + 8 * l:56 + 8 * l] = col(inp["xmem_norm"][l])
        gc[:, 64 + 8 * l:72 + 8 * l] = col(inp["xmem_mem_norm"][l])
        gc[:, 80 + 2 * l:82 + 2 * l] = col(inp["xmem_q_gain"][l])
        gc[:, 84 + 2 * l:86 + 2 * l] = col(inp["xmem_k_gain"][l])
    gc[:, 88:92] = col(inp["mla_q_lora_gain"][0])
    gc[:, 92:94] = col(inp["mla_kv_lora_gain"][0])
    perm = np.arange(96)
    perm[64:80] = np.arange(80, 96)
    perm[80:96] = np.arange(64, 80)
    gc[:96, 94] = inp["mla_q_gain"][0]
    gc[:96, 95] = inp["mla_k_gain"][0]
    gc[:96, 96] = inp["mla_q_gain"][0][perm]
    gc[:96, 97] = inp["mla_k_gain"][0][perm]
    half = 16
    invf = (f32(10000.0) ** (-np.arange(half, dtype=f32) / f32(half))).astype(f32)
    gc[64:80, 98] = invf
    gc[80:96, 98] = invf
    gc[64:80, 99] = -1.0
    gc[80:96, 99] = 1.0
    gc[:, 100] = EPS
    gc[:, 101] = np.pi / 2
    gc[:, 102] = 1.0
    bc0 = np.zeros((128, 1536), f32)
    bc0[:, 0:512] = inp["sgu_ln_gain"][0][None, :]
    bc0[:, 512:1024] = inp["sgu_ln_bias"][0][None, :]
    sb = inp["sgu_b"][0]
    for cc in range(4):
        bc0[0:64, 1024 + cc * 128:1024 + (cc + 1) * 128] = sb[2 * cc][None, :]
        bc0[64:128, 1024 + cc * 128:1024 + (cc + 1) * 128] = sb[2 * cc + 1][None, :]
    trim = (np.arange(128)[:, None] <= np.arange(128)[None, :]).astype(f32)
    sguwT = np.ascontiguousarray(inp["sgu_w"][0].transpose(2, 0, 1)).reshape(128, 8 * 128)
    common = {"consts_bf": cb, "ident_f": ident, "gcols": gc, "bc0": bc0, "trimask": trim, "sgu_wT": sguwT}
    for l in range(2):
        for which in ("pre", "post"):
            common["w_%s_gu_%d" % (which, l)] = np.ascontiguousarray(inp["ffn_%s_w_gu" % which][l])
            common["w_%s_d_%d" % (which, l)] = np.ascontiguousarray(inp["ffn_%s_w_down" % which][l])
        common["w_xq_%d" % l] = np.ascontiguousarray(inp["xmem_wq"][l])
        wkv = inp["xmem_wkv"][l].reshape(D, 4, 2, 256)
        common["w_xk_%d" % l] = np.ascontiguousarray(wkv[:, :, 0, :].reshape(D, D))
        common["w_xv_%d" % l] = np.ascontiguousarray(wkv[:, :, 1, :].reshape(D, D))
        common["w_xo_%d" % l] = np.ascontiguousarray(inp["xmem_wo"][l])
    common["w_sbg_in"] = np.ascontiguousarray(inp["sbg_w_in"][0])
    common["w_sbg_out"] = np.ascontiguousarray(inp["sbg_w_out"][0])
    common["w_mla_in"] = np.ascontiguousarray(inp["mla_w_in"][0])
    uq = inp["mla_w_uq"][0]
    common["w_mla_uq"] = np.ascontiguousarray(uq)
    uq_h = uq.reshape(512, 16, 96)
    common["w_mla_uq_sw"] = np.ascontiguousarray(np.concatenate([uq_h[:, :, 80:96], uq_h[:, :, 64:80]], axis=2).reshape(512, 512))
    ukv = inp["mla_w_ukv"][0].reshape(256, 16, 128)
    common["w_mla_uk"] = np.ascontiguousarray(ukv[:, :, 0:64].reshape(256, 1024))
    common["w_mla_uv"] = np.ascontiguousarray(ukv[:, :, 64:128].reshape(256, 1024))
    common["w_mla_out"] = np.ascontiguousarray(inp["mla_w_out"][0])
    xs = shard_tokens(inp["x"])
    pos = inp["positions"].astype(np.int32)
    in_maps = []
    for c in range(NCORES):
        b, r = c // 4, c % 4
        blocks = [zig(t, r) for t in range(NT)]
        pl = pos[b].reshape(64, 128)[blocks].reshape(T)
        m = dict(common)
        m["x_in"] = xs[c]
        m["masks"] = make_masks(r)
        m["posrep"] = np.ascontiguousarray(np.broadcast_to(pl[None, :], (32, T))).astype(np.int32)
        m["mem"] = np.ascontiguousarray(inp["mem"][b])
        in_maps.append(m)
    return in_maps


EXCH = "cc"
DEBUG_X = {}


def _run(key, in_maps):
    if key not in _PROG_CACHE:
        _PROG_CACHE[key] = build_program(*key)
    nc, used, outs = _PROG_CACHE[key]
    maps = [{k: v for k, v in m.items() if k in used} for m in in_maps]
    res = run_bass_kernel_spmd(nc, maps, core_ids=list(range(NCORES)))
    return res.results


def _gather(results, name, rc):
    out = []
    for c in range(NCORES):
        b = c // 4
        rows = results[c][name].shape[0]
        parts = []
        for i in range(rows // rc):
            for rk in range(4):
                parts.append(results[4 * b + rk][name][i * rc:(i + 1) * rc])
        out.append(np.ascontiguousarray(np.concatenate(parts, axis=0)))
    return out


def kernel(**inputs):
    inp = {k: np.asarray(v) for k, v in inputs.items()}
    in_maps = prepare_inputs(inp)
    if EXCH == "cc":
        res = _run((0, 10, "cc"), in_maps)
        return unshard_tokens([r["x_out"] for r in res])
    r1 = _run((0, 2, "host"), in_maps)
    kg0, vg0 = _gather(r1, "k0_d", RC_K0), _gather(r1, "v0_d", RC_V0)
    for c in range(NCORES):
        in_maps[c].update({"x_in": r1[c]["x_out"], "q0_d": r1[c]["q0_d"], "osg_d": r1[c]["osg_d"], "kg0_d": kg0[c], "vg0_d": vg0[c]})
    r2 = _run((2, 7, "host"), in_maps)
    DEBUG_X["p2"] = [r["x_out"] for r in r2]
    kg1, vg1 = _gather(r2, "k1_d", RC_K1), _gather(r2, "v1_d", RC_V1)
    for c in range(NCORES):
        in_maps[c].update({"x_in": r2[c]["x_out"], "q1_d": r2[c]["q1_d"], "kg1_d": kg1[c], "vg1_d": vg1[c]})
    r3 = _run((7, 10, "host"), in_maps)
    return unshard_tokens([r["x_out"] for r in r3])
```
